# Optimizing a Trainium2 kernel written in Bass

```python
import jax, jax.numpy as jnp
from jax import lax
import numpy as np

D_MODEL = 1024
BATCH = 2
SEQ = 8192
DEPTH = 2

D_MIX = D_MODEL
C_CONV = D_MIX // 4
NSA_HEADS = 4
HEAD_DIM = 64
C_NSA = NSA_HEADS * HEAD_DIM
C_SGU = D_MIX // 4
C_POOL = D_MIX // 4
CONV_K = 31
CMP_LEN = 32
CMP_STRIDE = 16
SEL_LEN = 64
SEL_TOPK = 16
WINDOW = 512
Q_BLOCK = 128
SGU_CHUNK = 128
SGU_GROUPS = 4
POOL_WINDOWS = (2, 4, 8, 16)
FFN_DIM = 2816
FFN_CONV_K = 3
ROPE_THETA = 10000.0
RMS_EPS = 1e-6
LN_EPS = 1e-5
NEG_INF = -1e30
FORCE_SCORE = 1e6
N_IN = 2 * C_CONV + C_NSA + 6 * HEAD_DIM + 3 * NSA_HEADS + 2 * C_SGU + C_POOL

kernel_name = 'hybrid_parallel_nsa_conv_sgu_pool'


def rms_norm(x, g):
    xf = x.astype(jnp.float32)
    y = xf * lax.rsqrt(jnp.mean(xf * xf, axis=-1, keepdims=True) + RMS_EPS)
    return y.astype(x.dtype) * g


def layer_norm(x, g, b):
    xf = x.astype(jnp.float32)
    mu = jnp.mean(xf, axis=-1, keepdims=True)
    var = jnp.mean(jnp.square(xf - mu), axis=-1, keepdims=True)
    return ((xf - mu) * lax.rsqrt(var + LN_EPS)).astype(x.dtype) * g + b


def causal_dwconv(x, w, b):
    k, c = w.shape
    y = lax.conv_general_dilated(x, w[:, None, :].astype(x.dtype), window_strides=(1,),
                                 padding=[(k - 1, 0)], dimension_numbers=('NWC', 'WIO', 'NWC'),
                                 feature_group_count=c)
    return y + b


def rope_tables(seq):
    half = HEAD_DIM // 2
    inv = ROPE_THETA ** (-jnp.arange(half, dtype=jnp.float32) * 2.0 / HEAD_DIM)
    ang = jnp.arange(seq, dtype=jnp.float32)[:, None] * inv[None, :]
    return jnp.cos(ang), jnp.sin(ang)


def apply_rope(x, cos, sin):
    half = HEAD_DIM // 2
    shp = (cos.shape[0],) + (1,) * (x.ndim - 3) + (half,)
    c = cos.reshape(shp).astype(x.dtype)
    s = sin.reshape(shp).astype(x.dtype)
    x1, x2 = x[..., :half], x[..., half:]
    return jnp.concatenate([x1 * c - x2 * s, x2 * c + x1 * s], axis=-1)


def masked_softmax(s, mask):
    p = jax.nn.softmax(jnp.where(mask, s, NEG_INF), axis=-1)
    return jnp.where(mask, p, 0.0)


def conformer_conv(z, w, b, ln_g, ln_b):
    a, gate = jnp.split(z, 2, axis=-1)
    y = causal_dwconv(a * jax.nn.sigmoid(gate), w, b)
    return jax.nn.silu(layer_norm(y, ln_g, ln_b))


def nsa_attention(q, k_cmp, v_cmp, k_slc, v_slc, k_win, v_win, gates,
                  pe_k, pe_v, ck_w1, ck_w2, cv_w1, cv_w2, cos, sin):
    bsz, seq = q.shape[0], q.shape[1]
    scale = HEAD_DIM ** -0.5
    n_cmp = (seq - CMP_LEN) // CMP_STRIDE + 1
    n_sel = seq // SEL_LEN
    top_n = min(SEL_TOPK, n_sel)
    ratio = SEL_LEN // CMP_STRIDE
    n_ov = CMP_LEN // CMP_STRIDE
    front = n_ov - 1
    right = ratio * n_sel + ratio - front - n_cmp

    blk_idx = jnp.arange(n_cmp)[:, None] * CMP_STRIDE + jnp.arange(CMP_LEN)[None, :]

    def compress(t, pe, w1, w2):
        blocks = (t[:, blk_idx] + pe).reshape(bsz, n_cmp, CMP_LEN * HEAD_DIM)
        return jax.nn.gelu(blocks @ w1) @ w2

    kc = compress(k_cmp, pe_k, ck_w1, ck_w2)
    vc = compress(v_cmp, pe_v, cv_w1, cv_w2)
    cmp_end = jnp.arange(n_cmp) * CMP_STRIDE + CMP_LEN - 1

    q_cmp = q * scale
    q_rot = apply_rope(q, cos, sin) * scale
    ks_blocks = apply_rope(k_slc, cos, sin).reshape(bsz, n_sel, SEL_LEN, HEAD_DIM)
    vs_blocks = v_slc.reshape(bsz, n_sel, SEL_LEN, HEAD_DIM)
    kw_pad = jnp.pad(apply_rope(k_win, cos, sin), ((0, 0), (WINDOW, 0), (0, 0)))
    vw_pad = jnp.pad(v_win, ((0, 0), (WINDOW, 0), (0, 0)))
    sel_j = jnp.arange(n_sel)
    gather = jax.vmap(lambda blocks, idx: blocks[idx])

    def query_block(i):
        t0 = i * Q_BLOCK
        pos = t0 + jnp.arange(Q_BLOCK)
        qc = lax.dynamic_slice_in_dim(q_cmp, t0, Q_BLOCK, axis=1)
        qr = lax.dynamic_slice_in_dim(q_rot, t0, Q_BLOCK, axis=1)
        g = lax.dynamic_slice_in_dim(gates, t0, Q_BLOCK, axis=1)

        s = jnp.einsum('bthd,bnd->bhtn', qc, kc, preferred_element_type=jnp.float32)
        p_cmp = masked_softmax(s, cmp_end[None, :] <= pos[:, None])
        o_cmp = jnp.einsum('bhtn,bnd->bthd', p_cmp.astype(vc.dtype), vc)

        pp = jnp.pad(p_cmp.sum(axis=1), ((0, 0), (0, 0), (front, right)))
        imp = jnp.zeros((bsz, Q_BLOCK, n_sel), jnp.float32)
        for m in range(ratio):
            for n in range(n_ov):
                imp = imp + pp[..., m - n + front::ratio][..., :n_sel]
        cur = (pos // SEL_LEN)[:, None]
        forced = (sel_j[None, :] == 0) | (sel_j[None, :] == cur) | (sel_j[None, :] == cur - 1)
        valid = sel_j[None, :] * SEL_LEN <= pos[:, None]
        imp = jnp.where(valid, jnp.where(forced, FORCE_SCORE, imp), NEG_INF)
        top_val, top_idx = lax.top_k(imp, top_n)

        ksel = gather(ks_blocks, top_idx).reshape(bsz, Q_BLOCK, top_n * SEL_LEN, HEAD_DIM)
        vsel = gather(vs_blocks, top_idx).reshape(bsz, Q_BLOCK, top_n * SEL_LEN, HEAD_DIM)
        tok = (top_idx[..., None] * SEL_LEN + jnp.arange(SEL_LEN)).reshape(bsz, Q_BLOCK, top_n * SEL_LEN)
        sel_ok = jnp.repeat(top_val > 0.5 * NEG_INF, SEL_LEN, axis=-1)
        mask_sel = sel_ok & (tok <= pos[None, :, None])
        s = jnp.einsum('bthd,btkd->bhtk', qr, ksel, preferred_element_type=jnp.float32)
        p = masked_softmax(s, mask_sel[:, None])
        o_slc = jnp.einsum('bhtk,btkd->bthd', p.astype(vsel.dtype), vsel)

        kw = lax.dynamic_slice_in_dim(kw_pad, t0, Q_BLOCK + WINDOW, axis=1)
        vw = lax.dynamic_slice_in_dim(vw_pad, t0, Q_BLOCK + WINDOW, axis=1)
        kpos = t0 - WINDOW + jnp.arange(Q_BLOCK + WINDOW)
        mask_win = ((kpos[None, :] >= 0) & (kpos[None, :] <= pos[:, None])
                    & (kpos[None, :] > pos[:, None] - WINDOW))
        s = jnp.einsum('bthd,bkd->bhtk', qr, kw, preferred_element_type=jnp.float32)
        p = masked_softmax(s, mask_win)
        o_win = jnp.einsum('bhtk,bkd->bthd', p.astype(vw.dtype), vw)

        return g[..., 0:1] * o_cmp + g[..., 1:2] * o_slc + g[..., 2:3] * o_win

    out = lax.map(query_block, jnp.arange(seq // Q_BLOCK))
    return out.transpose(1, 0, 2, 3, 4).reshape(bsz, seq, C_NSA)


def spatial_gating(z, ln_g, ln_b, w_s, b_s):
    z = jax.nn.gelu(z)
    u, v = jnp.split(z, 2, axis=-1)
    v = layer_norm(v, ln_g, ln_b)
    bsz, seq, c = v.shape
    vr = v.reshape(bsz, seq // SGU_CHUNK, SGU_CHUNK, SGU_GROUPS, c // SGU_GROUPS)
    causal = jnp.tril(jnp.ones((SGU_CHUNK, SGU_CHUNK), dtype=bool))
    ws = jnp.where(causal, w_s, 0.0).astype(v.dtype)
    f = jnp.einsum('gts,bcsgd->bctgd', ws, vr) + b_s.T[:, :, None]
    return u * f.reshape(bsz, seq, c)


def multiscale_pool(z, w_p, scale):
    bsz, seq, c = z.shape
    cg = c // len(POOL_WINDOWS)
    zf = z.astype(jnp.float32)
    cs = jnp.pad(jnp.cumsum(zf, axis=1), ((0, 0), (1, 0), (0, 0)))
    t = jnp.arange(seq)
    outs = []
    for gi, w in enumerate(POOL_WINDOWS):
        c_g = cs[..., gi * cg:(gi + 1) * cg]
        lag = jnp.pad(c_g[:, :seq + 1 - w], ((0, 0), (w - 1, 0), (0, 0)))
        cnt = jnp.minimum(t + 1, w).astype(jnp.float32)[:, None]
        outs.append((c_g[:, 1:] - lag) / cnt - zf[..., gi * cg:(gi + 1) * cg])
    pooled = jnp.stack(outs, axis=2).astype(z.dtype)
    y = jnp.einsum('bsgc,gcd->bsgd', pooled, w_p).reshape(bsz, seq, c)
    return y * scale


def conv_ffn(h, w_up, cw, cb, w_down):
    gu = causal_dwconv(h @ w_up, cw, cb)
    g, u = jnp.split(gu, 2, axis=-1)
    return (jax.nn.gelu(g, approximate=True) * u) @ w_down


def setup_inputs(seed: int = 0) -> dict:
    key = jax.random.key(seed)
    ks = jax.random.split(key, 32)
    L = DEPTH
    hd = HEAD_DIM

    def nrm(k, shape, s):
        return jax.random.normal(k, shape, jnp.float32) * s

    def gain(k, shape):
        return 1.0 + nrm(k, shape, 0.05)

    return {
        'x': nrm(ks[0], (BATCH, SEQ, D_MODEL), 1.0),
        'norm_mix_pre': gain(ks[1], (L, D_MODEL)),
        'norm_mix_post': gain(ks[2], (L, D_MODEL)),
        'norm_ffn_pre': gain(ks[3], (L, D_MODEL)),
        'norm_ffn_post': gain(ks[4], (L, D_MODEL)),
        'w_in': nrm(ks[5], (L, D_MODEL, N_IN), D_MODEL ** -0.5),
        'w_out': nrm(ks[6], (L, D_MIX, D_MODEL), D_MIX ** -0.5),
        'conv_dw_w': nrm(ks[7], (L, CONV_K, C_CONV), CONV_K ** -0.5),
        'conv_dw_b': nrm(ks[8], (L, C_CONV), 0.02),
        'conv_ln_g': gain(ks[9], (L, C_CONV)),
        'conv_ln_b': nrm(ks[10], (L, C_CONV), 0.02),
        'nsa_pe_k': nrm(ks[11], (L, CMP_LEN, hd), 0.1),
        'nsa_pe_v': nrm(ks[12], (L, CMP_LEN, hd), 0.1),
        'nsa_ck_w1': nrm(ks[13], (L, CMP_LEN * hd, hd), (CMP_LEN * hd) ** -0.5),
        'nsa_ck_w2': nrm(ks[14], (L, hd, hd), hd ** -0.5),
        'nsa_cv_w1': nrm(ks[15], (L, CMP_LEN * hd, hd), (CMP_LEN * hd) ** -0.5),
        'nsa_cv_w2': nrm(ks[16], (L, hd, hd), hd ** -0.5),
        'sgu_ln_g': gain(ks[17], (L, C_SGU)),
        'sgu_ln_b': nrm(ks[18], (L, C_SGU), 0.02),
        'sgu_w': nrm(ks[19], (L, SGU_GROUPS, SGU_CHUNK, SGU_CHUNK), SGU_CHUNK ** -0.5),
        'sgu_b': 1.0 + nrm(ks[20], (L, SGU_GROUPS, SGU_CHUNK), 0.1),
        'pool_w': nrm(ks[21], (L, len(POOL_WINDOWS), C_POOL // 4, C_POOL // 4), (C_POOL // 4) ** -0.5),
        'pool_scale': gain(ks[22], (L, C_POOL)),
        'ffn_up': nrm(ks[23], (L, D_MODEL, 2 * FFN_DIM), D_MODEL ** -0.5),
        'ffn_conv_w': nrm(ks[24], (L, FFN_CONV_K, 2 * FFN_DIM), FFN_CONV_K ** -0.5),
        'ffn_conv_b': nrm(ks[25], (L, 2 * FFN_DIM), 0.02),
        'ffn_down': nrm(ks[26], (L, FFN_DIM, D_MODEL), FFN_DIM ** -0.5),
    }


def reference(x, norm_mix_pre, norm_mix_post, norm_ffn_pre, norm_ffn_post, w_in, w_out,
              conv_dw_w, conv_dw_b, conv_ln_g, conv_ln_b,
              nsa_pe_k, nsa_pe_v, nsa_ck_w1, nsa_ck_w2, nsa_cv_w1, nsa_cv_w2,
              sgu_ln_g, sgu_ln_b, sgu_w, sgu_b, pool_w, pool_scale,
              ffn_up, ffn_conv_w, ffn_conv_b, ffn_down):
    bsz, seq, _ = x.shape
    cos, sin = rope_tables(seq)
    sizes = (2 * C_CONV, C_NSA, 6 * HEAD_DIM, 3 * NSA_HEADS, 2 * C_SGU, C_POOL)
    offsets = [int(o) for o in np.cumsum(sizes)[:-1]]
    for l in range(DEPTH):
        h = rms_norm(x, norm_mix_pre[l])
        z = h @ w_in[l]
        za, zq, zkv, zg, zc, zd = jnp.split(z, offsets, axis=-1)
        y_conv = conformer_conv(za, conv_dw_w[l], conv_dw_b[l], conv_ln_g[l], conv_ln_b[l])
        q = zq.reshape(bsz, seq, NSA_HEADS, HEAD_DIM)
        kv = zkv.reshape(bsz, seq, 6, HEAD_DIM)
        gates = jax.nn.sigmoid(zg).reshape(bsz, seq, NSA_HEADS, 3)
        y_nsa = nsa_attention(q, kv[:, :, 0], kv[:, :, 1], kv[:, :, 2], kv[:, :, 3], kv[:, :, 4], kv[:, :, 5],
                              gates, nsa_pe_k[l], nsa_pe_v[l], nsa_ck_w1[l], nsa_ck_w2[l],
                              nsa_cv_w1[l], nsa_cv_w2[l], cos, sin)
        y_sgu = spatial_gating(zc, sgu_ln_g[l], sgu_ln_b[l], sgu_w[l], sgu_b[l])
        y_pool = multiscale_pool(zd, pool_w[l], pool_scale[l])
        y = jnp.concatenate([y_conv, y_nsa, y_sgu, y_pool], axis=-1) @ w_out[l]
        x = x + rms_norm(y, norm_mix_post[l])
        h = rms_norm(x, norm_ffn_pre[l])
        y = conv_ffn(h, ffn_up[l], ffn_conv_w[l], ffn_conv_b[l], ffn_down[l])
        x = x + rms_norm(y, norm_ffn_post[l])
    return x
```

```python
import numpy as np
from contextlib import ExitStack
import concourse.bass as bass
import concourse.mybir as mybir
from concourse.bass_utils import run_bass_kernel_spmd

F32 = mybir.dt.float32
BF16 = mybir.dt.bfloat16
AF = mybir.ActivationFunctionType
ALU = mybir.AluOpType
AX = mybir.AxisListType

D = 1024
SEQ = 8192
NB = 2
NT = 64
NS = 16
FF = 2816
NFC = 44
RMS_EPS = 1e-6
LN_EPS = 1e-5
NEG = -30000.0


class Sched:
    NDS = 8

    def __init__(self, nc, sems):
        self.nc = nc
        self.eng = {"pe": nc.tensor, "act": nc.scalar, "dve": nc.vector,
                    "pool": nc.gpsimd, "sp": nc.sync}
        self.ops = []
        self.last_writer = {}
        self.readers = {}
        self.sems = sems

    def cc_op(self, fn, reads=(), writes=()):
        idx = self.op("pool", fn, reads, writes)
        self.ops[idx].append("cc")
        return idx

    def op(self, engine, fn, reads=(), writes=()):
        idx = len(self.ops)
        is_dma = engine == "sp"
        deps = {}

        def add(d, kind):
            if d is None:
                return
            if deps.get(d) is None or kind == "raw":
                deps[d] = kind

        for k in reads:
            add(self.last_writer.get(k), "raw")
        for k in writes:
            add(self.last_writer.get(k), "waw")
            for r in self.readers.get(k, ()):
                add(r, "war")
        for k in writes:
            self.last_writer[k] = idx
            self.readers[k] = []
        for k in reads:
            if k not in writes:
                lst = self.readers.setdefault(k, [])
                if not is_dma:
                    lst[:] = [r_ for r_ in lst if self.ops[r_][0] != engine]
                lst.append(idx)
        keep = []
        for d, kind in deps.items():
            de = self.ops[d][0]
            if de == engine and not is_dma:
                if engine == "pe":
                    continue
            keep.append(d)
        self.ops.append([engine, fn, keep, is_dma])
        for d in keep:
            self.ops[d][3] = True
        return idx

    def _init_emit_state(self):
        self.counts = {e: 0 for e in self.eng}
        self.dma_counts = [0] * self.NDS
        self.n_dma = 0
        self.waited = {e: {} for e in self.eng}
        self.sigval = {}
        self.emitted = 0
        self.cc_count = 0

    def flush(self):
        if not hasattr(self, "counts"):
            self._init_emit_state()
        last = {}
        for idx in range(self.emitted, len(self.ops)):
            if len(self.ops[idx]) == 4:
                last[self.ops[idx][0]] = idx
        for e, idx in last.items():
            if e != "sp" and len(self.ops[idx]) == 4:
                self.ops[idx][3] = True
        for idx in range(self.emitted, len(self.ops)):
            e, fn, deps, signal = self.ops[idx][:4]
            is_cc = len(self.ops[idx]) > 4
            eng = self.eng[e]
            need = {}
            for d in deps:
                sem, val = self.sigval[d]
                if need.get(id(sem), (None, 0))[1] < val:
                    need[id(sem)] = (sem, val)
            if e == "sp":
                slot = self.n_dma % self.NDS
                if self.dma_counts[slot] > 0:
                    sem = self.sems["dma"][slot]
                    val = self.dma_counts[slot] * 16
                    if need.get(id(sem), (None, 0))[1] < val:
                        need[id(sem)] = (sem, val)
            for key, (sem, val) in need.items():
                if self.waited[e].get(key, 0) >= val:
                    continue
                eng.wait_ge(sem, val)
                self.waited[e][key] = val
            ins = fn(eng)
            if is_cc:
                self.cc_count += 1
                ins.then_inc(self.sems["cc"])
                self.sigval[idx] = (self.sems["cc"], self.cc_count)
                continue
            if e == "sp":
                slot = self.n_dma % self.NDS
                self.n_dma += 1
                self.dma_counts[slot] += 1
                sem = self.sems["dma"][slot]
                ins.then_inc(sem, 16)
                self.sigval[idx] = (sem, self.dma_counts[slot] * 16)
            elif signal:
                self.counts[e] += 1
                ins.then_inc(self.sems[e], 1)
                self.sigval[idx] = (self.sems[e], self.counts[e])
        self.emitted = len(self.ops)

    def barrier(self, engines=None):
        self.flush()
        for e, eng in self.eng.items():
            for x in ("pe", "act", "dve", "pool"):
                if x != e and self.counts[x] > 0 and self.waited[e].get(id(self.sems[x]), 0) < self.counts[x]:
                    eng.wait_ge(self.sems[x], self.counts[x])
                    self.waited[e][id(self.sems[x])] = self.counts[x]
            for slot in range(self.NDS):
                if self.dma_counts[slot] > 0:
                    sem = self.sems["dma"][slot]
                    val = self.dma_counts[slot] * 16
                    if self.waited[e].get(id(sem), 0) < val:
                        eng.wait_ge(sem, val)
                        self.waited[e][id(sem)] = val
            if self.cc_count > 0 and self.waited[e].get("cc", 0) < self.cc_count:
                eng.wait_ge(self.sems["cc"], self.cc_count)
                self.waited[e]["cc"] = self.cc_count
        self.last_writer = {}
        self.readers = {}

    def collective(self, fn):
        self.barrier()
        self.cc_count += 1
        fn(self.eng["pool"]).then_inc(self.sems["cc"])
        for e, eng in self.eng.items():
            eng.wait_ge(self.sems["cc"], self.cc_count)

    def emit(self):
        self.flush()
        sp = self.eng["sp"]
        for slot in range(self.NDS):
            if self.dma_counts[slot] > 0:
                sp.wait_ge(self.sems["dma"][slot], self.dma_counts[slot] * 16)
        return self.counts, self.n_dma


class Ctx:
    def __init__(self):
        self.nc = bass.Bass("TRN2", target_bir_lowering=False)
        self.es = ExitStack()
        nc = self.nc
        sems = {e: self.es.enter_context(nc.semaphore("s_" + e)) for e in ["pe", "act", "dve", "pool"]}
        sems["dma"] = [self.es.enter_context(nc.semaphore("s_dma%d" % i)) for i in range(Sched.NDS)]
        sems["cc"] = self.es.enter_context(nc.semaphore("s_cc"))
        self.S = Sched(nc, sems)
        self.pes = self.es

    def begin_phase(self, sfx):
        self.pes = ExitStack()
        self.sfx = sfx

    def end_phase(self):
        self.S.barrier()
        self.pes.close()
        self.pes = self.es

    def dram_int(self, name, shape, dt=F32):
        return self.nc.dram_tensor(name, list(shape), dt).ap()

    def dram_in(self, name, shape, dt=F32):
        return self.nc.dram_tensor(name, list(shape), dt, kind="ExternalInput").ap()

    def dram_out(self, name, shape, dt=F32):
        return self.nc.dram_tensor(name, list(shape), dt, kind="ExternalOutput").ap()

    def sb(self, name, shape, dt=F32):
        return self.pes.enter_context(self.nc.sbuf_tensor(name + getattr(self, "sfx", ""), list(shape), dt))

    def ps(self, name, shape, dt=F32):
        return self.pes.enter_context(self.nc.psum_tensor(name + getattr(self, "sfx", ""), list(shape), dt))

    def dma(self, out, in_, r=(), w=()):
        self.S.op("sp", lambda e: e.dma_start(out=out, in_=in_), r, w)

    def mm(self, out, lhsT, rhs, start=True, stop=True, r=(), w=(), skip=False):
        self.S.op("pe", lambda e: e.matmul(out, lhsT=lhsT, rhs=rhs, start=start, stop=stop,
                                           skip_group_check=skip), r, w)

    def tr(self, out, in_, ident, r=(), w=()):
        self.S.op("pe", lambda e: e.transpose(out, in_, ident), r, w)

    def act(self, out, in_, func, r=(), w=(), scale=1.0, bias=0.0, accum=None, eng="act"):
        if accum is None:
            self.S.op("act", lambda e: e.activation(out=out, in_=in_, func=func, bias=bias, scale=scale), r, w)
        else:
            self.S.op("act", lambda e: e.activation(out=out, in_=in_, func=func, bias=bias, scale=scale,
                                                    accum_out=accum), r, w)

    def cp(self, eng, out, in_, r=(), w=()):
        if eng == "act":
            self.S.op("act", lambda e: e.activation(out=out, in_=in_, func=AF.Copy), r, w)
        else:
            self.S.op(eng, lambda e: e.tensor_copy(out=out, in_=in_), r, w)

    def tt(self, eng, out, in0, in1, op, r=(), w=()):
        self.S.op(eng, lambda e: e.tensor_tensor(out=out, in0=in0, in1=in1, op=op), r, w)

    def ts(self, eng, out, in0, s1, s2, op0, op1=None, r=(), w=()):
        if op1 is None:
            self.S.op(eng, lambda e: e.tensor_scalar(out=out, in0=in0, scalar1=s1, scalar2=None, op0=op0), r, w)
        else:
            self.S.op(eng, lambda e: e.tensor_scalar(out=out, in0=in0, scalar1=s1, scalar2=s2, op0=op0, op1=op1), r, w)

    def stt(self, out, in0, scalar, in1, op0, op1, r=(), w=()):
        self.S.op("dve", lambda e: e.scalar_tensor_tensor(out=out, in0=in0, scalar=scalar, in1=in1,
                                                          op0=op0, op1=op1), r, w)

    def memset(self, eng, ap, val, w=()):
        self.S.op(eng, lambda e: e.memset(ap, val), (), w)

    def recip(self, out, in_, r=(), w=()):
        self.S.op("dve", lambda e: e.reciprocal(out=out, in_=in_), r, w)

    def finish(self):
        res = self.S.emit()
        self.es.close()
        return res


def load_convert(C, dst_bf, src_dram, stage, stage_key, dst_key, ncols, scale_ap=None, parity=0, scale_key="gscale"):
    C.dma(stage[:, 0:ncols], src_dram, w=[stage_key])
    if scale_ap is not None:
        C.ts("dve", dst_bf, stage[:, 0:ncols], scale_ap, None, ALU.mult, r=[stage_key, scale_key], w=[dst_key])
    elif parity % 2 == 0:
        C.cp("dve", dst_bf, stage[:, 0:ncols], r=[stage_key], w=[dst_key])
    else:
        C.cp("pool", dst_bf, stage[:, 0:ncols], r=[stage_key], w=[dst_key])


NLOC = 1804


MIX_W = ["w_kv", "w_loc", "w_out", "gpre", "gpost", "convw", "convv", "w1d", "w2d", "ped", "sgv", "sgw", "sgb",
         "poolw", "poolsc"]
MIX_W_SHAPES = {"w_kv": [D, 512], "w_loc": [D, NLOC], "w_out": [D, D], "gpre": [128, 8], "gpost": [1, D],
                "convw": [128, 2, 31], "convv": [128, 2, 3], "w1d": [128, 2048], "w2d": [64, 128], "ped": [128, 32],
                "sgv": [2, 256], "sgw": [128, 4, 128], "sgb": [4, 128], "poolw": [128, 2, 128], "poolsc": [128, 2]}
MIX_C_SHAPES = {"ropeC": [128, SEQ], "ropeS": [128, SEQ], "qC": [NS, 128, 128], "qS": [NS, 128, 128], "indp": [64, SEQ],
                "identd": [128, 128], "wmaskd": [8, 128, 128], "cmaskd": [5, 128, 128], "selA": [NS, 128, 128],
                "selB": [NS, 128, 128], "impM": [4, 128, 128], "trild": [128, 128], "rcnt0": [128, 2, 128],
                "rcntg": [128, 2, 128], "selm": [128, 32], "selm2": [8, 2]}
FFN_W_SHAPES = {"w_up": [D, 2 * FF], "w_dn": [FF, D], "g2": [128, 8], "gpost2": [1, D], "cwd": [128, NFC, 4]}


def mixer_phase(C, T, l, nslots=NS, stages=5):
    C.begin_phase("_m%d" % l)
    nc = C.nc
    xg_tile = T["xg_tile"]; xown_tile = T["xown_tile"]; xm_own = T["xm_own"]; xm_tail = T["xm_tail"]
    w_kv, w_loc, w_out, gpre, gpost = (T[k + "_%d" % l] for k in ("w_kv", "w_loc", "w_out", "gpre", "gpost"))
    convw, convv, w1d, w2d, ped = (T[k + "_%d" % l] for k in ("convw", "convv", "w1d", "w2d", "ped"))
    sgv, sgw, sgb, poolw, poolsc = (T[k + "_%d" % l] for k in ("sgv", "sgw", "sgb", "poolw", "poolsc"))
    ropeC, ropeS, qC, qS, indp, identd = (T[k] for k in ("ropeC", "ropeS", "qC", "qS", "indp", "identd"))
    wmaskd, cmaskd, selA, selB, impM, trild = (T[k] for k in ("wmaskd", "cmaskd", "selA", "selB", "impM", "trild"))
    rcnt0, rcntg, selmd = T["rcnt0"], T["rcntg"], T["selm"]

    sb = C.sb
    wkv = sb("wkv", [128, 8, 512], BF16)
    wloc = sb("wloc", [128, 8, NLOC], BF16)
    wo = sb("wo", [128, 8, D], BF16)
    xt = [sb("xt0", [128, D]), sb("xt1", [128, D])]
    stage = xt
    gpre_t = sb("gpre_t", [128, 8])
    gpost_bc = sb("gpost_bc", [128, D])
    ident = sb("ident", [128, 128], BF16)
    ones_b = sb("ones_b", [128, 128], BF16)
    shi = sb("shi", [128, 2, 128], BF16)
    slo = sb("slo", [128, 2, 128], BF16)
    Dg = sb("Dg", [128, 2, 31, 128], BF16)
    convw_t = sb("convw_t", [128, 2, 31])
    convv_t = sb("convv_t", [128, 2, 3])
    W1k = sb("W1k", [64, 32, 64], BF16)
    W1v = sb("W1v", [64, 32, 64], BF16)
    W2 = sb("W2", [64, 128], BF16)
    peTk = sb("peTk", [64, 32], BF16)
    peTv = sb("peTv", [64, 32], BF16)
    hnc = sb("hnc", [128, D], BF16)
    cbias = sb("cbias", [64, 2])
    Kaug = sb("Kaug", [128, SEQ], BF16)
    Kw = sb("Kw", [128, 12, 128], BF16)
    Vs = sb("Vs", [128, NT, 65], BF16)
    Vw = sb("Vw", [128, 12, 65], BF16)
    kvk = [sb("kvk0", [64, 528], BF16), sb("kvk1", [64, 528], BF16)]
    kvv = [sb("kvv0", [64, 528], BF16), sb("kvv1", [64, 528], BF16)]
    gkT = sb("gkT", [64, 576], BF16)
    gvT = sb("gvT", [64, 576], BF16)
    kcT = sb("kcT", [128, 576], BF16)
    rhsC = sb("rhsC", [128, 4, 193], BF16)
    wmask = sb("wmask", [128, 8, 128], BF16)
    cmask = sb("cmask", [128, 5, 128], BF16)
    wsT = sb("wsT", [128, 4, 128], BF16)
    tril = sb("tril", [128, 128])
    Bs = sb("Bs", [128, 2, 128])
    Wbd = sb("Wbd", [128, 2, 128], BF16)
    poolsc_t = sb("poolsc_t", [128, 2])
    rc0 = sb("rc0", [128, 2, 128])
    rcg = sb("rcg", [128, 2, 128])
    lng_bc = sb("lng_bc", [128, 256])
    lnb_bc = sb("lnb_bc", [128, 256])
    hnb = [sb("hnb0", [128, D], BF16), sb("hnb1", [128, D], BF16)]
    hT = [sb("hT0", [128, 8, 128], BF16), sb("hT1", [128, 8, 128], BF16)]
    rC = [sb("rC0", [128, 128]), sb("rC1", [128, 128])]
    rS = [sb("rS0", [128, 128]), sb("rS1", [128, 128])]
    junk = sb("junk", [128, D], BF16)
    ssq = sb("ssq", [128, 8])
    rstd = sb("rstd", [128, 8])
    ropet = sb("ropet", [128, 2, 128])
    xo = sb("xo", [128, D])
    hno = sb("hno", [128, D], BF16)
    hTo = sb("hTo", [128, 8, 160], BF16)
    sig = sb("sig", [128, 2, 160])
    ub = sb("ub", [128, 2, 160], BF16)
    zdT = sb("zdT", [128, 2, 160])
    s2 = sb("s2", [128, 2, 160]); s4 = sb("s4", [128, 2, 160])
    s8 = sb("s8", [128, 2, 160]); s16 = sb("s16", [128, 2, 160])
    plf = sb("plf", [128, 2, 128]); plb = sb("plb", [128, 2, 128], BF16)
    usg = sb("usg", [128, 2, 128])
    qCt = sb("qCt", [128, 128]); qSt = sb("qSt", [128, 128])
    qt1 = sb("qt1", [128, 128]); qt2 = sb("qt2", [128, 128]); qsum = sb("qsum", [128, 128])
    QR = sb("QR", [128, 2, 4, 128], BF16)
    QC = sb("QC", [128, 4, 128], BF16)
    QW = sb("QW", [128, 4, 128], BF16)
    gts = sb("gts", [128, 12])
    vg = sb("vg", [128, 256]); vn = sb("vn", [128, 256]); vbf = sb("vbf", [128, 256], BF16)
    bnst = sb("bnst", [128, 6]); bnag = sb("bnag", [128, 2]); lnr = sb("lnr", [128, 2])
    ycv = sb("ycv", [128, 2, 128]); ysq = sb("ysq", [128, 2, 128])
    cmean = sb("cmean", [128, 128]); cmsq = sb("cmsq", [128, 128]); cvar = sb("cvar", [128, 128])
    crstd = sb("crstd", [128, 128]); cyn = sb("cyn", [128, 2, 128])
    sgt = sb("sgt", [128, 2, 128])
    yT = sb("yT", [128, 8, 128], BF16)
    Pb = [sb("Pb%d" % i, [128, 4, 128], BF16) for i in range(4)]
    selAt = sb("selAt", [128, 128]); selBt = sb("selBt", [128, 128])
    imp = sb("imp", [128, 128]); imp2 = sb("imp2", [128, 128]); impw = sb("impw", [128, 128])
    top8 = sb("top8", [128, 16]); selt1 = sb("selt1", [128, 128]); selt2 = sb("selt2", [128, 128])
    selbf = sb("selbf", [128, 128], BF16)
    den = sb("den", [128, 12]); rden = sb("rden", [128, 12]); gsc = sb("gsc", [128, 12])
    Ynsa = sb("Ynsa", [128, 256]); Ynb = sb("Ynb", [128, 256], BF16)
    yo = sb("yo", [128, D])
    oss = sb("oss", [128, 4])
    selm = sb("selm", [128, 32], BF16)
    cg1 = sb("cg1", [64, 64]); cg2 = sb("cg2", [64, 64])

    psS = [C.ps("psS0", [128, 512]), C.ps("psS1", [128, 512])]
    psC = C.ps("psC", [128, 512])
    psK = C.ps("psK", [128, 512])
    psSel = C.ps("psSel", [128, 512])
    psWin = C.ps("psWin", [128, 512])
    psG = C.ps("psG", [128, 512])
    psT = C.ps("psT", [128, 1024], BF16)

    C.dma(gpre_t[:], gpre[:, :], w=["gscale"])
    C.dma(gpost_bc[:], gpost.partition_broadcast(128), w=["gpost_bc"])
    C.dma(stage[0][:, 0:128], identd[:, :], w=["xt0"])
    C.cp("dve", ident[:], stage[0][:, 0:128], r=["xt0"], w=["ident"])
    C.memset("dve", ones_b[:], 1.0, w=["ones_b"])
    C.dma(stage[1][:, 0:32], selmd[:, :], w=["xt1"])
    C.cp("dve", selm[:], stage[1][:, 0:32], r=["xt1"], w=["selm"])
    par = 0
    for kc in range(8):
        s = stage[par % 2]; sk = "xt%d" % (par % 2); par += 1
        load_convert(C, wkv[:, kc, :], w_kv[kc * 128:(kc + 1) * 128, :], s, sk, "wkv", 512, scale_ap=gpre_t[:, kc:kc + 1])
        s = stage[par % 2]; sk = "xt%d" % (par % 2); par += 1
        load_convert(C, wloc[:, kc, 0:902], w_loc[kc * 128:(kc + 1) * 128, 0:902], s, sk, "wloc", 902, scale_ap=gpre_t[:, kc:kc + 1])
        s = stage[par % 2]; sk = "xt%d" % (par % 2); par += 1
        load_convert(C, wloc[:, kc, 902:NLOC], w_loc[kc * 128:(kc + 1) * 128, 902:NLOC], s, sk, "wloc", 902, scale_ap=gpre_t[:, kc:kc + 1])
        s = stage[par % 2]; sk = "xt%d" % (par % 2); par += 1
        load_convert(C, wo[:, kc, :], w_out[kc * 128:(kc + 1) * 128, :], s, sk, "wo", D, parity=kc)
    for q in range(8):
        s = stage[par % 2]; sk = "xt%d" % (par % 2); par += 1
        C.dma(s[64:128, 0:1024], indp[:, q * 1024:(q + 1) * 1024], w=[sk])
        C.cp("pool", Kaug[64:128, q * 1024:(q + 1) * 1024], s[64:128, 0:1024], r=[sk], w=["Kaug_ind"])
    C.memset("pool", Kw[64:128, :, :], 0.0, w=["Kw"])
    C.memset("pool", kcT[:, :], 0.0, w=["kcT"])
    C.memset("pool", gkT[:, :], 0.0, w=["gkT"])
    C.memset("pool", gvT[:, :], 0.0, w=["gvT"])
    for q_ in range(2):
        C.memset("pool", kvk[q_][:, :], 0.0, w=["kvk%d" % q_])
        C.memset("pool", kvv[q_][:, :], 0.0, w=["kvv%d" % q_])
    C.memset("pool", QC[:, :, :], 0.0, w=["QC"])
    C.memset("pool", QW[:, :, :], 0.0, w=["QW"])
    C.memset("pool", Vs[:, :, 64:65], 1.0, w=["Vs"])
    C.memset("pool", Vw[:, :, 64:65], 1.0, w=["Vw"])
    C.memset("pool", rhsC[:, :, :], 0.0, w=["rhsC"])
    C.memset("pool", rhsC[:, :, 192:193], 1.0, w=["rhsC"])
    for q in range(8):
        s = stage[par % 2]; sk = "xt%d" % (par % 2); par += 1
        C.dma(s[:, 0:128], wmaskd[q, :, :], w=[sk])
        C.cp("dve", wmask[:, q, :], s[:, 0:128], r=[sk], w=["wmask"])
    for q in range(4):
        s = stage[par % 2]; sk = "xt%d" % (par % 2); par += 1
        C.dma(s[:, 0:128], cmaskd[q, :, :], w=[sk])
        C.cp("dve", cmask[:, q, :], s[:, 0:128], r=[sk], w=["cmask"])
        if q == 0:
            s = stage[par % 2]; sk = "xt%d" % (par % 2); par += 1
            C.dma(s[:, 0:128], cmaskd[4, :, :], w=[sk])
            C.cp("dve", cmask[:, 4, :], s[:, 0:128], r=[sk], w=["cmask"])
        s = stage[par % 2]; sk = "xt%d" % (par % 2); par += 1
        C.dma(s[:, 0:128], impM[q, :, :], w=[sk])
        C.cp("dve", rhsC[:, q, 64:192], s[:, 0:128], r=[sk], w=["rhsC"])
    C.dma(convw_t[:], convw[:, :, :], w=["convw_t"])
    C.dma(convv_t[:], convv[:, :, :], w=["convv_t"])
    C.dma(stage[0][:, 0:128], identd[:, :], w=["xt0"])
    for cc in range(2):
        for k in range(31):
            C.ts("dve" if k % 2 == 0 else "pool", Dg[:, cc, k, :], stage[0][:, 0:128], convw_t[:, cc, k:k + 1], None, ALU.mult,
                 r=["xt0", "convw_t"], w=["Dg"])
    for wi_, (Wt_, wk_) in enumerate(((W1k, "W1k"), (W1v, "W1v"))):
        for q in range(2):
            C.dma(stage[q][0:64, 0:1024], w1d[64 * wi_:64 * wi_ + 64, q * 1024:(q + 1) * 1024], w=["xt%d" % q])
            C.cp("dve", Wt_[:].rearrange("p r e -> p (r e)")[:, q * 1024:(q + 1) * 1024], stage[q][0:64, 0:1024],
                 r=["xt%d" % q], w=[wk_])
    C.dma(stage[0][0:64, 0:128], w2d[:, :], w=["xt0"])
    C.cp("dve", W2[:], stage[0][0:64, 0:128], r=["xt0"], w=["W2"])
    C.dma(stage[0][0:64, 0:32], ped[0:64, :], w=["xt0"])
    C.cp("dve", peTk[:], stage[0][0:64, 0:32], r=["xt0"], w=["peTk"])
    C.dma(stage[1][0:64, 0:32], ped[64:128, :], w=["xt1"])
    C.cp("dve", peTv[:], stage[1][0:64, 0:32], r=["xt1"], w=["peTv"])
    for wi_, (Wt_, wk_, pt_, pk_) in enumerate(((W1k, "W1k", peTk, "peTk"), (W1v, "W1v", peTv, "peTv"))):
        for r_ in range(32):
            C.mm(psK[0:64, 0:1], Wt_[:, r_, :], pt_[:, r_:r_ + 1], start=(r_ == 0), stop=(r_ == 31),
                 r=[wk_, pk_], w=["psK"])
        C.cp("dve", cbias[:, wi_:wi_ + 1], psK[0:64, 0:1], w=["psK", "cbias"])
    C.dma(lng_bc[:], sgv[0:1, :].partition_broadcast(128), w=["lng_bc"])
    C.dma(lnb_bc[:], sgv[1:2, :].partition_broadcast(128), w=["lnb_bc"])
    C.dma(tril[:], trild[:, :], w=["tril"])
    C.dma(stage[0][:, 0:512], sgw.rearrange("p g t -> p (g t)"), w=["xt0"])
    for g in range(4):
        C.tt("dve", wsT[:, g, :], stage[0][:, g * 128:(g + 1) * 128], tril[:], ALU.mult, r=["xt0", "tril"], w=["wsT"])
        C.dma(Bs[(g % 2) * 64:(g % 2) * 64 + 64, g // 2, :], sgb[g:g + 1, :].partition_broadcast(64), w=["Bs"])
    C.dma(stage[1][:, 0:256], poolw.rearrange("p q d -> p (q d)"), w=["xt1"])
    C.cp("dve", Wbd[:].rearrange("p q d -> p (q d)"), stage[1][:, 0:256], r=["xt1"], w=["Wbd"])
    C.dma(poolsc_t[:], poolsc[:, :], w=["poolsc_t"])
    C.dma(rc0[:], rcnt0[:, :, :], w=["rc0"])
    C.dma(rcg[:], rcntg[:, :, :], w=["rcg"])

    def rms_to_bf(xt_ap, xk, out_bf, ok, np_, col):
        C.act(junk[0:np_, :], xt_ap, AF.Square, r=[xk], w=["junk", "ssq%d" % col], accum=ssq[0:np_, col:col + 1])
        C.act(rstd[0:np_, col:col + 1], ssq[0:np_, col:col + 1], AF.Ln, r=["ssq%d" % col, "epsb"], w=["rstd%d" % col],
              scale=1.0 / D, bias=epsb[0:np_, 0:1])
        C.act(rstd[0:np_, col:col + 1], rstd[0:np_, col:col + 1], AF.Exp, r=["rstd%d" % col], w=["rstd%d" % col], scale=-0.5)
        C.act(out_bf, xt_ap, AF.Copy, r=[xk, "rstd%d" % col], w=[ok], scale=rstd[0:np_, col:col + 1])

    epsb = sb("epsb", [128, 2])
    C.memset("dve", epsb[:, 0:1], RMS_EPS, w=["epsb"])
    C.memset("dve", epsb[:, 1:2], LN_EPS, w=["epsb"])

    def gelu_inplace(t_ap, tkeys, tmp_ap, tmpkeys, out_ap, okeys):
        C.tt("pool", tmp_ap, t_ap, t_ap, ALU.mult, r=tkeys, w=tmpkeys)
        C.ts("dve", tmp_ap, tmp_ap, 0.044715, 1.0, ALU.mult, ALU.add, r=tmpkeys, w=tmpkeys)
        C.tt("pool", tmp_ap, tmp_ap, t_ap, ALU.mult, r=tmpkeys + tkeys, w=tmpkeys)
        C.act(tmp_ap, tmp_ap, AF.Exp, r=tmpkeys, w=tmpkeys, scale=-1.5957691216057308)
        C.ts("dve", tmp_ap, tmp_ap, 1.0, None, ALU.add, r=tmpkeys, w=tmpkeys)
        C.recip(tmp_ap, tmp_ap, r=tmpkeys, w=tmpkeys)
        C.tt("pool", out_ap, t_ap, tmp_ap, ALU.mult, r=tkeys + tmpkeys, w=okeys)

    def kv_tile(m):
        p = m % 2
        xk, hk, tk = "xt%d" % p, "hnb%d" % p, "hT%d" % p
        C.dma(xt[p][:], xg_tile(m)[:, :], w=[xk])
        C.dma(rC[p][:], ropeC[:, m * 128:(m + 1) * 128], w=["rC%d" % p])
        C.dma(rS[p][:], ropeS[:, m * 128:(m + 1) * 128], w=["rS%d" % p])
        yield
        C.act(junk[:, :], xt[p][:], AF.Square, r=[xk], w=["junk", "ssq%d" % p], accum=ssq[:, p:p + 1])
        yield
        C.act(rstd[:, p:p + 1], ssq[:, p:p + 1], AF.Ln, r=["ssq%d" % p, "epsb"], w=["rstd%d" % p],
              scale=1.0 / D, bias=epsb[:, 0:1])
        yield
        C.act(rstd[:, p:p + 1], rstd[:, p:p + 1], AF.Exp, r=["rstd%d" % p], w=["rstd%d" % p], scale=-0.5)
        yield
        C.ts("dve", hnb[p][:], xt[p][:], rstd[:, p:p + 1], None, ALU.mult, r=[xk, "rstd%d" % p], w=[hk])
        yield
        for half in range(2):
            for q in range(4):
                kc = 4 * half + q
                C.tr(psT[:, q * 128:(q + 1) * 128], hnb[p][:, kc * 128:(kc + 1) * 128], ident[:], r=[hk, "ident"], w=["psT"])
            yield
            C.cp("dve", hT[p][:, 4 * half:4 * half + 4, :], psT[:, 0:512].rearrange("p (k t) -> p k t", k=4), w=["psT", tk])
            yield
        for f in range(3):
            for kc in range(8):
                C.mm(psK[:, f * 128:(f + 1) * 128], wkv[:, kc, f * 128:(f + 1) * 128], hT[p][:, kc, :],
                     start=(kc == 0), stop=(kc == 7), r=["wkv", tk], w=["psK"])
            yield
        for kc in range(8):
            C.mm(psK[:, 384:512], hT[p][:, kc, :], wkv[:, kc, 384:512], start=(kc == 0), stop=(kc == 7),
                 r=["wkv", tk], w=["psK"])
        yield
        grp = (m // 4) % 2
        col0 = 16 + (m % 4) * 128
        C.cp("act", kvk[grp][:, col0:col0 + 128], psK[0:64, 0:128], w=["psK", "kvk%d" % grp])
        C.cp("act", kvv[grp][:, col0:col0 + 128], psK[64:128, 0:128], w=["psK", "kvv%d" % grp])
        C.tt("dve", ropet[:, 0, :], psK[:, 128:256], rC[p][:], ALU.mult, r=["rC%d" % p], w=["psK", "ropet0"])
        C.tt("dve", ropet[:, 1, :], psK[:, 256:384], rS[p][:], ALU.mult, r=["rS%d" % p], w=["psK", "ropet1"])
        yield
        C.cp("dve", Vs[:, m, 0:64], psK[:, 384:448], w=["psK", "Vs%d" % m])
        C.cp("dve", Vw[:, m % 12, 0:64], psK[:, 448:512], w=["psK", "Vw%d" % (m % 12)])
        yield
        C.tt("pool", Kaug[0:64, m * 128:(m + 1) * 128], ropet[0:64, 0, :], ropet[0:64, 1, :], ALU.add,
             r=["ropet0", "ropet1"], w=["Kaug%d" % m])
        C.tt("pool", Kw[0:64, m % 12, :], ropet[64:128, 0, :], ropet[64:128, 1, :], ALU.add,
             r=["ropet0", "ropet1"], w=["Kw%d" % (m % 12)])
        yield

    def compress(j):
        g = j % 2
        n0 = 32 * j - 1
        c0 = 32 + n0
        for which, (Wt_, wk_, kb, kk, gT, gk_) in enumerate(((W1k, "W1k", kvk[g], "kvk%d" % g, gkT, "gkT"),
                                                           (W1v, "W1v", kvv[g], "kvv%d" % g, gvT, "gvT"))):
            for r_ in range(32):
                C.mm(psK[0:64, which * 32:(which + 1) * 32], Wt_[:, r_, :], kb[:, r_:r_ + 16 * 31 + 1:16],
                     start=(r_ == 0), stop=(r_ == 31), r=[wk_, kk], w=["psK"])
            yield
            cgt = cg1[:, which * 32:(which + 1) * 32]
            cgs = cg2[:, which * 32:(which + 1) * 32]
            C.ts("dve", cgt, psK[0:64, which * 32:(which + 1) * 32], cbias[:, which:which + 1], None, ALU.add,
                 r=["cbias"], w=["psK", "cg1_%d" % which])
            gelu_inplace(cgt, ["cg1_%d" % which], cgs, ["cg2_%d" % which], gT[:, c0:c0 + 32], [gk_])
            other = (kvk, kvv)[which][1 - g]
            C.cp("pool", other[:, 0:16], kb[:, 512:528], r=[kk], w=[("kvk%d", "kvv%d")[which] % (1 - g)])
            yield
        C.mm(psK[0:64, 64:96], W2[:, 0:64], gkT[:, c0:c0 + 32], r=["W2", "gkT"], w=["psK"])
        yield
        C.cp("dve", kcT[0:64, c0:c0 + 32], psK[0:64, 64:96], w=["psK", "kcT"])
        yield
        chunks = sorted(set([max(n0, 0) // 128, (n0 + 31) // 128]))
        for ci in chunks:
            C.mm(psK[:, 128:192], gvT[:, 32 + ci * 128:32 + (ci + 1) * 128], W2[:, 64:128], r=["W2", "gvT"], w=["psK"])
            yield
            C.cp("dve", rhsC[:, ci, 0:64], psK[:, 128:192], w=["psK", "rhsC"])
            yield

    def local(j):
        yield
        C.dma(xo[:], xown_tile(j)[:, :], w=["xo"])
        for cand in range(4):
            m_ = 4 * j - 1 + cand
            if m_ < 0:
                C.memset("pool", yo[0:32, :], 0.0, w=["yo"])
            else:
                C.dma(yo[cand * 32:(cand + 1) * 32, :], xg_tile(m_)[96:128, :], w=["yo"])
        C.dma(qCt[:], qC[j, :, :], w=["qCt"])
        C.dma(qSt[:], qS[j, :, :], w=["qSt"])
        C.dma(selAt[:], selA[j, :, :], w=["selAt"])
        C.dma(selBt[:], selB[j, :, :], w=["selBt"])
        rms_to_bf(xo[:], "xo", hno[:], "hno", 128, 2)
        yield
        rms_to_bf(yo[:], "yo", hnc[:], "hnc", 128, 3)
        yield
        for half in range(2):
            for q in range(4):
                kc = 4 * half + q
                C.tr(psT[:, 512 + q * 128:512 + (q + 1) * 128], hno[:, kc * 128:(kc + 1) * 128], ident[:],
                     r=["hno", "ident"], w=["psT"])
            yield
            C.cp("act", hTo[:, 4 * half:4 * half + 4, 32:160], psT[:, 512:1024].rearrange("p (k t) -> p k t", k=4),
                 w=["psT", "hTo"])
            yield
        for kc in range(8):
            C.mm(psG[:, kc * 32:(kc + 1) * 32], hnc[:, kc * 128:(kc + 1) * 128], selm[:], r=["hnc", "selm"], w=["psG"])
        yield
        C.cp("act", hTo[:, :, 0:32], psG[:, 0:256].rearrange("p (k t) -> p k t", k=8), w=["psG", "hTo"])
        yield

        def proj(chunk, ncol, out_ap):
            c0 = 160 - ncol
            for kc in range(8):
                C.mm(out_ap, wloc[:, kc, chunk * 128:(chunk + 1) * 128], hTo[:, kc, c0:160],
                     start=(kc == 0), stop=(kc == 7), r=["wloc", "hTo"], w=["psG"])

        yield
        for cc in range(2):
            proj(2 + cc, 160, psG[:, 0:160])
            C.act(sig[:, cc, :], psG[:, 0:160], AF.Exp, w=["psG", "sig%d" % cc], scale=-1.0)
            C.ts("dve", sig[:, cc, :], sig[:, cc, :], 1.0, None, ALU.add, r=["sig%d" % cc], w=["sig%d" % cc])
            C.recip(sig[:, cc, :], sig[:, cc, :], r=["sig%d" % cc], w=["sig%d" % cc])
        for cc in range(2):
            proj(cc, 160, psG[:, 0:160])
            C.tt("dve", ub[:, cc, :], psG[:, 0:160], sig[:, cc, :], ALU.mult, r=["sig%d" % cc], w=["psG", "ub"])
        yield
        for cc in range(2):
            proj(4 + cc, 160, psG[:, 0:160])
            C.cp("act", zdT[:, cc, :], psG[:, 0:160], w=["psG", "zdT"])
        yield
        for cc in range(2):
            proj(6 + cc, 128, psG[:, 0:128])
            sk2 = ["sgt%d" % (2 * cc), "sgt%d" % (2 * cc + 1)]
            C.cp("act", sgt[:, cc, :], psG[:, 0:128], w=["psG"] + sk2)
            gelu_inplace(sgt[:, cc, :], sk2, vn[:, cc * 128:(cc + 1) * 128], ["vn"], usg[:, cc, :], ["usg"])
        yield
        for hp in range(2):
            for kc in range(8):
                C.mm(psG[:, 0:128], wloc[:, kc, 1024 + hp * 128:1024 + (hp + 1) * 128], hTo[:, kc, 32:160],
                     start=(kc == 0), stop=(kc == 7), r=["wloc", "hTo"], w=["psG"])
            for kc in range(8):
                C.mm(psG[:, 128:256], wloc[:, kc, 1280 + hp * 128:1280 + (hp + 1) * 128], hTo[:, kc, 32:160],
                     start=(kc == 0), stop=(kc == 7), r=["wloc", "hTo"], w=["psG"])
            C.tt("dve", qt1[:], psG[:, 0:128], qCt[:], ALU.mult, r=["qCt"], w=["psG", "qt1"])
            C.tt("dve", qt2[:], psG[:, 128:256], qSt[:], ALU.mult, r=["qSt"], w=["psG", "qt2"])
            C.act(QC[0:64, 2 * hp, :], psG[0:64, 0:128], AF.Copy, w=["psG", "QC"], scale=0.125)
            C.act(QC[0:64, 2 * hp + 1, :], psG[64:128, 0:128], AF.Copy, w=["psG", "QC"], scale=0.125)
            yield
            C.tt("pool", qsum[:], qt1[:], qt2[:], ALU.add, r=["qt1", "qt2"], w=["qsum"])
            yield
            for v in range(2):
                C.cp("dve", QR[0:64, v, 2 * hp, :], qsum[0:64, :], r=["qsum"], w=["QRq"])
                C.cp("act", QR[0:64, v, 2 * hp + 1, :], qsum[64:128, :], r=["qsum"], w=["QRq"])
            C.cp("pool", QW[0:64, 2 * hp, :], qsum[0:64, :], r=["qsum"], w=["QW"])
            C.cp("act", QW[0:64, 2 * hp + 1, :], qsum[64:128, :], r=["qsum"], w=["QW"])
        yield
        for kc in range(8):
            C.mm(psG[:, 0:268], hTo[:, kc, 32:160], wloc[:, kc, 1536:1804], start=(kc == 0), stop=(kc == 7),
                 r=["wloc", "hTo"], w=["psG"])
        C.act(gts[:], psG[:, 256:268], AF.Exp, w=["psG", "gts"], scale=-1.0)
        C.ts("dve", gts[:], gts[:], 1.0, None, ALU.add, r=["gts"], w=["gts"])
        C.recip(gts[:], gts[:], r=["gts"], w=["gts"])
        C.cp("act", vn[:], psG[:, 0:256], w=["psG", "vn"])
        gelu_inplace(vn[:], ["vn"], cyn[:].rearrange("p c t -> p (c t)"), ["cyn0", "cyn1"], vg[:], ["vg"])
        C.S.op("dve", lambda e: e.bn_stats(out=bnst[:], in_=vg[:]), ["vg"], ["bnst"])
        C.S.op("dve", lambda e: e.bn_aggr(out=bnag[:], in_=bnst[:]), ["bnst"], ["bnag"])
        C.act(lnr[:, 0:1], bnag[:, 1:2], AF.Ln, r=["bnag", "epsb"], w=["lnr"], bias=epsb[:, 1:2])
        C.act(lnr[:, 1:2], lnr[:, 0:1], AF.Exp, r=["lnr"], w=["lnr1"], scale=-0.5)
        C.ts("dve", vn[:], vg[:], bnag[:, 0:1], lnr[:, 1:2], ALU.subtract, ALU.mult, r=["vg", "bnag", "lnr1"], w=["vn"])
        C.tt("pool", vn[:], vn[:], lng_bc[:], ALU.mult, r=["vn", "lng_bc"], w=["vn"])
        C.tt("pool", vbf[:], vn[:], lnb_bc[:], ALU.add, r=["vn", "lnb_bc"], w=["vbf"])
        yield
        for cc in range(2):
            for k in range(31):
                C.mm(psG[:, 0:128], Dg[:, cc, k, :], ub[:, cc, 2 + k:2 + k + 128], start=(k == 0), stop=(k == 30),
                     r=["Dg", "ub"], w=["psG"])
            C.act(ycv[:, cc, :], psG[:, 0:128], AF.Identity, r=["convv_t"], w=["psG", "ycv"], bias=convv_t[:, cc, 0:1])
            C.act(ysq[:, cc, :], psG[:, 0:128], AF.Square, r=["convv_t"], w=["psG", "ysq"], bias=convv_t[:, cc, 0:1])
        for src, sk_, c0_ in ((ycv, "ycv", 0), (ysq, "ysq", 128)):
            C.cp("dve", shi[:], src[:], r=[sk_], w=["shi"])
            C.tt("dve", slo[:], src[:], shi[:], ALU.subtract, r=[sk_, "shi"], w=["slo"])
            n_ = 0
            for part, pk_ in ((shi, "shi"), (slo, "slo")):
                for cc in range(2):
                    C.mm(psG[:, c0_:c0_ + 128], ones_b[:], part[:, cc, :], start=(n_ == 0), stop=(n_ == 3),
                         r=["ones_b", pk_], w=["psG"])
                    n_ += 1
        C.act(cmean[:], psG[:, 0:128], AF.Copy, w=["psG", "cmean"], scale=1.0 / 256)
        C.tt("pool", cmsq[:], cmean[:], cmean[:], ALU.mult, r=["cmean"], w=["cmsq"])
        C.stt(cvar[:], psG[:, 128:256], 1.0 / 256, cmsq[:], ALU.mult, ALU.subtract, r=["cmsq"], w=["psG", "cvar"])
        C.act(crstd[:], cvar[:], AF.Ln, r=["cvar", "epsb"], w=["crstd"], bias=epsb[:, 1:2])
        C.act(crstd[:], crstd[:], AF.Exp, r=["crstd"], w=["crstd"], scale=-0.5)
        for cc in range(2):
            C.tt("pool", cyn[:, cc, :], ycv[:, cc, :], cmean[:], ALU.subtract, r=["ycv", "cmean"], w=["cyn%d" % cc])
            C.tt("dve", cyn[:, cc, :], cyn[:, cc, :], crstd[:], ALU.mult, r=["cyn%d" % cc, "crstd"], w=["cyn%d" % cc])
            C.ts("dve", cyn[:, cc, :], cyn[:, cc, :], convv_t[:, cc, 1:2], convv_t[:, cc, 2:3], ALU.mult, ALU.add,
                 r=["cyn%d" % cc, "convv_t"], w=["cyn%d" % cc])
            C.act(ysq[:, cc, :], cyn[:, cc, :], AF.Exp, r=["cyn%d" % cc], w=["ysq"], scale=-1.0)
            C.ts("dve", ysq[:, cc, :], ysq[:, cc, :], 1.0, None, ALU.add, r=["ysq"], w=["ysq"])
            C.recip(ysq[:, cc, :], ysq[:, cc, :], r=["ysq"], w=["ysq"])
            C.tt("pool", yT[:, cc, :], cyn[:, cc, :], ysq[:, cc, :], ALU.mult, r=["cyn%d" % cc, "ysq"], w=["yT%d" % cc])
        yield
        for g in range(4):
            q = g // 2
            C.mm(psG[:, g * 128:(g + 1) * 128], vbf[:, q * 128:(q + 1) * 128], wsT[:, g, :], r=["vbf", "wsT"], w=["psG"])
        for g in range(4):
            q = g // 2
            lo = (g % 2) * 64
            C.tt("dve", sgt[lo:lo + 64, q, :], psG[lo:lo + 64, g * 128:(g + 1) * 128], Bs[lo:lo + 64, q, :], ALU.add,
                 r=["Bs"], w=["psG", "sgt%d" % g])
            C.tt("pool", yT[lo:lo + 64, 4 + q, :], sgt[lo:lo + 64, q, :], usg[lo:lo + 64, q, :], ALU.mult,
                 r=["sgt%d" % g, "usg"], w=["yT%d" % (4 + q)])
        yield
        C.tt("pool", s2[:, :, 1:160], zdT[:, :, 1:160], zdT[:, :, 0:159], ALU.add, r=["zdT"], w=["s2"])
        C.tt("pool", s4[:, :, 3:160], s2[:, :, 3:160], s2[:, :, 1:158], ALU.add, r=["s2"], w=["s4"])
        C.tt("pool", s8[:, :, 7:160], s4[:, :, 7:160], s4[:, :, 3:156], ALU.add, r=["s4"], w=["s8"])
        C.tt("pool", s16[:, :, 15:160], s8[:, :, 15:160], s8[:, :, 7:152], ALU.add, r=["s8"], w=["s16"])
        rc = rc0 if j == 0 else rcg
        srcs = [(s2, "s2", 0, 0), (s4, "s4", 64, 0), (s8, "s8", 0, 1), (s16, "s16", 64, 1)]
        for gi, (sbuf_, sk, lo, q) in enumerate(srcs):
            C.tt("dve", plf[lo:lo + 64, q, :], sbuf_[lo:lo + 64, q, 32:160], rc[lo:lo + 64, q, :], ALU.mult,
                 r=[sk, "rc0", "rcg"], w=["plf%d" % gi])
            C.tt("dve", plb[lo:lo + 64, q, :], plf[lo:lo + 64, q, :], zdT[lo:lo + 64, q, 32:160], ALU.subtract,
                 r=["plf%d" % gi, "zdT"], w=["plb%d" % q])
        for q in range(2):
            C.mm(psG[:, q * 128:(q + 1) * 128], Wbd[:, q, :], plb[:, q, :], r=["Wbd", "plb%d" % q], w=["psG"])
        for q in range(2):
            C.act(yT[:, 6 + q, :], psG[:, q * 128:(q + 1) * 128], AF.Copy, r=["poolsc_t"], w=["psG", "yT%d" % (6 + q)],
                  scale=poolsc_t[:, q:q + 1])

    pcount = [0]

    def nextP():
        i = pcount[0] % 4
        pcount[0] += 1
        return Pb[i], "Pb%d" % i

    scount = [0]

    def nextS(nb=2):
        i = scount[0] % nb
        scount[0] += 1
        if i == 2:
            return psC, "psC"
        return psS[i], "psS%d" % i

    def nsa(j):
        L = (32 * j + 30) // 128

        def pipeline(chunks, depth=1):
            state = []
            for i, (A, B, Cc) in enumerate(chunks):
                while len(state) < min(len(chunks), i + depth + 1):
                    state.append(chunks[len(state)][0]())
                st = state[i]
                B(st)
                yield
                Cc(st)
                yield

        st_p = {}

        def mk_cmp(ci):
            def A():
                S_, sk = nextS()
                C.mm(S_[:, :], kcT[:, 32 + ci * 128:32 + (ci + 1) * 128], QC[:].rearrange("p h t -> p (h t)"),
                     r=["kcT", "QC"], w=[sk])
                return S_, sk

            def B(st):
                S_, sk = st
                P_, pk = Pb[ci], "Pb%d" % ci
                C.act(P_[:].rearrange("p h t -> p (h t)"), S_[:, :], AF.Exp, w=[sk, pk])
                if ci == L:
                    C.tt("dve", P_[:], P_[:], cmask[:, j % 4:j % 4 + 1, :].to_broadcast([128, 4, 128]), ALU.mult,
                         r=[pk, "cmask"], w=[pk])
                elif ci == L - 1 and j % 4 == 0:
                    C.tt("dve", P_[:], P_[:], cmask[:, 4:5, :].to_broadcast([128, 4, 128]), ALU.mult,
                         r=[pk, "cmask"], w=[pk])

            def Cc(st):
                pass
            return A, B, Cc

        yield from pipeline([mk_cmp(ci) for ci in range(L + 1)])
        for hp in range(2):
            for ci in range(L + 1):
                for h in (2 * hp, 2 * hp + 1):
                    c0 = (h % 2) * 193
                    C.mm(psC[:, c0:c0 + 193], Pb[ci][:, h, :], rhsC[:, ci, :], start=(ci == 0 and h % 2 == 0), stop=(ci == L),
                         r=["Pb%d" % ci, "rhsC"], w=["psC"], skip=True)
            yield
            for h in (2 * hp, 2 * hp + 1):
                c0 = (h % 2) * 193
                C.ts("dve", den[:, h:h + 1], psC[:, c0 + 192:c0 + 193], 1e-30, None, ALU.max, w=["psC", "den%d" % hp])
            C.recip(rden[:, 2 * hp:2 * hp + 2], den[:, 2 * hp:2 * hp + 2], r=["den%d" % hp], w=["rden%d" % hp])
            yield
            for h in (2 * hp, 2 * hp + 1):
                c0 = (h % 2) * 193
                if h == 0:
                    C.ts("dve", imp[:], psC[:, c0 + 64:c0 + 192], rden[:, 0:1], None, ALU.mult, r=["rden0"],
                         w=["psC", "imp"])
                else:
                    C.stt(imp[:], psC[:, c0 + 64:c0 + 192], rden[:, h:h + 1], imp[:], ALU.mult, ALU.add,
                          r=["rden%d" % hp, "imp"], w=["psC", "imp"])
            C.tt("dve", gsc[:, 2 * hp:2 * hp + 2], rden[:, 2 * hp:2 * hp + 2], gts[:, 6 * hp:6 * hp + 6:3], ALU.mult,
                 r=["rden%d" % hp, "gts"], w=["gscc%d" % hp])
            for h in (2 * hp, 2 * hp + 1):
                c0 = (h % 2) * 193
                C.ts("dve", Ynsa[:, h * 64:(h + 1) * 64], psC[:, c0:c0 + 64], gsc[:, h:h + 1], None, ALU.mult,
                     r=["gscc%d" % hp], w=["psC", "Ynsa"])
            yield
        C.tt("dve", imp2[:], imp[:], selAt[:], ALU.mult, r=["imp", "selAt"], w=["imp2"])
        C.tt("dve", imp2[:], imp2[:], selBt[:], ALU.add, r=["imp2", "selBt"], w=["imp2"])
        yield
        C.S.op("dve", lambda e: e.max(out=top8[:, 0:8], in_=imp2[:]), ["imp2"], ["top8a"])
        C.S.op("dve", lambda e: e.match_replace(out=impw[:], in_to_replace=top8[:, 0:8], in_values=imp2[:],
                                                imm_value=-3e38), ["imp2", "top8a"], ["impw"])
        yield
        C.S.op("dve", lambda e: e.max(out=top8[:, 8:16], in_=impw[:]), ["impw"], ["top8b"])
        C.ts("dve", selt1[:], imp2[:], top8[:, 15:16], None, ALU.is_ge, r=["imp2", "top8b"], w=["selt1"])
        C.ts("dve", selt2[:], imp2[:], -1e29, -NEG, ALU.is_gt, ALU.mult, r=["imp2"], w=["selt2"])
        yield
        C.tt("dve", selt1[:], selt1[:], selt2[:], ALU.mult, r=["selt1", "selt2"], w=["selt1"])
        C.ts("dve", selbf[:], selt1[:], NEG, None, ALU.add, r=["selt1"], w=["selbf"])
        yield

        wl = list(range(max(0, 4 * j - 4), 4 * j + 4))

        def mk_win(wi, m):
            def A():
                S_, sk = nextS(3)
                C.mm(S_[:, :], Kw[:, m % 12, :], QW[:].rearrange("p h t -> p (h t)"),
                     r=["Kw%d" % (m % 12), "Kw", "QW"], w=[sk])
                return S_, sk

            def B(st):
                S_, sk = st
                P_, pk = nextP()
                C.act(P_[:].rearrange("p h t -> p (h t)"), S_[:, :], AF.Exp, w=[sk, pk])
                C.tt("dve", P_[:], P_[:], wmask[:, 4 + m - 4 * j:5 + m - 4 * j, :].to_broadcast([128, 4, 128]), ALU.mult,
                     r=[pk, "wmask"], w=[pk])
                st_p[("w", wi)] = (P_, pk)

            def Cc(st):
                P_, pk = st_p[("w", wi)]
                for h in range(4):
                    C.mm(psWin[:, h * 65:(h + 1) * 65], P_[:, h, :], Vw[:, m % 12, :], start=(wi == 0 and h == 0),
                         stop=(wi == len(wl) - 1), r=[pk, "Vw%d" % (m % 12), "Vw"], w=["psWin"], skip=True)
            return A, B, Cc

        scount[0] = 0
        yield from pipeline([mk_win(wi, m) for wi, m in enumerate(wl)], depth=2)
        C.tr(psT[:, 512:640], selbf[:], ident[:], r=["selbf", "ident"], w=["psT"])
        yield
        C.cp("dve", QR[64:128, 0, :, :], psT[0:64, 512:640].unsqueeze(1).to_broadcast([64, 4, 128]), w=["psT", "QRb"])
        C.cp("act", QR[64:128, 1, :, :], psT[64:128, 512:640].unsqueeze(1).to_broadcast([64, 4, 128]), w=["psT", "QRb"])
        yield

        nsel = 4 * j + 4

        def mk_sel(m):
            def A():
                S_, sk = nextS(3)
                v = 0 if m < 32 else 1
                C.mm(S_[:, :], Kaug[:, m * 128:(m + 1) * 128], QR[:, v, :, :].rearrange("p h t -> p (h t)"),
                     r=["Kaug%d" % m, "Kaug_ind", "QRq", "QRb"], w=[sk])
                return S_, sk

            def B(st):
                S_, sk = st
                P_, pk = nextP()
                C.act(P_[:].rearrange("p h t -> p (h t)"), S_[:, :], AF.Exp, w=[sk, pk])
                if m >= 4 * j:
                    C.tt("dve", P_[:], P_[:], wmask[:, 4 + m - 4 * j:5 + m - 4 * j, :].to_broadcast([128, 4, 128]), ALU.mult,
                         r=[pk, "wmask"], w=[pk])
                st_p[("s", m)] = (P_, pk)

            def Cc(st):
                P_, pk = st_p[("s", m)]
                for h in range(4):
                    C.mm(psSel[:, h * 65:(h + 1) * 65], P_[:, h, :], Vs[:, m, :], start=(m == 0 and h == 0),
                         stop=(m == nsel - 1), r=[pk, "Vs%d" % m, "Vs"], w=["psSel"], skip=True)
            return A, B, Cc

        yield from pipeline([mk_sel(m) for m in range(nsel)], depth=2)
        scount[0] = 0
        C.cp("dve", den[:, 4:8], psSel[:, 64:260:65], w=["psSel", "den"])
        C.cp("dve", den[:, 8:12], psWin[:, 64:260:65], w=["psWin", "den"])
        C.recip(rden[:, 4:12], den[:, 4:12], r=["den"], w=["rden2"])
        C.tt("dve", gsc[:, 4:8], rden[:, 4:8], gts[:, 1:12:3], ALU.mult, r=["rden2", "gts"], w=["gsc1"])
        C.tt("dve", gsc[:, 8:12], rden[:, 8:12], gts[:, 2:12:3], ALU.mult, r=["rden2", "gts"], w=["gsc1"])
        for h in range(4):
            C.stt(Ynsa[:, h * 64:(h + 1) * 64], psSel[:, h * 65:h * 65 + 64], gsc[:, 4 + h:5 + h], Ynsa[:, h * 64:(h + 1) * 64],
                  ALU.mult, ALU.add, r=["gsc1", "Ynsa"], w=["psSel", "Ynsa"])
        for h in range(4):
            C.stt(Ynb[:, h * 64:(h + 1) * 64], psWin[:, h * 65:h * 65 + 64], gsc[:, 8 + h:9 + h], Ynsa[:, h * 64:(h + 1) * 64],
                  ALU.mult, ALU.add, r=["gsc1", "Ynsa"], w=["psWin", "Ynb"])
        yield
        for q in range(2):
            C.tr(psT[:, 512 + q * 128:512 + (q + 1) * 128], Ynb[:, q * 128:(q + 1) * 128], ident[:], r=["Ynb", "ident"], w=["psT"])
        yield
        C.cp("act", yT[:, 2:4, :], psT[:, 512:768].rearrange("p (q t) -> p q t", q=2), w=["psT", "yT2", "yT3"])
        yield

    def outproj(j):
        yk = ["yT%d" % f for f in range(8)]
        for half in range(2):
            for f in range(8):
                C.mm(psG[:, :], yT[:, f, :], wo[:, f, half * 512:(half + 1) * 512], start=(f == 0), stop=(f == 7),
                     r=yk + ["wo"], w=["psG"])
            yield
            C.cp("dve", yo[:, half * 512:(half + 1) * 512], psG[:, :], w=["psG", "yo"])
            yield
        C.act(junk[:, :], yo[:], AF.Square, r=["yo"], w=["junk", "oss"], accum=oss[:, 0:1])
        yield
        C.act(oss[:, 1:2], oss[:, 0:1], AF.Ln, r=["oss", "epsb"], w=["oss1"], scale=1.0 / D, bias=epsb[:, 0:1])
        C.act(oss[:, 2:3], oss[:, 1:2], AF.Exp, r=["oss1"], w=["oss2"], scale=-0.5)
        C.stt(yo[:], yo[:], oss[:, 2:3], gpost_bc[:], ALU.mult, ALU.mult, r=["yo", "oss2", "gpost_bc"], w=["yo"])
        C.tt("pool", yo[:], yo[:], xo[:], ALU.add, r=["yo", "xo"], w=["yo"])
        C.dma(xm_own[j, :, :], yo[:], r=["yo"])
        C.dma(xm_tail[2 * j:2 * j + 2, :], yo[126:128, :], r=["yo"])
        yield

    def stream_b(j):
        for m in range(4 * j, 4 * j + 4):
            yield from kv_tile(m)
        yield from compress(j)

    def stream_a(j):
        yield from local(j)
        yield from nsa(j)
        yield from outproj(j)

    for _ in stream_b(0):
        pass
    for j in range(nslots):
        A_ = stream_a(j)
        B_ = stream_b(j + 1) if j + 1 < nslots else None
        doneA = doneB = B_ is None and False
        doneB = B_ is None
        while not doneA:
            try:
                next(A_)
            except StopIteration:
                doneA = True
            if not doneB:
                try:
                    next(B_)
                except StopIteration:
                    doneB = True
        while not doneB:
            try:
                next(B_)
            except StopIteration:
                doneB = True
    C.end_phase()


def ffn_phase(C, T, l):
    C.begin_phase("_f%d" % l)
    nc = C.nc
    xm_own = T["xm_own"]; tail_g = T["tail_g_%d" % l]; out_tile = T["ffn_out_tile"]
    w_up, w_dn, g2, gpost, cwd = (T[k + "_%d" % l] for k in ("w_up", "w_dn", "g2", "gpost2", "cwd"))
    identd = T["identd"]; selm2d = T["selm2"]

    sb = C.sb
    wu = sb("wu", [128, 8, 2 * FF], BF16)
    wd = sb("wd", [128, 22, D], BF16)
    stage = [sb("stage%d" % i, [128, 1024]) for i in range(4)]
    g2_t = sb("g2_t", [128, 8])
    gpost_bc = sb("gpost_bc", [128, D])
    cw = sb("cw", [128, NFC, 4])
    ident = sb("ident", [128, 128], BF16)
    epsb = sb("epsb", [128, 1])
    xs = [sb("xs%d" % i, [128, D]) for i in range(2)]
    xh = sb("xh", [8, D])
    selm2 = sb("selm2", [8, 2], BF16)
    hn = sb("hn", [128, D], BF16)
    hnh = sb("hnh", [8, D], BF16)
    junk = sb("junk", [128, D], BF16)
    ssq = sb("ssq", [128, 4]); rstd = sb("rstd", [128, 4])
    h2T = sb("h2T", [128, 8, 2, 130], BF16)
    actT = sb("actT", [128, 22, 2, 128], BF16)
    gc = [sb("gc%d" % i, [128, 2, 128]) for i in range(2)]
    uc = [sb("uc%d" % i, [128, 2, 128]) for i in range(2)]
    gg = [sb("gg%d" % i, [128, 2, 128]) for i in range(2)]
    yo = sb("yo", [128, D])
    oss = sb("oss", [128, 4])

    psU = [C.ps("psU%d" % i, [128, 512]) for i in range(4)]
    psD = [C.ps("psD%d" % i, [128, 512]) for i in range(2)]
    psT = C.ps("psT", [128, 1024], BF16)
    psH = C.ps("psH", [128, 512])

    C.dma(stage[1][0:8, 0:2], selm2d[:, :], w=["stage1"])
    C.cp("dve", selm2[:], stage[1][0:8, 0:2], r=["stage1"], w=["selm2"])
    C.dma(g2_t[:], g2[:, :], w=["gscale"])
    C.dma(gpost_bc[:], gpost.partition_broadcast(128), w=["gpost_bc"])
    C.dma(cw[:], cwd[:, :, :], w=["cw"])
    C.dma(stage[0][:, 0:128], identd[:, :], w=["stage0"])
    C.cp("dve", ident[:], stage[0][:, 0:128], r=["stage0"], w=["ident"])
    C.memset("dve", epsb[:, 0:1], RMS_EPS, w=["epsb"])
    par = 0
    for kc in range(8):
        for q in range(6):
            c0 = q * 1024
            c1 = min(2 * FF, c0 + 1024)
            s = stage[par % 4]; sk = "stage%d" % (par % 4); par += 1
            load_convert(C, wu[:, kc, c0:c1], w_up[kc * 128:(kc + 1) * 128, c0:c1], s, sk, "wu", c1 - c0,
                         scale_ap=g2_t[:, kc:kc + 1])
    for f in range(22):
        s = stage[par % 4]; sk = "stage%d" % (par % 4); par += 1
        load_convert(C, wd[:, f, :], w_dn[f * 128:(f + 1) * 128, :], s, sk, "wd", D, parity=f)

    def rms_to_bf(x_ap, xk, out_bf, ok, np_, col):
        C.act(junk[0:np_, :], x_ap, AF.Square, r=[xk], w=["junk", "ssq%d" % col], accum=ssq[0:np_, col:col + 1])
        C.act(rstd[0:np_, col:col + 1], ssq[0:np_, col:col + 1], AF.Ln, r=["ssq%d" % col, "epsb"], w=["rstd%d" % col],
              scale=1.0 / D, bias=epsb[0:np_, 0:1])
        C.act(rstd[0:np_, col:col + 1], rstd[0:np_, col:col + 1], AF.Exp, r=["rstd%d" % col], w=["rstd%d" % col], scale=-0.5)
        C.act(out_bf, x_ap, AF.Copy, r=[xk, "rstd%d" % col], w=[ok], scale=rstd[0:np_, col:col + 1])

    ucount = [0]
    for grp in range(NS // 2):
        for sl in range(2):
            j = grp * 2 + sl
            C.dma(xs[sl][:], xm_own[j, :, :], w=["xs%d" % sl])
            for cand in range(4):
                m_ = 4 * j - 1 + cand
                if m_ < 0:
                    C.memset("pool", xh[0:2, :], 0.0, w=["xh"])
                else:
                    row0 = ((m_ % 4) * NS + m_ // 4) * 2
                    C.dma(xh[cand * 2:(cand + 1) * 2, :], tail_g[row0:row0 + 2, :], w=["xh"])
            rms_to_bf(xs[sl][:], "xs%d" % sl, hn[:], "hn", 128, 0)
            rms_to_bf(xh[:], "xh", hnh[:], "hnh", 8, 1)
            for kc in range(8):
                C.tr(psT[:, kc * 128:(kc + 1) * 128], hn[:, kc * 128:(kc + 1) * 128], ident[:], r=["hn", "ident"], w=["psT"])
            C.cp("act", h2T[:, :, sl, 2:130], psT[:, :].rearrange("p (k t) -> p k t", k=8), w=["psT", "h2T"])
            for kc in range(8):
                C.mm(psH[:, kc * 2:(kc + 1) * 2], hnh[:, kc * 128:(kc + 1) * 128], selm2[:], r=["hnh", "selm2"], w=["psH"])
            C.cp("act", h2T[:, :, sl, 0:2], psH[:, 0:16].rearrange("p (k t) -> p k t", k=8), w=["psH", "h2T"])
        def up_mm(fc):
            banks = []
            for ch in (fc, 22 + fc):
                bi = ucount[0] % 4
                ucount[0] += 1
                bank = psU[bi]
                bk = "psU%d" % bi
                for kc in range(8):
                    C.mm(bank[:, 0:260], wu[:, kc, ch * 128:(ch + 1) * 128],
                         h2T[:, kc, :, :].rearrange("p s t -> p (s t)"),
                         start=(kc == 0), stop=(kc == 7), r=["wu", "h2T"], w=[bk])
                banks.append((bank, bk))
            return banks

        def conv_ops(fc, banks):
            p = fc % 2
            for (bank, bk), ch, dst, dk in ((banks[0], fc, gc[p], "gc%d" % p), (banks[1], 22 + fc, uc[p], "uc%d" % p)):
                bv = bank[:, 0:260].rearrange("p (s t) -> p s t", s=2)
                C.act(dst[:], bv[:, :, 2:130], AF.Identity, r=["cw"], w=[bk, dk], scale=cw[:, ch, 2:3], bias=cw[:, ch, 3:4])
                C.stt(dst[:], bv[:, :, 1:129], cw[:, ch, 1:2], dst[:], ALU.mult, ALU.add, r=["cw", dk], w=[bk, dk])
                C.stt(dst[:], bv[:, :, 0:128], cw[:, ch, 0:1], dst[:], ALU.mult, ALU.add, r=["cw", dk], w=[bk, dk])

        def gate_ops(fc):
            p = fc % 2
            C.act(gg[p][:], gc[p][:], AF.Gelu_apprx_tanh, r=["gc%d" % p], w=["gg%d" % p])
            C.tt("pool", actT[:, fc, :, :], gg[p][:], uc[p][:], ALU.mult,
                 r=["gg%d" % p, "uc%d" % p], w=["actT"])

        nxt = up_mm(0)
        for fc in range(22):
            cur = nxt
            if fc + 1 < 22:
                nxt = up_mm(fc + 1)
            conv_ops(fc, cur)
            if fc >= 1:
                gate_ops(fc - 1)
            if fc == 12 and grp >= 1 and T.get("post_group") is not None:
                T["post_group"](grp - 1)
        gate_ops(21)
        for sl in range(2):
            j = grp * 2 + sl
            for half in range(2):
                for f in range(22):
                    C.mm(psD[half][:, :], actT[:, f, sl, :], wd[:, f, half * 512:(half + 1) * 512],
                         start=(f == 0), stop=(f == 21), r=["actT", "wd"], w=["psD%d" % half])
                C.cp("dve", yo[:, half * 512:(half + 1) * 512], psD[half][:, :], w=["psD%d" % half, "yo"])
            C.act(junk[:, :], yo[:], AF.Square, r=["yo"], w=["junk", "oss"], accum=oss[:, 0:1])
            C.act(oss[:, 1:2], oss[:, 0:1], AF.Ln, r=["oss", "epsb"], w=["oss1"], scale=1.0 / D, bias=epsb[:, 0:1])
            C.act(oss[:, 2:3], oss[:, 1:2], AF.Exp, r=["oss1"], w=["oss2"], scale=-0.5)
            C.stt(yo[:], yo[:], oss[:, 2:3], gpost_bc[:], ALU.mult, ALU.mult, r=["yo", "oss2", "gpost_bc"], w=["yo"])
            C.tt("pool", yo[:], yo[:], xs[sl][:], ALU.add, r=["yo", "xs%d" % sl], w=["yo"])
            C.dma(out_tile(j)[:, :], yo[:], r=["yo"], w=["ffnout%d" % j])
    if T.get("post_group") is not None:
        T["post_group"](NS // 2 - 1)
    C.end_phase()


OFF_Q, OFF_KV, OFF_G, OFF_C, OFF_D = 512, 768, 1152, 1164, 1676


def _consts():
    c = {}
    half = 32
    inv = (10000.0 ** (-np.arange(half, dtype=np.float32) * 2.0 / 64)).astype(np.float32)
    ang = np.arange(SEQ, dtype=np.float32)[:, None] * inv[None, :]
    cos = np.cos(ang).astype(np.float32).T
    sin = np.sin(ang).astype(np.float32).T
    c64 = np.concatenate([cos, cos], 0)
    s64 = np.concatenate([-sin, sin], 0)
    c["ropeC"] = np.ascontiguousarray(np.concatenate([c64, c64], 0))
    c["ropeS"] = np.ascontiguousarray(np.concatenate([s64, s64], 0))
    blk = np.arange(SEQ) // 64
    c["indp"] = (np.arange(64)[:, None] == (blk % 64)[None, :]).astype(np.float32)
    c["identd"] = np.eye(128, dtype=np.float32)
    n = np.arange(512)[:, None]
    b = np.arange(128)[None, :]
    off = n - 4 * b
    M = np.where((off == -1) | (off == 3), 1.0, np.where((off >= 0) & (off <= 2), 2.0, 0.0)).astype(np.float32)
    M[511, :] = 0.0
    c["impM"] = np.ascontiguousarray(M.reshape(4, 128, 128))
    c["trild"] = (np.arange(128)[:, None] <= np.arange(128)[None, :]).astype(np.float32)
    rg = np.zeros((128, 2, 128), np.float32)
    for gi, w in enumerate((2, 4, 8, 16)):
        rg[(gi % 2) * 64:(gi % 2) * 64 + 64, gi // 2, :] = 1.0 / w
    c["rcntg"] = rg
    return c


def _core_consts(cidx):
    c = cidx
    o = {}
    k = np.arange(128)[:, None]
    t = np.arange(128)[None, :]
    wm = np.zeros((8, 128, 128), np.float32)
    for q in range(8):
        mm = q - 4
        rel = mm - c
        if rel == 0:
            wm[q] = (k <= t)
        elif rel == -4:
            wm[q] = (k > t)
        elif -4 < rel < 0:
            wm[q] = 1.0
    o["wmaskd"] = wm
    cm = np.zeros((5, 128, 128), np.float32)
    for jm in range(4):
        nprime = k - 32 * jm
        cm[jm] = (16 * nprime + 31 <= 128 * c + t)
    cm[4] = 1.0
    if c == 0:
        cm[4][127, :15] = 0.0
    o["cmaskd"] = cm
    A = np.zeros((NS, 128, 128), np.float32)
    B = np.zeros((NS, 128, 128), np.float32)
    tt = np.arange(128)[:, None]
    bb = np.arange(128)[None, :]
    for j in range(NS):
        i = 4 * j + c
        cur = 2 * i + (tt >= 64)
        valid = bb <= cur
        forced = (bb == 0) | (bb == cur) | (bb == cur - 1)
        A[j] = (valid & ~forced)
        B[j] = np.where(valid, np.where(forced, 1e6, 0.0), -1e30)
    o["selA"] = A
    o["selB"] = B
    inv = (10000.0 ** (-np.arange(32, dtype=np.float32) * 2.0 / 64)).astype(np.float32)
    qC = np.zeros((NS, 128, 128), np.float32)
    qS = np.zeros((NS, 128, 128), np.float32)
    for j in range(NS):
        pos = (128 * (4 * j + c) + np.arange(128)).astype(np.float32)
        ang = pos[:, None] * inv[None, :]
        cs = np.cos(ang).astype(np.float32).T * np.float32(0.125)
        sn = np.sin(ang).astype(np.float32).T * np.float32(0.125)
        c64 = np.concatenate([cs, cs], 0)
        s64 = np.concatenate([-sn, sn], 0)
        qC[j] = np.concatenate([c64, c64], 0)
        qS[j] = np.concatenate([s64, s64], 0)
    o["qC"] = qC
    o["qS"] = qS
    r0 = np.zeros((128, 2, 128), np.float32)
    for gi, w in enumerate((2, 4, 8, 16)):
        pos = 128 * c + np.arange(128)
        cnt = np.minimum(pos + 1, w).astype(np.float32)
        r0[(gi % 2) * 64:(gi % 2) * 64 + 64, gi // 2, :] = (1.0 / cnt)[None, :]
    o["rcnt0"] = r0
    sm = np.zeros((128, 32), np.float32)
    sm[c * 32:(c + 1) * 32, :] = np.eye(32, dtype=np.float32)
    o["selm"] = sm
    sm2 = np.zeros((8, 2), np.float32)
    sm2[c * 2:(c + 1) * 2, :] = np.eye(2, dtype=np.float32)
    o["selm2"] = sm2
    return o


def _sw(idx):
    return np.concatenate([idx[32:], idx[:32]])


def _mixer_weights(P, l):
    w_in = P["w_in"][l]
    kv = lambda s: np.arange(OFF_KV + 64 * s, OFF_KV + 64 * s + 64)
    cols_kv = np.concatenate([kv(0), kv(1), kv(2), kv(4), _sw(kv(2)), _sw(kv(4)), kv(3), kv(5)])
    qcols = np.arange(OFF_Q, OFF_Q + 256)
    qsw = np.concatenate([_sw(qcols[h * 64:(h + 1) * 64]) for h in range(4)])
    cols_loc = np.concatenate([np.arange(0, 512), np.arange(OFF_D, OFF_D + 256), np.arange(OFF_C, OFF_C + 256),
                               qcols, qsw, np.arange(OFF_C + 256, OFF_C + 512), np.arange(OFF_G, OFF_G + 12)])
    o = {}
    o["w_kv"] = np.ascontiguousarray(w_in[:, cols_kv])
    o["w_loc"] = np.ascontiguousarray(w_in[:, cols_loc])
    o["w_out"] = np.ascontiguousarray(P["w_out"][l])
    o["gpre"] = np.ascontiguousarray(P["norm_mix_pre"][l].reshape(8, 128).T)
    o["gpost"] = np.ascontiguousarray(P["norm_mix_post"][l].reshape(1, D))
    o["convw"] = np.ascontiguousarray(P["conv_dw_w"][l].reshape(31, 2, 128).transpose(2, 1, 0))
    cv = np.stack([P["conv_dw_b"][l], P["conv_ln_g"][l], P["conv_ln_b"][l]], -1)
    o["convv"] = np.ascontiguousarray(cv.reshape(2, 128, 3).transpose(1, 0, 2))
    w1k = P["nsa_ck_w1"][l].reshape(32, 64, 64).transpose(1, 0, 2).reshape(64, 2048)
    w1v = P["nsa_cv_w1"][l].reshape(32, 64, 64).transpose(1, 0, 2).reshape(64, 2048)
    o["w1d"] = np.ascontiguousarray(np.concatenate([w1k, w1v], 0))
    o["w2d"] = np.ascontiguousarray(np.concatenate([P["nsa_ck_w2"][l], P["nsa_cv_w2"][l]], 1))
    o["ped"] = np.ascontiguousarray(np.concatenate([P["nsa_pe_k"][l].T, P["nsa_pe_v"][l].T], 0))
    o["sgv"] = np.ascontiguousarray(np.stack([P["sgu_ln_g"][l], P["sgu_ln_b"][l]], 0))
    o["sgw"] = np.ascontiguousarray(P["sgu_w"][l].transpose(2, 0, 1))
    o["sgb"] = np.ascontiguousarray(P["sgu_b"][l])
    pw = np.zeros((128, 2, 128), np.float32)
    for gi in range(4):
        lo = (gi % 2) * 64
        pw[lo:lo + 64, gi // 2, lo:lo + 64] = P["pool_w"][l][gi]
    o["poolw"] = pw
    o["poolsc"] = np.ascontiguousarray(P["pool_scale"][l].reshape(2, 128).T)
    return o


def _ffn_weights(P, l):
    o = {}
    o["w_up"] = np.ascontiguousarray(P["ffn_up"][l])
    o["w_dn"] = np.ascontiguousarray(P["ffn_down"][l])
    o["g2"] = np.ascontiguousarray(P["norm_ffn_pre"][l].reshape(8, 128).T)
    o["gpost2"] = np.ascontiguousarray(P["norm_ffn_post"][l].reshape(1, D))
    cw = np.concatenate([P["ffn_conv_w"][l], P["ffn_conv_b"][l][None, :]], 0)
    o["cwd"] = np.ascontiguousarray(cw.reshape(4, NFC, 128).transpose(2, 1, 0))
    return o


def _own_tiles(xb, c, halo):
    pad = np.concatenate([np.zeros((halo, D), np.float32), xb], 0)
    out = np.empty((NS, halo + 128, D), np.float32)
    for j in range(NS):
        i = 4 * j + c
        out[j] = pad[128 * i:128 * i + 128 + halo]
    return out


def _scatter_own(res_list, key):
    x = np.empty((NB, SEQ, D), np.float32)
    for core in range(8):
        b, c = divmod(core, 4)
        r = res_list[core][key]
        for j in range(NS):
            i = 4 * j + c
            x[b, 128 * i:128 * (i + 1)] = r[j]
    return x


def _chunk_row(m):
    r, sl = m % 4, m // 4
    return sl // 2, r * 256 + (sl % 2) * 128


def build_all(nslots=NS):
    C = Ctx()
    T = {}
    xg0 = C.dram_in("xg0", [8 * 1024, D])
    xown0 = C.dram_in("xown0", [NS, 128, D])
    for k, shp in MIX_C_SHAPES.items():
        T[k] = C.dram_in(k, shp)
    for l in range(2):
        for k, shp in MIX_W_SHAPES.items():
            T["%s_%d" % (k, l)] = C.dram_in("%s_%d" % (k, l), shp)
        for k, shp in FFN_W_SHAPES.items():
            T["%s_%d" % (k, l)] = C.dram_in("%s_%d" % (k, l), shp)
    out = C.dram_out("out", [NS, 128, D])
    xm_own = C.dram_int("xm_own", [NS, 128, D])
    tails = [C.dram_int("xm_tail%d" % l, [NS * 2, D]) for l in range(2)]
    tail_g = [C.dram_int("tail_g%d" % l, [4 * NS * 2, D]) for l in range(2)]
    x1_own = [C.dram_int("x1_own%d" % g, [256, D]) for g in range(8)]
    x1_g = [C.dram_int("x1_g%d" % g, [1024, D]) for g in range(8)]
    RG = [[0, 1, 2, 3], [4, 5, 6, 7]]

    def gather(src, dst):
        C.S.collective(lambda e: e.collective_compute("AllGather", ALU.bypass, replica_groups=RG,
                                                      ins=[src.opt()], outs=[dst.opt()]))

    def xg_tile0(m):
        g, ro = _chunk_row(m)
        return xg0[g * 1024 + ro:g * 1024 + ro + 128, :]

    def xg_tile1(m):
        g, ro = _chunk_row(m)
        return x1_g[g][ro:ro + 128, :]

    for l in range(2):
        T["xg_tile"] = xg_tile0 if l == 0 else xg_tile1
        T["xown_tile"] = (lambda j: xown0[j, :, :]) if l == 0 else (lambda j: x1_own[j // 2][(j % 2) * 128:(j % 2) * 128 + 128, :])
        T["xm_own"] = xm_own
        T["xm_tail"] = tails[l]
        mixer_phase(C, T, l, nslots=nslots)
        gather(tails[l], tail_g[l])
        T["tail_g_%d" % l] = tail_g[l]
        T["ffn_out_tile"] = (lambda j: x1_own[j // 2][(j % 2) * 128:(j % 2) * 128 + 128, :]) if l == 0 else (lambda j: out[j, :, :])
        if l == 0:
            def post_group(g):
                C.S.cc_op(lambda e: e.collective_compute("AllGather", ALU.bypass, replica_groups=RG,
                                                         ins=[x1_own[g].opt()], outs=[x1_g[g].opt()]),
                          reads=["ffnout%d" % (2 * g), "ffnout%d" % (2 * g + 1)], writes=["x1g%d" % g])
            T["post_group"] = post_group
        else:
            T["post_group"] = None
        ffn_phase(C, T, l)
    info = C.finish()
    return C.nc, info


_CACHE = {}


def kernel(**inputs):
    P = {k: np.asarray(v, dtype=np.float32) for k, v in inputs.items()}
    x = P["x"]
    if "nc" not in _CACHE:
        _CACHE["nc"] = build_all()[0]
    nc = _CACHE["nc"]
    consts = _consts()
    shared = {k: consts[k] for k in MIX_C_SHAPES if k in consts}
    for l in range(2):
        for k, v in _mixer_weights(P, l).items():
            shared["%s_%d" % (k, l)] = v
        for k, v in _ffn_weights(P, l).items():
            shared["%s_%d" % (k, l)] = v
    in_maps = []
    for core in range(8):
        b, c = divmod(core, 4)
        m = dict(shared)
        m.update(_core_consts(c))
        xt = x[b].reshape(8, 2, 4, 128, D)
        m["xg0"] = np.ascontiguousarray(xt.transpose(0, 2, 1, 3, 4).reshape(8 * 1024, D))
        m["xown0"] = np.ascontiguousarray(x[b].reshape(NS, 4, 128, D)[:, c])
        in_maps.append(m)
    res = run_bass_kernel_spmd(nc, in_maps, core_ids=list(range(8)))
    return _scatter_own(res.results, "out").astype(np.float32)
```

```python
import numpy as np
from contextlib import ExitStack
import concourse.bass as bass
import concourse.mybir as mybir
from concourse.bass_utils import run_bass_kernel_spmd

F32 = mybir.dt.float32
BF16 = mybir.dt.bfloat16
AF = mybir.ActivationFunctionType
ALU = mybir.AluOpType
AX = mybir.AxisListType

D = 1024
SEQ = 8192
NB = 2
NT = 64
NS = 16
FF = 2816
NFC = 44
RMS_EPS = 1e-6
LN_EPS = 1e-5
NEG = -30000.0
GELU_C = 1.5957691216057308


class Sched:
    NDS = 8

    def __init__(self, nc, sems):
        self.nc = nc
        self.eng = {"pe": nc.tensor, "act": nc.scalar, "dve": nc.vector,
                    "pool": nc.gpsimd, "sp": nc.sync}
        self.ops = []
        self.last_writer = {}
        self.readers = {}
        self.sems = sems

    def cc_op(self, fn, reads=(), writes=()):
        idx = self.op("pool", fn, reads, writes)
        self.ops[idx].append("cc")
        return idx

    def op(self, engine, fn, reads=(), writes=()):
        idx = len(self.ops)
        is_dma = engine == "sp"
        deps = {}

        def add(d, kind):
            if d is None:
                return
            if deps.get(d) is None or kind == "raw":
                deps[d] = kind

        for k in reads:
            add(self.last_writer.get(k), "raw")
        for k in writes:
            add(self.last_writer.get(k), "waw")
            for r in self.readers.get(k, ()):
                add(r, "war")
        for k in writes:
            self.last_writer[k] = idx
            self.readers[k] = []
        for k in reads:
            if k not in writes:
                lst = self.readers.setdefault(k, [])
                if not is_dma:
                    lst[:] = [r_ for r_ in lst if self.ops[r_][0] != engine]
                lst.append(idx)
        keep = []
        for d, kind in deps.items():
            de = self.ops[d][0]
            if de == engine and not is_dma:
                if engine == "pe":
                    continue
            keep.append(d)
        self.ops.append([engine, fn, keep, is_dma])
        for d in keep:
            self.ops[d][3] = True
        return idx

    def _init_emit_state(self):
        self.counts = {e: 0 for e in self.eng}
        self.dma_counts = [0] * self.NDS
        self.n_dma = 0
        self.waited = {e: {} for e in self.eng}
        self.sigval = {}
        self.emitted = 0
        self.cc_count = 0

    def flush(self):
        if not hasattr(self, "counts"):
            self._init_emit_state()
        last = {}
        for idx in range(self.emitted, len(self.ops)):
            if len(self.ops[idx]) == 4:
                last[self.ops[idx][0]] = idx
        for e, idx in last.items():
            if e != "sp" and len(self.ops[idx]) == 4:
                self.ops[idx][3] = True
        for idx in range(self.emitted, len(self.ops)):
            e, fn, deps, signal = self.ops[idx][:4]
            is_cc = len(self.ops[idx]) > 4
            eng = self.eng[e]
            need = {}
            for d in deps:
                sem, val = self.sigval[d]
                if need.get(id(sem), (None, 0))[1] < val:
                    need[id(sem)] = (sem, val)
            if e == "sp":
                slot = self.n_dma % self.NDS
                if self.dma_counts[slot] > 0:
                    sem = self.sems["dma"][slot]
                    val = self.dma_counts[slot] * 16
                    if need.get(id(sem), (None, 0))[1] < val:
                        need[id(sem)] = (sem, val)
            for key, (sem, val) in need.items():
                if self.waited[e].get(key, 0) >= val:
                    continue
                eng.wait_ge(sem, val)
                self.waited[e][key] = val
            ins = fn(eng)
            if is_cc:
                self.cc_count += 1
                ins.then_inc(self.sems["cc"])
                self.sigval[idx] = (self.sems["cc"], self.cc_count)
                continue
            if e == "sp":
                slot = self.n_dma % self.NDS
                self.n_dma += 1
                self.dma_counts[slot] += 1
                sem = self.sems["dma"][slot]
                ins.then_inc(sem, 16)
                self.sigval[idx] = (sem, self.dma_counts[slot] * 16)
            elif signal:
                self.counts[e] += 1
                ins.then_inc(self.sems[e], 1)
                self.sigval[idx] = (self.sems[e], self.counts[e])
        self.emitted = len(self.ops)

    def barrier(self, engines=None):
        self.flush()
        for e, eng in self.eng.items():
            for x in ("pe", "act", "dve", "pool"):
                if x != e and self.counts[x] > 0 and self.waited[e].get(id(self.sems[x]), 0) < self.counts[x]:
                    eng.wait_ge(self.sems[x], self.counts[x])
                    self.waited[e][id(self.sems[x])] = self.counts[x]
            for slot in range(self.NDS):
                if self.dma_counts[slot] > 0:
                    sem = self.sems["dma"][slot]
                    val = self.dma_counts[slot] * 16
                    if self.waited[e].get(id(sem), 0) < val:
                        eng.wait_ge(sem, val)
                        self.waited[e][id(sem)] = val
            if self.cc_count > 0 and self.waited[e].get("cc", 0) < self.cc_count:
                eng.wait_ge(self.sems["cc"], self.cc_count)
                self.waited[e]["cc"] = self.cc_count
        self.last_writer = {}
        self.readers = {}

    def collective(self, fn):
        self.barrier()
        self.cc_count += 1
        fn(self.eng["pool"]).then_inc(self.sems["cc"])
        for e, eng in self.eng.items():
            eng.wait_ge(self.sems["cc"], self.cc_count)

    def emit(self):
        self.flush()
        sp = self.eng["sp"]
        for slot in range(self.NDS):
            if self.dma_counts[slot] > 0:
                sp.wait_ge(self.sems["dma"][slot], self.dma_counts[slot] * 16)
        return self.counts, self.n_dma


class Ctx:
    def __init__(self):
        self.nc = bass.Bass("TRN2", target_bir_lowering=False)
        self.es = ExitStack()
        nc = self.nc
        sems = {e: self.es.enter_context(nc.semaphore("s_" + e)) for e in ["pe", "act", "dve", "pool"]}
        sems["dma"] = [self.es.enter_context(nc.semaphore("s_dma%d" % i)) for i in range(Sched.NDS)]
        sems["cc"] = self.es.enter_context(nc.semaphore("s_cc"))
        self.S = Sched(nc, sems)
        self.pes = self.es

    def begin_phase(self, sfx):
        self.pes = ExitStack()
        self.sfx = sfx

    def end_phase(self):
        self.S.barrier()
        self.pes.close()
        self.pes = self.es

    def dram_int(self, name, shape, dt=F32):
        return self.nc.dram_tensor(name, list(shape), dt).ap()

    def dram_in(self, name, shape, dt=F32):
        return self.nc.dram_tensor(name, list(shape), dt, kind="ExternalInput").ap()

    def dram_out(self, name, shape, dt=F32):
        return self.nc.dram_tensor(name, list(shape), dt, kind="ExternalOutput").ap()

    def sb(self, name, shape, dt=F32):
        return self.pes.enter_context(self.nc.sbuf_tensor(name + getattr(self, "sfx", ""), list(shape), dt))

    def ps(self, name, shape, dt=F32):
        return self.pes.enter_context(self.nc.psum_tensor(name + getattr(self, "sfx", ""), list(shape), dt))

    def dma(self, out, in_, r=(), w=()):
        self.S.op("sp", lambda e: e.dma_start(out=out, in_=in_), r, w)

    def mm(self, out, lhsT, rhs, start=True, stop=True, r=(), w=(), skip=False):
        self.S.op("pe", lambda e: e.matmul(out, lhsT=lhsT, rhs=rhs, start=start, stop=stop,
                                           skip_group_check=skip), r, w)

    def tr(self, out, in_, ident, r=(), w=()):
        self.S.op("pe", lambda e: e.transpose(out, in_, ident), r, w)

    def act(self, out, in_, func, r=(), w=(), scale=1.0, bias=0.0, accum=None, eng="act"):
        if accum is None:
            self.S.op("act", lambda e: e.activation(out=out, in_=in_, func=func, bias=bias, scale=scale), r, w)
        else:
            self.S.op("act", lambda e: e.activation(out=out, in_=in_, func=func, bias=bias, scale=scale,
                                                    accum_out=accum), r, w)

    def cp(self, eng, out, in_, r=(), w=()):
        if eng == "act":
            self.S.op("act", lambda e: e.activation(out=out, in_=in_, func=AF.Copy), r, w)
        else:
            self.S.op(eng, lambda e: e.tensor_copy(out=out, in_=in_), r, w)

    def tt(self, eng, out, in0, in1, op, r=(), w=()):
        self.S.op(eng, lambda e: e.tensor_tensor(out=out, in0=in0, in1=in1, op=op), r, w)

    def ts(self, eng, out, in0, s1, s2, op0, op1=None, r=(), w=()):
        if op1 is None:
            self.S.op(eng, lambda e: e.tensor_scalar(out=out, in0=in0, scalar1=s1, scalar2=None, op0=op0), r, w)
        else:
            self.S.op(eng, lambda e: e.tensor_scalar(out=out, in0=in0, scalar1=s1, scalar2=s2, op0=op0, op1=op1), r, w)

    def stt(self, out, in0, scalar, in1, op0, op1, r=(), w=()):
        self.S.op("dve", lambda e: e.scalar_tensor_tensor(out=out, in0=in0, scalar=scalar, in1=in1,
                                                          op0=op0, op1=op1), r, w)

    def memset(self, eng, ap, val, w=()):
        self.S.op(eng, lambda e: e.memset(ap, val), (), w)

    def recip(self, out, in_, r=(), w=()):
        self.S.op("dve", lambda e: e.reciprocal(out=out, in_=in_), r, w)

    def finish(self):
        res = self.S.emit()
        self.es.close()
        return res


def load_convert(C, dst_bf, src_dram, stage, stage_key, dst_key, ncols, scale_ap=None, parity=0, scale_key="gscale"):
    C.dma(stage[:, 0:ncols], src_dram, w=[stage_key])
    if scale_ap is not None:
        C.ts("dve", dst_bf, stage[:, 0:ncols], scale_ap, None, ALU.mult, r=[stage_key, scale_key], w=[dst_key])
    elif parity % 2 == 0:
        C.cp("dve", dst_bf, stage[:, 0:ncols], r=[stage_key], w=[dst_key])
    else:
        C.cp("pool", dst_bf, stage[:, 0:ncols], r=[stage_key], w=[dst_key])


NLOC = 1804


MIX_W = ["w_kv", "w_loc", "w_out", "gpre", "gpost", "convw", "convv", "w1d", "w2d", "ped", "sgv", "sgw", "sgb",
         "poolw", "poolsc"]
MIX_W_SHAPES = {"w_kv": [D, 512], "w_loc": [D, NLOC], "w_out": [D, D], "gpre": [128, 8], "gpost": [1, D],
                "convw": [128, 2, 31], "convv": [128, 2, 3], "w1d": [128, 2048], "w2d": [64, 128], "ped": [128, 32],
                "sgv": [2, 256], "sgw": [128, 4, 128], "sgb": [4, 128], "poolw": [128, 2, 128], "poolsc": [128, 2]}
MIX_C_SHAPES = {"ropeC": [128, SEQ], "ropeS": [128, SEQ], "qC": [NS, 128, 128], "qS": [NS, 128, 128], "indp": [64, SEQ],
                "identd": [128, 128], "wmaskd": [8, 128, 128], "cmaskd": [5, 128, 128], "selA": [NS, 128, 128],
                "selB": [NS, 128, 128], "impM": [4, 128, 128], "trild": [128, 128], "rcnt0": [128, 2, 128],
                "rcntg": [128, 2, 128], "selm": [128, 32], "selm2": [8, 2]}
FFN_W_SHAPES = {"w_up": [D, 2 * FF], "w_dn": [FF, D], "g2": [128, 8], "gpost2": [1, D], "cwd": [128, NFC, 4]}


def mixer_phase(C, T, l, nslots=NS, stages=5):
    C.begin_phase("_m%d" % l)
    nc = C.nc
    xg_tile = T["xg_tile"]; xown_tile = T["xown_tile"]; xm_own = T["xm_own"]; xm_tail = T["xm_tail"]
    w_kv, w_loc, w_out, gpre, gpost = (T[k + "_%d" % l] for k in ("w_kv", "w_loc", "w_out", "gpre", "gpost"))
    convw, convv, w1d, w2d, ped = (T[k + "_%d" % l] for k in ("convw", "convv", "w1d", "w2d", "ped"))
    sgv, sgw, sgb, poolw, poolsc = (T[k + "_%d" % l] for k in ("sgv", "sgw", "sgb", "poolw", "poolsc"))
    ropeC, ropeS, qC, qS, indp, identd = (T[k] for k in ("ropeC", "ropeS", "qC", "qS", "indp", "identd"))
    wmaskd, cmaskd, selA, selB, impM, trild = (T[k] for k in ("wmaskd", "cmaskd", "selA", "selB", "impM", "trild"))
    rcnt0, rcntg, selmd = T["rcnt0"], T["rcntg"], T["selm"]

    sb = C.sb
    wkv = sb("wkv", [128, 8, 512], BF16)
    wloc = sb("wloc", [128, 8, NLOC], BF16)
    wo = sb("wo", [128, 8, D], BF16)
    xt = [sb("xt0", [128, D]), sb("xt1", [128, D])]
    stage = xt
    gpre_t = sb("gpre_t", [128, 8])
    gpost_bc = sb("gpost_bc", [128, D])
    ident = sb("ident", [128, 128], BF16)
    ones_b = sb("ones_b", [128, 128], BF16)
    shi = sb("shi", [128, 2, 128], BF16)
    slo = sb("slo", [128, 2, 128], BF16)
    Dg = sb("Dg", [128, 2, 31, 128], BF16)
    convw_t = sb("convw_t", [128, 2, 31])
    convv_t = sb("convv_t", [128, 2, 3])
    W1k = sb("W1k", [64, 32, 64], BF16)
    W1v = sb("W1v", [64, 32, 64], BF16)
    W2 = sb("W2", [64, 128], BF16)
    peTk = sb("peTk", [64, 32], BF16)
    peTv = sb("peTv", [64, 32], BF16)
    hnc = sb("hnc", [128, D], BF16)
    cbias = sb("cbias", [64, 2])
    Kaug = sb("Kaug", [128, SEQ], BF16)
    Kw = sb("Kw", [128, 12, 128], BF16)
    Vs = sb("Vs", [128, NT, 65], BF16)
    Vw = sb("Vw", [128, 12, 65], BF16)
    kvk = [sb("kvk0", [64, 528], BF16), sb("kvk1", [64, 528], BF16)]
    kvv = [sb("kvv0", [64, 528], BF16), sb("kvv1", [64, 528], BF16)]
    gkT = sb("gkT", [64, 576], BF16)
    gvT = sb("gvT", [64, 576], BF16)
    kcT = sb("kcT", [128, 576], BF16)
    rhsC = sb("rhsC", [128, 4, 193], BF16)
    wmask = sb("wmask", [128, 8, 128], BF16)
    cmask = sb("cmask", [128, 5, 128], BF16)
    wsT = sb("wsT", [128, 4, 128], BF16)
    tril = sb("tril", [128, 128])
    Bs = sb("Bs", [128, 2, 128])
    Wbd = sb("Wbd", [128, 2, 128], BF16)
    poolsc_t = sb("poolsc_t", [128, 2])
    rc0 = sb("rc0", [128, 2, 128])
    rcg = sb("rcg", [128, 2, 128])
    lng_bc = sb("lng_bc", [128, 256])
    lnb_bc = sb("lnb_bc", [128, 256])
    hnb = [sb("hnb0", [128, D], BF16), sb("hnb1", [128, D], BF16)]
    hT = [sb("hT0", [128, 8, 128], BF16), sb("hT1", [128, 8, 128], BF16)]
    rC = [sb("rC0", [128, 128]), sb("rC1", [128, 128])]
    rS = [sb("rS0", [128, 128]), sb("rS1", [128, 128])]
    junk = sb("junk", [128, D], BF16)
    ssq = sb("ssq", [128, 8])
    rstd = sb("rstd", [128, 8])
    ropet = sb("ropet", [128, 2, 128])
    xo = sb("xo", [128, D])
    hno = sb("hno", [128, D], BF16)
    hTo = sb("hTo", [128, 8, 160], BF16)
    sig = sb("sig", [128, 2, 160])
    ub = sb("ub", [128, 2, 160], BF16)
    zdT = sb("zdT", [128, 2, 160])
    s2 = sb("s2", [128, 2, 160]); s4 = sb("s4", [128, 2, 160])
    s8 = sb("s8", [128, 2, 160]); s16 = sb("s16", [128, 2, 160])
    plf = sb("plf", [128, 2, 128]); plb = sb("plb", [128, 2, 128], BF16)
    usg = sb("usg", [128, 2, 128])
    qCt = sb("qCt", [128, 128]); qSt = sb("qSt", [128, 128])
    qt1 = sb("qt1", [128, 128]); qt2 = sb("qt2", [128, 128]); qsum = sb("qsum", [128, 128])
    QR = sb("QR", [128, 2, 4, 128], BF16)
    QC = sb("QC", [128, 4, 128], BF16)
    QW = sb("QW", [128, 4, 128], BF16)
    gts = sb("gts", [128, 12])
    vg = sb("vg", [128, 256]); vn = sb("vn", [128, 256]); vbf = sb("vbf", [128, 256], BF16)
    bnst = sb("bnst", [128, 6]); bnag = sb("bnag", [128, 2]); lnr = sb("lnr", [128, 2])
    ycv = sb("ycv", [128, 2, 128]); ysq = sb("ysq", [128, 2, 128])
    cmean = sb("cmean", [128, 128]); cmsq = sb("cmsq", [128, 128]); cvar = sb("cvar", [128, 128])
    crstd = sb("crstd", [128, 128]); cyn = sb("cyn", [128, 2, 128])
    sgt = sb("sgt", [128, 2, 128])
    yT = sb("yT", [128, 8, 128], BF16)
    Pb = [sb("Pb%d" % i, [128, 4, 128], BF16) for i in range(4)]
    selAt = sb("selAt", [128, 128]); selBt = sb("selBt", [128, 128])
    imp = sb("imp", [128, 128]); imp2 = sb("imp2", [128, 128]); impw = sb("impw", [128, 128])
    top8 = sb("top8", [128, 16]); selt1 = sb("selt1", [128, 128]); selt2 = sb("selt2", [128, 128])
    selbf = sb("selbf", [128, 128], BF16)
    den = sb("den", [128, 12]); rden = sb("rden", [128, 12]); gsc = sb("gsc", [128, 12])
    Ynsa = sb("Ynsa", [128, 256]); Ynb = sb("Ynb", [128, 256], BF16)
    yo = sb("yo", [128, D])
    oss = sb("oss", [128, 4])
    selm = sb("selm", [128, 32], BF16)
    cg1 = sb("cg1", [64, 64]); cg2 = sb("cg2", [64, 64])

    psS = [C.ps("psS0", [128, 512]), C.ps("psS1", [128, 512])]
    psC = C.ps("psC", [128, 512])
    psK = C.ps("psK", [128, 512])
    psSel = C.ps("psSel", [128, 512])
    psWin = C.ps("psWin", [128, 512])
    psG = C.ps("psG", [128, 512])
    psT = C.ps("psT", [128, 1024], BF16)

    C.dma(gpre_t[:], gpre[:, :], w=["gscale"])
    C.dma(gpost_bc[:], gpost.partition_broadcast(128), w=["gpost_bc"])
    C.dma(stage[0][:, 0:128], identd[:, :], w=["xt0"])
    C.cp("dve", ident[:], stage[0][:, 0:128], r=["xt0"], w=["ident"])
    C.memset("dve", ones_b[:], 1.0, w=["ones_b"])
    C.dma(stage[1][:, 0:32], selmd[:, :], w=["xt1"])
    C.cp("dve", selm[:], stage[1][:, 0:32], r=["xt1"], w=["selm"])
    par = 0
    for kc in range(8):
        s = stage[par % 2]; sk = "xt%d" % (par % 2); par += 1
        load_convert(C, wkv[:, kc, :], w_kv[kc * 128:(kc + 1) * 128, :], s, sk, "wkv", 512, scale_ap=gpre_t[:, kc:kc + 1])
        s = stage[par % 2]; sk = "xt%d" % (par % 2); par += 1
        load_convert(C, wloc[:, kc, 0:902], w_loc[kc * 128:(kc + 1) * 128, 0:902], s, sk, "wloc", 902, scale_ap=gpre_t[:, kc:kc + 1])
        s = stage[par % 2]; sk = "xt%d" % (par % 2); par += 1
        load_convert(C, wloc[:, kc, 902:NLOC], w_loc[kc * 128:(kc + 1) * 128, 902:NLOC], s, sk, "wloc", 902, scale_ap=gpre_t[:, kc:kc + 1])
        s = stage[par % 2]; sk = "xt%d" % (par % 2); par += 1
        load_convert(C, wo[:, kc, :], w_out[kc * 128:(kc + 1) * 128, :], s, sk, "wo", D, parity=kc)
    for q in range(8):
        s = stage[par % 2]; sk = "xt%d" % (par % 2); par += 1
        C.dma(s[64:128, 0:1024], indp[:, q * 1024:(q + 1) * 1024], w=[sk])
        C.cp("pool", Kaug[64:128, q * 1024:(q + 1) * 1024], s[64:128, 0:1024], r=[sk], w=["Kaug_ind"])
    C.memset("pool", Kw[64:128, :, :], 0.0, w=["Kw"])
    C.memset("pool", kcT[:, :], 0.0, w=["kcT"])
    C.memset("pool", gkT[:, :], 0.0, w=["gkT"])
    C.memset("pool", gvT[:, :], 0.0, w=["gvT"])
    for q_ in range(2):
        C.memset("pool", kvk[q_][:, :], 0.0, w=["kvk%d" % q_])
        C.memset("pool", kvv[q_][:, :], 0.0, w=["kvv%d" % q_])
    C.memset("pool", QC[:, :, :], 0.0, w=["QC"])
    C.memset("pool", QW[:, :, :], 0.0, w=["QW"])
    C.memset("pool", Vs[:, :, 64:65], 1.0, w=["Vs"])
    C.memset("pool", Vw[:, :, 64:65], 1.0, w=["Vw"])
    C.memset("pool", rhsC[:, :, :], 0.0, w=["rhsC"])
    C.memset("pool", rhsC[:, :, 192:193], 1.0, w=["rhsC"])
    for q in range(8):
        s = stage[par % 2]; sk = "xt%d" % (par % 2); par += 1
        C.dma(s[:, 0:128], wmaskd[q, :, :], w=[sk])
        C.cp("dve", wmask[:, q, :], s[:, 0:128], r=[sk], w=["wmask"])
    for q in range(4):
        s = stage[par % 2]; sk = "xt%d" % (par % 2); par += 1
        C.dma(s[:, 0:128], cmaskd[q, :, :], w=[sk])
        C.cp("dve", cmask[:, q, :], s[:, 0:128], r=[sk], w=["cmask"])
        if q == 0:
            s = stage[par % 2]; sk = "xt%d" % (par % 2); par += 1
            C.dma(s[:, 0:128], cmaskd[4, :, :], w=[sk])
            C.cp("dve", cmask[:, 4, :], s[:, 0:128], r=[sk], w=["cmask"])
        s = stage[par % 2]; sk = "xt%d" % (par % 2); par += 1
        C.dma(s[:, 0:128], impM[q, :, :], w=[sk])
        C.cp("dve", rhsC[:, q, 64:192], s[:, 0:128], r=[sk], w=["rhsC"])
    C.dma(convw_t[:], convw[:, :, :], w=["convw_t"])
    C.dma(convv_t[:], convv[:, :, :], w=["convv_t"])
    C.dma(stage[0][:, 0:128], identd[:, :], w=["xt0"])
    for cc in range(2):
        for k in range(31):
            C.ts("dve" if k % 2 == 0 else "pool", Dg[:, cc, k, :], stage[0][:, 0:128], convw_t[:, cc, k:k + 1], None, ALU.mult,
                 r=["xt0", "convw_t"], w=["Dg"])
    for wi_, (Wt_, wk_) in enumerate(((W1k, "W1k"), (W1v, "W1v"))):
        for q in range(2):
            C.dma(stage[q][0:64, 0:1024], w1d[64 * wi_:64 * wi_ + 64, q * 1024:(q + 1) * 1024], w=["xt%d" % q])
            C.cp("dve", Wt_[:].rearrange("p r e -> p (r e)")[:, q * 1024:(q + 1) * 1024], stage[q][0:64, 0:1024],
                 r=["xt%d" % q], w=[wk_])
    C.dma(stage[0][0:64, 0:128], w2d[:, :], w=["xt0"])
    C.cp("dve", W2[:], stage[0][0:64, 0:128], r=["xt0"], w=["W2"])
    C.dma(stage[0][0:64, 0:32], ped[0:64, :], w=["xt0"])
    C.cp("dve", peTk[:], stage[0][0:64, 0:32], r=["xt0"], w=["peTk"])
    C.dma(stage[1][0:64, 0:32], ped[64:128, :], w=["xt1"])
    C.cp("dve", peTv[:], stage[1][0:64, 0:32], r=["xt1"], w=["peTv"])
    for wi_, (Wt_, wk_, pt_, pk_) in enumerate(((W1k, "W1k", peTk, "peTk"), (W1v, "W1v", peTv, "peTv"))):
        for r_ in range(32):
            C.mm(psK[0:64, 0:1], Wt_[:, r_, :], pt_[:, r_:r_ + 1], start=(r_ == 0), stop=(r_ == 31),
                 r=[wk_, pk_], w=["psK"])
        C.cp("dve", cbias[:, wi_:wi_ + 1], psK[0:64, 0:1], w=["psK", "cbias"])
    C.dma(lng_bc[:], sgv[0:1, :].partition_broadcast(128), w=["lng_bc"])
    C.dma(lnb_bc[:], sgv[1:2, :].partition_broadcast(128), w=["lnb_bc"])
    C.dma(tril[:], trild[:, :], w=["tril"])
    C.dma(stage[0][:, 0:512], sgw.rearrange("p g t -> p (g t)"), w=["xt0"])
    for g in range(4):
        C.tt("dve", wsT[:, g, :], stage[0][:, g * 128:(g + 1) * 128], tril[:], ALU.mult, r=["xt0", "tril"], w=["wsT"])
        C.dma(Bs[(g % 2) * 64:(g % 2) * 64 + 64, g // 2, :], sgb[g:g + 1, :].partition_broadcast(64), w=["Bs"])
    C.dma(stage[1][:, 0:256], poolw.rearrange("p q d -> p (q d)"), w=["xt1"])
    C.cp("dve", Wbd[:].rearrange("p q d -> p (q d)"), stage[1][:, 0:256], r=["xt1"], w=["Wbd"])
    C.dma(poolsc_t[:], poolsc[:, :], w=["poolsc_t"])
    C.dma(rc0[:], rcnt0[:, :, :], w=["rc0"])
    C.dma(rcg[:], rcntg[:, :, :], w=["rcg"])

    def rms_to_bf(xt_ap, xk, out_bf, ok, np_, col):
        C.act(junk[0:np_, :], xt_ap, AF.Square, r=[xk], w=["junk", "ssq%d" % col], accum=ssq[0:np_, col:col + 1])
        C.act(rstd[0:np_, col:col + 1], ssq[0:np_, col:col + 1], AF.Ln, r=["ssq%d" % col, "epsb"], w=["rstd%d" % col],
              scale=1.0 / D, bias=epsb[0:np_, 0:1])
        C.act(rstd[0:np_, col:col + 1], rstd[0:np_, col:col + 1], AF.Exp, r=["rstd%d" % col], w=["rstd%d" % col], scale=-0.5)
        C.act(out_bf, xt_ap, AF.Copy, r=[xk, "rstd%d" % col], w=[ok], scale=rstd[0:np_, col:col + 1])

    epsb = sb("epsb", [128, 2])
    C.memset("dve", epsb[:, 0:1], RMS_EPS, w=["epsb"])
    C.memset("dve", epsb[:, 1:2], LN_EPS, w=["epsb"])

    def kv_tile(m):
        p = m % 2
        xk, hk, tk = "xt%d" % p, "hnb%d" % p, "hT%d" % p
        C.dma(xt[p][:], xg_tile(m)[:, :], w=[xk])
        C.dma(rC[p][:], ropeC[:, m * 128:(m + 1) * 128], w=["rC%d" % p])
        C.dma(rS[p][:], ropeS[:, m * 128:(m + 1) * 128], w=["rS%d" % p])
        yield
        C.act(junk[:, :], xt[p][:], AF.Square, r=[xk], w=["junk", "ssq%d" % p], accum=ssq[:, p:p + 1])
        yield
        C.act(rstd[:, p:p + 1], ssq[:, p:p + 1], AF.Ln, r=["ssq%d" % p, "epsb"], w=["rstd%d" % p],
              scale=1.0 / D, bias=epsb[:, 0:1])
        yield
        C.act(rstd[:, p:p + 1], rstd[:, p:p + 1], AF.Exp, r=["rstd%d" % p], w=["rstd%d" % p], scale=-0.5)
        yield
        C.ts("dve", hnb[p][:], xt[p][:], rstd[:, p:p + 1], None, ALU.mult, r=[xk, "rstd%d" % p], w=[hk])
        yield
        for half in range(2):
            for q in range(4):
                kc = 4 * half + q
                C.tr(psT[:, q * 128:(q + 1) * 128], hnb[p][:, kc * 128:(kc + 1) * 128], ident[:], r=[hk, "ident"], w=["psT"])
            yield
            C.cp("dve", hT[p][:, 4 * half:4 * half + 4, :], psT[:, 0:512].rearrange("p (k t) -> p k t", k=4), w=["psT", tk])
            yield
        for f in range(3):
            for kc in range(8):
                C.mm(psK[:, f * 128:(f + 1) * 128], wkv[:, kc, f * 128:(f + 1) * 128], hT[p][:, kc, :],
                     start=(kc == 0), stop=(kc == 7), r=["wkv", tk], w=["psK"])
            yield
        for kc in range(8):
            C.mm(psK[:, 384:512], hT[p][:, kc, :], wkv[:, kc, 384:512], start=(kc == 0), stop=(kc == 7),
                 r=["wkv", tk], w=["psK"])
        yield
        grp = (m // 4) % 2
        col0 = 16 + (m % 4) * 128
        C.cp("act", kvk[grp][:, col0:col0 + 128], psK[0:64, 0:128], w=["psK", "kvk%d" % grp])
        C.cp("act", kvv[grp][:, col0:col0 + 128], psK[64:128, 0:128], w=["psK", "kvv%d" % grp])
        C.tt("dve", ropet[:, 0, :], psK[:, 128:256], rC[p][:], ALU.mult, r=["rC%d" % p], w=["psK", "ropet0"])
        C.tt("dve", ropet[:, 1, :], psK[:, 256:384], rS[p][:], ALU.mult, r=["rS%d" % p], w=["psK", "ropet1"])
        yield
        C.cp("dve", Vs[:, m, 0:64], psK[:, 384:448], w=["psK", "Vs%d" % m])
        C.cp("dve", Vw[:, m % 12, 0:64], psK[:, 448:512], w=["psK", "Vw%d" % (m % 12)])
        yield
        C.tt("pool", Kaug[0:64, m * 128:(m + 1) * 128], ropet[0:64, 0, :], ropet[0:64, 1, :], ALU.add,
             r=["ropet0", "ropet1"], w=["Kaug%d" % m])
        C.tt("pool", Kw[0:64, m % 12, :], ropet[64:128, 0, :], ropet[64:128, 1, :], ALU.add,
             r=["ropet0", "ropet1"], w=["Kw%d" % (m % 12)])
        yield

    def compress(j):
        g = j % 2
        n0 = 32 * j - 1
        c0 = 32 + n0
        for which, (Wt_, wk_, kb, kk, gT, gk_) in enumerate(((W1k, "W1k", kvk[g], "kvk%d" % g, gkT, "gkT"),
                                                           (W1v, "W1v", kvv[g], "kvv%d" % g, gvT, "gvT"))):
            for r_ in range(32):
                C.mm(psK[0:64, which * 32:(which + 1) * 32], Wt_[:, r_, :], kb[:, r_:r_ + 16 * 31 + 1:16],
                     start=(r_ == 0), stop=(r_ == 31), r=[wk_, kk], w=["psK"])
            yield
            cgt = cg1[:, which * 32:(which + 1) * 32]
            cgs = cg2[:, which * 32:(which + 1) * 32]
            k1, k2 = ["cg1_%d" % which], ["cg2_%d" % which]
            C.ts("dve", cgt, psK[0:64, which * 32:(which + 1) * 32], cbias[:, which:which + 1], None, ALU.add,
                 r=["cbias"], w=["psK"] + k1)
            C.tt("dve", cgs, cgt, cgt, ALU.mult, r=k1, w=k2)
            C.ts("dve", cgs, cgs, 0.044715, 1.0, ALU.mult, ALU.add, r=k2, w=k2)
            C.tt("dve", cgs, cgs, cgt, ALU.mult, r=k2 + k1, w=k2)
            C.act(cgs, cgs, AF.Exp, r=k2, w=k2, scale=-GELU_C)
            C.ts("dve", cgs, cgs, 1.0, None, ALU.add, r=k2, w=k2)
            C.recip(cgs, cgs, r=k2, w=k2)
            C.tt("dve", gT[:, c0:c0 + 32], cgt, cgs, ALU.mult, r=k1 + k2, w=[gk_])
            other = (kvk, kvv)[which][1 - g]
            C.cp("pool", other[:, 0:16], kb[:, 512:528], r=[kk], w=[("kvk%d", "kvv%d")[which] % (1 - g)])
            yield
        C.mm(psK[0:64, 64:96], W2[:, 0:64], gkT[:, c0:c0 + 32], r=["W2", "gkT"], w=["psK"])
        yield
        C.cp("dve", kcT[0:64, c0:c0 + 32], psK[0:64, 64:96], w=["psK", "kcT"])
        yield
        chunks = sorted(set([max(n0, 0) // 128, (n0 + 31) // 128]))
        for ci in chunks:
            C.mm(psK[:, 128:192], gvT[:, 32 + ci * 128:32 + (ci + 1) * 128], W2[:, 64:128], r=["W2", "gvT"], w=["psK"])
            yield
            C.cp("dve", rhsC[:, ci, 0:64], psK[:, 128:192], w=["psK", "rhsC"])
            yield

    def local(j):
        yield
        C.dma(xo[:], xown_tile(j)[:, :], w=["xo"])
        for cand in range(4):
            m_ = 4 * j - 1 + cand
            if m_ < 0:
                C.memset("pool", yo[0:32, :], 0.0, w=["yo"])
            else:
                C.dma(yo[cand * 32:(cand + 1) * 32, :], xg_tile(m_)[96:128, :], w=["yo"])
        C.dma(qCt[:], qC[j, :, :], w=["qCt"])
        C.dma(qSt[:], qS[j, :, :], w=["qSt"])
        C.dma(selAt[:], selA[j, :, :], w=["selAt"])
        C.dma(selBt[:], selB[j, :, :], w=["selBt"])
        rms_to_bf(xo[:], "xo", hno[:], "hno", 128, 2)
        yield
        rms_to_bf(yo[:], "yo", hnc[:], "hnc", 128, 3)
        yield
        for half in range(2):
            for q in range(4):
                kc = 4 * half + q
                C.tr(psT[:, 512 + q * 128:512 + (q + 1) * 128], hno[:, kc * 128:(kc + 1) * 128], ident[:],
                     r=["hno", "ident"], w=["psT"])
            yield
            C.cp("act", hTo[:, 4 * half:4 * half + 4, 32:160], psT[:, 512:1024].rearrange("p (k t) -> p k t", k=4),
                 w=["psT", "hTo"])
            yield
        for kc in range(8):
            C.mm(psG[:, kc * 32:(kc + 1) * 32], hnc[:, kc * 128:(kc + 1) * 128], selm[:], r=["hnc", "selm"], w=["psG"])
        yield
        C.cp("act", hTo[:, :, 0:32], psG[:, 0:256].rearrange("p (k t) -> p k t", k=8), w=["psG", "hTo"])
        yield

        def proj(chunk, ncol, out_ap):
            c0 = 160 - ncol
            for kc in range(8):
                C.mm(out_ap, wloc[:, kc, chunk * 128:(chunk + 1) * 128], hTo[:, kc, c0:160],
                     start=(kc == 0), stop=(kc == 7), r=["wloc", "hTo"], w=["psG"])

        yield
        for cc in range(2):
            proj(2 + cc, 160, psG[:, 0:160])
            C.act(sig[:, cc, :], psG[:, 0:160], AF.Sigmoid, w=["psG", "sig"])
        for cc in range(2):
            proj(cc, 160, psG[:, 0:160])
            C.tt("dve", ub[:, cc, :], psG[:, 0:160], sig[:, cc, :], ALU.mult, r=["sig"], w=["psG", "ub"])
        yield
        for cc in range(2):
            proj(4 + cc, 160, psG[:, 0:160])
            C.cp("act", zdT[:, cc, :], psG[:, 0:160], w=["psG", "zdT"])
        yield
        for cc in range(2):
            proj(6 + cc, 128, psG[:, 0:128])
            C.act(usg[:, cc, :], psG[:, 0:128], AF.Gelu_apprx_tanh, w=["psG", "usg"])
        yield
        for hp in range(2):
            for kc in range(8):
                C.mm(psG[:, 0:128], wloc[:, kc, 1024 + hp * 128:1024 + (hp + 1) * 128], hTo[:, kc, 32:160],
                     start=(kc == 0), stop=(kc == 7), r=["wloc", "hTo"], w=["psG"])
            for kc in range(8):
                C.mm(psG[:, 128:256], wloc[:, kc, 1280 + hp * 128:1280 + (hp + 1) * 128], hTo[:, kc, 32:160],
                     start=(kc == 0), stop=(kc == 7), r=["wloc", "hTo"], w=["psG"])
            C.tt("dve", qt1[:], psG[:, 0:128], qCt[:], ALU.mult, r=["qCt"], w=["psG", "qt1"])
            C.tt("dve", qt2[:], psG[:, 128:256], qSt[:], ALU.mult, r=["qSt"], w=["psG", "qt2"])
            C.act(QC[0:64, 2 * hp, :], psG[0:64, 0:128], AF.Copy, w=["psG", "QC"], scale=0.125)
            C.act(QC[0:64, 2 * hp + 1, :], psG[64:128, 0:128], AF.Copy, w=["psG", "QC"], scale=0.125)
            yield
            C.tt("pool", qsum[:], qt1[:], qt2[:], ALU.add, r=["qt1", "qt2"], w=["qsum"])
            yield
            for v in range(2):
                C.cp("dve", QR[0:64, v, 2 * hp, :], qsum[0:64, :], r=["qsum"], w=["QRq"])
                C.cp("act", QR[0:64, v, 2 * hp + 1, :], qsum[64:128, :], r=["qsum"], w=["QRq"])
            C.cp("pool", QW[0:64, 2 * hp, :], qsum[0:64, :], r=["qsum"], w=["QW"])
            C.cp("act", QW[0:64, 2 * hp + 1, :], qsum[64:128, :], r=["qsum"], w=["QW"])
        yield
        for kc in range(8):
            C.mm(psG[:, 0:268], hTo[:, kc, 32:160], wloc[:, kc, 1536:1804], start=(kc == 0), stop=(kc == 7),
                 r=["wloc", "hTo"], w=["psG"])
        C.act(gts[:], psG[:, 256:268], AF.Sigmoid, w=["psG", "gts"])
        C.act(vg[:], psG[:, 0:256], AF.Gelu_apprx_tanh, w=["psG", "vg"])
        C.S.op("dve", lambda e: e.bn_stats(out=bnst[:], in_=vg[:]), ["vg"], ["bnst"])
        C.S.op("dve", lambda e: e.bn_aggr(out=bnag[:], in_=bnst[:]), ["bnst"], ["bnag"])
        C.act(lnr[:, 0:1], bnag[:, 1:2], AF.Ln, r=["bnag", "epsb"], w=["lnr"], bias=epsb[:, 1:2])
        C.act(lnr[:, 1:2], lnr[:, 0:1], AF.Exp, r=["lnr"], w=["lnr1"], scale=-0.5)
        C.ts("dve", vn[:], vg[:], bnag[:, 0:1], lnr[:, 1:2], ALU.subtract, ALU.mult, r=["vg", "bnag", "lnr1"], w=["vn"])
        C.tt("pool", vn[:], vn[:], lng_bc[:], ALU.mult, r=["vn", "lng_bc"], w=["vn"])
        C.tt("pool", vbf[:], vn[:], lnb_bc[:], ALU.add, r=["vn", "lnb_bc"], w=["vbf"])
        yield
        for cc in range(2):
            for k in range(31):
                C.mm(psG[:, 0:128], Dg[:, cc, k, :], ub[:, cc, 2 + k:2 + k + 128], start=(k == 0), stop=(k == 30),
                     r=["Dg", "ub"], w=["psG"])
            C.act(ycv[:, cc, :], psG[:, 0:128], AF.Identity, r=["convv_t"], w=["psG", "ycv"], bias=convv_t[:, cc, 0:1])
            C.act(ysq[:, cc, :], psG[:, 0:128], AF.Square, r=["convv_t"], w=["psG", "ysq"], bias=convv_t[:, cc, 0:1])
        for src, sk_, c0_ in ((ycv, "ycv", 0), (ysq, "ysq", 128)):
            C.cp("dve", shi[:], src[:], r=[sk_], w=["shi"])
            C.tt("dve", slo[:], src[:], shi[:], ALU.subtract, r=[sk_, "shi"], w=["slo"])
            n_ = 0
            for part, pk_ in ((shi, "shi"), (slo, "slo")):
                for cc in range(2):
                    C.mm(psG[:, c0_:c0_ + 128], ones_b[:], part[:, cc, :], start=(n_ == 0), stop=(n_ == 3),
                         r=["ones_b", pk_], w=["psG"])
                    n_ += 1
        C.act(cmean[:], psG[:, 0:128], AF.Copy, w=["psG", "cmean"], scale=1.0 / 256)
        C.tt("pool", cmsq[:], cmean[:], cmean[:], ALU.mult, r=["cmean"], w=["cmsq"])
        C.stt(cvar[:], psG[:, 128:256], 1.0 / 256, cmsq[:], ALU.mult, ALU.subtract, r=["cmsq"], w=["psG", "cvar"])
        C.act(crstd[:], cvar[:], AF.Ln, r=["cvar", "epsb"], w=["crstd"], bias=epsb[:, 1:2])
        C.act(crstd[:], crstd[:], AF.Exp, r=["crstd"], w=["crstd"], scale=-0.5)
        for cc in range(2):
            C.tt("pool", cyn[:, cc, :], ycv[:, cc, :], cmean[:], ALU.subtract, r=["ycv", "cmean"], w=["cyn%d" % cc])
            C.tt("dve", cyn[:, cc, :], cyn[:, cc, :], crstd[:], ALU.mult, r=["cyn%d" % cc, "crstd"], w=["cyn%d" % cc])
            C.act(yT[:, cc, :], cyn[:, cc, :], AF.Silu, r=["cyn%d" % cc, "convv_t"], w=["yT%d" % cc],
                  scale=convv_t[:, cc, 1:2], bias=convv_t[:, cc, 2:3])
        yield
        for g in range(4):
            q = g // 2
            C.mm(psG[:, g * 128:(g + 1) * 128], vbf[:, q * 128:(q + 1) * 128], wsT[:, g, :], r=["vbf", "wsT"], w=["psG"])
        for g in range(4):
            q = g // 2
            lo = (g % 2) * 64
            C.tt("dve", sgt[lo:lo + 64, q, :], psG[lo:lo + 64, g * 128:(g + 1) * 128], Bs[lo:lo + 64, q, :], ALU.add,
                 r=["Bs"], w=["psG", "sgt%d" % g])
            C.tt("pool", yT[lo:lo + 64, 4 + q, :], sgt[lo:lo + 64, q, :], usg[lo:lo + 64, q, :], ALU.mult,
                 r=["sgt%d" % g, "usg"], w=["yT%d" % (4 + q)])
        yield
        C.tt("pool", s2[:, :, 1:160], zdT[:, :, 1:160], zdT[:, :, 0:159], ALU.add, r=["zdT"], w=["s2"])
        C.tt("pool", s4[:, :, 3:160], s2[:, :, 3:160], s2[:, :, 1:158], ALU.add, r=["s2"], w=["s4"])
        C.tt("pool", s8[:, :, 7:160], s4[:, :, 7:160], s4[:, :, 3:156], ALU.add, r=["s4"], w=["s8"])
        C.tt("pool", s16[:, :, 15:160], s8[:, :, 15:160], s8[:, :, 7:152], ALU.add, r=["s8"], w=["s16"])
        rc = rc0 if j == 0 else rcg
        srcs = [(s2, "s2", 0, 0), (s4, "s4", 64, 0), (s8, "s8", 0, 1), (s16, "s16", 64, 1)]
        for gi, (sbuf_, sk, lo, q) in enumerate(srcs):
            C.tt("dve", plf[lo:lo + 64, q, :], sbuf_[lo:lo + 64, q, 32:160], rc[lo:lo + 64, q, :], ALU.mult,
                 r=[sk, "rc0", "rcg"], w=["plf%d" % gi])
            C.tt("dve", plb[lo:lo + 64, q, :], plf[lo:lo + 64, q, :], zdT[lo:lo + 64, q, 32:160], ALU.subtract,
                 r=["plf%d" % gi, "zdT"], w=["plb%d" % q])
        for q in range(2):
            C.mm(psG[:, q * 128:(q + 1) * 128], Wbd[:, q, :], plb[:, q, :], r=["Wbd", "plb%d" % q], w=["psG"])
        for q in range(2):
            C.act(yT[:, 6 + q, :], psG[:, q * 128:(q + 1) * 128], AF.Copy, r=["poolsc_t"], w=["psG", "yT%d" % (6 + q)],
                  scale=poolsc_t[:, q:q + 1])

    pcount = [0]

    def nextP():
        i = pcount[0] % 4
        pcount[0] += 1
        return Pb[i], "Pb%d" % i

    scount = [0]

    def nextS(nb=2):
        i = scount[0] % nb
        scount[0] += 1
        if i == 2:
            return psC, "psC"
        return psS[i], "psS%d" % i

    def nsa(j):
        L = (32 * j + 30) // 128

        def pipeline(chunks, depth=1):
            state = []
            for i, (A, B, Cc) in enumerate(chunks):
                while len(state) < min(len(chunks), i + depth + 1):
                    state.append(chunks[len(state)][0]())
                st = state[i]
                B(st)
                yield
                Cc(st)
                yield

        st_p = {}

        def mk_cmp(ci):
            def A():
                S_, sk = nextS()
                C.mm(S_[:, :], kcT[:, 32 + ci * 128:32 + (ci + 1) * 128], QC[:].rearrange("p h t -> p (h t)"),
                     r=["kcT", "QC"], w=[sk])
                return S_, sk

            def B(st):
                S_, sk = st
                P_, pk = Pb[ci], "Pb%d" % ci
                C.act(P_[:].rearrange("p h t -> p (h t)"), S_[:, :], AF.Exp, w=[sk, pk])
                if ci == L:
                    C.tt("dve", P_[:], P_[:], cmask[:, j % 4:j % 4 + 1, :].to_broadcast([128, 4, 128]), ALU.mult,
                         r=[pk, "cmask"], w=[pk])
                elif ci == L - 1 and j % 4 == 0:
                    C.tt("dve", P_[:], P_[:], cmask[:, 4:5, :].to_broadcast([128, 4, 128]), ALU.mult,
                         r=[pk, "cmask"], w=[pk])

            def Cc(st):
                pass
            return A, B, Cc

        yield from pipeline([mk_cmp(ci) for ci in range(L + 1)])
        for hp in range(2):
            for ci in range(L + 1):
                for h in (2 * hp, 2 * hp + 1):
                    c0 = (h % 2) * 193
                    C.mm(psC[:, c0:c0 + 193], Pb[ci][:, h, :], rhsC[:, ci, :], start=(ci == 0 and h % 2 == 0), stop=(ci == L),
                         r=["Pb%d" % ci, "rhsC"], w=["psC"], skip=True)
            yield
            for h in (2 * hp, 2 * hp + 1):
                c0 = (h % 2) * 193
                C.ts("dve", den[:, h:h + 1], psC[:, c0 + 192:c0 + 193], 1e-30, None, ALU.max, w=["psC", "den%d" % hp])
            C.recip(rden[:, 2 * hp:2 * hp + 2], den[:, 2 * hp:2 * hp + 2], r=["den%d" % hp], w=["rden%d" % hp])
            yield
            for h in (2 * hp, 2 * hp + 1):
                c0 = (h % 2) * 193
                if h == 0:
                    C.ts("dve", imp[:], psC[:, c0 + 64:c0 + 192], rden[:, 0:1], None, ALU.mult, r=["rden0"],
                         w=["psC", "imp"])
                else:
                    C.stt(imp[:], psC[:, c0 + 64:c0 + 192], rden[:, h:h + 1], imp[:], ALU.mult, ALU.add,
                          r=["rden%d" % hp, "imp"], w=["psC", "imp"])
            C.tt("dve", gsc[:, 2 * hp:2 * hp + 2], rden[:, 2 * hp:2 * hp + 2], gts[:, 6 * hp:6 * hp + 6:3], ALU.mult,
                 r=["rden%d" % hp, "gts"], w=["gscc%d" % hp])
            for h in (2 * hp, 2 * hp + 1):
                c0 = (h % 2) * 193
                C.ts("dve", Ynsa[:, h * 64:(h + 1) * 64], psC[:, c0:c0 + 64], gsc[:, h:h + 1], None, ALU.mult,
                     r=["gscc%d" % hp], w=["psC", "Ynsa"])
            yield
        C.tt("dve", imp2[:], imp[:], selAt[:], ALU.mult, r=["imp", "selAt"], w=["imp2"])
        C.tt("dve", imp2[:], imp2[:], selBt[:], ALU.add, r=["imp2", "selBt"], w=["imp2"])
        yield
        C.S.op("dve", lambda e: e.max(out=top8[:, 0:8], in_=imp2[:]), ["imp2"], ["top8a"])
        C.S.op("dve", lambda e: e.match_replace(out=impw[:], in_to_replace=top8[:, 0:8], in_values=imp2[:],
                                                imm_value=-3e38), ["imp2", "top8a"], ["impw"])
        yield
        C.S.op("dve", lambda e: e.max(out=top8[:, 8:16], in_=impw[:]), ["impw"], ["top8b"])
        C.ts("dve", selt1[:], imp2[:], top8[:, 15:16], None, ALU.is_ge, r=["imp2", "top8b"], w=["selt1"])
        C.ts("dve", selt2[:], imp2[:], -1e29, -NEG, ALU.is_gt, ALU.mult, r=["imp2"], w=["selt2"])
        yield
        C.tt("dve", selt1[:], selt1[:], selt2[:], ALU.mult, r=["selt1", "selt2"], w=["selt1"])
        C.ts("dve", selbf[:], selt1[:], NEG, None, ALU.add, r=["selt1"], w=["selbf"])
        yield

        wl = list(range(max(0, 4 * j - 4), 4 * j + 4))

        def mk_win(wi, m):
            def A():
                S_, sk = nextS(3)
                C.mm(S_[:, :], Kw[:, m % 12, :], QW[:].rearrange("p h t -> p (h t)"),
                     r=["Kw%d" % (m % 12), "Kw", "QW"], w=[sk])
                return S_, sk

            def B(st):
                S_, sk = st
                P_, pk = nextP()
                C.act(P_[:].rearrange("p h t -> p (h t)"), S_[:, :], AF.Exp, w=[sk, pk])
                C.tt("dve", P_[:], P_[:], wmask[:, 4 + m - 4 * j:5 + m - 4 * j, :].to_broadcast([128, 4, 128]), ALU.mult,
                     r=[pk, "wmask"], w=[pk])
                st_p[("w", wi)] = (P_, pk)

            def Cc(st):
                P_, pk = st_p[("w", wi)]
                for h in range(4):
                    C.mm(psWin[:, h * 65:(h + 1) * 65], P_[:, h, :], Vw[:, m % 12, :], start=(wi == 0 and h == 0),
                         stop=(wi == len(wl) - 1), r=[pk, "Vw%d" % (m % 12), "Vw"], w=["psWin"], skip=True)
            return A, B, Cc

        scount[0] = 0
        yield from pipeline([mk_win(wi, m) for wi, m in enumerate(wl)], depth=2)
        C.tr(psT[:, 512:640], selbf[:], ident[:], r=["selbf", "ident"], w=["psT"])
        yield
        C.cp("dve", QR[64:128, 0, :, :], psT[0:64, 512:640].unsqueeze(1).to_broadcast([64, 4, 128]), w=["psT", "QRb"])
        C.cp("act", QR[64:128, 1, :, :], psT[64:128, 512:640].unsqueeze(1).to_broadcast([64, 4, 128]), w=["psT", "QRb"])
        yield

        nsel = 4 * j + 4

        def mk_sel(m):
            def A():
                S_, sk = nextS(3)
                v = 0 if m < 32 else 1
                C.mm(S_[:, :], Kaug[:, m * 128:(m + 1) * 128], QR[:, v, :, :].rearrange("p h t -> p (h t)"),
                     r=["Kaug%d" % m, "Kaug_ind", "QRq", "QRb"], w=[sk])
                return S_, sk

            def B(st):
                S_, sk = st
                P_, pk = nextP()
                C.act(P_[:].rearrange("p h t -> p (h t)"), S_[:, :], AF.Exp, w=[sk, pk])
                if m >= 4 * j:
                    C.tt("dve", P_[:], P_[:], wmask[:, 4 + m - 4 * j:5 + m - 4 * j, :].to_broadcast([128, 4, 128]), ALU.mult,
                         r=[pk, "wmask"], w=[pk])
                st_p[("s", m)] = (P_, pk)

            def Cc(st):
                P_, pk = st_p[("s", m)]
                for h in range(4):
                    C.mm(psSel[:, h * 65:(h + 1) * 65], P_[:, h, :], Vs[:, m, :], start=(m == 0 and h == 0),
                         stop=(m == nsel - 1), r=[pk, "Vs%d" % m, "Vs"], w=["psSel"], skip=True)
            return A, B, Cc

        yield from pipeline([mk_sel(m) for m in range(nsel)], depth=2)
        scount[0] = 0
        C.cp("dve", den[:, 4:8], psSel[:, 64:260:65], w=["psSel", "den"])
        C.cp("dve", den[:, 8:12], psWin[:, 64:260:65], w=["psWin", "den"])
        C.recip(rden[:, 4:12], den[:, 4:12], r=["den"], w=["rden2"])
        C.tt("dve", gsc[:, 4:8], rden[:, 4:8], gts[:, 1:12:3], ALU.mult, r=["rden2", "gts"], w=["gsc1"])
        C.tt("dve", gsc[:, 8:12], rden[:, 8:12], gts[:, 2:12:3], ALU.mult, r=["rden2", "gts"], w=["gsc1"])
        for h in range(4):
            C.stt(Ynsa[:, h * 64:(h + 1) * 64], psSel[:, h * 65:h * 65 + 64], gsc[:, 4 + h:5 + h], Ynsa[:, h * 64:(h + 1) * 64],
                  ALU.mult, ALU.add, r=["gsc1", "Ynsa"], w=["psSel", "Ynsa"])
        for h in range(4):
            C.stt(Ynb[:, h * 64:(h + 1) * 64], psWin[:, h * 65:h * 65 + 64], gsc[:, 8 + h:9 + h], Ynsa[:, h * 64:(h + 1) * 64],
                  ALU.mult, ALU.add, r=["gsc1", "Ynsa"], w=["psWin", "Ynb"])
        yield
        for q in range(2):
            C.tr(psT[:, 512 + q * 128:512 + (q + 1) * 128], Ynb[:, q * 128:(q + 1) * 128], ident[:], r=["Ynb", "ident"], w=["psT"])
        yield
        C.cp("act", yT[:, 2:4, :], psT[:, 512:768].rearrange("p (q t) -> p q t", q=2), w=["psT", "yT2", "yT3"])
        yield

    def outproj(j):
        yk = ["yT%d" % f for f in range(8)]
        for half in range(2):
            for f in range(8):
                C.mm(psG[:, :], yT[:, f, :], wo[:, f, half * 512:(half + 1) * 512], start=(f == 0), stop=(f == 7),
                     r=yk + ["wo"], w=["psG"])
            yield
            C.cp("dve", yo[:, half * 512:(half + 1) * 512], psG[:, :], w=["psG", "yo"])
            yield
        C.act(junk[:, :], yo[:], AF.Square, r=["yo"], w=["junk", "oss"], accum=oss[:, 0:1])
        yield
        C.act(oss[:, 1:2], oss[:, 0:1], AF.Ln, r=["oss", "epsb"], w=["oss1"], scale=1.0 / D, bias=epsb[:, 0:1])
        C.act(oss[:, 2:3], oss[:, 1:2], AF.Exp, r=["oss1"], w=["oss2"], scale=-0.5)
        C.stt(yo[:], yo[:], oss[:, 2:3], gpost_bc[:], ALU.mult, ALU.mult, r=["yo", "oss2", "gpost_bc"], w=["yo"])
        C.tt("pool", yo[:], yo[:], xo[:], ALU.add, r=["yo", "xo"], w=["yo"])
        C.dma(xm_own[j, :, :], yo[:], r=["yo"])
        C.dma(xm_tail[2 * j:2 * j + 2, :], yo[126:128, :], r=["yo"])
        yield

    def stream_b(j):
        for m in range(4 * j, 4 * j + 4):
            yield from kv_tile(m)
        yield from compress(j)

    def stream_a(j):
        yield from local(j)
        yield from nsa(j)
        yield from outproj(j)

    for _ in stream_b(0):
        pass
    for j in range(nslots):
        A_ = stream_a(j)
        B_ = stream_b(j + 1) if j + 1 < nslots else None
        doneA = doneB = B_ is None and False
        doneB = B_ is None
        while not doneA:
            try:
                next(A_)
            except StopIteration:
                doneA = True
            if not doneB:
                try:
                    next(B_)
                except StopIteration:
                    doneB = True
        while not doneB:
            try:
                next(B_)
            except StopIteration:
                doneB = True
    C.end_phase()


def ffn_phase(C, T, l):
    C.begin_phase("_f%d" % l)
    nc = C.nc
    xm_own = T["xm_own"]; tail_g = T["tail_g_%d" % l]; out_tile = T["ffn_out_tile"]
    w_up, w_dn, g2, gpost, cwd = (T[k + "_%d" % l] for k in ("w_up", "w_dn", "g2", "gpost2", "cwd"))
    identd = T["identd"]; selm2d = T["selm2"]

    sb = C.sb
    wu = sb("wu", [128, 8, 2 * FF], BF16)
    wd = sb("wd", [128, 22, D], BF16)
    stage = [sb("stage%d" % i, [128, 1024]) for i in range(4)]
    g2_t = sb("g2_t", [128, 8])
    gpost_bc = sb("gpost_bc", [128, D])
    cw = sb("cw", [128, NFC, 4])
    ident = sb("ident", [128, 128], BF16)
    epsb = sb("epsb", [128, 1])
    xs = [sb("xs%d" % i, [128, D]) for i in range(2)]
    xh = sb("xh", [8, D])
    selm2 = sb("selm2", [8, 2], BF16)
    hn = sb("hn", [128, D], BF16)
    hnh = sb("hnh", [8, D], BF16)
    junk = sb("junk", [128, D], BF16)
    ssq = sb("ssq", [128, 4]); rstd = sb("rstd", [128, 4])
    h2T = sb("h2T", [128, 8, 2, 130], BF16)
    actT = sb("actT", [128, 22, 2, 128], BF16)
    gc = [sb("gc%d" % i, [128, 2, 128]) for i in range(2)]
    uc = [sb("uc%d" % i, [128, 2, 128]) for i in range(2)]
    gg = [sb("gg%d" % i, [128, 2, 128]) for i in range(2)]
    yo = sb("yo", [128, D])
    oss = sb("oss", [128, 4])

    psU = [C.ps("psU%d" % i, [128, 512]) for i in range(4)]
    psD = [C.ps("psD%d" % i, [128, 512]) for i in range(2)]
    psT = C.ps("psT", [128, 1024], BF16)
    psH = C.ps("psH", [128, 512])

    C.dma(stage[1][0:8, 0:2], selm2d[:, :], w=["stage1"])
    C.cp("dve", selm2[:], stage[1][0:8, 0:2], r=["stage1"], w=["selm2"])
    C.dma(g2_t[:], g2[:, :], w=["gscale"])
    C.dma(gpost_bc[:], gpost.partition_broadcast(128), w=["gpost_bc"])
    C.dma(cw[:], cwd[:, :, :], w=["cw"])
    C.dma(stage[0][:, 0:128], identd[:, :], w=["stage0"])
    C.cp("dve", ident[:], stage[0][:, 0:128], r=["stage0"], w=["ident"])
    C.memset("dve", epsb[:, 0:1], RMS_EPS, w=["epsb"])
    par = 0
    for kc in range(8):
        for q in range(6):
            c0 = q * 1024
            c1 = min(2 * FF, c0 + 1024)
            s = stage[par % 4]; sk = "stage%d" % (par % 4); par += 1
            load_convert(C, wu[:, kc, c0:c1], w_up[kc * 128:(kc + 1) * 128, c0:c1], s, sk, "wu", c1 - c0,
                         scale_ap=g2_t[:, kc:kc + 1])
    for f in range(22):
        s = stage[par % 4]; sk = "stage%d" % (par % 4); par += 1
        load_convert(C, wd[:, f, :], w_dn[f * 128:(f + 1) * 128, :], s, sk, "wd", D, parity=f)

    def rms_to_bf(x_ap, xk, out_bf, ok, np_, col):
        C.act(junk[0:np_, :], x_ap, AF.Square, r=[xk], w=["junk", "ssq%d" % col], accum=ssq[0:np_, col:col + 1])
        C.act(rstd[0:np_, col:col + 1], ssq[0:np_, col:col + 1], AF.Ln, r=["ssq%d" % col, "epsb"], w=["rstd%d" % col],
              scale=1.0 / D, bias=epsb[0:np_, 0:1])
        C.act(rstd[0:np_, col:col + 1], rstd[0:np_, col:col + 1], AF.Exp, r=["rstd%d" % col], w=["rstd%d" % col], scale=-0.5)
        C.act(out_bf, x_ap, AF.Copy, r=[xk, "rstd%d" % col], w=[ok], scale=rstd[0:np_, col:col + 1])

    ucount = [0]
    for grp in range(NS // 2):
        for sl in range(2):
            j = grp * 2 + sl
            C.dma(xs[sl][:], xm_own[j, :, :], w=["xs%d" % sl])
            for cand in range(4):
                m_ = 4 * j - 1 + cand
                if m_ < 0:
                    C.memset("pool", xh[0:2, :], 0.0, w=["xh"])
                else:
                    row0 = ((m_ % 4) * NS + m_ // 4) * 2
                    C.dma(xh[cand * 2:(cand + 1) * 2, :], tail_g[row0:row0 + 2, :], w=["xh"])
            rms_to_bf(xs[sl][:], "xs%d" % sl, hn[:], "hn", 128, 0)
            rms_to_bf(xh[:], "xh", hnh[:], "hnh", 8, 1)
            for kc in range(8):
                C.tr(psT[:, kc * 128:(kc + 1) * 128], hn[:, kc * 128:(kc + 1) * 128], ident[:], r=["hn", "ident"], w=["psT"])
            C.cp("act", h2T[:, :, sl, 2:130], psT[:, :].rearrange("p (k t) -> p k t", k=8), w=["psT", "h2T"])
            for kc in range(8):
                C.mm(psH[:, kc * 2:(kc + 1) * 2], hnh[:, kc * 128:(kc + 1) * 128], selm2[:], r=["hnh", "selm2"], w=["psH"])
            C.cp("act", h2T[:, :, sl, 0:2], psH[:, 0:16].rearrange("p (k t) -> p k t", k=8), w=["psH", "h2T"])
        def up_mm(fc):
            banks = []
            for ch in (fc, 22 + fc):
                bi = ucount[0] % 4
                ucount[0] += 1
                bank = psU[bi]
                bk = "psU%d" % bi
                for kc in range(8):
                    C.mm(bank[:, 0:260], wu[:, kc, ch * 128:(ch + 1) * 128],
                         h2T[:, kc, :, :].rearrange("p s t -> p (s t)"),
                         start=(kc == 0), stop=(kc == 7), r=["wu", "h2T"], w=[bk])
                banks.append((bank, bk))
            return banks

        def conv_ops(fc, banks):
            p = fc % 2
            for (bank, bk), ch, dst, dk in ((banks[0], fc, gc[p], "gc%d" % p), (banks[1], 22 + fc, uc[p], "uc%d" % p)):
                bv = bank[:, 0:260].rearrange("p (s t) -> p s t", s=2)
                C.act(dst[:], bv[:, :, 2:130], AF.Identity, r=["cw"], w=[bk, dk], scale=cw[:, ch, 2:3], bias=cw[:, ch, 3:4])
                C.stt(dst[:], bv[:, :, 1:129], cw[:, ch, 1:2], dst[:], ALU.mult, ALU.add, r=["cw", dk], w=[bk, dk])
                C.stt(dst[:], bv[:, :, 0:128], cw[:, ch, 0:1], dst[:], ALU.mult, ALU.add, r=["cw", dk], w=[bk, dk])

        def gate_ops(fc):
            p = fc % 2
            C.act(gg[p][:], gc[p][:], AF.Gelu_apprx_tanh, r=["gc%d" % p], w=["gg%d" % p])
            C.tt("pool", actT[:, fc, :, :], gg[p][:], uc[p][:], ALU.mult,
                 r=["gg%d" % p, "uc%d" % p], w=["actT"])

        nxt = up_mm(0)
        for fc in range(22):
            cur = nxt
            if fc + 1 < 22:
                nxt = up_mm(fc + 1)
            conv_ops(fc, cur)
            if fc >= 1:
                gate_ops(fc - 1)
            if fc == 12 and grp >= 1 and T.get("post_group") is not None:
                T["post_group"](grp - 1)
        gate_ops(21)
        for sl in range(2):
            j = grp * 2 + sl
            for half in range(2):
                for f in range(22):
                    C.mm(psD[half][:, :], actT[:, f, sl, :], wd[:, f, half * 512:(half + 1) * 512],
                         start=(f == 0), stop=(f == 21), r=["actT", "wd"], w=["psD%d" % half])
                C.cp("dve", yo[:, half * 512:(half + 1) * 512], psD[half][:, :], w=["psD%d" % half, "yo"])
            C.act(junk[:, :], yo[:], AF.Square, r=["yo"], w=["junk", "oss"], accum=oss[:, 0:1])
            C.act(oss[:, 1:2], oss[:, 0:1], AF.Ln, r=["oss", "epsb"], w=["oss1"], scale=1.0 / D, bias=epsb[:, 0:1])
            C.act(oss[:, 2:3], oss[:, 1:2], AF.Exp, r=["oss1"], w=["oss2"], scale=-0.5)
            C.stt(yo[:], yo[:], oss[:, 2:3], gpost_bc[:], ALU.mult, ALU.mult, r=["yo", "oss2", "gpost_bc"], w=["yo"])
            C.tt("pool", yo[:], yo[:], xs[sl][:], ALU.add, r=["yo", "xs%d" % sl], w=["yo"])
            C.dma(out_tile(j)[:, :], yo[:], r=["yo"], w=["ffnout%d" % j])
    if T.get("post_group") is not None:
        T["post_group"](NS // 2 - 1)
    C.end_phase()


OFF_Q, OFF_KV, OFF_G, OFF_C, OFF_D = 512, 768, 1152, 1164, 1676


def _consts():
    c = {}
    half = 32
    inv = (10000.0 ** (-np.arange(half, dtype=np.float32) * 2.0 / 64)).astype(np.float32)
    ang = np.arange(SEQ, dtype=np.float32)[:, None] * inv[None, :]
    cos = np.cos(ang).astype(np.float32).T
    sin = np.sin(ang).astype(np.float32).T
    c64 = np.concatenate([cos, cos], 0)
    s64 = np.concatenate([-sin, sin], 0)
    c["ropeC"] = np.ascontiguousarray(np.concatenate([c64, c64], 0))
    c["ropeS"] = np.ascontiguousarray(np.concatenate([s64, s64], 0))
    blk = np.arange(SEQ) // 64
    c["indp"] = (np.arange(64)[:, None] == (blk % 64)[None, :]).astype(np.float32)
    c["identd"] = np.eye(128, dtype=np.float32)
    n = np.arange(512)[:, None]
    b = np.arange(128)[None, :]
    off = n - 4 * b
    M = np.where((off == -1) | (off == 3), 1.0, np.where((off >= 0) & (off <= 2), 2.0, 0.0)).astype(np.float32)
    M[511, :] = 0.0
    c["impM"] = np.ascontiguousarray(M.reshape(4, 128, 128))
    c["trild"] = (np.arange(128)[:, None] <= np.arange(128)[None, :]).astype(np.float32)
    rg = np.zeros((128, 2, 128), np.float32)
    for gi, w in enumerate((2, 4, 8, 16)):
        rg[(gi % 2) * 64:(gi % 2) * 64 + 64, gi // 2, :] = 1.0 / w
    c["rcntg"] = rg
    return c


def _core_consts(cidx):
    c = cidx
    o = {}
    k = np.arange(128)[:, None]
    t = np.arange(128)[None, :]
    wm = np.zeros((8, 128, 128), np.float32)
    for q in range(8):
        mm = q - 4
        rel = mm - c
        if rel == 0:
            wm[q] = (k <= t)
        elif rel == -4:
            wm[q] = (k > t)
        elif -4 < rel < 0:
            wm[q] = 1.0
    o["wmaskd"] = wm
    cm = np.zeros((5, 128, 128), np.float32)
    for jm in range(4):
        nprime = k - 32 * jm
        cm[jm] = (16 * nprime + 31 <= 128 * c + t)
    cm[4] = 1.0
    if c == 0:
        cm[4][127, :15] = 0.0
    o["cmaskd"] = cm
    A = np.zeros((NS, 128, 128), np.float32)
    B = np.zeros((NS, 128, 128), np.float32)
    tt = np.arange(128)[:, None]
    bb = np.arange(128)[None, :]
    for j in range(NS):
        i = 4 * j + c
        cur = 2 * i + (tt >= 64)
        valid = bb <= cur
        forced = (bb == 0) | (bb == cur) | (bb == cur - 1)
        A[j] = (valid & ~forced)
        B[j] = np.where(valid, np.where(forced, 1e6, 0.0), -1e30)
    o["selA"] = A
    o["selB"] = B
    inv = (10000.0 ** (-np.arange(32, dtype=np.float32) * 2.0 / 64)).astype(np.float32)
    qC = np.zeros((NS, 128, 128), np.float32)
    qS = np.zeros((NS, 128, 128), np.float32)
    for j in range(NS):
        pos = (128 * (4 * j + c) + np.arange(128)).astype(np.float32)
        ang = pos[:, None] * inv[None, :]
        cs = np.cos(ang).astype(np.float32).T * np.float32(0.125)
        sn = np.sin(ang).astype(np.float32).T * np.float32(0.125)
        c64 = np.concatenate([cs, cs], 0)
        s64 = np.concatenate([-sn, sn], 0)
        qC[j] = np.concatenate([c64, c64], 0)
        qS[j] = np.concatenate([s64, s64], 0)
    o["qC"] = qC
    o["qS"] = qS
    r0 = np.zeros((128, 2, 128), np.float32)
    for gi, w in enumerate((2, 4, 8, 16)):
        pos = 128 * c + np.arange(128)
        cnt = np.minimum(pos + 1, w).astype(np.float32)
        r0[(gi % 2) * 64:(gi % 2) * 64 + 64, gi // 2, :] = (1.0 / cnt)[None, :]
    o["rcnt0"] = r0
    sm = np.zeros((128, 32), np.float32)
    sm[c * 32:(c + 1) * 32, :] = np.eye(32, dtype=np.float32)
    o["selm"] = sm
    sm2 = np.zeros((8, 2), np.float32)
    sm2[c * 2:(c + 1) * 2, :] = np.eye(2, dtype=np.float32)
    o["selm2"] = sm2
    return o


def _sw(idx):
    return np.concatenate([idx[32:], idx[:32]])


def _mixer_weights(P, l):
    w_in = P["w_in"][l]
    kv = lambda s: np.arange(OFF_KV + 64 * s, OFF_KV + 64 * s + 64)
    cols_kv = np.concatenate([kv(0), kv(1), kv(2), kv(4), _sw(kv(2)), _sw(kv(4)), kv(3), kv(5)])
    qcols = np.arange(OFF_Q, OFF_Q + 256)
    qsw = np.concatenate([_sw(qcols[h * 64:(h + 1) * 64]) for h in range(4)])
    cols_loc = np.concatenate([np.arange(0, 512), np.arange(OFF_D, OFF_D + 256), np.arange(OFF_C, OFF_C + 256),
                               qcols, qsw, np.arange(OFF_C + 256, OFF_C + 512), np.arange(OFF_G, OFF_G + 12)])
    o = {}
    o["w_kv"] = np.ascontiguousarray(w_in[:, cols_kv])
    o["w_loc"] = np.ascontiguousarray(w_in[:, cols_loc])
    o["w_out"] = np.ascontiguousarray(P["w_out"][l])
    o["gpre"] = np.ascontiguousarray(P["norm_mix_pre"][l].reshape(8, 128).T)
    o["gpost"] = np.ascontiguousarray(P["norm_mix_post"][l].reshape(1, D))
    o["convw"] = np.ascontiguousarray(P["conv_dw_w"][l].reshape(31, 2, 128).transpose(2, 1, 0))
    cv = np.stack([P["conv_dw_b"][l], P["conv_ln_g"][l], P["conv_ln_b"][l]], -1)
    o["convv"] = np.ascontiguousarray(cv.reshape(2, 128, 3).transpose(1, 0, 2))
    w1k = P["nsa_ck_w1"][l].reshape(32, 64, 64).transpose(1, 0, 2).reshape(64, 2048)
    w1v = P["nsa_cv_w1"][l].reshape(32, 64, 64).transpose(1, 0, 2).reshape(64, 2048)
    o["w1d"] = np.ascontiguousarray(np.concatenate([w1k, w1v], 0))
    o["w2d"] = np.ascontiguousarray(np.concatenate([P["nsa_ck_w2"][l], P["nsa_cv_w2"][l]], 1))
    o["ped"] = np.ascontiguousarray(np.concatenate([P["nsa_pe_k"][l].T, P["nsa_pe_v"][l].T], 0))
    o["sgv"] = np.ascontiguousarray(np.stack([P["sgu_ln_g"][l], P["sgu_ln_b"][l]], 0))
    o["sgw"] = np.ascontiguousarray(P["sgu_w"][l].transpose(2, 0, 1))
    o["sgb"] = np.ascontiguousarray(P["sgu_b"][l])
    pw = np.zeros((128, 2, 128), np.float32)
    for gi in range(4):
        lo = (gi % 2) * 64
        pw[lo:lo + 64, gi // 2, lo:lo + 64] = P["pool_w"][l][gi]
    o["poolw"] = pw
    o["poolsc"] = np.ascontiguousarray(P["pool_scale"][l].reshape(2, 128).T)
    return o


def _ffn_weights(P, l):
    o = {}
    o["w_up"] = np.ascontiguousarray(P["ffn_up"][l])
    o["w_dn"] = np.ascontiguousarray(P["ffn_down"][l])
    o["g2"] = np.ascontiguousarray(P["norm_ffn_pre"][l].reshape(8, 128).T)
    o["gpost2"] = np.ascontiguousarray(P["norm_ffn_post"][l].reshape(1, D))
    cw = np.concatenate([P["ffn_conv_w"][l], P["ffn_conv_b"][l][None, :]], 0)
    o["cwd"] = np.ascontiguousarray(cw.reshape(4, NFC, 128).transpose(2, 1, 0))
    return o


def _own_tiles(xb, c, halo):
    pad = np.concatenate([np.zeros((halo, D), np.float32), xb], 0)
    out = np.empty((NS, halo + 128, D), np.float32)
    for j in range(NS):
        i = 4 * j + c
        out[j] = pad[128 * i:128 * i + 128 + halo]
    return out


def _scatter_own(res_list, key):
    x = np.empty((NB, SEQ, D), np.float32)
    for core in range(8):
        b, c = divmod(core, 4)
        r = res_list[core][key]
        for j in range(NS):
            i = 4 * j + c
            x[b, 128 * i:128 * (i + 1)] = r[j]
    return x


def _chunk_row(m):
    r, sl = m % 4, m // 4
    return sl // 2, r * 256 + (sl % 2) * 128


def build_all(nslots=NS):
    C = Ctx()
    T = {}
    xg0 = C.dram_in("xg0", [8 * 1024, D])
    xown0 = C.dram_in("xown0", [NS, 128, D])
    for k, shp in MIX_C_SHAPES.items():
        T[k] = C.dram_in(k, shp)
    for l in range(2):
        for k, shp in MIX_W_SHAPES.items():
            T["%s_%d" % (k, l)] = C.dram_in("%s_%d" % (k, l), shp)
        for k, shp in FFN_W_SHAPES.items():
            T["%s_%d" % (k, l)] = C.dram_in("%s_%d" % (k, l), shp)
    out = C.dram_out("out", [NS, 128, D])
    xm_own = C.dram_int("xm_own", [NS, 128, D])
    tails = [C.dram_int("xm_tail%d" % l, [NS * 2, D]) for l in range(2)]
    tail_g = [C.dram_int("tail_g%d" % l, [4 * NS * 2, D]) for l in range(2)]
    x1_own = [C.dram_int("x1_own%d" % g, [256, D]) for g in range(8)]
    x1_g = [C.dram_int("x1_g%d" % g, [1024, D]) for g in range(8)]
    RG = [[0, 1, 2, 3], [4, 5, 6, 7]]

    def gather(src, dst):
        C.S.collective(lambda e: e.collective_compute("AllGather", ALU.bypass, replica_groups=RG,
                                                      ins=[src.opt()], outs=[dst.opt()]))

    def xg_tile0(m):
        g, ro = _chunk_row(m)
        return xg0[g * 1024 + ro:g * 1024 + ro + 128, :]

    def xg_tile1(m):
        g, ro = _chunk_row(m)
        return x1_g[g][ro:ro + 128, :]

    for l in range(2):
        T["xg_tile"] = xg_tile0 if l == 0 else xg_tile1
        T["xown_tile"] = (lambda j: xown0[j, :, :]) if l == 0 else (lambda j: x1_own[j // 2][(j % 2) * 128:(j % 2) * 128 + 128, :])
        T["xm_own"] = xm_own
        T["xm_tail"] = tails[l]
        mixer_phase(C, T, l, nslots=nslots)
        gather(tails[l], tail_g[l])
        T["tail_g_%d" % l] = tail_g[l]
        T["ffn_out_tile"] = (lambda j: x1_own[j // 2][(j % 2) * 128:(j % 2) * 128 + 128, :]) if l == 0 else (lambda j: out[j, :, :])
        if l == 0:
            def post_group(g):
                C.S.cc_op(lambda e: e.collective_compute("AllGather", ALU.bypass, replica_groups=RG,
                                                         ins=[x1_own[g].opt()], outs=[x1_g[g].opt()]),
                          reads=["ffnout%d" % (2 * g), "ffnout%d" % (2 * g + 1)], writes=["x1g%d" % g])
            T["post_group"] = post_group
        else:
            T["post_group"] = None
        ffn_phase(C, T, l)
    info = C.finish()
    return C.nc, info


_CACHE = {}


def kernel(**inputs):
    P = {k: np.asarray(v, dtype=np.float32) for k, v in inputs.items()}
    x = P["x"]
    if "nc" not in _CACHE:
        _CACHE["nc"] = build_all()[0]
    nc = _CACHE["nc"]
    consts = _consts()
    shared = {k: consts[k] for k in MIX_C_SHAPES if k in consts}
    for l in range(2):
        for k, v in _mixer_weights(P, l).items():
            shared["%s_%d" % (k, l)] = v
        for k, v in _ffn_weights(P, l).items():
            shared["%s_%d" % (k, l)] = v
    in_maps = []
    for core in range(8):
        b, c = divmod(core, 4)
        m = dict(shared)
        m.update(_core_consts(c))
        xt = x[b].reshape(8, 2, 4, 128, D)
        m["xg0"] = np.ascontiguousarray(xt.transpose(0, 2, 1, 3, 4).reshape(8 * 1024, D))
        m["xown0"] = np.ascontiguousarray(x[b].reshape(NS, 4, 128, D)[:, c])
        in_maps.append(m)
    res = run_bass_kernel_spmd(nc, in_maps, core_ids=list(range(8)))
    return _scatter_own(res.results, "out").astype(np.float32)
```

```python
import numpy as np
from contextlib import ExitStack
import concourse.bass as bass
import concourse.mybir as mybir
from concourse.bass_utils import run_bass_kernel_spmd

F32 = mybir.dt.float32
BF16 = mybir.dt.bfloat16
AF = mybir.ActivationFunctionType
ALU = mybir.AluOpType
AX = mybir.AxisListType

D = 1024
SEQ = 8192
NB = 2
NT = 64
NS = 16
FF = 2816
NFC = 44
RMS_EPS = 1e-6
LN_EPS = 1e-5
NEG = -30000.0
GELU_C = 1.5957691216057308


class Sched:
    NDS = 8

    def __init__(self, nc, sems):
        self.nc = nc
        self.eng = {"pe": nc.tensor, "act": nc.scalar, "dve": nc.vector,
                    "pool": nc.gpsimd, "sp": nc.sync}
        self.ops = []
        self.last_writer = {}
        self.readers = {}
        self.sems = sems

    def cc_op(self, fn, reads=(), writes=()):
        idx = self.op("pool", fn, reads, writes)
        self.ops[idx].append("cc")
        return idx

    def op(self, engine, fn, reads=(), writes=()):
        idx = len(self.ops)
        is_dma = engine == "sp"
        deps = {}

        def add(d, kind):
            if d is None:
                return
            if deps.get(d) is None or kind == "raw":
                deps[d] = kind

        for k in reads:
            add(self.last_writer.get(k), "raw")
        for k in writes:
            add(self.last_writer.get(k), "waw")
            for r in self.readers.get(k, ()):
                add(r, "war")
        for k in writes:
            self.last_writer[k] = idx
            self.readers[k] = []
        for k in reads:
            if k not in writes:
                lst = self.readers.setdefault(k, [])
                if not is_dma:
                    lst[:] = [r_ for r_ in lst if self.ops[r_][0] != engine]
                lst.append(idx)
        keep = []
        for d, kind in deps.items():
            de = self.ops[d][0]
            if de == engine and not is_dma:
                if engine == "pe":
                    continue
            keep.append(d)
        self.ops.append([engine, fn, keep, is_dma])
        for d in keep:
            self.ops[d][3] = True
        return idx

    def _init_emit_state(self):
        self.counts = {e: 0 for e in self.eng}
        self.dma_counts = [0] * self.NDS
        self.n_dma = 0
        self.waited = {e: {} for e in self.eng}
        self.sigval = {}
        self.emitted = 0
        self.cc_count = 0

    def flush(self):
        if not hasattr(self, "counts"):
            self._init_emit_state()
        last = {}
        for idx in range(self.emitted, len(self.ops)):
            if len(self.ops[idx]) == 4:
                last[self.ops[idx][0]] = idx
        for e, idx in last.items():
            if e != "sp" and len(self.ops[idx]) == 4:
                self.ops[idx][3] = True
        for idx in range(self.emitted, len(self.ops)):
            e, fn, deps, signal = self.ops[idx][:4]
            is_cc = len(self.ops[idx]) > 4
            eng = self.eng[e]
            need = {}
            for d in deps:
                sem, val = self.sigval[d]
                if need.get(id(sem), (None, 0))[1] < val:
                    need[id(sem)] = (sem, val)
            if e == "sp":
                slot = self.n_dma % self.NDS
                if self.dma_counts[slot] > 0:
                    sem = self.sems["dma"][slot]
                    val = self.dma_counts[slot] * 16
                    if need.get(id(sem), (None, 0))[1] < val:
                        need[id(sem)] = (sem, val)
            for key, (sem, val) in need.items():
                if self.waited[e].get(key, 0) >= val:
                    continue
                eng.wait_ge(sem, val)
                self.waited[e][key] = val
            ins = fn(eng)
            if is_cc:
                self.cc_count += 1
                ins.then_inc(self.sems["cc"])
                self.sigval[idx] = (self.sems["cc"], self.cc_count)
                continue
            if e == "sp":
                slot = self.n_dma % self.NDS
                self.n_dma += 1
                self.dma_counts[slot] += 1
                sem = self.sems["dma"][slot]
                ins.then_inc(sem, 16)
                self.sigval[idx] = (sem, self.dma_counts[slot] * 16)
            elif signal:
                self.counts[e] += 1
                ins.then_inc(self.sems[e], 1)
                self.sigval[idx] = (self.sems[e], self.counts[e])
        self.emitted = len(self.ops)

    def barrier(self, engines=None):
        self.flush()
        for e, eng in self.eng.items():
            for x in ("pe", "act", "dve", "pool"):
                if x != e and self.counts[x] > 0 and self.waited[e].get(id(self.sems[x]), 0) < self.counts[x]:
                    eng.wait_ge(self.sems[x], self.counts[x])
                    self.waited[e][id(self.sems[x])] = self.counts[x]
            for slot in range(self.NDS):
                if self.dma_counts[slot] > 0:
                    sem = self.sems["dma"][slot]
                    val = self.dma_counts[slot] * 16
                    if self.waited[e].get(id(sem), 0) < val:
                        eng.wait_ge(sem, val)
                        self.waited[e][id(sem)] = val
            if self.cc_count > 0 and self.waited[e].get("cc", 0) < self.cc_count:
                eng.wait_ge(self.sems["cc"], self.cc_count)
                self.waited[e]["cc"] = self.cc_count
        self.last_writer = {}
        self.readers = {}

    def collective(self, fn):
        self.barrier()
        self.cc_count += 1
        fn(self.eng["pool"]).then_inc(self.sems["cc"])
        for e, eng in self.eng.items():
            eng.wait_ge(self.sems["cc"], self.cc_count)

    def emit(self):
        self.flush()
        sp = self.eng["sp"]
        for slot in range(self.NDS):
            if self.dma_counts[slot] > 0:
                sp.wait_ge(self.sems["dma"][slot], self.dma_counts[slot] * 16)
        return self.counts, self.n_dma


class Ctx:
    def __init__(self):
        self.nc = bass.Bass("TRN2", target_bir_lowering=False)
        self.es = ExitStack()
        nc = self.nc
        sems = {e: self.es.enter_context(nc.semaphore("s_" + e)) for e in ["pe", "act", "dve", "pool"]}
        sems["dma"] = [self.es.enter_context(nc.semaphore("s_dma%d" % i)) for i in range(Sched.NDS)]
        sems["cc"] = self.es.enter_context(nc.semaphore("s_cc"))
        self.S = Sched(nc, sems)
        self.pes = self.es

    def begin_phase(self, sfx):
        self.pes = ExitStack()
        self.sfx = sfx

    def end_phase(self):
        self.S.barrier()
        self.pes.close()
        self.pes = self.es

    def dram_int(self, name, shape, dt=F32):
        return self.nc.dram_tensor(name, list(shape), dt).ap()

    def dram_in(self, name, shape, dt=F32):
        return self.nc.dram_tensor(name, list(shape), dt, kind="ExternalInput").ap()

    def dram_out(self, name, shape, dt=F32):
        return self.nc.dram_tensor(name, list(shape), dt, kind="ExternalOutput").ap()

    def sb(self, name, shape, dt=F32):
        return self.pes.enter_context(self.nc.sbuf_tensor(name + getattr(self, "sfx", ""), list(shape), dt))

    def ps(self, name, shape, dt=F32):
        return self.pes.enter_context(self.nc.psum_tensor(name + getattr(self, "sfx", ""), list(shape), dt))

    def dma(self, out, in_, r=(), w=()):
        self.S.op("sp", lambda e: e.dma_start(out=out, in_=in_), r, w)

    def mm(self, out, lhsT, rhs, start=True, stop=True, r=(), w=(), skip=False):
        self.S.op("pe", lambda e: e.matmul(out, lhsT=lhsT, rhs=rhs, start=start, stop=stop,
                                           skip_group_check=skip), r, w)

    def tr(self, out, in_, ident, r=(), w=()):
        self.S.op("pe", lambda e: e.transpose(out, in_, ident), r, w)

    def act(self, out, in_, func, r=(), w=(), scale=1.0, bias=0.0, accum=None, eng="act"):
        if accum is None:
            self.S.op("act", lambda e: e.activation(out=out, in_=in_, func=func, bias=bias, scale=scale), r, w)
        else:
            self.S.op("act", lambda e: e.activation(out=out, in_=in_, func=func, bias=bias, scale=scale,
                                                    accum_out=accum), r, w)

    def cp(self, eng, out, in_, r=(), w=()):
        if eng == "act":
            self.S.op("act", lambda e: e.activation(out=out, in_=in_, func=AF.Copy), r, w)
        else:
            self.S.op(eng, lambda e: e.tensor_copy(out=out, in_=in_), r, w)

    def tt(self, eng, out, in0, in1, op, r=(), w=()):
        self.S.op(eng, lambda e: e.tensor_tensor(out=out, in0=in0, in1=in1, op=op), r, w)

    def ts(self, eng, out, in0, s1, s2, op0, op1=None, r=(), w=()):
        if op1 is None:
            self.S.op(eng, lambda e: e.tensor_scalar(out=out, in0=in0, scalar1=s1, scalar2=None, op0=op0), r, w)
        else:
            self.S.op(eng, lambda e: e.tensor_scalar(out=out, in0=in0, scalar1=s1, scalar2=s2, op0=op0, op1=op1), r, w)

    def stt(self, out, in0, scalar, in1, op0, op1, r=(), w=()):
        self.S.op("dve", lambda e: e.scalar_tensor_tensor(out=out, in0=in0, scalar=scalar, in1=in1,
                                                          op0=op0, op1=op1), r, w)

    def memset(self, eng, ap, val, w=()):
        self.S.op(eng, lambda e: e.memset(ap, val), (), w)

    def recip(self, out, in_, r=(), w=()):
        self.S.op("dve", lambda e: e.reciprocal(out=out, in_=in_), r, w)

    def finish(self):
        res = self.S.emit()
        self.es.close()
        return res


def load_convert(C, dst_bf, src_dram, stage, stage_key, dst_key, ncols, scale_ap=None, parity=0, scale_key="gscale"):
    C.dma(stage[:, 0:ncols], src_dram, w=[stage_key])
    if scale_ap is not None:
        C.ts("dve", dst_bf, stage[:, 0:ncols], scale_ap, None, ALU.mult, r=[stage_key, scale_key], w=[dst_key])
    elif parity % 2 == 0:
        C.cp("dve", dst_bf, stage[:, 0:ncols], r=[stage_key], w=[dst_key])
    else:
        C.cp("pool", dst_bf, stage[:, 0:ncols], r=[stage_key], w=[dst_key])


NLOC = 1804


MIX_W = ["w_kv", "w_loc", "w_out", "gpre", "gpost", "convw", "convv", "w1d", "w2d", "ped", "sgv", "sgw", "sgb",
         "poolw", "poolsc"]
MIX_W_SHAPES = {"w_kv": [D, 512], "w_loc": [D, NLOC], "w_out": [D, D], "gpre": [128, 8], "gpost": [1, D],
                "convw": [128, 2, 31], "convv": [128, 2, 3], "w1d": [128, 2048], "w2d": [64, 128], "ped": [128, 32],
                "sgv": [2, 256], "sgw": [128, 4, 128], "sgb": [4, 128], "poolw": [128, 2, 128], "poolsc": [128, 2]}
MIX_C_SHAPES = {"ropeC": [128, SEQ], "ropeS": [128, SEQ], "qC": [NS, 128, 128], "qS": [NS, 128, 128], "indp": [64, SEQ],
                "identd": [128, 128], "wmaskd": [8, 128, 128], "cmaskd": [5, 128, 128], "selA": [NS, 128, 128],
                "selB": [NS, 128, 128], "impM": [4, 128, 128], "trild": [128, 128], "rcnt0": [128, 2, 128],
                "rcntg": [128, 2, 128], "selm": [128, 32], "selm2": [8, 2]}
FFN_W_SHAPES = {"w_up": [D, 2 * FF], "w_dn": [FF, D], "g2": [128, 8], "gpost2": [1, D], "cwd": [128, NFC, 4]}


def mixer_phase(C, T, l, nslots=NS, stages=5):
    C.begin_phase("_m%d" % l)
    nc = C.nc
    xg_tile = T["xg_tile"]; xown_tile = T["xown_tile"]; xm_own = T["xm_own"]; xm_tail = T["xm_tail"]
    w_kv, w_loc, w_out, gpre, gpost = (T[k + "_%d" % l] for k in ("w_kv", "w_loc", "w_out", "gpre", "gpost"))
    convw, convv, w1d, w2d, ped = (T[k + "_%d" % l] for k in ("convw", "convv", "w1d", "w2d", "ped"))
    sgv, sgw, sgb, poolw, poolsc = (T[k + "_%d" % l] for k in ("sgv", "sgw", "sgb", "poolw", "poolsc"))
    ropeC, ropeS, qC, qS, indp, identd = (T[k] for k in ("ropeC", "ropeS", "qC", "qS", "indp", "identd"))
    wmaskd, cmaskd, selA, selB, impM, trild = (T[k] for k in ("wmaskd", "cmaskd", "selA", "selB", "impM", "trild"))
    rcnt0, rcntg, selmd = T["rcnt0"], T["rcntg"], T["selm"]

    sb = C.sb
    wkv = sb("wkv", [128, 8, 512], BF16)
    wloc = sb("wloc", [128, 8, NLOC], BF16)
    wo = sb("wo", [128, 8, D], BF16)
    xt = [sb("xt0", [128, D]), sb("xt1", [128, D])]
    stage = xt
    gpre_t = sb("gpre_t", [128, 8])
    gpost_bc = sb("gpost_bc", [128, D])
    ident = sb("ident", [128, 128], BF16)
    ones_b = sb("ones_b", [128, 128], BF16)
    shi = sb("shi", [128, 2, 128], BF16)
    slo = sb("slo", [128, 2, 128], BF16)
    Dg = sb("Dg", [128, 2, 31, 128], BF16)
    convw_t = sb("convw_t", [128, 2, 31])
    convv_t = sb("convv_t", [128, 2, 3])
    W1k = sb("W1k", [64, 32, 64], BF16)
    W1v = sb("W1v", [64, 32, 64], BF16)
    W2 = sb("W2", [64, 128], BF16)
    peTk = sb("peTk", [64, 32], BF16)
    peTv = sb("peTv", [64, 32], BF16)
    hnc = sb("hnc", [128, D], BF16)
    cbias = sb("cbias", [64, 2])
    Kaug = sb("Kaug", [128, SEQ], BF16)
    Kw = sb("Kw", [128, 12, 128], BF16)
    Vs = sb("Vs", [128, NT, 65], BF16)
    Vw = sb("Vw", [128, 12, 65], BF16)
    kvk = [sb("kvk0", [64, 528], BF16), sb("kvk1", [64, 528], BF16)]
    kvv = [sb("kvv0", [64, 528], BF16), sb("kvv1", [64, 528], BF16)]
    gkT = sb("gkT", [64, 576], BF16)
    gvT = sb("gvT", [64, 576], BF16)
    kcT = sb("kcT", [128, 576], BF16)
    rhsC = sb("rhsC", [128, 4, 193], BF16)
    wmask = sb("wmask", [128, 8, 128], BF16)
    cmask = sb("cmask", [128, 5, 128], BF16)
    wsT = sb("wsT", [128, 4, 128], BF16)
    tril = sb("tril", [128, 128])
    Bs = sb("Bs", [128, 2, 128])
    Wbd = sb("Wbd", [128, 2, 128], BF16)
    poolsc_t = sb("poolsc_t", [128, 2])
    rc0 = sb("rc0", [128, 2, 128])
    rcg = sb("rcg", [128, 2, 128])
    lng_bc = sb("lng_bc", [128, 256])
    lnb_bc = sb("lnb_bc", [128, 256])
    hnb = [sb("hnb0", [128, D], BF16), sb("hnb1", [128, D], BF16)]
    hT = [sb("hT0", [128, 8, 128], BF16), sb("hT1", [128, 8, 128], BF16)]
    rC = [sb("rC0", [128, 128]), sb("rC1", [128, 128])]
    rS = [sb("rS0", [128, 128]), sb("rS1", [128, 128])]
    junk = sb("junk", [128, D], BF16)
    ssq = sb("ssq", [128, 8])
    rstd = sb("rstd", [128, 8])
    ropet = sb("ropet", [128, 2, 128])
    xo = sb("xo", [128, D])
    hno = sb("hno", [128, D], BF16)
    hTo = sb("hTo", [128, 8, 160], BF16)
    sig = sb("sig", [128, 2, 160])
    ub = sb("ub", [128, 2, 160], BF16)
    zdT = sb("zdT", [128, 2, 160])
    s2 = sb("s2", [128, 2, 160]); s4 = sb("s4", [128, 2, 160])
    s8 = sb("s8", [128, 2, 160]); s16 = sb("s16", [128, 2, 160])
    plf = sb("plf", [128, 2, 128]); plb = sb("plb", [128, 2, 128], BF16)
    usg = sb("usg", [128, 2, 128])
    qCt = sb("qCt", [128, 128]); qSt = sb("qSt", [128, 128])
    qt1 = sb("qt1", [128, 128]); qt2 = sb("qt2", [128, 128]); qsum = sb("qsum", [128, 128])
    QR = sb("QR", [128, 2, 4, 128], BF16)
    QC = sb("QC", [128, 4, 128], BF16)
    QW = sb("QW", [128, 4, 128], BF16)
    gts = sb("gts", [128, 12])
    vg = sb("vg", [128, 256]); vn = sb("vn", [128, 256]); vbf = sb("vbf", [128, 256], BF16)
    bnst = sb("bnst", [128, 6]); bnag = sb("bnag", [128, 2]); lnr = sb("lnr", [128, 2])
    ycv = sb("ycv", [128, 2, 128]); ysq = sb("ysq", [128, 2, 128])
    cmean = sb("cmean", [128, 128]); cmsq = sb("cmsq", [128, 128]); cvar = sb("cvar", [128, 128])
    crstd = sb("crstd", [128, 128]); cyn = sb("cyn", [128, 2, 128])
    sgt = sb("sgt", [128, 2, 128])
    yT = sb("yT", [128, 8, 128], BF16)
    Pb = [sb("Pb%d" % i, [128, 4, 128], BF16) for i in range(4)]
    selAt = sb("selAt", [128, 128]); selBt = sb("selBt", [128, 128])
    imp = sb("imp", [128, 128]); imp2 = sb("imp2", [128, 128]); impw = sb("impw", [128, 128])
    top8 = sb("top8", [128, 16]); selt1 = sb("selt1", [128, 128]); selt2 = sb("selt2", [128, 128])
    selbf = sb("selbf", [128, 128], BF16)
    den = sb("den", [128, 12]); rden = sb("rden", [128, 12]); gsc = sb("gsc", [128, 12])
    Ynsa = sb("Ynsa", [128, 256]); Ynb = sb("Ynb", [128, 256], BF16)
    yo = sb("yo", [128, D])
    oss = sb("oss", [128, 4])
    selm = sb("selm", [128, 32], BF16)
    cg1 = sb("cg1", [64, 64]); cg2 = sb("cg2", [64, 64])

    psS = [C.ps("psS0", [128, 512]), C.ps("psS1", [128, 512])]
    psC = C.ps("psC", [128, 512])
    psK = C.ps("psK", [128, 512])
    psSel = C.ps("psSel", [128, 512])
    psWin = C.ps("psWin", [128, 512])
    psG = C.ps("psG", [128, 512])
    psT = C.ps("psT", [128, 1024], BF16)

    C.dma(gpre_t[:], gpre[:, :], w=["gscale"])
    C.dma(gpost_bc[:], gpost.partition_broadcast(128), w=["gpost_bc"])
    C.dma(stage[0][:, 0:128], identd[:, :], w=["xt0"])
    C.cp("dve", ident[:], stage[0][:, 0:128], r=["xt0"], w=["ident"])
    C.memset("dve", ones_b[:], 1.0, w=["ones_b"])
    C.dma(stage[1][:, 0:32], selmd[:, :], w=["xt1"])
    C.cp("dve", selm[:], stage[1][:, 0:32], r=["xt1"], w=["selm"])
    par = 0
    for kc in range(8):
        s = stage[par % 2]; sk = "xt%d" % (par % 2); par += 1
        load_convert(C, wkv[:, kc, :], w_kv[kc * 128:(kc + 1) * 128, :], s, sk, "wkv", 512, scale_ap=gpre_t[:, kc:kc + 1])
        s = stage[par % 2]; sk = "xt%d" % (par % 2); par += 1
        load_convert(C, wloc[:, kc, 0:902], w_loc[kc * 128:(kc + 1) * 128, 0:902], s, sk, "wloc", 902, scale_ap=gpre_t[:, kc:kc + 1])
        s = stage[par % 2]; sk = "xt%d" % (par % 2); par += 1
        load_convert(C, wloc[:, kc, 902:NLOC], w_loc[kc * 128:(kc + 1) * 128, 902:NLOC], s, sk, "wloc", 902, scale_ap=gpre_t[:, kc:kc + 1])
        s = stage[par % 2]; sk = "xt%d" % (par % 2); par += 1
        load_convert(C, wo[:, kc, :], w_out[kc * 128:(kc + 1) * 128, :], s, sk, "wo", D, parity=kc)
    for q in range(8):
        s = stage[par % 2]; sk = "xt%d" % (par % 2); par += 1
        C.dma(s[64:128, 0:1024], indp[:, q * 1024:(q + 1) * 1024], w=[sk])
        C.cp("pool", Kaug[64:128, q * 1024:(q + 1) * 1024], s[64:128, 0:1024], r=[sk], w=["Kaug_ind"])
    C.memset("pool", Kw[64:128, :, :], 0.0, w=["Kw"])
    C.memset("pool", kcT[:, :], 0.0, w=["kcT"])
    C.memset("pool", gkT[:, :], 0.0, w=["gkT"])
    C.memset("pool", gvT[:, :], 0.0, w=["gvT"])
    for q_ in range(2):
        C.memset("pool", kvk[q_][:, :], 0.0, w=["kvk%d" % q_])
        C.memset("pool", kvv[q_][:, :], 0.0, w=["kvv%d" % q_])
    C.memset("pool", QC[:, :, :], 0.0, w=["QC"])
    C.memset("pool", QW[:, :, :], 0.0, w=["QW"])
    C.memset("pool", Vs[:, :, 64:65], 1.0, w=["Vs"])
    C.memset("pool", Vw[:, :, 64:65], 1.0, w=["Vw"])
    C.memset("pool", rhsC[:, :, :], 0.0, w=["rhsC"])
    C.memset("pool", rhsC[:, :, 192:193], 1.0, w=["rhsC"])
    for q in range(8):
        s = stage[par % 2]; sk = "xt%d" % (par % 2); par += 1
        C.dma(s[:, 0:128], wmaskd[q, :, :], w=[sk])
        C.cp("dve", wmask[:, q, :], s[:, 0:128], r=[sk], w=["wmask"])
    for q in range(4):
        s = stage[par % 2]; sk = "xt%d" % (par % 2); par += 1
        C.dma(s[:, 0:128], cmaskd[q, :, :], w=[sk])
        C.cp("dve", cmask[:, q, :], s[:, 0:128], r=[sk], w=["cmask"])
        if q == 0:
            s = stage[par % 2]; sk = "xt%d" % (par % 2); par += 1
            C.dma(s[:, 0:128], cmaskd[4, :, :], w=[sk])
            C.cp("dve", cmask[:, 4, :], s[:, 0:128], r=[sk], w=["cmask"])
        s = stage[par % 2]; sk = "xt%d" % (par % 2); par += 1
        C.dma(s[:, 0:128], impM[q, :, :], w=[sk])
        C.cp("dve", rhsC[:, q, 64:192], s[:, 0:128], r=[sk], w=["rhsC"])
    C.dma(convw_t[:], convw[:, :, :], w=["convw_t"])
    C.dma(convv_t[:], convv[:, :, :], w=["convv_t"])
    C.dma(stage[0][:, 0:128], identd[:, :], w=["xt0"])
    for cc in range(2):
        for k in range(31):
            C.ts("dve" if k % 2 == 0 else "pool", Dg[:, cc, k, :], stage[0][:, 0:128], convw_t[:, cc, k:k + 1], None, ALU.mult,
                 r=["xt0", "convw_t"], w=["Dg"])
    for wi_, (Wt_, wk_) in enumerate(((W1k, "W1k"), (W1v, "W1v"))):
        for q in range(2):
            C.dma(stage[q][0:64, 0:1024], w1d[64 * wi_:64 * wi_ + 64, q * 1024:(q + 1) * 1024], w=["xt%d" % q])
            C.cp("dve", Wt_[:].rearrange("p r e -> p (r e)")[:, q * 1024:(q + 1) * 1024], stage[q][0:64, 0:1024],
                 r=["xt%d" % q], w=[wk_])
    C.dma(stage[0][0:64, 0:128], w2d[:, :], w=["xt0"])
    C.cp("dve", W2[:], stage[0][0:64, 0:128], r=["xt0"], w=["W2"])
    C.dma(stage[0][0:64, 0:32], ped[0:64, :], w=["xt0"])
    C.cp("dve", peTk[:], stage[0][0:64, 0:32], r=["xt0"], w=["peTk"])
    C.dma(stage[1][0:64, 0:32], ped[64:128, :], w=["xt1"])
    C.cp("dve", peTv[:], stage[1][0:64, 0:32], r=["xt1"], w=["peTv"])
    for wi_, (Wt_, wk_, pt_, pk_) in enumerate(((W1k, "W1k", peTk, "peTk"), (W1v, "W1v", peTv, "peTv"))):
        for r_ in range(32):
            C.mm(psK[0:64, 0:1], Wt_[:, r_, :], pt_[:, r_:r_ + 1], start=(r_ == 0), stop=(r_ == 31),
                 r=[wk_, pk_], w=["psK"])
        C.cp("dve", cbias[:, wi_:wi_ + 1], psK[0:64, 0:1], w=["psK", "cbias"])
    C.dma(lng_bc[:], sgv[0:1, :].partition_broadcast(128), w=["lng_bc"])
    C.dma(lnb_bc[:], sgv[1:2, :].partition_broadcast(128), w=["lnb_bc"])
    C.dma(tril[:], trild[:, :], w=["tril"])
    C.dma(stage[0][:, 0:512], sgw.rearrange("p g t -> p (g t)"), w=["xt0"])
    for g in range(4):
        C.tt("dve", wsT[:, g, :], stage[0][:, g * 128:(g + 1) * 128], tril[:], ALU.mult, r=["xt0", "tril"], w=["wsT"])
        C.dma(Bs[(g % 2) * 64:(g % 2) * 64 + 64, g // 2, :], sgb[g:g + 1, :].partition_broadcast(64), w=["Bs"])
    C.dma(stage[1][:, 0:256], poolw.rearrange("p q d -> p (q d)"), w=["xt1"])
    C.cp("dve", Wbd[:].rearrange("p q d -> p (q d)"), stage[1][:, 0:256], r=["xt1"], w=["Wbd"])
    C.dma(poolsc_t[:], poolsc[:, :], w=["poolsc_t"])
    C.dma(rc0[:], rcnt0[:, :, :], w=["rc0"])
    C.dma(rcg[:], rcntg[:, :, :], w=["rcg"])

    def rms_to_bf(xt_ap, xk, out_bf, ok, np_, col):
        C.act(junk[0:np_, :], xt_ap, AF.Square, r=[xk], w=["junk", "ssq%d" % col], accum=ssq[0:np_, col:col + 1])
        C.act(rstd[0:np_, col:col + 1], ssq[0:np_, col:col + 1], AF.Ln, r=["ssq%d" % col, "epsb"], w=["rstd%d" % col],
              scale=1.0 / D, bias=epsb[0:np_, 0:1])
        C.act(rstd[0:np_, col:col + 1], rstd[0:np_, col:col + 1], AF.Exp, r=["rstd%d" % col], w=["rstd%d" % col], scale=-0.5)
        C.act(out_bf, xt_ap, AF.Copy, r=[xk, "rstd%d" % col], w=[ok], scale=rstd[0:np_, col:col + 1])

    epsb = sb("epsb", [128, 2])
    C.memset("dve", epsb[:, 0:1], RMS_EPS, w=["epsb"])
    C.memset("dve", epsb[:, 1:2], LN_EPS, w=["epsb"])

    def kv_tile(m):
        p = m % 2
        xk, hk, tk = "xt%d" % p, "hnb%d" % p, "hT%d" % p
        C.dma(xt[p][:], xg_tile(m)[:, :], w=[xk])
        C.dma(rC[p][:], ropeC[:, m * 128:(m + 1) * 128], w=["rC%d" % p])
        C.dma(rS[p][:], ropeS[:, m * 128:(m + 1) * 128], w=["rS%d" % p])
        yield
        C.act(junk[:, :], xt[p][:], AF.Square, r=[xk], w=["junk", "ssq%d" % p], accum=ssq[:, p:p + 1])
        yield
        C.act(rstd[:, p:p + 1], ssq[:, p:p + 1], AF.Ln, r=["ssq%d" % p, "epsb"], w=["rstd%d" % p],
              scale=1.0 / D, bias=epsb[:, 0:1])
        yield
        C.act(rstd[:, p:p + 1], rstd[:, p:p + 1], AF.Exp, r=["rstd%d" % p], w=["rstd%d" % p], scale=-0.5)
        yield
        C.ts("dve", hnb[p][:], xt[p][:], rstd[:, p:p + 1], None, ALU.mult, r=[xk, "rstd%d" % p], w=[hk])
        yield
        for half in range(2):
            for q in range(4):
                kc = 4 * half + q
                C.tr(psT[:, q * 128:(q + 1) * 128], hnb[p][:, kc * 128:(kc + 1) * 128], ident[:], r=[hk, "ident"], w=["psT"])
            yield
            C.cp("dve", hT[p][:, 4 * half:4 * half + 4, :], psT[:, 0:512].rearrange("p (k t) -> p k t", k=4), w=["psT", tk])
            yield
        for f in range(3):
            for kc in range(8):
                C.mm(psK[:, f * 128:(f + 1) * 128], wkv[:, kc, f * 128:(f + 1) * 128], hT[p][:, kc, :],
                     start=(kc == 0), stop=(kc == 7), r=["wkv", tk], w=["psK"])
            yield
        for kc in range(8):
            C.mm(psK[:, 384:512], hT[p][:, kc, :], wkv[:, kc, 384:512], start=(kc == 0), stop=(kc == 7),
                 r=["wkv", tk], w=["psK"])
        yield
        grp = (m // 4) % 2
        col0 = 16 + (m % 4) * 128
        C.cp("act", kvk[grp][:, col0:col0 + 128], psK[0:64, 0:128], w=["psK", "kvk%d" % grp])
        C.cp("act", kvv[grp][:, col0:col0 + 128], psK[64:128, 0:128], w=["psK", "kvv%d" % grp])
        C.tt("dve", ropet[:, 0, :], psK[:, 128:256], rC[p][:], ALU.mult, r=["rC%d" % p], w=["psK", "ropet0"])
        C.tt("dve", ropet[:, 1, :], psK[:, 256:384], rS[p][:], ALU.mult, r=["rS%d" % p], w=["psK", "ropet1"])
        yield
        C.cp("dve", Vs[:, m, 0:64], psK[:, 384:448], w=["psK", "Vs%d" % m])
        C.cp("dve", Vw[:, m % 12, 0:64], psK[:, 448:512], w=["psK", "Vw%d" % (m % 12)])
        yield
        C.tt("pool", Kaug[0:64, m * 128:(m + 1) * 128], ropet[0:64, 0, :], ropet[0:64, 1, :], ALU.add,
             r=["ropet0", "ropet1"], w=["Kaug%d" % m])
        C.tt("pool", Kw[0:64, m % 12, :], ropet[64:128, 0, :], ropet[64:128, 1, :], ALU.add,
             r=["ropet0", "ropet1"], w=["Kw%d" % (m % 12)])
        yield

    def compress(j):
        g = j % 2
        n0 = 32 * j - 1
        c0 = 32 + n0
        for which, (Wt_, wk_, kb, kk, gT, gk_) in enumerate(((W1k, "W1k", kvk[g], "kvk%d" % g, gkT, "gkT"),
                                                           (W1v, "W1v", kvv[g], "kvv%d" % g, gvT, "gvT"))):
            for r_ in range(32):
                C.mm(psK[0:64, which * 32:(which + 1) * 32], Wt_[:, r_, :], kb[:, r_:r_ + 16 * 31 + 1:16],
                     start=(r_ == 0), stop=(r_ == 31), r=[wk_, kk], w=["psK"])
            yield
            cgt = cg1[:, which * 32:(which + 1) * 32]
            cgs = cg2[:, which * 32:(which + 1) * 32]
            k1, k2 = ["cg1_%d" % which], ["cg2_%d" % which]
            C.ts("dve", cgt, psK[0:64, which * 32:(which + 1) * 32], cbias[:, which:which + 1], None, ALU.add,
                 r=["cbias"], w=["psK"] + k1)
            C.tt("dve", cgs, cgt, cgt, ALU.mult, r=k1, w=k2)
            C.ts("dve", cgs, cgs, 0.044715, 1.0, ALU.mult, ALU.add, r=k2, w=k2)
            C.tt("dve", cgs, cgs, cgt, ALU.mult, r=k2 + k1, w=k2)
            C.act(cgs, cgs, AF.Exp, r=k2, w=k2, scale=-GELU_C)
            C.ts("dve", cgs, cgs, 1.0, None, ALU.add, r=k2, w=k2)
            C.recip(cgs, cgs, r=k2, w=k2)
            C.tt("dve", gT[:, c0:c0 + 32], cgt, cgs, ALU.mult, r=k1 + k2, w=[gk_])
            other = (kvk, kvv)[which][1 - g]
            C.cp("pool", other[:, 0:16], kb[:, 512:528], r=[kk], w=[("kvk%d", "kvv%d")[which] % (1 - g)])
            yield
        C.mm(psK[0:64, 64:96], W2[:, 0:64], gkT[:, c0:c0 + 32], r=["W2", "gkT"], w=["psK"])
        yield
        C.cp("dve", kcT[0:64, c0:c0 + 32], psK[0:64, 64:96], w=["psK", "kcT"])
        yield
        chunks = sorted(set([max(n0, 0) // 128, (n0 + 31) // 128]))
        for ci in chunks:
            C.mm(psK[:, 128:192], gvT[:, 32 + ci * 128:32 + (ci + 1) * 128], W2[:, 64:128], r=["W2", "gvT"], w=["psK"])
            yield
            C.cp("dve", rhsC[:, ci, 0:64], psK[:, 128:192], w=["psK", "rhsC"])
            yield

    def local(j):
        yield
        C.dma(xo[:], xown_tile(j)[:, :], w=["xo"])
        for cand in range(4):
            m_ = 4 * j - 1 + cand
            if m_ < 0:
                C.memset("pool", yo[0:32, :], 0.0, w=["yo"])
            else:
                C.dma(yo[cand * 32:(cand + 1) * 32, :], xg_tile(m_)[96:128, :], w=["yo"])
        C.dma(qCt[:], qC[j, :, :], w=["qCt"])
        C.dma(qSt[:], qS[j, :, :], w=["qSt"])
        C.dma(selAt[:], selA[j, :, :], w=["selAt"])
        C.dma(selBt[:], selB[j, :, :], w=["selBt"])
        rms_to_bf(xo[:], "xo", hno[:], "hno", 128, 2)
        yield
        rms_to_bf(yo[:], "yo", hnc[:], "hnc", 128, 3)
        yield
        for half in range(2):
            for q in range(4):
                kc = 4 * half + q
                C.tr(psT[:, 512 + q * 128:512 + (q + 1) * 128], hno[:, kc * 128:(kc + 1) * 128], ident[:],
                     r=["hno", "ident"], w=["psT"])
            yield
            C.cp("act", hTo[:, 4 * half:4 * half + 4, 32:160], psT[:, 512:1024].rearrange("p (k t) -> p k t", k=4),
                 w=["psT", "hTo"])
            yield
        for kc in range(8):
            C.mm(psG[:, kc * 32:(kc + 1) * 32], hnc[:, kc * 128:(kc + 1) * 128], selm[:], r=["hnc", "selm"], w=["psG"])
        yield
        C.cp("act", hTo[:, :, 0:32], psG[:, 0:256].rearrange("p (k t) -> p k t", k=8), w=["psG", "hTo"])
        yield

        def proj(chunk, ncol, out_ap):
            c0 = 160 - ncol
            for kc in range(8):
                C.mm(out_ap, wloc[:, kc, chunk * 128:(chunk + 1) * 128], hTo[:, kc, c0:160],
                     start=(kc == 0), stop=(kc == 7), r=["wloc", "hTo"], w=["psG"])

        yield
        for cc in range(2):
            proj(2 + cc, 160, psG[:, 0:160])
            C.act(sig[:, cc, :], psG[:, 0:160], AF.Sigmoid, w=["psG", "sig"])
        for cc in range(2):
            proj(cc, 160, psG[:, 0:160])
            C.tt("dve", ub[:, cc, :], psG[:, 0:160], sig[:, cc, :], ALU.mult, r=["sig"], w=["psG", "ub"])
        yield
        for cc in range(2):
            proj(4 + cc, 160, psG[:, 0:160])
            C.cp("act", zdT[:, cc, :], psG[:, 0:160], w=["psG", "zdT"])
        yield
        for cc in range(2):
            proj(6 + cc, 128, psG[:, 0:128])
            C.act(usg[:, cc, :], psG[:, 0:128], AF.Gelu_apprx_tanh, w=["psG", "usg"])
        yield
        for hp in range(2):
            for kc in range(8):
                C.mm(psG[:, 0:128], wloc[:, kc, 1024 + hp * 128:1024 + (hp + 1) * 128], hTo[:, kc, 32:160],
                     start=(kc == 0), stop=(kc == 7), r=["wloc", "hTo"], w=["psG"])
            for kc in range(8):
                C.mm(psG[:, 128:256], wloc[:, kc, 1280 + hp * 128:1280 + (hp + 1) * 128], hTo[:, kc, 32:160],
                     start=(kc == 0), stop=(kc == 7), r=["wloc", "hTo"], w=["psG"])
            C.tt("dve", qt1[:], psG[:, 0:128], qCt[:], ALU.mult, r=["qCt"], w=["psG", "qt1"])
            C.tt("dve", qt2[:], psG[:, 128:256], qSt[:], ALU.mult, r=["qSt"], w=["psG", "qt2"])
            C.act(QC[0:64, 2 * hp, :], psG[0:64, 0:128], AF.Copy, w=["psG", "QC"], scale=0.125)
            C.act(QC[0:64, 2 * hp + 1, :], psG[64:128, 0:128], AF.Copy, w=["psG", "QC"], scale=0.125)
            yield
            C.tt("dve", qsum[:], qt1[:], qt2[:], ALU.add, r=["qt1", "qt2"], w=["qsum"])
            yield
            for v in range(2):
                C.cp("dve", QR[0:64, v, 2 * hp, :], qsum[0:64, :], r=["qsum"], w=["QRq"])
                C.cp("act", QR[0:64, v, 2 * hp + 1, :], qsum[64:128, :], r=["qsum"], w=["QRq"])
            C.cp("pool", QW[0:64, 2 * hp, :], qsum[0:64, :], r=["qsum"], w=["QW"])
            C.cp("act", QW[0:64, 2 * hp + 1, :], qsum[64:128, :], r=["qsum"], w=["QW"])
        yield
        for kc in range(8):
            C.mm(psG[:, 0:268], hTo[:, kc, 32:160], wloc[:, kc, 1536:1804], start=(kc == 0), stop=(kc == 7),
                 r=["wloc", "hTo"], w=["psG"])
        C.act(gts[:], psG[:, 256:268], AF.Sigmoid, w=["psG", "gts"])
        C.act(vg[:], psG[:, 0:256], AF.Gelu_apprx_tanh, w=["psG", "vg"])
        C.S.op("dve", lambda e: e.bn_stats(out=bnst[:], in_=vg[:]), ["vg"], ["bnst"])
        C.S.op("dve", lambda e: e.bn_aggr(out=bnag[:], in_=bnst[:]), ["bnst"], ["bnag"])
        C.act(lnr[:, 0:1], bnag[:, 1:2], AF.Ln, r=["bnag", "epsb"], w=["lnr"], bias=epsb[:, 1:2])
        C.act(lnr[:, 1:2], lnr[:, 0:1], AF.Exp, r=["lnr"], w=["lnr1"], scale=-0.5)
        C.ts("dve", vn[:], vg[:], bnag[:, 0:1], lnr[:, 1:2], ALU.subtract, ALU.mult, r=["vg", "bnag", "lnr1"], w=["vn"])
        C.tt("dve", vn[:], vn[:], lng_bc[:], ALU.mult, r=["vn", "lng_bc"], w=["vn"])
        C.tt("dve", vbf[:], vn[:], lnb_bc[:], ALU.add, r=["vn", "lnb_bc"], w=["vbf"])
        yield
        for cc in range(2):
            for k in range(31):
                C.mm(psG[:, 0:128], Dg[:, cc, k, :], ub[:, cc, 2 + k:2 + k + 128], start=(k == 0), stop=(k == 30),
                     r=["Dg", "ub"], w=["psG"])
            C.act(ycv[:, cc, :], psG[:, 0:128], AF.Identity, r=["convv_t"], w=["psG", "ycv"], bias=convv_t[:, cc, 0:1])
            C.act(ysq[:, cc, :], psG[:, 0:128], AF.Square, r=["convv_t"], w=["psG", "ysq"], bias=convv_t[:, cc, 0:1])
        for src, sk_, c0_ in ((ycv, "ycv", 0), (ysq, "ysq", 128)):
            C.cp("dve", shi[:], src[:], r=[sk_], w=["shi"])
            C.tt("dve", slo[:], src[:], shi[:], ALU.subtract, r=[sk_, "shi"], w=["slo"])
            n_ = 0
            for part, pk_ in ((shi, "shi"), (slo, "slo")):
                for cc in range(2):
                    C.mm(psG[:, c0_:c0_ + 128], ones_b[:], part[:, cc, :], start=(n_ == 0), stop=(n_ == 3),
                         r=["ones_b", pk_], w=["psG"])
                    n_ += 1
        C.act(cmean[:], psG[:, 0:128], AF.Copy, w=["psG", "cmean"], scale=1.0 / 256)
        C.tt("dve", cmsq[:], cmean[:], cmean[:], ALU.mult, r=["cmean"], w=["cmsq"])
        C.stt(cvar[:], psG[:, 128:256], 1.0 / 256, cmsq[:], ALU.mult, ALU.subtract, r=["cmsq"], w=["psG", "cvar"])
        C.act(crstd[:], cvar[:], AF.Ln, r=["cvar", "epsb"], w=["crstd"], bias=epsb[:, 1:2])
        C.act(crstd[:], crstd[:], AF.Exp, r=["crstd"], w=["crstd"], scale=-0.5)
        for cc in range(2):
            C.tt("dve", cyn[:, cc, :], ycv[:, cc, :], cmean[:], ALU.subtract, r=["ycv", "cmean"], w=["cyn%d" % cc])
            C.tt("dve", cyn[:, cc, :], cyn[:, cc, :], crstd[:], ALU.mult, r=["cyn%d" % cc, "crstd"], w=["cyn%d" % cc])
            C.act(yT[:, cc, :], cyn[:, cc, :], AF.Silu, r=["cyn%d" % cc, "convv_t"], w=["yT%d" % cc],
                  scale=convv_t[:, cc, 1:2], bias=convv_t[:, cc, 2:3])
        yield
        for g in range(4):
            q = g // 2
            C.mm(psG[:, g * 128:(g + 1) * 128], vbf[:, q * 128:(q + 1) * 128], wsT[:, g, :], r=["vbf", "wsT"], w=["psG"])
        for g in range(4):
            q = g // 2
            lo = (g % 2) * 64
            C.tt("dve", sgt[lo:lo + 64, q, :], psG[lo:lo + 64, g * 128:(g + 1) * 128], Bs[lo:lo + 64, q, :], ALU.add,
                 r=["Bs"], w=["psG", "sgt%d" % g])
            C.tt("pool", yT[lo:lo + 64, 4 + q, :], sgt[lo:lo + 64, q, :], usg[lo:lo + 64, q, :], ALU.mult,
                 r=["sgt%d" % g, "usg"], w=["yT%d" % (4 + q)])
        yield
        C.tt("pool", s2[:, :, 1:160], zdT[:, :, 1:160], zdT[:, :, 0:159], ALU.add, r=["zdT"], w=["s2"])
        C.tt("pool", s4[:, :, 3:160], s2[:, :, 3:160], s2[:, :, 1:158], ALU.add, r=["s2"], w=["s4"])
        C.tt("pool", s8[:, :, 7:160], s4[:, :, 7:160], s4[:, :, 3:156], ALU.add, r=["s4"], w=["s8"])
        C.tt("pool", s16[:, :, 15:160], s8[:, :, 15:160], s8[:, :, 7:152], ALU.add, r=["s8"], w=["s16"])
        rc = rc0 if j == 0 else rcg
        srcs = [(s2, "s2", 0, 0), (s4, "s4", 64, 0), (s8, "s8", 0, 1), (s16, "s16", 64, 1)]
        for gi, (sbuf_, sk, lo, q) in enumerate(srcs):
            C.tt("dve", plf[lo:lo + 64, q, :], sbuf_[lo:lo + 64, q, 32:160], rc[lo:lo + 64, q, :], ALU.mult,
                 r=[sk, "rc0", "rcg"], w=["plf%d" % gi])
            C.tt("dve", plb[lo:lo + 64, q, :], plf[lo:lo + 64, q, :], zdT[lo:lo + 64, q, 32:160], ALU.subtract,
                 r=["plf%d" % gi, "zdT"], w=["plb%d" % q])
        for q in range(2):
            C.mm(psG[:, q * 128:(q + 1) * 128], Wbd[:, q, :], plb[:, q, :], r=["Wbd", "plb%d" % q], w=["psG"])
        for q in range(2):
            C.act(yT[:, 6 + q, :], psG[:, q * 128:(q + 1) * 128], AF.Copy, r=["poolsc_t"], w=["psG", "yT%d" % (6 + q)],
                  scale=poolsc_t[:, q:q + 1])

    pcount = [0]

    def nextP():
        i = pcount[0] % 4
        pcount[0] += 1
        return Pb[i], "Pb%d" % i

    scount = [0]

    def nextS(nb=2):
        i = scount[0] % nb
        scount[0] += 1
        if i == 2:
            return psC, "psC"
        return psS[i], "psS%d" % i

    def nsa(j):
        L = (32 * j + 30) // 128

        def pipeline(chunks, depth=1):
            state = []
            for i, (A, B, Cc) in enumerate(chunks):
                while len(state) < min(len(chunks), i + depth + 1):
                    state.append(chunks[len(state)][0]())
                st = state[i]
                B(st)
                yield
                Cc(st)
                yield

        st_p = {}

        def mk_cmp(ci):
            def A():
                S_, sk = nextS()
                C.mm(S_[:, :], kcT[:, 32 + ci * 128:32 + (ci + 1) * 128], QC[:].rearrange("p h t -> p (h t)"),
                     r=["kcT", "QC"], w=[sk])
                return S_, sk

            def B(st):
                S_, sk = st
                P_, pk = Pb[ci], "Pb%d" % ci
                C.act(P_[:].rearrange("p h t -> p (h t)"), S_[:, :], AF.Exp, w=[sk, pk])
                if ci == L:
                    C.tt("dve", P_[:], P_[:], cmask[:, j % 4:j % 4 + 1, :].to_broadcast([128, 4, 128]), ALU.mult,
                         r=[pk, "cmask"], w=[pk])
                elif ci == L - 1 and j % 4 == 0:
                    C.tt("dve", P_[:], P_[:], cmask[:, 4:5, :].to_broadcast([128, 4, 128]), ALU.mult,
                         r=[pk, "cmask"], w=[pk])

            def Cc(st):
                pass
            return A, B, Cc

        yield from pipeline([mk_cmp(ci) for ci in range(L + 1)])
        for hp in range(2):
            for ci in range(L + 1):
                for h in (2 * hp, 2 * hp + 1):
                    c0 = (h % 2) * 193
                    C.mm(psC[:, c0:c0 + 193], Pb[ci][:, h, :], rhsC[:, ci, :], start=(ci == 0 and h % 2 == 0), stop=(ci == L),
                         r=["Pb%d" % ci, "rhsC"], w=["psC"], skip=True)
            yield
            for h in (2 * hp, 2 * hp + 1):
                c0 = (h % 2) * 193
                C.ts("dve", den[:, h:h + 1], psC[:, c0 + 192:c0 + 193], 1e-30, None, ALU.max, w=["psC", "den%d" % hp])
            C.recip(rden[:, 2 * hp:2 * hp + 2], den[:, 2 * hp:2 * hp + 2], r=["den%d" % hp], w=["rden%d" % hp])
            yield
            for h in (2 * hp, 2 * hp + 1):
                c0 = (h % 2) * 193
                if h == 0:
                    C.ts("dve", imp[:], psC[:, c0 + 64:c0 + 192], rden[:, 0:1], None, ALU.mult, r=["rden0"],
                         w=["psC", "imp"])
                else:
                    C.stt(imp[:], psC[:, c0 + 64:c0 + 192], rden[:, h:h + 1], imp[:], ALU.mult, ALU.add,
                          r=["rden%d" % hp, "imp"], w=["psC", "imp"])
            C.tt("dve", gsc[:, 2 * hp:2 * hp + 2], rden[:, 2 * hp:2 * hp + 2], gts[:, 6 * hp:6 * hp + 6:3], ALU.mult,
                 r=["rden%d" % hp, "gts"], w=["gscc%d" % hp])
            for h in (2 * hp, 2 * hp + 1):
                c0 = (h % 2) * 193
                C.ts("dve", Ynsa[:, h * 64:(h + 1) * 64], psC[:, c0:c0 + 64], gsc[:, h:h + 1], None, ALU.mult,
                     r=["gscc%d" % hp], w=["psC", "Ynsa"])
            yield
        C.tt("dve", imp2[:], imp[:], selAt[:], ALU.mult, r=["imp", "selAt"], w=["imp2"])
        C.tt("dve", imp2[:], imp2[:], selBt[:], ALU.add, r=["imp2", "selBt"], w=["imp2"])
        yield
        C.S.op("dve", lambda e: e.max(out=top8[:, 0:8], in_=imp2[:]), ["imp2"], ["top8a"])
        C.S.op("dve", lambda e: e.match_replace(out=impw[:], in_to_replace=top8[:, 0:8], in_values=imp2[:],
                                                imm_value=-3e38), ["imp2", "top8a"], ["impw"])
        yield
        C.S.op("dve", lambda e: e.max(out=top8[:, 8:16], in_=impw[:]), ["impw"], ["top8b"])
        C.ts("dve", selt1[:], imp2[:], top8[:, 15:16], None, ALU.is_ge, r=["imp2", "top8b"], w=["selt1"])
        C.ts("dve", selt2[:], imp2[:], -1e29, -NEG, ALU.is_gt, ALU.mult, r=["imp2"], w=["selt2"])
        yield
        C.tt("dve", selt1[:], selt1[:], selt2[:], ALU.mult, r=["selt1", "selt2"], w=["selt1"])
        C.ts("dve", selbf[:], selt1[:], NEG, None, ALU.add, r=["selt1"], w=["selbf"])
        yield

        wl = list(range(max(0, 4 * j - 4), 4 * j + 4))

        def mk_win(wi, m):
            def A():
                S_, sk = nextS(3)
                C.mm(S_[:, :], Kw[:, m % 12, :], QW[:].rearrange("p h t -> p (h t)"),
                     r=["Kw%d" % (m % 12), "Kw", "QW"], w=[sk])
                return S_, sk

            def B(st):
                S_, sk = st
                P_, pk = nextP()
                C.act(P_[:].rearrange("p h t -> p (h t)"), S_[:, :], AF.Exp, w=[sk, pk])
                C.tt("dve", P_[:], P_[:], wmask[:, 4 + m - 4 * j:5 + m - 4 * j, :].to_broadcast([128, 4, 128]), ALU.mult,
                     r=[pk, "wmask"], w=[pk])
                st_p[("w", wi)] = (P_, pk)

            def Cc(st):
                P_, pk = st_p[("w", wi)]
                for h in range(4):
                    C.mm(psWin[:, h * 65:(h + 1) * 65], P_[:, h, :], Vw[:, m % 12, :], start=(wi == 0 and h == 0),
                         stop=(wi == len(wl) - 1), r=[pk, "Vw%d" % (m % 12), "Vw"], w=["psWin"], skip=True)
            return A, B, Cc

        scount[0] = 0
        yield from pipeline([mk_win(wi, m) for wi, m in enumerate(wl)], depth=2)
        C.tr(psT[:, 512:640], selbf[:], ident[:], r=["selbf", "ident"], w=["psT"])
        yield
        C.cp("dve", QR[64:128, 0, :, :], psT[0:64, 512:640].unsqueeze(1).to_broadcast([64, 4, 128]), w=["psT", "QRb"])
        C.cp("act", QR[64:128, 1, :, :], psT[64:128, 512:640].unsqueeze(1).to_broadcast([64, 4, 128]), w=["psT", "QRb"])
        yield

        nsel = 4 * j + 4

        def mk_sel(m):
            def A():
                S_, sk = nextS(3)
                v = 0 if m < 32 else 1
                C.mm(S_[:, :], Kaug[:, m * 128:(m + 1) * 128], QR[:, v, :, :].rearrange("p h t -> p (h t)"),
                     r=["Kaug%d" % m, "Kaug_ind", "QRq", "QRb"], w=[sk])
                return S_, sk

            def B(st):
                S_, sk = st
                P_, pk = nextP()
                C.act(P_[:].rearrange("p h t -> p (h t)"), S_[:, :], AF.Exp, w=[sk, pk])
                if m >= 4 * j:
                    C.tt("dve", P_[:], P_[:], wmask[:, 4 + m - 4 * j:5 + m - 4 * j, :].to_broadcast([128, 4, 128]), ALU.mult,
                         r=[pk, "wmask"], w=[pk])
                st_p[("s", m)] = (P_, pk)

            def Cc(st):
                P_, pk = st_p[("s", m)]
                for h in range(4):
                    C.mm(psSel[:, h * 65:(h + 1) * 65], P_[:, h, :], Vs[:, m, :], start=(m == 0 and h == 0),
                         stop=(m == nsel - 1), r=[pk, "Vs%d" % m, "Vs"], w=["psSel"], skip=True)
            return A, B, Cc

        yield from pipeline([mk_sel(m) for m in range(nsel)], depth=2)
        scount[0] = 0
        C.cp("dve", den[:, 4:8], psSel[:, 64:260:65], w=["psSel", "den"])
        C.cp("dve", den[:, 8:12], psWin[:, 64:260:65], w=["psWin", "den"])
        C.recip(rden[:, 4:12], den[:, 4:12], r=["den"], w=["rden2"])
        C.tt("dve", gsc[:, 4:8], rden[:, 4:8], gts[:, 1:12:3], ALU.mult, r=["rden2", "gts"], w=["gsc1"])
        C.tt("dve", gsc[:, 8:12], rden[:, 8:12], gts[:, 2:12:3], ALU.mult, r=["rden2", "gts"], w=["gsc1"])
        for h in range(4):
            C.stt(Ynsa[:, h * 64:(h + 1) * 64], psSel[:, h * 65:h * 65 + 64], gsc[:, 4 + h:5 + h], Ynsa[:, h * 64:(h + 1) * 64],
                  ALU.mult, ALU.add, r=["gsc1", "Ynsa"], w=["psSel", "Ynsa"])
        for h in range(4):
            C.stt(Ynb[:, h * 64:(h + 1) * 64], psWin[:, h * 65:h * 65 + 64], gsc[:, 8 + h:9 + h], Ynsa[:, h * 64:(h + 1) * 64],
                  ALU.mult, ALU.add, r=["gsc1", "Ynsa"], w=["psWin", "Ynb"])
        yield
        for q in range(2):
            C.tr(psT[:, 512 + q * 128:512 + (q + 1) * 128], Ynb[:, q * 128:(q + 1) * 128], ident[:], r=["Ynb", "ident"], w=["psT"])
        yield
        C.cp("act", yT[:, 2:4, :], psT[:, 512:768].rearrange("p (q t) -> p q t", q=2), w=["psT", "yT2", "yT3"])
        yield

    def outproj(j):
        yk = ["yT%d" % f for f in range(8)]
        for half in range(2):
            for f in range(8):
                C.mm(psG[:, :], yT[:, f, :], wo[:, f, half * 512:(half + 1) * 512], start=(f == 0), stop=(f == 7),
                     r=yk + ["wo"], w=["psG"])
            yield
            C.cp("dve", yo[:, half * 512:(half + 1) * 512], psG[:, :], w=["psG", "yo"])
            yield
        C.act(junk[:, :], yo[:], AF.Square, r=["yo"], w=["junk", "oss"], accum=oss[:, 0:1])
        yield
        C.act(oss[:, 1:2], oss[:, 0:1], AF.Ln, r=["oss", "epsb"], w=["oss1"], scale=1.0 / D, bias=epsb[:, 0:1])
        C.act(oss[:, 2:3], oss[:, 1:2], AF.Exp, r=["oss1"], w=["oss2"], scale=-0.5)
        C.stt(yo[:], yo[:], oss[:, 2:3], gpost_bc[:], ALU.mult, ALU.mult, r=["yo", "oss2", "gpost_bc"], w=["yo"])
        C.tt("dve", yo[:], yo[:], xo[:], ALU.add, r=["yo", "xo"], w=["yo"])
        C.dma(xm_own[j, :, :], yo[:], r=["yo"])
        C.dma(xm_tail[2 * j:2 * j + 2, :], yo[126:128, :], r=["yo"])
        yield

    def stream_b(j):
        for m in range(4 * j, 4 * j + 4):
            yield from kv_tile(m)
        yield from compress(j)

    def stream_a(j):
        yield from local(j)
        yield from nsa(j)
        yield from outproj(j)

    for _ in stream_b(0):
        pass
    for j in range(nslots):
        A_ = stream_a(j)
        B_ = stream_b(j + 1) if j + 1 < nslots else None
        doneA = doneB = B_ is None and False
        doneB = B_ is None
        while not doneA:
            try:
                next(A_)
            except StopIteration:
                doneA = True
            if not doneB:
                try:
                    next(B_)
                except StopIteration:
                    doneB = True
        while not doneB:
            try:
                next(B_)
            except StopIteration:
                doneB = True
    C.end_phase()


def ffn_phase(C, T, l):
    C.begin_phase("_f%d" % l)
    nc = C.nc
    xm_own = T["xm_own"]; tail_g = T["tail_g_%d" % l]; out_tile = T["ffn_out_tile"]
    w_up, w_dn, g2, gpost, cwd = (T[k + "_%d" % l] for k in ("w_up", "w_dn", "g2", "gpost2", "cwd"))
    identd = T["identd"]; selm2d = T["selm2"]

    sb = C.sb
    wu = sb("wu", [128, 8, 2 * FF], BF16)
    wd = sb("wd", [128, 22, D], BF16)
    stage = [sb("stage%d" % i, [128, 1024]) for i in range(4)]
    g2_t = sb("g2_t", [128, 8])
    gpost_bc = sb("gpost_bc", [128, D])
    cw = sb("cw", [128, NFC, 4])
    ident = sb("ident", [128, 128], BF16)
    epsb = sb("epsb", [128, 1])
    xs = [sb("xs%d" % i, [128, D]) for i in range(2)]
    xh = sb("xh", [8, D])
    selm2 = sb("selm2", [8, 2], BF16)
    hn = sb("hn", [128, D], BF16)
    hnh = sb("hnh", [8, D], BF16)
    junk = sb("junk", [128, D], BF16)
    ssq = sb("ssq", [128, 4]); rstd = sb("rstd", [128, 4])
    h2T = sb("h2T", [128, 8, 2, 130], BF16)
    actT = sb("actT", [128, 22, 2, 128], BF16)
    gc = [sb("gc%d" % i, [128, 2, 128]) for i in range(2)]
    uc = [sb("uc%d" % i, [128, 2, 128]) for i in range(2)]
    gg = [sb("gg%d" % i, [128, 2, 128]) for i in range(2)]
    yo = sb("yo", [128, D])
    oss = sb("oss", [128, 4])

    psU = [C.ps("psU%d" % i, [128, 512]) for i in range(4)]
    psD = [C.ps("psD%d" % i, [128, 512]) for i in range(2)]
    psT = C.ps("psT", [128, 1024], BF16)
    psH = C.ps("psH", [128, 512])

    C.dma(stage[1][0:8, 0:2], selm2d[:, :], w=["stage1"])
    C.cp("dve", selm2[:], stage[1][0:8, 0:2], r=["stage1"], w=["selm2"])
    C.dma(g2_t[:], g2[:, :], w=["gscale"])
    C.dma(gpost_bc[:], gpost.partition_broadcast(128), w=["gpost_bc"])
    C.dma(cw[:], cwd[:, :, :], w=["cw"])
    C.dma(stage[0][:, 0:128], identd[:, :], w=["stage0"])
    C.cp("dve", ident[:], stage[0][:, 0:128], r=["stage0"], w=["ident"])
    C.memset("dve", epsb[:, 0:1], RMS_EPS, w=["epsb"])
    par = 0
    for kc in range(8):
        for q in range(6):
            c0 = q * 1024
            c1 = min(2 * FF, c0 + 1024)
            s = stage[par % 4]; sk = "stage%d" % (par % 4); par += 1
            load_convert(C, wu[:, kc, c0:c1], w_up[kc * 128:(kc + 1) * 128, c0:c1], s, sk, "wu", c1 - c0,
                         scale_ap=g2_t[:, kc:kc + 1])
    for f in range(22):
        s = stage[par % 4]; sk = "stage%d" % (par % 4); par += 1
        load_convert(C, wd[:, f, :], w_dn[f * 128:(f + 1) * 128, :], s, sk, "wd", D, parity=f)

    def rms_to_bf(x_ap, xk, out_bf, ok, np_, col):
        C.act(junk[0:np_, :], x_ap, AF.Square, r=[xk], w=["junk", "ssq%d" % col], accum=ssq[0:np_, col:col + 1])
        C.act(rstd[0:np_, col:col + 1], ssq[0:np_, col:col + 1], AF.Ln, r=["ssq%d" % col, "epsb"], w=["rstd%d" % col],
              scale=1.0 / D, bias=epsb[0:np_, 0:1])
        C.act(rstd[0:np_, col:col + 1], rstd[0:np_, col:col + 1], AF.Exp, r=["rstd%d" % col], w=["rstd%d" % col], scale=-0.5)
        C.act(out_bf, x_ap, AF.Copy, r=[xk, "rstd%d" % col], w=[ok], scale=rstd[0:np_, col:col + 1])

    ucount = [0]
    for grp in range(NS // 2):
        for sl in range(2):
            j = grp * 2 + sl
            C.dma(xs[sl][:], xm_own[j, :, :], w=["xs%d" % sl])
            for cand in range(4):
                m_ = 4 * j - 1 + cand
                if m_ < 0:
                    C.memset("pool", xh[0:2, :], 0.0, w=["xh"])
                else:
                    row0 = ((m_ % 4) * NS + m_ // 4) * 2
                    C.dma(xh[cand * 2:(cand + 1) * 2, :], tail_g[row0:row0 + 2, :], w=["xh"])
            rms_to_bf(xs[sl][:], "xs%d" % sl, hn[:], "hn", 128, 0)
            rms_to_bf(xh[:], "xh", hnh[:], "hnh", 8, 1)
            for kc in range(8):
                C.tr(psT[:, kc * 128:(kc + 1) * 128], hn[:, kc * 128:(kc + 1) * 128], ident[:], r=["hn", "ident"], w=["psT"])
            C.cp("act", h2T[:, :, sl, 2:130], psT[:, :].rearrange("p (k t) -> p k t", k=8), w=["psT", "h2T"])
            for kc in range(8):
                C.mm(psH[:, kc * 2:(kc + 1) * 2], hnh[:, kc * 128:(kc + 1) * 128], selm2[:], r=["hnh", "selm2"], w=["psH"])
            C.cp("act", h2T[:, :, sl, 0:2], psH[:, 0:16].rearrange("p (k t) -> p k t", k=8), w=["psH", "h2T"])
        def up_mm(fc):
            banks = []
            for ch in (fc, 22 + fc):
                bi = ucount[0] % 4
                ucount[0] += 1
                bank = psU[bi]
                bk = "psU%d" % bi
                for kc in range(8):
                    C.mm(bank[:, 0:260], wu[:, kc, ch * 128:(ch + 1) * 128],
                         h2T[:, kc, :, :].rearrange("p s t -> p (s t)"),
                         start=(kc == 0), stop=(kc == 7), r=["wu", "h2T"], w=[bk])
                banks.append((bank, bk))
            return banks

        def conv_ops(fc, banks):
            p = fc % 2
            for (bank, bk), ch, dst, dk in ((banks[0], fc, gc[p], "gc%d" % p), (banks[1], 22 + fc, uc[p], "uc%d" % p)):
                bv = bank[:, 0:260].rearrange("p (s t) -> p s t", s=2)
                C.act(dst[:], bv[:, :, 2:130], AF.Identity, r=["cw"], w=[bk, dk], scale=cw[:, ch, 2:3], bias=cw[:, ch, 3:4])
                C.stt(dst[:], bv[:, :, 1:129], cw[:, ch, 1:2], dst[:], ALU.mult, ALU.add, r=["cw", dk], w=[bk, dk])
                C.stt(dst[:], bv[:, :, 0:128], cw[:, ch, 0:1], dst[:], ALU.mult, ALU.add, r=["cw", dk], w=[bk, dk])

        def gate_ops(fc):
            p = fc % 2
            C.act(gg[p][:], gc[p][:], AF.Gelu_apprx_tanh, r=["gc%d" % p], w=["gg%d" % p])
            C.tt("pool", actT[:, fc, :, :], gg[p][:], uc[p][:], ALU.mult,
                 r=["gg%d" % p, "uc%d" % p], w=["actT"])

        nxt = up_mm(0)
        for fc in range(22):
            cur = nxt
            if fc + 1 < 22:
                nxt = up_mm(fc + 1)
            conv_ops(fc, cur)
            if fc >= 1:
                gate_ops(fc - 1)
            if fc == 12 and grp >= 1 and T.get("post_group") is not None:
                T["post_group"](grp - 1)
        gate_ops(21)
        for sl in range(2):
            j = grp * 2 + sl
            for half in range(2):
                for f in range(22):
                    C.mm(psD[half][:, :], actT[:, f, sl, :], wd[:, f, half * 512:(half + 1) * 512],
                         start=(f == 0), stop=(f == 21), r=["actT", "wd"], w=["psD%d" % half])
                C.cp("dve", yo[:, half * 512:(half + 1) * 512], psD[half][:, :], w=["psD%d" % half, "yo"])
            C.act(junk[:, :], yo[:], AF.Square, r=["yo"], w=["junk", "oss"], accum=oss[:, 0:1])
            C.act(oss[:, 1:2], oss[:, 0:1], AF.Ln, r=["oss", "epsb"], w=["oss1"], scale=1.0 / D, bias=epsb[:, 0:1])
            C.act(oss[:, 2:3], oss[:, 1:2], AF.Exp, r=["oss1"], w=["oss2"], scale=-0.5)
            C.stt(yo[:], yo[:], oss[:, 2:3], gpost_bc[:], ALU.mult, ALU.mult, r=["yo", "oss2", "gpost_bc"], w=["yo"])
            C.tt("dve", yo[:], yo[:], xs[sl][:], ALU.add, r=["yo", "xs%d" % sl], w=["yo"])
            C.dma(out_tile(j)[:, :], yo[:], r=["yo"], w=["ffnout%d" % j])
    if T.get("post_group") is not None:
        T["post_group"](NS // 2 - 1)
    C.end_phase()


OFF_Q, OFF_KV, OFF_G, OFF_C, OFF_D = 512, 768, 1152, 1164, 1676


def _consts():
    c = {}
    half = 32
    inv = (10000.0 ** (-np.arange(half, dtype=np.float32) * 2.0 / 64)).astype(np.float32)
    ang = np.arange(SEQ, dtype=np.float32)[:, None] * inv[None, :]
    cos = np.cos(ang).astype(np.float32).T
    sin = np.sin(ang).astype(np.float32).T
    c64 = np.concatenate([cos, cos], 0)
    s64 = np.concatenate([-sin, sin], 0)
    c["ropeC"] = np.ascontiguousarray(np.concatenate([c64, c64], 0))
    c["ropeS"] = np.ascontiguousarray(np.concatenate([s64, s64], 0))
    blk = np.arange(SEQ) // 64
    c["indp"] = (np.arange(64)[:, None] == (blk % 64)[None, :]).astype(np.float32)
    c["identd"] = np.eye(128, dtype=np.float32)
    n = np.arange(512)[:, None]
    b = np.arange(128)[None, :]
    off = n - 4 * b
    M = np.where((off == -1) | (off == 3), 1.0, np.where((off >= 0) & (off <= 2), 2.0, 0.0)).astype(np.float32)
    M[511, :] = 0.0
    c["impM"] = np.ascontiguousarray(M.reshape(4, 128, 128))
    c["trild"] = (np.arange(128)[:, None] <= np.arange(128)[None, :]).astype(np.float32)
    rg = np.zeros((128, 2, 128), np.float32)
    for gi, w in enumerate((2, 4, 8, 16)):
        rg[(gi % 2) * 64:(gi % 2) * 64 + 64, gi // 2, :] = 1.0 / w
    c["rcntg"] = rg
    return c


def _core_consts(cidx):
    c = cidx
    o = {}
    k = np.arange(128)[:, None]
    t = np.arange(128)[None, :]
    wm = np.zeros((8, 128, 128), np.float32)
    for q in range(8):
        mm = q - 4
        rel = mm - c
        if rel == 0:
            wm[q] = (k <= t)
        elif rel == -4:
            wm[q] = (k > t)
        elif -4 < rel < 0:
            wm[q] = 1.0
    o["wmaskd"] = wm
    cm = np.zeros((5, 128, 128), np.float32)
    for jm in range(4):
        nprime = k - 32 * jm
        cm[jm] = (16 * nprime + 31 <= 128 * c + t)
    cm[4] = 1.0
    if c == 0:
        cm[4][127, :15] = 0.0
    o["cmaskd"] = cm
    A = np.zeros((NS, 128, 128), np.float32)
    B = np.zeros((NS, 128, 128), np.float32)
    tt = np.arange(128)[:, None]
    bb = np.arange(128)[None, :]
    for j in range(NS):
        i = 4 * j + c
        cur = 2 * i + (tt >= 64)
        valid = bb <= cur
        forced = (bb == 0) | (bb == cur) | (bb == cur - 1)
        A[j] = (valid & ~forced)
        B[j] = np.where(valid, np.where(forced, 1e6, 0.0), -1e30)
    o["selA"] = A
    o["selB"] = B
    inv = (10000.0 ** (-np.arange(32, dtype=np.float32) * 2.0 / 64)).astype(np.float32)
    qC = np.zeros((NS, 128, 128), np.float32)
    qS = np.zeros((NS, 128, 128), np.float32)
    for j in range(NS):
        pos = (128 * (4 * j + c) + np.arange(128)).astype(np.float32)
        ang = pos[:, None] * inv[None, :]
        cs = np.cos(ang).astype(np.float32).T * np.float32(0.125)
        sn = np.sin(ang).astype(np.float32).T * np.float32(0.125)
        c64 = np.concatenate([cs, cs], 0)
        s64 = np.concatenate([-sn, sn], 0)
        qC[j] = np.concatenate([c64, c64], 0)
        qS[j] = np.concatenate([s64, s64], 0)
    o["qC"] = qC
    o["qS"] = qS
    r0 = np.zeros((128, 2, 128), np.float32)
    for gi, w in enumerate((2, 4, 8, 16)):
        pos = 128 * c + np.arange(128)
        cnt = np.minimum(pos + 1, w).astype(np.float32)
        r0[(gi % 2) * 64:(gi % 2) * 64 + 64, gi // 2, :] = (1.0 / cnt)[None, :]
    o["rcnt0"] = r0
    sm = np.zeros((128, 32), np.float32)
    sm[c * 32:(c + 1) * 32, :] = np.eye(32, dtype=np.float32)
    o["selm"] = sm
    sm2 = np.zeros((8, 2), np.float32)
    sm2[c * 2:(c + 1) * 2, :] = np.eye(2, dtype=np.float32)
    o["selm2"] = sm2
    return o


def _sw(idx):
    return np.concatenate([idx[32:], idx[:32]])


def _mixer_weights(P, l):
    w_in = P["w_in"][l]
    kv = lambda s: np.arange(OFF_KV + 64 * s, OFF_KV + 64 * s + 64)
    cols_kv = np.concatenate([kv(0), kv(1), kv(2), kv(4), _sw(kv(2)), _sw(kv(4)), kv(3), kv(5)])
    qcols = np.arange(OFF_Q, OFF_Q + 256)
    qsw = np.concatenate([_sw(qcols[h * 64:(h + 1) * 64]) for h in range(4)])
    cols_loc = np.concatenate([np.arange(0, 512), np.arange(OFF_D, OFF_D + 256), np.arange(OFF_C, OFF_C + 256),
                               qcols, qsw, np.arange(OFF_C + 256, OFF_C + 512), np.arange(OFF_G, OFF_G + 12)])
    o = {}
    o["w_kv"] = np.ascontiguousarray(w_in[:, cols_kv])
    o["w_loc"] = np.ascontiguousarray(w_in[:, cols_loc])
    o["w_out"] = np.ascontiguousarray(P["w_out"][l])
    o["gpre"] = np.ascontiguousarray(P["norm_mix_pre"][l].reshape(8, 128).T)
    o["gpost"] = np.ascontiguousarray(P["norm_mix_post"][l].reshape(1, D))
    o["convw"] = np.ascontiguousarray(P["conv_dw_w"][l].reshape(31, 2, 128).transpose(2, 1, 0))
    cv = np.stack([P["conv_dw_b"][l], P["conv_ln_g"][l], P["conv_ln_b"][l]], -1)
    o["convv"] = np.ascontiguousarray(cv.reshape(2, 128, 3).transpose(1, 0, 2))
    w1k = P["nsa_ck_w1"][l].reshape(32, 64, 64).transpose(1, 0, 2).reshape(64, 2048)
    w1v = P["nsa_cv_w1"][l].reshape(32, 64, 64).transpose(1, 0, 2).reshape(64, 2048)
    o["w1d"] = np.ascontiguousarray(np.concatenate([w1k, w1v], 0))
    o["w2d"] = np.ascontiguousarray(np.concatenate([P["nsa_ck_w2"][l], P["nsa_cv_w2"][l]], 1))
    o["ped"] = np.ascontiguousarray(np.concatenate([P["nsa_pe_k"][l].T, P["nsa_pe_v"][l].T], 0))
    o["sgv"] = np.ascontiguousarray(np.stack([P["sgu_ln_g"][l], P["sgu_ln_b"][l]], 0))
    o["sgw"] = np.ascontiguousarray(P["sgu_w"][l].transpose(2, 0, 1))
    o["sgb"] = np.ascontiguousarray(P["sgu_b"][l])
    pw = np.zeros((128, 2, 128), np.float32)
    for gi in range(4):
        lo = (gi % 2) * 64
        pw[lo:lo + 64, gi // 2, lo:lo + 64] = P["pool_w"][l][gi]
    o["poolw"] = pw
    o["poolsc"] = np.ascontiguousarray(P["pool_scale"][l].reshape(2, 128).T)
    return o


def _ffn_weights(P, l):
    o = {}
    o["w_up"] = np.ascontiguousarray(P["ffn_up"][l])
    o["w_dn"] = np.ascontiguousarray(P["ffn_down"][l])
    o["g2"] = np.ascontiguousarray(P["norm_ffn_pre"][l].reshape(8, 128).T)
    o["gpost2"] = np.ascontiguousarray(P["norm_ffn_post"][l].reshape(1, D))
    cw = np.concatenate([P["ffn_conv_w"][l], P["ffn_conv_b"][l][None, :]], 0)
    o["cwd"] = np.ascontiguousarray(cw.reshape(4, NFC, 128).transpose(2, 1, 0))
    return o


def _own_tiles(xb, c, halo):
    pad = np.concatenate([np.zeros((halo, D), np.float32), xb], 0)
    out = np.empty((NS, halo + 128, D), np.float32)
    for j in range(NS):
        i = 4 * j + c
        out[j] = pad[128 * i:128 * i + 128 + halo]
    return out


def _scatter_own(res_list, key):
    x = np.empty((NB, SEQ, D), np.float32)
    for core in range(8):
        b, c = divmod(core, 4)
        r = res_list[core][key]
        for j in range(NS):
            i = 4 * j + c
            x[b, 128 * i:128 * (i + 1)] = r[j]
    return x


def _chunk_row(m):
    r, sl = m % 4, m // 4
    return sl // 2, r * 256 + (sl % 2) * 128


def build_all(nslots=NS):
    C = Ctx()
    T = {}
    xg0 = C.dram_in("xg0", [8 * 1024, D])
    xown0 = C.dram_in("xown0", [NS, 128, D])
    for k, shp in MIX_C_SHAPES.items():
        T[k] = C.dram_in(k, shp)
    for l in range(2):
        for k, shp in MIX_W_SHAPES.items():
            T["%s_%d" % (k, l)] = C.dram_in("%s_%d" % (k, l), shp)
        for k, shp in FFN_W_SHAPES.items():
            T["%s_%d" % (k, l)] = C.dram_in("%s_%d" % (k, l), shp)
    out = C.dram_out("out", [NS, 128, D])
    xm_own = C.dram_int("xm_own", [NS, 128, D])
    tails = [C.dram_int("xm_tail%d" % l, [NS * 2, D]) for l in range(2)]
    tail_g = [C.dram_int("tail_g%d" % l, [4 * NS * 2, D]) for l in range(2)]
    x1_own = [C.dram_int("x1_own%d" % g, [256, D]) for g in range(8)]
    x1_g = [C.dram_int("x1_g%d" % g, [1024, D]) for g in range(8)]
    RG = [[0, 1, 2, 3], [4, 5, 6, 7]]

    def gather(src, dst):
        C.S.collective(lambda e: e.collective_compute("AllGather", ALU.bypass, replica_groups=RG,
                                                      ins=[src.opt()], outs=[dst.opt()]))

    def xg_tile0(m):
        g, ro = _chunk_row(m)
        return xg0[g * 1024 + ro:g * 1024 + ro + 128, :]

    def xg_tile1(m):
        g, ro = _chunk_row(m)
        return x1_g[g][ro:ro + 128, :]

    for l in range(2):
        T["xg_tile"] = xg_tile0 if l == 0 else xg_tile1
        T["xown_tile"] = (lambda j: xown0[j, :, :]) if l == 0 else (lambda j: x1_own[j // 2][(j % 2) * 128:(j % 2) * 128 + 128, :])
        T["xm_own"] = xm_own
        T["xm_tail"] = tails[l]
        mixer_phase(C, T, l, nslots=nslots)
        gather(tails[l], tail_g[l])
        T["tail_g_%d" % l] = tail_g[l]
        T["ffn_out_tile"] = (lambda j: x1_own[j // 2][(j % 2) * 128:(j % 2) * 128 + 128, :]) if l == 0 else (lambda j: out[j, :, :])
        if l == 0:
            def post_group(g):
                C.S.cc_op(lambda e: e.collective_compute("AllGather", ALU.bypass, replica_groups=RG,
                                                         ins=[x1_own[g].opt()], outs=[x1_g[g].opt()]),
                          reads=["ffnout%d" % (2 * g), "ffnout%d" % (2 * g + 1)], writes=["x1g%d" % g])
            T["post_group"] = post_group
        else:
            T["post_group"] = None
        ffn_phase(C, T, l)
    info = C.finish()
    return C.nc, info


_CACHE = {}


def kernel(**inputs):
    P = {k: np.asarray(v, dtype=np.float32) for k, v in inputs.items()}
    x = P["x"]
    if "nc" not in _CACHE:
        _CACHE["nc"] = build_all()[0]
    nc = _CACHE["nc"]
    consts = _consts()
    shared = {k: consts[k] for k in MIX_C_SHAPES if k in consts}
    for l in range(2):
        for k, v in _mixer_weights(P, l).items():
            shared["%s_%d" % (k, l)] = v
        for k, v in _ffn_weights(P, l).items():
            shared["%s_%d" % (k, l)] = v
    in_maps = []
    for core in range(8):
        b, c = divmod(core, 4)
        m = dict(shared)
        m.update(_core_consts(c))
        xt = x[b].reshape(8, 2, 4, 128, D)
        m["xg0"] = np.ascontiguousarray(xt.transpose(0, 2, 1, 3, 4).reshape(8 * 1024, D))
        m["xown0"] = np.ascontiguousarray(x[b].reshape(NS, 4, 128, D)[:, c])
        in_maps.append(m)
    res = run_bass_kernel_spmd(nc, in_maps, core_ids=list(range(8)))
    return _scatter_own(res.results, "out").astype(np.float32)
```

```python
import numpy as np
from contextlib import ExitStack
import concourse.bass as bass
import concourse.mybir as mybir
from concourse.bass_utils import run_bass_kernel_spmd

F32 = mybir.dt.float32
BF16 = mybir.dt.bfloat16
AF = mybir.ActivationFunctionType
ALU = mybir.AluOpType
AX = mybir.AxisListType

D = 1024
SEQ = 8192
NB = 2
NT = 64
NS = 16
FF = 2816
NFC = 44
RMS_EPS = 1e-6
LN_EPS = 1e-5
NEG = -30000.0
GELU_C = 1.5957691216057308


class Sched:
    NDS = 8

    def __init__(self, nc, sems):
        self.nc = nc
        self.eng = {"pe": nc.tensor, "act": nc.scalar, "dve": nc.vector,
                    "pool": nc.gpsimd, "sp": nc.sync}
        self.ops = []
        self.last_writer = {}
        self.readers = {}
        self.sems = sems

    def cc_op(self, fn, reads=(), writes=()):
        idx = self.op("pool", fn, reads, writes)
        self.ops[idx].append("cc")
        return idx

    def op(self, engine, fn, reads=(), writes=()):
        idx = len(self.ops)
        is_dma = engine == "sp"
        deps = {}

        def add(d, kind):
            if d is None:
                return
            if deps.get(d) is None or kind == "raw":
                deps[d] = kind

        for k in reads:
            add(self.last_writer.get(k), "raw")
        for k in writes:
            add(self.last_writer.get(k), "waw")
            for r in self.readers.get(k, ()):
                add(r, "war")
        for k in writes:
            self.last_writer[k] = idx
            self.readers[k] = []
        for k in reads:
            if k not in writes:
                lst = self.readers.setdefault(k, [])
                if not is_dma:
                    lst[:] = [r_ for r_ in lst if self.ops[r_][0] != engine]
                lst.append(idx)
        keep = []
        for d, kind in deps.items():
            de = self.ops[d][0]
            if de == engine and not is_dma:
                if engine == "pe":
                    continue
            keep.append(d)
        self.ops.append([engine, fn, keep, is_dma])
        for d in keep:
            self.ops[d][3] = True
        return idx

    def _init_emit_state(self):
        self.counts = {e: 0 for e in self.eng}
        self.dma_counts = [0] * self.NDS
        self.n_dma = 0
        self.waited = {e: {} for e in self.eng}
        self.sigval = {}
        self.emitted = 0
        self.cc_count = 0

    def flush(self):
        if not hasattr(self, "counts"):
            self._init_emit_state()
        last = {}
        for idx in range(self.emitted, len(self.ops)):
            if len(self.ops[idx]) == 4 and self.ops[idx][0] != "__ccwait__":
                last[self.ops[idx][0]] = idx
        for e, idx in last.items():
            if e != "sp" and len(self.ops[idx]) == 4:
                self.ops[idx][3] = True
        for idx in range(self.emitted, len(self.ops)):
            e, fn, deps, signal = self.ops[idx][:4]
            if e == "__ccwait__":
                if self.cc_count > 0:
                    for e2, eng2 in self.eng.items():
                        if self.waited[e2].get("cc", 0) < self.cc_count:
                            eng2.wait_ge(self.sems["cc"], self.cc_count)
                            self.waited[e2]["cc"] = self.cc_count
                continue
            is_cc = len(self.ops[idx]) > 4
            eng = self.eng[e]
            need = {}
            for d in deps:
                sem, val = self.sigval[d]
                if need.get(id(sem), (None, 0))[1] < val:
                    need[id(sem)] = (sem, val)
            if e == "sp":
                slot = self.n_dma % self.NDS
                if self.dma_counts[slot] > 0:
                    sem = self.sems["dma"][slot]
                    val = self.dma_counts[slot] * 16
                    if need.get(id(sem), (None, 0))[1] < val:
                        need[id(sem)] = (sem, val)
            for key, (sem, val) in need.items():
                if self.waited[e].get(key, 0) >= val:
                    continue
                eng.wait_ge(sem, val)
                self.waited[e][key] = val
            ins = fn(eng)
            if is_cc:
                self.cc_count += 1
                ins.then_inc(self.sems["cc"])
                self.sigval[idx] = (self.sems["cc"], self.cc_count)
                continue
            if e == "sp":
                slot = self.n_dma % self.NDS
                self.n_dma += 1
                self.dma_counts[slot] += 1
                sem = self.sems["dma"][slot]
                ins.then_inc(sem, 16)
                self.sigval[idx] = (sem, self.dma_counts[slot] * 16)
            elif signal:
                self.counts[e] += 1
                ins.then_inc(self.sems[e], 1)
                self.sigval[idx] = (self.sems[e], self.counts[e])
        self.emitted = len(self.ops)

    def barrier(self, engines=None, wait_cc=True):
        self.flush()
        for e, eng in self.eng.items():
            for x in ("pe", "act", "dve", "pool"):
                if x != e and self.counts[x] > 0 and self.waited[e].get(id(self.sems[x]), 0) < self.counts[x]:
                    eng.wait_ge(self.sems[x], self.counts[x])
                    self.waited[e][id(self.sems[x])] = self.counts[x]
            for slot in range(self.NDS):
                if self.dma_counts[slot] > 0:
                    sem = self.sems["dma"][slot]
                    val = self.dma_counts[slot] * 16
                    if self.waited[e].get(id(sem), 0) < val:
                        eng.wait_ge(sem, val)
                        self.waited[e][id(sem)] = val
            if wait_cc and self.cc_count > 0 and self.waited[e].get("cc", 0) < self.cc_count:
                eng.wait_ge(self.sems["cc"], self.cc_count)
                self.waited[e]["cc"] = self.cc_count
        self.last_writer = {}
        self.readers = {}

    def wait_cc(self):
        self.ops.append(["__ccwait__", None, [], False])

    def collective_nowait(self, fn):
        self.barrier()
        self.cc_count += 1
        fn(self.eng["pool"]).then_inc(self.sems["cc"])

    def collective(self, fn):
        self.barrier()
        self.cc_count += 1
        fn(self.eng["pool"]).then_inc(self.sems["cc"])
        for e, eng in self.eng.items():
            eng.wait_ge(self.sems["cc"], self.cc_count)

    def emit(self):
        self.flush()
        sp = self.eng["sp"]
        for slot in range(self.NDS):
            if self.dma_counts[slot] > 0:
                sp.wait_ge(self.sems["dma"][slot], self.dma_counts[slot] * 16)
        return self.counts, self.n_dma


class Ctx:
    def __init__(self):
        self.nc = bass.Bass("TRN2", target_bir_lowering=False)
        self.es = ExitStack()
        nc = self.nc
        sems = {e: self.es.enter_context(nc.semaphore("s_" + e)) for e in ["pe", "act", "dve", "pool"]}
        sems["dma"] = [self.es.enter_context(nc.semaphore("s_dma%d" % i)) for i in range(Sched.NDS)]
        sems["cc"] = self.es.enter_context(nc.semaphore("s_cc"))
        self.S = Sched(nc, sems)
        self.pes = self.es

    def begin_phase(self, sfx):
        self.pes = ExitStack()
        self.sfx = sfx

    def end_phase(self, wait_cc=True):
        self.S.barrier(wait_cc=wait_cc)
        self.pes.close()
        self.pes = self.es

    def dram_int(self, name, shape, dt=F32):
        return self.nc.dram_tensor(name, list(shape), dt).ap()

    def dram_in(self, name, shape, dt=F32):
        return self.nc.dram_tensor(name, list(shape), dt, kind="ExternalInput").ap()

    def dram_out(self, name, shape, dt=F32):
        return self.nc.dram_tensor(name, list(shape), dt, kind="ExternalOutput").ap()

    def sb(self, name, shape, dt=F32):
        return self.pes.enter_context(self.nc.sbuf_tensor(name + getattr(self, "sfx", ""), list(shape), dt))

    def ps(self, name, shape, dt=F32):
        return self.pes.enter_context(self.nc.psum_tensor(name + getattr(self, "sfx", ""), list(shape), dt))

    def dma(self, out, in_, r=(), w=()):
        self.S.op("sp", lambda e: e.dma_start(out=out, in_=in_), r, w)

    def mm(self, out, lhsT, rhs, start=True, stop=True, r=(), w=(), skip=False):
        self.S.op("pe", lambda e: e.matmul(out, lhsT=lhsT, rhs=rhs, start=start, stop=stop,
                                           skip_group_check=skip), r, w)

    def tr(self, out, in_, ident, r=(), w=()):
        self.S.op("pe", lambda e: e.transpose(out, in_, ident), r, w)

    def act(self, out, in_, func, r=(), w=(), scale=1.0, bias=0.0, accum=None, eng="act"):
        if accum is None:
            self.S.op("act", lambda e: e.activation(out=out, in_=in_, func=func, bias=bias, scale=scale), r, w)
        else:
            self.S.op("act", lambda e: e.activation(out=out, in_=in_, func=func, bias=bias, scale=scale,
                                                    accum_out=accum), r, w)

    def cp(self, eng, out, in_, r=(), w=()):
        if eng == "act":
            self.S.op("act", lambda e: e.activation(out=out, in_=in_, func=AF.Copy), r, w)
        else:
            self.S.op(eng, lambda e: e.tensor_copy(out=out, in_=in_), r, w)

    def tt(self, eng, out, in0, in1, op, r=(), w=()):
        self.S.op(eng, lambda e: e.tensor_tensor(out=out, in0=in0, in1=in1, op=op), r, w)

    def ts(self, eng, out, in0, s1, s2, op0, op1=None, r=(), w=()):
        if op1 is None:
            self.S.op(eng, lambda e: e.tensor_scalar(out=out, in0=in0, scalar1=s1, scalar2=None, op0=op0), r, w)
        else:
            self.S.op(eng, lambda e: e.tensor_scalar(out=out, in0=in0, scalar1=s1, scalar2=s2, op0=op0, op1=op1), r, w)

    def stt(self, out, in0, scalar, in1, op0, op1, r=(), w=()):
        self.S.op("dve", lambda e: e.scalar_tensor_tensor(out=out, in0=in0, scalar=scalar, in1=in1,
                                                          op0=op0, op1=op1), r, w)

    def memset(self, eng, ap, val, w=()):
        self.S.op(eng, lambda e: e.memset(ap, val), (), w)

    def recip(self, out, in_, r=(), w=()):
        self.S.op("dve", lambda e: e.reciprocal(out=out, in_=in_), r, w)

    def finish(self):
        res = self.S.emit()
        self.es.close()
        return res


def load_convert(C, dst_bf, src_dram, stage, stage_key, dst_key, ncols, scale_ap=None, parity=0, scale_key="gscale"):
    C.dma(stage[:, 0:ncols], src_dram, w=[stage_key])
    if scale_ap is not None:
        C.ts("dve", dst_bf, stage[:, 0:ncols], scale_ap, None, ALU.mult, r=[stage_key, scale_key], w=[dst_key])
    elif parity % 2 == 0:
        C.cp("dve", dst_bf, stage[:, 0:ncols], r=[stage_key], w=[dst_key])
    else:
        C.cp("act", dst_bf, stage[:, 0:ncols], r=[stage_key], w=[dst_key])


NLOC = 1804


MIX_W = ["w_kv", "w_loc", "w_out", "gpre", "gpost", "convw", "convv", "w1d", "w2d", "ped", "sgv", "sgw", "sgb",
         "poolw", "poolsc"]
MIX_W_SHAPES = {"w_kv": [D, 512], "w_loc": [D, NLOC], "w_out": [D, D], "gpre": [128, 8], "gpost": [1, D],
                "convw": [128, 2, 31], "convv": [128, 2, 3], "w1d": [128, 2048], "w2d": [64, 128], "ped": [128, 32],
                "sgv": [2, 256], "sgw": [128, 4, 128], "sgb": [4, 128], "poolw": [128, 2, 128], "poolsc": [128, 2]}
MIX_C_SHAPES = {"ropeC": [128, SEQ], "ropeS": [128, SEQ], "qC": [NS, 128, 128], "qS": [NS, 128, 128], "indp": [64, SEQ],
                "identd": [128, 128], "wmaskd": [8, 128, 128], "cmaskd": [5, 128, 128], "selA": [NS, 128, 128],
                "selB": [NS, 128, 128], "impM": [4, 128, 128], "trild": [128, 128], "rcnt0": [128, 2, 128],
                "rcntg": [128, 2, 128], "selm": [128, 32], "selm2": [8, 2]}
FFN_W_SHAPES = {"w_up": [D, 2 * FF], "w_dn": [FF, D], "g2": [128, 8], "gpost2": [1, D], "cwd": [128, NFC, 4]}


def mixer_phase(C, T, l, nslots=NS, stages=5):
    C.begin_phase("_m%d" % l)
    nc = C.nc
    xg_tile = T["xg_tile"]; xown_tile = T["xown_tile"]; xm_own = T["xm_own"]; xm_tail = T["xm_tail"]
    w_kv, w_loc, w_out, gpre, gpost = (T[k + "_%d" % l] for k in ("w_kv", "w_loc", "w_out", "gpre", "gpost"))
    convw, convv, w1d, w2d, ped = (T[k + "_%d" % l] for k in ("convw", "convv", "w1d", "w2d", "ped"))
    sgv, sgw, sgb, poolw, poolsc = (T[k + "_%d" % l] for k in ("sgv", "sgw", "sgb", "poolw", "poolsc"))
    ropeC, ropeS, qC, qS, indp, identd = (T[k] for k in ("ropeC", "ropeS", "qC", "qS", "indp", "identd"))
    wmaskd, cmaskd, selA, selB, impM, trild = (T[k] for k in ("wmaskd", "cmaskd", "selA", "selB", "impM", "trild"))
    rcnt0, rcntg, selmd = T["rcnt0"], T["rcntg"], T["selm"]

    sb = C.sb
    wkv = sb("wkv", [128, 8, 512], BF16)
    wloc = sb("wloc", [128, 8, NLOC], BF16)
    wo = sb("wo", [128, 8, D], BF16)
    xt = [sb("xt0", [128, D]), sb("xt1", [128, D])]
    stage = xt
    gpre_t = sb("gpre_t", [128, 8])
    gpost_bc = sb("gpost_bc", [128, D])
    ident = sb("ident", [128, 128], BF16)
    ones_b = sb("ones_b", [128, 128], BF16)
    shi = sb("shi", [128, 2, 128], BF16)
    slo = sb("slo", [128, 2, 128], BF16)
    Dg = sb("Dg", [128, 2, 31, 128], BF16)
    convw_t = sb("convw_t", [128, 2, 31])
    convv_t = sb("convv_t", [128, 2, 3])
    W1k = sb("W1k", [64, 32, 64], BF16)
    W1v = sb("W1v", [64, 32, 64], BF16)
    W2 = sb("W2", [64, 128], BF16)
    peTk = sb("peTk", [64, 32], BF16)
    peTv = sb("peTv", [64, 32], BF16)
    hnc = sb("hnc", [128, D], BF16)
    cbias = sb("cbias", [64, 2])
    Kaug = sb("Kaug", [128, SEQ], BF16)
    Kw = sb("Kw", [128, 12, 128], BF16)
    Vs = sb("Vs", [128, NT, 65], BF16)
    Vw = sb("Vw", [128, 12, 65], BF16)
    kvk = [sb("kvk0", [64, 528], BF16), sb("kvk1", [64, 528], BF16)]
    kvv = [sb("kvv0", [64, 528], BF16), sb("kvv1", [64, 528], BF16)]
    gkT = sb("gkT", [64, 576], BF16)
    gvT = sb("gvT", [64, 576], BF16)
    kcT = sb("kcT", [128, 576], BF16)
    rhsC = sb("rhsC", [128, 4, 193], BF16)
    wmask = sb("wmask", [128, 8, 128], BF16)
    cmask = sb("cmask", [128, 5, 128], BF16)
    wsT = sb("wsT", [128, 4, 128], BF16)
    tril = sb("tril", [128, 128])
    Bs = sb("Bs", [128, 2, 128])
    Wbd = sb("Wbd", [128, 2, 128], BF16)
    poolsc_t = sb("poolsc_t", [128, 2])
    rc0 = sb("rc0", [128, 2, 128])
    rcg = sb("rcg", [128, 2, 128])
    lng_bc = sb("lng_bc", [128, 256])
    lnb_bc = sb("lnb_bc", [128, 256])
    hnb = [sb("hnb0", [128, D], BF16), sb("hnb1", [128, D], BF16)]
    hT = [sb("hT0", [128, 8, 128], BF16), sb("hT1", [128, 8, 128], BF16)]
    rC = [sb("rC0", [128, 128]), sb("rC1", [128, 128])]
    rS = [sb("rS0", [128, 128]), sb("rS1", [128, 128])]
    junk = sb("junk", [128, D], BF16)
    ssq = sb("ssq", [128, 8])
    rstd = sb("rstd", [128, 8])
    ropet = sb("ropet", [128, 2, 128])
    xo = sb("xo", [128, D])
    hno = sb("hno", [128, D], BF16)
    hTo = sb("hTo", [128, 8, 160], BF16)
    sig = sb("sig", [128, 2, 160])
    ub = sb("ub", [128, 2, 160], BF16)
    zdT = sb("zdT", [128, 2, 160])
    s2 = sb("s2", [128, 2, 160]); s4 = sb("s4", [128, 2, 160])
    s8 = sb("s8", [128, 2, 160]); s16 = sb("s16", [128, 2, 160])
    plf = sb("plf", [128, 2, 128]); plb = sb("plb", [128, 2, 128], BF16)
    usg = sb("usg", [128, 2, 128])
    qCt = sb("qCt", [128, 128]); qSt = sb("qSt", [128, 128])
    qt1 = sb("qt1", [128, 128]); qt2 = sb("qt2", [128, 128]); qsum = sb("qsum", [128, 128])
    QR = sb("QR", [128, 2, 4, 128], BF16)
    QC = sb("QC", [128, 4, 128], BF16)
    QW = sb("QW", [128, 4, 128], BF16)
    gts = sb("gts", [128, 12])
    vg = sb("vg", [128, 256]); vn = sb("vn", [128, 256]); vbf = sb("vbf", [128, 256], BF16)
    bnst = sb("bnst", [128, 6]); bnag = sb("bnag", [128, 2]); lnr = sb("lnr", [128, 2])
    ycv = sb("ycv", [128, 2, 128]); ysq = sb("ysq", [128, 2, 128])
    cmean = sb("cmean", [128, 128]); cmsq = sb("cmsq", [128, 128]); cvar = sb("cvar", [128, 128])
    crstd = sb("crstd", [128, 128]); cyn = sb("cyn", [128, 2, 128])
    sgt = sb("sgt", [128, 2, 128])
    yT = sb("yT", [128, 8, 128], BF16)
    Pb = [sb("Pb%d" % i, [128, 4, 128], BF16) for i in range(4)]
    selAt = sb("selAt", [128, 128]); selBt = sb("selBt", [128, 128])
    imp = sb("imp", [128, 128]); imp2 = sb("imp2", [128, 128]); impw = sb("impw", [128, 128])
    top8 = sb("top8", [128, 16]); selt1 = sb("selt1", [128, 128]); selt2 = sb("selt2", [128, 128])
    selbf = sb("selbf", [128, 128], BF16)
    den = sb("den", [128, 12]); rden = sb("rden", [128, 12]); gsc = sb("gsc", [128, 12])
    Ynsa = sb("Ynsa", [128, 256]); Ynb = sb("Ynb", [128, 256], BF16)
    yo = sb("yo", [128, D])
    oss = sb("oss", [128, 4])
    selm = sb("selm", [128, 32], BF16)
    cg1 = sb("cg1", [64, 64]); cg2 = sb("cg2", [64, 64])

    psS = [C.ps("psS0", [128, 512]), C.ps("psS1", [128, 512])]
    psC = C.ps("psC", [128, 512])
    psK = C.ps("psK", [128, 512])
    psSel = C.ps("psSel", [128, 512])
    psWin = C.ps("psWin", [128, 512])
    psG = C.ps("psG", [128, 512])
    psT = C.ps("psT", [128, 1024], BF16)

    C.dma(gpre_t[:], gpre[:, :], w=["gscale"])
    C.dma(gpost_bc[:], gpost.partition_broadcast(128), w=["gpost_bc"])
    C.dma(stage[0][:, 0:128], identd[:, :], w=["xt0"])
    C.cp("dve", ident[:], stage[0][:, 0:128], r=["xt0"], w=["ident"])
    C.memset("dve", ones_b[:], 1.0, w=["ones_b"])
    C.dma(stage[1][:, 0:32], selmd[:, :], w=["xt1"])
    C.cp("dve", selm[:], stage[1][:, 0:32], r=["xt1"], w=["selm"])
    par = 0
    for kc in range(8):
        s = stage[par % 2]; sk = "xt%d" % (par % 2); par += 1
        load_convert(C, wkv[:, kc, :], w_kv[kc * 128:(kc + 1) * 128, :], s, sk, "wkv", 512, scale_ap=gpre_t[:, kc:kc + 1])
        s = stage[par % 2]; sk = "xt%d" % (par % 2); par += 1
        load_convert(C, wloc[:, kc, 0:902], w_loc[kc * 128:(kc + 1) * 128, 0:902], s, sk, "wloc", 902, scale_ap=gpre_t[:, kc:kc + 1])
        s = stage[par % 2]; sk = "xt%d" % (par % 2); par += 1
        load_convert(C, wloc[:, kc, 902:NLOC], w_loc[kc * 128:(kc + 1) * 128, 902:NLOC], s, sk, "wloc", 902, scale_ap=gpre_t[:, kc:kc + 1])
        s = stage[par % 2]; sk = "xt%d" % (par % 2); par += 1
        load_convert(C, wo[:, kc, :], w_out[kc * 128:(kc + 1) * 128, :], s, sk, "wo", D, parity=kc)
    for q in range(8):
        s = stage[par % 2]; sk = "xt%d" % (par % 2); par += 1
        C.dma(s[64:128, 0:1024], indp[:, q * 1024:(q + 1) * 1024], w=[sk])
        C.cp("act" if q % 2 else "dve", Kaug[64:128, q * 1024:(q + 1) * 1024], s[64:128, 0:1024], r=[sk], w=["Kaug_ind"])
    C.memset("pool", Kw[64:128, :, :], 0.0, w=["Kw"])
    C.memset("pool", kcT[:, :], 0.0, w=["kcT"])
    C.memset("pool", gkT[:, :], 0.0, w=["gkT"])
    C.memset("pool", gvT[:, :], 0.0, w=["gvT"])
    for q_ in range(2):
        C.memset("pool", kvk[q_][:, :], 0.0, w=["kvk%d" % q_])
        C.memset("pool", kvv[q_][:, :], 0.0, w=["kvv%d" % q_])
    C.memset("pool", QC[:, :, :], 0.0, w=["QC"])
    C.memset("pool", QW[:, :, :], 0.0, w=["QW"])
    C.memset("pool", Vs[:, :, 64:65], 1.0, w=["Vs"])
    C.memset("pool", Vw[:, :, 64:65], 1.0, w=["Vw"])
    C.memset("pool", rhsC[:, :, :], 0.0, w=["rhsC"])
    C.memset("pool", rhsC[:, :, 192:193], 1.0, w=["rhsC"])
    for q in range(8):
        s = stage[par % 2]; sk = "xt%d" % (par % 2); par += 1
        C.dma(s[:, 0:128], wmaskd[q, :, :], w=[sk])
        C.cp("dve", wmask[:, q, :], s[:, 0:128], r=[sk], w=["wmask"])
    for q in range(4):
        s = stage[par % 2]; sk = "xt%d" % (par % 2); par += 1
        C.dma(s[:, 0:128], cmaskd[q, :, :], w=[sk])
        C.cp("dve", cmask[:, q, :], s[:, 0:128], r=[sk], w=["cmask"])
        if q == 0:
            s = stage[par % 2]; sk = "xt%d" % (par % 2); par += 1
            C.dma(s[:, 0:128], cmaskd[4, :, :], w=[sk])
            C.cp("dve", cmask[:, 4, :], s[:, 0:128], r=[sk], w=["cmask"])
        s = stage[par % 2]; sk = "xt%d" % (par % 2); par += 1
        C.dma(s[:, 0:128], impM[q, :, :], w=[sk])
        C.cp("dve", rhsC[:, q, 64:192], s[:, 0:128], r=[sk], w=["rhsC"])
    C.dma(convw_t[:], convw[:, :, :], w=["convw_t"])
    C.dma(convv_t[:], convv[:, :, :], w=["convv_t"])
    C.dma(stage[0][:, 0:128], identd[:, :], w=["xt0"])
    for cc in range(2):
        for k in range(31):
            if k % 2 == 0:
                C.ts("dve", Dg[:, cc, k, :], stage[0][:, 0:128], convw_t[:, cc, k:k + 1], None, ALU.mult,
                     r=["xt0", "convw_t"], w=["Dg"])
            else:
                C.act(Dg[:, cc, k, :], stage[0][:, 0:128], AF.Copy, r=["xt0", "convw_t"], w=["Dg"],
                      scale=convw_t[:, cc, k:k + 1])
    for wi_, (Wt_, wk_) in enumerate(((W1k, "W1k"), (W1v, "W1v"))):
        for q in range(2):
            C.dma(stage[q][0:64, 0:1024], w1d[64 * wi_:64 * wi_ + 64, q * 1024:(q + 1) * 1024], w=["xt%d" % q])
            C.cp("dve", Wt_[:].rearrange("p r e -> p (r e)")[:, q * 1024:(q + 1) * 1024], stage[q][0:64, 0:1024],
                 r=["xt%d" % q], w=[wk_])
    C.dma(stage[0][0:64, 0:128], w2d[:, :], w=["xt0"])
    C.cp("dve", W2[:], stage[0][0:64, 0:128], r=["xt0"], w=["W2"])
    C.dma(stage[0][0:64, 0:32], ped[0:64, :], w=["xt0"])
    C.cp("dve", peTk[:], stage[0][0:64, 0:32], r=["xt0"], w=["peTk"])
    C.dma(stage[1][0:64, 0:32], ped[64:128, :], w=["xt1"])
    C.cp("dve", peTv[:], stage[1][0:64, 0:32], r=["xt1"], w=["peTv"])
    for wi_, (Wt_, wk_, pt_, pk_) in enumerate(((W1k, "W1k", peTk, "peTk"), (W1v, "W1v", peTv, "peTv"))):
        for r_ in range(32):
            C.mm(psK[0:64, 0:1], Wt_[:, r_, :], pt_[:, r_:r_ + 1], start=(r_ == 0), stop=(r_ == 31),
                 r=[wk_, pk_], w=["psK"])
        C.cp("dve", cbias[:, wi_:wi_ + 1], psK[0:64, 0:1], w=["psK", "cbias"])
    C.dma(lng_bc[:], sgv[0:1, :].partition_broadcast(128), w=["lng_bc"])
    C.dma(lnb_bc[:], sgv[1:2, :].partition_broadcast(128), w=["lnb_bc"])
    C.dma(tril[:], trild[:, :], w=["tril"])
    C.dma(stage[0][:, 0:512], sgw.rearrange("p g t -> p (g t)"), w=["xt0"])
    for g in range(4):
        C.tt("dve", wsT[:, g, :], stage[0][:, g * 128:(g + 1) * 128], tril[:], ALU.mult, r=["xt0", "tril"], w=["wsT"])
        C.dma(Bs[(g % 2) * 64:(g % 2) * 64 + 64, g // 2, :], sgb[g:g + 1, :].partition_broadcast(64), w=["Bs"])
    C.dma(stage[1][:, 0:256], poolw.rearrange("p q d -> p (q d)"), w=["xt1"])
    C.cp("dve", Wbd[:].rearrange("p q d -> p (q d)"), stage[1][:, 0:256], r=["xt1"], w=["Wbd"])
    C.dma(poolsc_t[:], poolsc[:, :], w=["poolsc_t"])
    C.dma(rc0[:], rcnt0[:, :, :], w=["rc0"])
    C.dma(rcg[:], rcntg[:, :, :], w=["rcg"])

    def rms_to_bf(xt_ap, xk, out_bf, ok, np_, col):
        C.act(junk[0:np_, :], xt_ap, AF.Square, r=[xk], w=["junk", "ssq%d" % col], accum=ssq[0:np_, col:col + 1])
        C.act(rstd[0:np_, col:col + 1], ssq[0:np_, col:col + 1], AF.Ln, r=["ssq%d" % col, "epsb"], w=["rstd%d" % col],
              scale=1.0 / D, bias=epsb[0:np_, 0:1])
        C.act(rstd[0:np_, col:col + 1], rstd[0:np_, col:col + 1], AF.Exp, r=["rstd%d" % col], w=["rstd%d" % col], scale=-0.5)
        C.act(out_bf, xt_ap, AF.Copy, r=[xk, "rstd%d" % col], w=[ok], scale=rstd[0:np_, col:col + 1])

    epsb = sb("epsb", [128, 2])
    C.memset("dve", epsb[:, 0:1], RMS_EPS, w=["epsb"])
    C.memset("dve", epsb[:, 1:2], LN_EPS, w=["epsb"])

    def kv_tile(m):
        p = m % 2
        xk, hk, tk = "xt%d" % p, "hnb%d" % p, "hT%d" % p
        C.dma(xt[p][:], xg_tile(m)[:, :], w=[xk])
        C.dma(rC[p][:], ropeC[:, m * 128:(m + 1) * 128], w=["rC%d" % p])
        C.dma(rS[p][:], ropeS[:, m * 128:(m + 1) * 128], w=["rS%d" % p])
        yield
        C.act(junk[:, :], xt[p][:], AF.Square, r=[xk], w=["junk", "ssq%d" % p], accum=ssq[:, p:p + 1])
        yield
        C.act(rstd[:, p:p + 1], ssq[:, p:p + 1], AF.Ln, r=["ssq%d" % p, "epsb"], w=["rstd%d" % p],
              scale=1.0 / D, bias=epsb[:, 0:1])
        yield
        C.act(rstd[:, p:p + 1], rstd[:, p:p + 1], AF.Exp, r=["rstd%d" % p], w=["rstd%d" % p], scale=-0.5)
        yield
        C.ts("dve", hnb[p][:], xt[p][:], rstd[:, p:p + 1], None, ALU.mult, r=[xk, "rstd%d" % p], w=[hk])
        yield
        for half in range(2):
            for q in range(4):
                kc = 4 * half + q
                C.tr(psT[:, q * 128:(q + 1) * 128], hnb[p][:, kc * 128:(kc + 1) * 128], ident[:], r=[hk, "ident"], w=["psT"])
            yield
            C.cp("dve", hT[p][:, 4 * half:4 * half + 4, :], psT[:, 0:512].rearrange("p (k t) -> p k t", k=4), w=["psT", tk])
            yield
        for f in range(3):
            for kc in range(8):
                C.mm(psK[:, f * 128:(f + 1) * 128], wkv[:, kc, f * 128:(f + 1) * 128], hT[p][:, kc, :],
                     start=(kc == 0), stop=(kc == 7), r=["wkv", tk], w=["psK"])
            yield
        for kc in range(8):
            C.mm(psK[:, 384:512], hT[p][:, kc, :], wkv[:, kc, 384:512], start=(kc == 0), stop=(kc == 7),
                 r=["wkv", tk], w=["psK"])
        yield
        grp = (m // 4) % 2
        col0 = 16 + (m % 4) * 128
        C.cp("act", kvk[grp][:, col0:col0 + 128], psK[0:64, 0:128], w=["psK", "kvk%d" % grp])
        C.cp("act", kvv[grp][:, col0:col0 + 128], psK[64:128, 0:128], w=["psK", "kvv%d" % grp])
        C.tt("dve", ropet[:, 0, :], psK[:, 128:256], rC[p][:], ALU.mult, r=["rC%d" % p], w=["psK", "ropet0"])
        C.tt("dve", ropet[:, 1, :], psK[:, 256:384], rS[p][:], ALU.mult, r=["rS%d" % p], w=["psK", "ropet1"])
        yield
        C.cp("dve", Vs[:, m, 0:64], psK[:, 384:448], w=["psK", "Vs%d" % m])
        C.cp("dve", Vw[:, m % 12, 0:64], psK[:, 448:512], w=["psK", "Vw%d" % (m % 12)])
        yield
        C.tt("pool", Kaug[0:64, m * 128:(m + 1) * 128], ropet[0:64, 0, :], ropet[0:64, 1, :], ALU.add,
             r=["ropet0", "ropet1"], w=["Kaug%d" % m])
        C.tt("pool", Kw[0:64, m % 12, :], ropet[64:128, 0, :], ropet[64:128, 1, :], ALU.add,
             r=["ropet0", "ropet1"], w=["Kw%d" % (m % 12)])
        yield

    def compress(j):
        g = j % 2
        n0 = 32 * j - 1
        c0 = 32 + n0
        for which, (Wt_, wk_, kb, kk, gT, gk_) in enumerate(((W1k, "W1k", kvk[g], "kvk%d" % g, gkT, "gkT"),
                                                           (W1v, "W1v", kvv[g], "kvv%d" % g, gvT, "gvT"))):
            for r_ in range(32):
                C.mm(psK[0:64, which * 32:(which + 1) * 32], Wt_[:, r_, :], kb[:, r_:r_ + 16 * 31 + 1:16],
                     start=(r_ == 0), stop=(r_ == 31), r=[wk_, kk], w=["psK"])
            yield
            cgt = cg1[:, which * 32:(which + 1) * 32]
            cgs = cg2[:, which * 32:(which + 1) * 32]
            k1, k2 = ["cg1_%d" % which], ["cg2_%d" % which]
            C.ts("dve", cgt, psK[0:64, which * 32:(which + 1) * 32], cbias[:, which:which + 1], None, ALU.add,
                 r=["cbias"], w=["psK"] + k1)
            C.tt("dve", cgs, cgt, cgt, ALU.mult, r=k1, w=k2)
            C.ts("dve", cgs, cgs, 0.044715, 1.0, ALU.mult, ALU.add, r=k2, w=k2)
            C.tt("dve", cgs, cgs, cgt, ALU.mult, r=k2 + k1, w=k2)
            C.act(cgs, cgs, AF.Exp, r=k2, w=k2, scale=-GELU_C)
            C.ts("dve", cgs, cgs, 1.0, None, ALU.add, r=k2, w=k2)
            C.recip(cgs, cgs, r=k2, w=k2)
            C.tt("dve", gT[:, c0:c0 + 32], cgt, cgs, ALU.mult, r=k1 + k2, w=[gk_])
            other = (kvk, kvv)[which][1 - g]
            C.cp("pool", other[:, 0:16], kb[:, 512:528], r=[kk], w=[("kvk%d", "kvv%d")[which] % (1 - g)])
            yield
        C.mm(psK[0:64, 64:96], W2[:, 0:64], gkT[:, c0:c0 + 32], r=["W2", "gkT"], w=["psK"])
        yield
        C.cp("dve", kcT[0:64, c0:c0 + 32], psK[0:64, 64:96], w=["psK", "kcT"])
        yield
        chunks = sorted(set([max(n0, 0) // 128, (n0 + 31) // 128]))
        for ci in chunks:
            C.mm(psK[:, 128:192], gvT[:, 32 + ci * 128:32 + (ci + 1) * 128], W2[:, 64:128], r=["W2", "gvT"], w=["psK"])
            yield
            C.cp("dve", rhsC[:, ci, 0:64], psK[:, 128:192], w=["psK", "rhsC"])
            yield

    def local(j):
        yield
        C.dma(xo[:], xown_tile(j)[:, :], w=["xo"])
        for cand in range(4):
            m_ = 4 * j - 1 + cand
            if m_ < 0:
                C.memset("pool", yo[0:32, :], 0.0, w=["yo"])
            else:
                C.dma(yo[cand * 32:(cand + 1) * 32, :], xg_tile(m_)[96:128, :], w=["yo"])
        C.dma(qCt[:], qC[j, :, :], w=["qCt"])
        C.dma(qSt[:], qS[j, :, :], w=["qSt"])
        C.dma(selAt[:], selA[j, :, :], w=["selAt"])
        C.dma(selBt[:], selB[j, :, :], w=["selBt"])
        rms_to_bf(xo[:], "xo", hno[:], "hno", 128, 2)
        yield
        rms_to_bf(yo[:], "yo", hnc[:], "hnc", 128, 3)
        yield
        for half in range(2):
            for q in range(4):
                kc = 4 * half + q
                C.tr(psT[:, 512 + q * 128:512 + (q + 1) * 128], hno[:, kc * 128:(kc + 1) * 128], ident[:],
                     r=["hno", "ident"], w=["psT"])
            yield
            C.cp("act", hTo[:, 4 * half:4 * half + 4, 32:160], psT[:, 512:1024].rearrange("p (k t) -> p k t", k=4),
                 w=["psT", "hTo"])
            yield
        for kc in range(8):
            C.mm(psG[:, kc * 32:(kc + 1) * 32], hnc[:, kc * 128:(kc + 1) * 128], selm[:], r=["hnc", "selm"], w=["psG"])
        yield
        C.cp("act", hTo[:, :, 0:32], psG[:, 0:256].rearrange("p (k t) -> p k t", k=8), w=["psG", "hTo"])
        yield

        def proj(chunk, ncol, out_ap):
            c0 = 160 - ncol
            for kc in range(8):
                C.mm(out_ap, wloc[:, kc, chunk * 128:(chunk + 1) * 128], hTo[:, kc, c0:160],
                     start=(kc == 0), stop=(kc == 7), r=["wloc", "hTo"], w=["psG"])

        yield
        for cc in range(2):
            proj(2 + cc, 160, psG[:, 0:160])
            C.act(sig[:, cc, :], psG[:, 0:160], AF.Sigmoid, w=["psG", "sig"])
        for cc in range(2):
            proj(cc, 160, psG[:, 0:160])
            C.tt("dve", ub[:, cc, :], psG[:, 0:160], sig[:, cc, :], ALU.mult, r=["sig"], w=["psG", "ub"])
        yield
        for cc in range(2):
            proj(4 + cc, 160, psG[:, 0:160])
            C.cp("act", zdT[:, cc, :], psG[:, 0:160], w=["psG", "zdT"])
        yield
        for cc in range(2):
            proj(6 + cc, 128, psG[:, 0:128])
            C.act(usg[:, cc, :], psG[:, 0:128], AF.Gelu_apprx_tanh, w=["psG", "usg"])
        yield
        for hp in range(2):
            for kc in range(8):
                C.mm(psG[:, 0:128], wloc[:, kc, 1024 + hp * 128:1024 + (hp + 1) * 128], hTo[:, kc, 32:160],
                     start=(kc == 0), stop=(kc == 7), r=["wloc", "hTo"], w=["psG"])
            for kc in range(8):
                C.mm(psG[:, 128:256], wloc[:, kc, 1280 + hp * 128:1280 + (hp + 1) * 128], hTo[:, kc, 32:160],
                     start=(kc == 0), stop=(kc == 7), r=["wloc", "hTo"], w=["psG"])
            C.tt("dve", qt1[:], psG[:, 0:128], qCt[:], ALU.mult, r=["qCt"], w=["psG", "qt1"])
            C.tt("dve", qt2[:], psG[:, 128:256], qSt[:], ALU.mult, r=["qSt"], w=["psG", "qt2"])
            C.act(QC[0:64, 2 * hp, :], psG[0:64, 0:128], AF.Copy, w=["psG", "QC"], scale=0.125)
            C.act(QC[0:64, 2 * hp + 1, :], psG[64:128, 0:128], AF.Copy, w=["psG", "QC"], scale=0.125)
            yield
            C.tt("dve", qsum[:], qt1[:], qt2[:], ALU.add, r=["qt1", "qt2"], w=["qsum"])
            yield
            for v in range(2):
                C.cp("dve", QR[0:64, v, 2 * hp, :], qsum[0:64, :], r=["qsum"], w=["QRq"])
                C.cp("act", QR[0:64, v, 2 * hp + 1, :], qsum[64:128, :], r=["qsum"], w=["QRq"])
            C.cp("pool", QW[0:64, 2 * hp, :], qsum[0:64, :], r=["qsum"], w=["QW"])
            C.cp("act", QW[0:64, 2 * hp + 1, :], qsum[64:128, :], r=["qsum"], w=["QW"])
        yield
        for kc in range(8):
            C.mm(psG[:, 0:268], hTo[:, kc, 32:160], wloc[:, kc, 1536:1804], start=(kc == 0), stop=(kc == 7),
                 r=["wloc", "hTo"], w=["psG"])
        C.act(gts[:], psG[:, 256:268], AF.Sigmoid, w=["psG", "gts"])
        C.act(vg[:], psG[:, 0:256], AF.Gelu_apprx_tanh, w=["psG", "vg"])
        C.S.op("dve", lambda e: e.bn_stats(out=bnst[:], in_=vg[:]), ["vg"], ["bnst"])
        C.S.op("dve", lambda e: e.bn_aggr(out=bnag[:], in_=bnst[:]), ["bnst"], ["bnag"])
        C.act(lnr[:, 0:1], bnag[:, 1:2], AF.Ln, r=["bnag", "epsb"], w=["lnr"], bias=epsb[:, 1:2])
        C.act(lnr[:, 1:2], lnr[:, 0:1], AF.Exp, r=["lnr"], w=["lnr1"], scale=-0.5)
        C.ts("dve", vn[:], vg[:], bnag[:, 0:1], lnr[:, 1:2], ALU.subtract, ALU.mult, r=["vg", "bnag", "lnr1"], w=["vn"])
        C.tt("dve", vn[:], vn[:], lng_bc[:], ALU.mult, r=["vn", "lng_bc"], w=["vn"])
        C.tt("dve", vbf[:], vn[:], lnb_bc[:], ALU.add, r=["vn", "lnb_bc"], w=["vbf"])
        yield
        for cc in range(2):
            for k in range(31):
                C.mm(psG[:, 0:128], Dg[:, cc, k, :], ub[:, cc, 2 + k:2 + k + 128], start=(k == 0), stop=(k == 30),
                     r=["Dg", "ub"], w=["psG"])
            C.act(ycv[:, cc, :], psG[:, 0:128], AF.Identity, r=["convv_t"], w=["psG", "ycv"], bias=convv_t[:, cc, 0:1])
            C.act(ysq[:, cc, :], psG[:, 0:128], AF.Square, r=["convv_t"], w=["psG", "ysq"], bias=convv_t[:, cc, 0:1])
        for src, sk_, c0_ in ((ycv, "ycv", 0), (ysq, "ysq", 128)):
            C.cp("dve", shi[:], src[:], r=[sk_], w=["shi"])
            C.tt("dve", slo[:], src[:], shi[:], ALU.subtract, r=[sk_, "shi"], w=["slo"])
            n_ = 0
            for part, pk_ in ((shi, "shi"), (slo, "slo")):
                for cc in range(2):
                    C.mm(psG[:, c0_:c0_ + 128], ones_b[:], part[:, cc, :], start=(n_ == 0), stop=(n_ == 3),
                         r=["ones_b", pk_], w=["psG"])
                    n_ += 1
        C.act(cmean[:], psG[:, 0:128], AF.Copy, w=["psG", "cmean"], scale=1.0 / 256)
        C.tt("dve", cmsq[:], cmean[:], cmean[:], ALU.mult, r=["cmean"], w=["cmsq"])
        C.stt(cvar[:], psG[:, 128:256], 1.0 / 256, cmsq[:], ALU.mult, ALU.subtract, r=["cmsq"], w=["psG", "cvar"])
        C.act(crstd[:], cvar[:], AF.Ln, r=["cvar", "epsb"], w=["crstd"], bias=epsb[:, 1:2])
        C.act(crstd[:], crstd[:], AF.Exp, r=["crstd"], w=["crstd"], scale=-0.5)
        for cc in range(2):
            C.tt("dve", cyn[:, cc, :], ycv[:, cc, :], cmean[:], ALU.subtract, r=["ycv", "cmean"], w=["cyn%d" % cc])
            C.tt("dve", cyn[:, cc, :], cyn[:, cc, :], crstd[:], ALU.mult, r=["cyn%d" % cc, "crstd"], w=["cyn%d" % cc])
            C.act(yT[:, cc, :], cyn[:, cc, :], AF.Silu, r=["cyn%d" % cc, "convv_t"], w=["yT%d" % cc],
                  scale=convv_t[:, cc, 1:2], bias=convv_t[:, cc, 2:3])
        yield
        for g in range(4):
            q = g // 2
            C.mm(psG[:, g * 128:(g + 1) * 128], vbf[:, q * 128:(q + 1) * 128], wsT[:, g, :], r=["vbf", "wsT"], w=["psG"])
        for g in range(4):
            q = g // 2
            lo = (g % 2) * 64
            C.tt("dve", sgt[lo:lo + 64, q, :], psG[lo:lo + 64, g * 128:(g + 1) * 128], Bs[lo:lo + 64, q, :], ALU.add,
                 r=["Bs"], w=["psG", "sgt%d" % g])
            C.tt("pool", yT[lo:lo + 64, 4 + q, :], sgt[lo:lo + 64, q, :], usg[lo:lo + 64, q, :], ALU.mult,
                 r=["sgt%d" % g, "usg"], w=["yT%d" % (4 + q)])
        yield
        C.tt("pool", s2[:, :, 1:160], zdT[:, :, 1:160], zdT[:, :, 0:159], ALU.add, r=["zdT"], w=["s2"])
        C.tt("pool", s4[:, :, 3:160], s2[:, :, 3:160], s2[:, :, 1:158], ALU.add, r=["s2"], w=["s4"])
        C.tt("pool", s8[:, :, 7:160], s4[:, :, 7:160], s4[:, :, 3:156], ALU.add, r=["s4"], w=["s8"])
        C.tt("pool", s16[:, :, 15:160], s8[:, :, 15:160], s8[:, :, 7:152], ALU.add, r=["s8"], w=["s16"])
        rc = rc0 if j == 0 else rcg
        srcs = [(s2, "s2", 0, 0), (s4, "s4", 64, 0), (s8, "s8", 0, 1), (s16, "s16", 64, 1)]
        for gi, (sbuf_, sk, lo, q) in enumerate(srcs):
            C.tt("dve", plf[lo:lo + 64, q, :], sbuf_[lo:lo + 64, q, 32:160], rc[lo:lo + 64, q, :], ALU.mult,
                 r=[sk, "rc0", "rcg"], w=["plf%d" % gi])
            C.tt("dve", plb[lo:lo + 64, q, :], plf[lo:lo + 64, q, :], zdT[lo:lo + 64, q, 32:160], ALU.subtract,
                 r=["plf%d" % gi, "zdT"], w=["plb%d" % q])
        for q in range(2):
            C.mm(psG[:, q * 128:(q + 1) * 128], Wbd[:, q, :], plb[:, q, :], r=["Wbd", "plb%d" % q], w=["psG"])
        for q in range(2):
            C.act(yT[:, 6 + q, :], psG[:, q * 128:(q + 1) * 128], AF.Copy, r=["poolsc_t"], w=["psG", "yT%d" % (6 + q)],
                  scale=poolsc_t[:, q:q + 1])

    pcount = [0]

    def nextP():
        i = pcount[0] % 4
        pcount[0] += 1
        return Pb[i], "Pb%d" % i

    scount = [0]

    def nextS(nb=2):
        i = scount[0] % nb
        scount[0] += 1
        if i == 2:
            return psC, "psC"
        return psS[i], "psS%d" % i

    def nsa(j):
        L = (32 * j + 30) // 128

        def pipeline(chunks, depth=1):
            state = []
            for i, (A, B, Cc) in enumerate(chunks):
                while len(state) < min(len(chunks), i + depth + 1):
                    state.append(chunks[len(state)][0]())
                st = state[i]
                B(st)
                yield
                Cc(st)
                yield

        st_p = {}

        def mk_cmp(ci):
            def A():
                S_, sk = nextS()
                C.mm(S_[:, :], kcT[:, 32 + ci * 128:32 + (ci + 1) * 128], QC[:].rearrange("p h t -> p (h t)"),
                     r=["kcT", "QC"], w=[sk])
                return S_, sk

            def B(st):
                S_, sk = st
                P_, pk = Pb[ci], "Pb%d" % ci
                C.act(P_[:].rearrange("p h t -> p (h t)"), S_[:, :], AF.Exp, w=[sk, pk])
                if ci == L:
                    C.tt("dve", P_[:], P_[:], cmask[:, j % 4:j % 4 + 1, :].to_broadcast([128, 4, 128]), ALU.mult,
                         r=[pk, "cmask"], w=[pk])
                elif ci == L - 1 and j % 4 == 0:
                    C.tt("dve", P_[:], P_[:], cmask[:, 4:5, :].to_broadcast([128, 4, 128]), ALU.mult,
                         r=[pk, "cmask"], w=[pk])

            def Cc(st):
                pass
            return A, B, Cc

        yield from pipeline([mk_cmp(ci) for ci in range(L + 1)])
        for hp in range(2):
            for ci in range(L + 1):
                for h in (2 * hp, 2 * hp + 1):
                    c0 = (h % 2) * 193
                    C.mm(psC[:, c0:c0 + 193], Pb[ci][:, h, :], rhsC[:, ci, :], start=(ci == 0 and h % 2 == 0), stop=(ci == L),
                         r=["Pb%d" % ci, "rhsC"], w=["psC"], skip=True)
            yield
            for h in (2 * hp, 2 * hp + 1):
                c0 = (h % 2) * 193
                C.ts("dve", den[:, h:h + 1], psC[:, c0 + 192:c0 + 193], 1e-30, None, ALU.max, w=["psC", "den%d" % hp])
            C.recip(rden[:, 2 * hp:2 * hp + 2], den[:, 2 * hp:2 * hp + 2], r=["den%d" % hp], w=["rden%d" % hp])
            yield
            for h in (2 * hp, 2 * hp + 1):
                c0 = (h % 2) * 193
                if h == 0:
                    C.ts("dve", imp[:], psC[:, c0 + 64:c0 + 192], rden[:, 0:1], None, ALU.mult, r=["rden0"],
                         w=["psC", "imp"])
                else:
                    C.stt(imp[:], psC[:, c0 + 64:c0 + 192], rden[:, h:h + 1], imp[:], ALU.mult, ALU.add,
                          r=["rden%d" % hp, "imp"], w=["psC", "imp"])
            C.tt("dve", gsc[:, 2 * hp:2 * hp + 2], rden[:, 2 * hp:2 * hp + 2], gts[:, 6 * hp:6 * hp + 6:3], ALU.mult,
                 r=["rden%d" % hp, "gts"], w=["gscc%d" % hp])
            for h in (2 * hp, 2 * hp + 1):
                c0 = (h % 2) * 193
                C.ts("dve", Ynsa[:, h * 64:(h + 1) * 64], psC[:, c0:c0 + 64], gsc[:, h:h + 1], None, ALU.mult,
                     r=["gscc%d" % hp], w=["psC", "Ynsa"])
            yield
        C.tt("dve", imp2[:], imp[:], selAt[:], ALU.mult, r=["imp", "selAt"], w=["imp2"])
        C.tt("dve", imp2[:], imp2[:], selBt[:], ALU.add, r=["imp2", "selBt"], w=["imp2"])
        yield
        C.S.op("dve", lambda e: e.max(out=top8[:, 0:8], in_=imp2[:]), ["imp2"], ["top8a"])
        C.S.op("dve", lambda e: e.match_replace(out=impw[:], in_to_replace=top8[:, 0:8], in_values=imp2[:],
                                                imm_value=-3e38), ["imp2", "top8a"], ["impw"])
        yield
        C.S.op("dve", lambda e: e.max(out=top8[:, 8:16], in_=impw[:]), ["impw"], ["top8b"])
        C.ts("dve", selt1[:], imp2[:], top8[:, 15:16], None, ALU.is_ge, r=["imp2", "top8b"], w=["selt1"])
        C.ts("dve", selt2[:], imp2[:], -1e29, -NEG, ALU.is_gt, ALU.mult, r=["imp2"], w=["selt2"])
        yield
        C.tt("dve", selt1[:], selt1[:], selt2[:], ALU.mult, r=["selt1", "selt2"], w=["selt1"])
        C.ts("dve", selbf[:], selt1[:], NEG, None, ALU.add, r=["selt1"], w=["selbf"])
        yield

        wl = list(range(max(0, 4 * j - 4), 4 * j + 4))

        def mk_win(wi, m):
            def A():
                S_, sk = nextS(3)
                C.mm(S_[:, :], Kw[:, m % 12, :], QW[:].rearrange("p h t -> p (h t)"),
                     r=["Kw%d" % (m % 12), "Kw", "QW"], w=[sk])
                return S_, sk

            def B(st):
                S_, sk = st
                P_, pk = nextP()
                C.act(P_[:].rearrange("p h t -> p (h t)"), S_[:, :], AF.Exp, w=[sk, pk])
                C.tt("dve", P_[:], P_[:], wmask[:, 4 + m - 4 * j:5 + m - 4 * j, :].to_broadcast([128, 4, 128]), ALU.mult,
                     r=[pk, "wmask"], w=[pk])
                st_p[("w", wi)] = (P_, pk)

            def Cc(st):
                P_, pk = st_p[("w", wi)]
                for h in range(4):
                    C.mm(psWin[:, h * 65:(h + 1) * 65], P_[:, h, :], Vw[:, m % 12, :], start=(wi == 0 and h == 0),
                         stop=(wi == len(wl) - 1), r=[pk, "Vw%d" % (m % 12), "Vw"], w=["psWin"], skip=True)
            return A, B, Cc

        scount[0] = 0
        yield from pipeline([mk_win(wi, m) for wi, m in enumerate(wl)], depth=2)
        C.tr(psT[:, 512:640], selbf[:], ident[:], r=["selbf", "ident"], w=["psT"])
        yield
        C.cp("dve", QR[64:128, 0, :, :], psT[0:64, 512:640].unsqueeze(1).to_broadcast([64, 4, 128]), w=["psT", "QRb"])
        C.cp("act", QR[64:128, 1, :, :], psT[64:128, 512:640].unsqueeze(1).to_broadcast([64, 4, 128]), w=["psT", "QRb"])
        yield

        nsel = 4 * j + 4

        def mk_sel(m):
            def A():
                S_, sk = nextS(3)
                v = 0 if m < 32 else 1
                C.mm(S_[:, :], Kaug[:, m * 128:(m + 1) * 128], QR[:, v, :, :].rearrange("p h t -> p (h t)"),
                     r=["Kaug%d" % m, "Kaug_ind", "QRq", "QRb"], w=[sk])
                return S_, sk

            def B(st):
                S_, sk = st
                P_, pk = nextP()
                C.act(P_[:].rearrange("p h t -> p (h t)"), S_[:, :], AF.Exp, w=[sk, pk])
                if m >= 4 * j:
                    C.tt("dve", P_[:], P_[:], wmask[:, 4 + m - 4 * j:5 + m - 4 * j, :].to_broadcast([128, 4, 128]), ALU.mult,
                         r=[pk, "wmask"], w=[pk])
                st_p[("s", m)] = (P_, pk)

            def Cc(st):
                P_, pk = st_p[("s", m)]
                for h in range(4):
                    C.mm(psSel[:, h * 65:(h + 1) * 65], P_[:, h, :], Vs[:, m, :], start=(m == 0 and h == 0),
                         stop=(m == nsel - 1), r=[pk, "Vs%d" % m, "Vs"], w=["psSel"], skip=True)
            return A, B, Cc

        yield from pipeline([mk_sel(m) for m in range(nsel)], depth=2)
        scount[0] = 0
        C.cp("dve", den[:, 4:8], psSel[:, 64:260:65], w=["psSel", "den"])
        C.cp("dve", den[:, 8:12], psWin[:, 64:260:65], w=["psWin", "den"])
        C.recip(rden[:, 4:12], den[:, 4:12], r=["den"], w=["rden2"])
        C.tt("dve", gsc[:, 4:8], rden[:, 4:8], gts[:, 1:12:3], ALU.mult, r=["rden2", "gts"], w=["gsc1"])
        C.tt("dve", gsc[:, 8:12], rden[:, 8:12], gts[:, 2:12:3], ALU.mult, r=["rden2", "gts"], w=["gsc1"])
        for h in range(4):
            C.stt(Ynsa[:, h * 64:(h + 1) * 64], psSel[:, h * 65:h * 65 + 64], gsc[:, 4 + h:5 + h], Ynsa[:, h * 64:(h + 1) * 64],
                  ALU.mult, ALU.add, r=["gsc1", "Ynsa"], w=["psSel", "Ynsa"])
        for h in range(4):
            C.stt(Ynb[:, h * 64:(h + 1) * 64], psWin[:, h * 65:h * 65 + 64], gsc[:, 8 + h:9 + h], Ynsa[:, h * 64:(h + 1) * 64],
                  ALU.mult, ALU.add, r=["gsc1", "Ynsa"], w=["psWin", "Ynb"])
        yield
        for q in range(2):
            C.tr(psT[:, 512 + q * 128:512 + (q + 1) * 128], Ynb[:, q * 128:(q + 1) * 128], ident[:], r=["Ynb", "ident"], w=["psT"])
        yield
        C.cp("act", yT[:, 2:4, :], psT[:, 512:768].rearrange("p (q t) -> p q t", q=2), w=["psT", "yT2", "yT3"])
        yield

    def outproj(j):
        yk = ["yT%d" % f for f in range(8)]
        for half in range(2):
            for f in range(8):
                C.mm(psG[:, :], yT[:, f, :], wo[:, f, half * 512:(half + 1) * 512], start=(f == 0), stop=(f == 7),
                     r=yk + ["wo"], w=["psG"])
            yield
            C.cp("dve", yo[:, half * 512:(half + 1) * 512], psG[:, :], w=["psG", "yo"])
            yield
        C.act(junk[:, :], yo[:], AF.Square, r=["yo"], w=["junk", "oss"], accum=oss[:, 0:1])
        yield
        C.act(oss[:, 1:2], oss[:, 0:1], AF.Ln, r=["oss", "epsb"], w=["oss1"], scale=1.0 / D, bias=epsb[:, 0:1])
        C.act(oss[:, 2:3], oss[:, 1:2], AF.Exp, r=["oss1"], w=["oss2"], scale=-0.5)
        C.stt(yo[:], yo[:], oss[:, 2:3], gpost_bc[:], ALU.mult, ALU.mult, r=["yo", "oss2", "gpost_bc"], w=["yo"])
        C.tt("dve", yo[:], yo[:], xo[:], ALU.add, r=["yo", "xo"], w=["yo"])
        C.dma(xm_own[j, :, :], yo[:], r=["yo"])
        C.dma(xm_tail[2 * j:2 * j + 2, :], yo[126:128, :], r=["yo"])
        yield

    def stream_b(j):
        for m in range(4 * j, 4 * j + 4):
            yield from kv_tile(m)
        yield from compress(j)

    def stream_a(j):
        yield from local(j)
        yield from nsa(j)
        yield from outproj(j)

    C.S.wait_cc()
    for _ in stream_b(0):
        pass
    for j in range(nslots):
        A_ = stream_a(j)
        B_ = stream_b(j + 1) if j + 1 < nslots else None
        doneA = doneB = B_ is None and False
        doneB = B_ is None
        while not doneA:
            try:
                next(A_)
            except StopIteration:
                doneA = True
            if not doneB:
                try:
                    next(B_)
                except StopIteration:
                    doneB = True
        while not doneB:
            try:
                next(B_)
            except StopIteration:
                doneB = True
    C.end_phase()


def ffn_phase(C, T, l):
    C.begin_phase("_f%d" % l)
    nc = C.nc
    xm_own = T["xm_own"]; tail_g = T["tail_g_%d" % l]; out_tile = T["ffn_out_tile"]
    w_up, w_dn, g2, gpost, cwd = (T[k + "_%d" % l] for k in ("w_up", "w_dn", "g2", "gpost2", "cwd"))
    identd = T["identd"]; selm2d = T["selm2"]

    sb = C.sb
    wu = sb("wu", [128, 8, 2 * FF], BF16)
    wd = sb("wd", [128, 22, D], BF16)
    stage = [sb("stage%d" % i, [128, 1024]) for i in range(4)]
    g2_t = sb("g2_t", [128, 8])
    gpost_bc = sb("gpost_bc", [128, D])
    cw = sb("cw", [128, NFC, 4])
    ident = sb("ident", [128, 128], BF16)
    epsb = sb("epsb", [128, 1])
    xs = [sb("xs%d" % i, [128, D]) for i in range(2)]
    xh = sb("xh", [8, D])
    selm2 = sb("selm2", [8, 2], BF16)
    hn = sb("hn", [128, D], BF16)
    hnh = sb("hnh", [8, D], BF16)
    junk = sb("junk", [128, D], BF16)
    ssq = sb("ssq", [128, 4]); rstd = sb("rstd", [128, 4])
    h2T = sb("h2T", [128, 8, 2, 130], BF16)
    actT = sb("actT", [128, 22, 2, 128], BF16)
    gc = [sb("gc%d" % i, [128, 2, 128]) for i in range(2)]
    uc = [sb("uc%d" % i, [128, 2, 128]) for i in range(2)]
    gg = [sb("gg%d" % i, [128, 2, 128]) for i in range(2)]
    yo = sb("yo", [128, D])
    oss = sb("oss", [128, 4])

    psU = [C.ps("psU%d" % i, [128, 512]) for i in range(4)]
    psD = [C.ps("psD%d" % i, [128, 512]) for i in range(2)]
    psT = C.ps("psT", [128, 1024], BF16)
    psH = C.ps("psH", [128, 512])

    C.dma(stage[1][0:8, 0:2], selm2d[:, :], w=["stage1"])
    C.cp("dve", selm2[:], stage[1][0:8, 0:2], r=["stage1"], w=["selm2"])
    C.dma(g2_t[:], g2[:, :], w=["gscale"])
    C.dma(gpost_bc[:], gpost.partition_broadcast(128), w=["gpost_bc"])
    C.dma(cw[:], cwd[:, :, :], w=["cw"])
    C.dma(stage[0][:, 0:128], identd[:, :], w=["stage0"])
    C.cp("dve", ident[:], stage[0][:, 0:128], r=["stage0"], w=["ident"])
    C.memset("dve", epsb[:, 0:1], RMS_EPS, w=["epsb"])
    par = 0
    for kc in range(8):
        for q in range(6):
            c0 = q * 1024
            c1 = min(2 * FF, c0 + 1024)
            s = stage[par % 4]; sk = "stage%d" % (par % 4); par += 1
            load_convert(C, wu[:, kc, c0:c1], w_up[kc * 128:(kc + 1) * 128, c0:c1], s, sk, "wu", c1 - c0,
                         scale_ap=g2_t[:, kc:kc + 1])
    for f in range(22):
        s = stage[par % 4]; sk = "stage%d" % (par % 4); par += 1
        load_convert(C, wd[:, f, :], w_dn[f * 128:(f + 1) * 128, :], s, sk, "wd", D, parity=f)

    def rms_to_bf(x_ap, xk, out_bf, ok, np_, col):
        C.act(junk[0:np_, :], x_ap, AF.Square, r=[xk], w=["junk", "ssq%d" % col], accum=ssq[0:np_, col:col + 1])
        C.act(rstd[0:np_, col:col + 1], ssq[0:np_, col:col + 1], AF.Ln, r=["ssq%d" % col, "epsb"], w=["rstd%d" % col],
              scale=1.0 / D, bias=epsb[0:np_, 0:1])
        C.act(rstd[0:np_, col:col + 1], rstd[0:np_, col:col + 1], AF.Exp, r=["rstd%d" % col], w=["rstd%d" % col], scale=-0.5)
        C.act(out_bf, x_ap, AF.Copy, r=[xk, "rstd%d" % col], w=[ok], scale=rstd[0:np_, col:col + 1])

    C.S.wait_cc()
    ucount = [0]
    for grp in range(NS // 2):
        for sl in range(2):
            j = grp * 2 + sl
            C.dma(xs[sl][:], xm_own[j, :, :], w=["xs%d" % sl])
            for cand in range(4):
                m_ = 4 * j - 1 + cand
                if m_ < 0:
                    C.memset("pool", xh[0:2, :], 0.0, w=["xh"])
                else:
                    row0 = ((m_ % 4) * NS + m_ // 4) * 2
                    C.dma(xh[cand * 2:(cand + 1) * 2, :], tail_g[row0:row0 + 2, :], w=["xh"])
            rms_to_bf(xs[sl][:], "xs%d" % sl, hn[:], "hn", 128, 0)
            rms_to_bf(xh[:], "xh", hnh[:], "hnh", 8, 1)
            for kc in range(8):
                C.tr(psT[:, kc * 128:(kc + 1) * 128], hn[:, kc * 128:(kc + 1) * 128], ident[:], r=["hn", "ident"], w=["psT"])
            C.cp("act", h2T[:, :, sl, 2:130], psT[:, :].rearrange("p (k t) -> p k t", k=8), w=["psT", "h2T"])
            for kc in range(8):
                C.mm(psH[:, kc * 2:(kc + 1) * 2], hnh[:, kc * 128:(kc + 1) * 128], selm2[:], r=["hnh", "selm2"], w=["psH"])
            C.cp("act", h2T[:, :, sl, 0:2], psH[:, 0:16].rearrange("p (k t) -> p k t", k=8), w=["psH", "h2T"])
        def up_mm(fc):
            banks = []
            for ch in (fc, 22 + fc):
                bi = ucount[0] % 4
                ucount[0] += 1
                bank = psU[bi]
                bk = "psU%d" % bi
                for kc in range(8):
                    C.mm(bank[:, 0:260], wu[:, kc, ch * 128:(ch + 1) * 128],
                         h2T[:, kc, :, :].rearrange("p s t -> p (s t)"),
                         start=(kc == 0), stop=(kc == 7), r=["wu", "h2T"], w=[bk])
                banks.append((bank, bk))
            return banks

        def conv_ops(fc, banks):
            p = fc % 2
            for (bank, bk), ch, dst, dk in ((banks[0], fc, gc[p], "gc%d" % p), (banks[1], 22 + fc, uc[p], "uc%d" % p)):
                bv = bank[:, 0:260].rearrange("p (s t) -> p s t", s=2)
                C.act(dst[:], bv[:, :, 2:130], AF.Identity, r=["cw"], w=[bk, dk], scale=cw[:, ch, 2:3], bias=cw[:, ch, 3:4])
                C.stt(dst[:], bv[:, :, 1:129], cw[:, ch, 1:2], dst[:], ALU.mult, ALU.add, r=["cw", dk], w=[bk, dk])
                C.stt(dst[:], bv[:, :, 0:128], cw[:, ch, 0:1], dst[:], ALU.mult, ALU.add, r=["cw", dk], w=[bk, dk])

        def gate_ops(fc):
            p = fc % 2
            C.act(gg[p][:], gc[p][:], AF.Gelu_apprx_tanh, r=["gc%d" % p], w=["gg%d" % p])
            C.tt("pool", actT[:, fc, :, :], gg[p][:], uc[p][:], ALU.mult,
                 r=["gg%d" % p, "uc%d" % p], w=["actT"])

        nxt = up_mm(0)
        for fc in range(22):
            cur = nxt
            if fc + 1 < 22:
                nxt = up_mm(fc + 1)
            conv_ops(fc, cur)
            if fc >= 1:
                gate_ops(fc - 1)
            if fc == 12 and grp >= 1 and T.get("post_group") is not None:
                T["post_group"](grp - 1)
        gate_ops(21)
        for sl in range(2):
            j = grp * 2 + sl
            for half in range(2):
                for f in range(22):
                    C.mm(psD[half][:, :], actT[:, f, sl, :], wd[:, f, half * 512:(half + 1) * 512],
                         start=(f == 0), stop=(f == 21), r=["actT", "wd"], w=["psD%d" % half])
                C.cp("dve", yo[:, half * 512:(half + 1) * 512], psD[half][:, :], w=["psD%d" % half, "yo"])
            C.act(junk[:, :], yo[:], AF.Square, r=["yo"], w=["junk", "oss"], accum=oss[:, 0:1])
            C.act(oss[:, 1:2], oss[:, 0:1], AF.Ln, r=["oss", "epsb"], w=["oss1"], scale=1.0 / D, bias=epsb[:, 0:1])
            C.act(oss[:, 2:3], oss[:, 1:2], AF.Exp, r=["oss1"], w=["oss2"], scale=-0.5)
            C.stt(yo[:], yo[:], oss[:, 2:3], gpost_bc[:], ALU.mult, ALU.mult, r=["yo", "oss2", "gpost_bc"], w=["yo"])
            C.tt("dve", yo[:], yo[:], xs[sl][:], ALU.add, r=["yo", "xs%d" % sl], w=["yo"])
            C.dma(out_tile(j)[:, :], yo[:], r=["yo"], w=["ffnout%d" % j])
    if T.get("post_group") is not None:
        T["post_group"](NS // 2 - 1)
        C.end_phase(wait_cc=False)
        return
    C.end_phase()


OFF_Q, OFF_KV, OFF_G, OFF_C, OFF_D = 512, 768, 1152, 1164, 1676


def _consts():
    c = {}
    half = 32
    inv = (10000.0 ** (-np.arange(half, dtype=np.float32) * 2.0 / 64)).astype(np.float32)
    ang = np.arange(SEQ, dtype=np.float32)[:, None] * inv[None, :]
    cos = np.cos(ang).astype(np.float32).T
    sin = np.sin(ang).astype(np.float32).T
    c64 = np.concatenate([cos, cos], 0)
    s64 = np.concatenate([-sin, sin], 0)
    c["ropeC"] = np.ascontiguousarray(np.concatenate([c64, c64], 0))
    c["ropeS"] = np.ascontiguousarray(np.concatenate([s64, s64], 0))
    blk = np.arange(SEQ) // 64
    c["indp"] = (np.arange(64)[:, None] == (blk % 64)[None, :]).astype(np.float32)
    c["identd"] = np.eye(128, dtype=np.float32)
    n = np.arange(512)[:, None]
    b = np.arange(128)[None, :]
    off = n - 4 * b
    M = np.where((off == -1) | (off == 3), 1.0, np.where((off >= 0) & (off <= 2), 2.0, 0.0)).astype(np.float32)
    M[511, :] = 0.0
    c["impM"] = np.ascontiguousarray(M.reshape(4, 128, 128))
    c["trild"] = (np.arange(128)[:, None] <= np.arange(128)[None, :]).astype(np.float32)
    rg = np.zeros((128, 2, 128), np.float32)
    for gi, w in enumerate((2, 4, 8, 16)):
        rg[(gi % 2) * 64:(gi % 2) * 64 + 64, gi // 2, :] = 1.0 / w
    c["rcntg"] = rg
    return c


def _core_consts(cidx):
    c = cidx
    o = {}
    k = np.arange(128)[:, None]
    t = np.arange(128)[None, :]
    wm = np.zeros((8, 128, 128), np.float32)
    for q in range(8):
        mm = q - 4
        rel = mm - c
        if rel == 0:
            wm[q] = (k <= t)
        elif rel == -4:
            wm[q] = (k > t)
        elif -4 < rel < 0:
            wm[q] = 1.0
    o["wmaskd"] = wm
    cm = np.zeros((5, 128, 128), np.float32)
    for jm in range(4):
        nprime = k - 32 * jm
        cm[jm] = (16 * nprime + 31 <= 128 * c + t)
    cm[4] = 1.0
    if c == 0:
        cm[4][127, :15] = 0.0
    o["cmaskd"] = cm
    A = np.zeros((NS, 128, 128), np.float32)
    B = np.zeros((NS, 128, 128), np.float32)
    tt = np.arange(128)[:, None]
    bb = np.arange(128)[None, :]
    for j in range(NS):
        i = 4 * j + c
        cur = 2 * i + (tt >= 64)
        valid = bb <= cur
        forced = (bb == 0) | (bb == cur) | (bb == cur - 1)
        A[j] = (valid & ~forced)
        B[j] = np.where(valid, np.where(forced, 1e6, 0.0), -1e30)
    o["selA"] = A
    o["selB"] = B
    inv = (10000.0 ** (-np.arange(32, dtype=np.float32) * 2.0 / 64)).astype(np.float32)
    qC = np.zeros((NS, 128, 128), np.float32)
    qS = np.zeros((NS, 128, 128), np.float32)
    for j in range(NS):
        pos = (128 * (4 * j + c) + np.arange(128)).astype(np.float32)
        ang = pos[:, None] * inv[None, :]
        cs = np.cos(ang).astype(np.float32).T * np.float32(0.125)
        sn = np.sin(ang).astype(np.float32).T * np.float32(0.125)
        c64 = np.concatenate([cs, cs], 0)
        s64 = np.concatenate([-sn, sn], 0)
        qC[j] = np.concatenate([c64, c64], 0)
        qS[j] = np.concatenate([s64, s64], 0)
    o["qC"] = qC
    o["qS"] = qS
    r0 = np.zeros((128, 2, 128), np.float32)
    for gi, w in enumerate((2, 4, 8, 16)):
        pos = 128 * c + np.arange(128)
        cnt = np.minimum(pos + 1, w).astype(np.float32)
        r0[(gi % 2) * 64:(gi % 2) * 64 + 64, gi // 2, :] = (1.0 / cnt)[None, :]
    o["rcnt0"] = r0
    sm = np.zeros((128, 32), np.float32)
    sm[c * 32:(c + 1) * 32, :] = np.eye(32, dtype=np.float32)
    o["selm"] = sm
    sm2 = np.zeros((8, 2), np.float32)
    sm2[c * 2:(c + 1) * 2, :] = np.eye(2, dtype=np.float32)
    o["selm2"] = sm2
    return o


def _sw(idx):
    return np.concatenate([idx[32:], idx[:32]])


def _mixer_weights(P, l):
    w_in = P["w_in"][l]
    kv = lambda s: np.arange(OFF_KV + 64 * s, OFF_KV + 64 * s + 64)
    cols_kv = np.concatenate([kv(0), kv(1), kv(2), kv(4), _sw(kv(2)), _sw(kv(4)), kv(3), kv(5)])
    qcols = np.arange(OFF_Q, OFF_Q + 256)
    qsw = np.concatenate([_sw(qcols[h * 64:(h + 1) * 64]) for h in range(4)])
    cols_loc = np.concatenate([np.arange(0, 512), np.arange(OFF_D, OFF_D + 256), np.arange(OFF_C, OFF_C + 256),
                               qcols, qsw, np.arange(OFF_C + 256, OFF_C + 512), np.arange(OFF_G, OFF_G + 12)])
    o = {}
    o["w_kv"] = np.ascontiguousarray(w_in[:, cols_kv])
    o["w_loc"] = np.ascontiguousarray(w_in[:, cols_loc])
    o["w_out"] = np.ascontiguousarray(P["w_out"][l])
    o["gpre"] = np.ascontiguousarray(P["norm_mix_pre"][l].reshape(8, 128).T)
    o["gpost"] = np.ascontiguousarray(P["norm_mix_post"][l].reshape(1, D))
    o["convw"] = np.ascontiguousarray(P["conv_dw_w"][l].reshape(31, 2, 128).transpose(2, 1, 0))
    cv = np.stack([P["conv_dw_b"][l], P["conv_ln_g"][l], P["conv_ln_b"][l]], -1)
    o["convv"] = np.ascontiguousarray(cv.reshape(2, 128, 3).transpose(1, 0, 2))
    w1k = P["nsa_ck_w1"][l].reshape(32, 64, 64).transpose(1, 0, 2).reshape(64, 2048)
    w1v = P["nsa_cv_w1"][l].reshape(32, 64, 64).transpose(1, 0, 2).reshape(64, 2048)
    o["w1d"] = np.ascontiguousarray(np.concatenate([w1k, w1v], 0))
    o["w2d"] = np.ascontiguousarray(np.concatenate([P["nsa_ck_w2"][l], P["nsa_cv_w2"][l]], 1))
    o["ped"] = np.ascontiguousarray(np.concatenate([P["nsa_pe_k"][l].T, P["nsa_pe_v"][l].T], 0))
    o["sgv"] = np.ascontiguousarray(np.stack([P["sgu_ln_g"][l], P["sgu_ln_b"][l]], 0))
    o["sgw"] = np.ascontiguousarray(P["sgu_w"][l].transpose(2, 0, 1))
    o["sgb"] = np.ascontiguousarray(P["sgu_b"][l])
    pw = np.zeros((128, 2, 128), np.float32)
    for gi in range(4):
        lo = (gi % 2) * 64
        pw[lo:lo + 64, gi // 2, lo:lo + 64] = P["pool_w"][l][gi]
    o["poolw"] = pw
    o["poolsc"] = np.ascontiguousarray(P["pool_scale"][l].reshape(2, 128).T)
    return o


def _ffn_weights(P, l):
    o = {}
    o["w_up"] = np.ascontiguousarray(P["ffn_up"][l])
    o["w_dn"] = np.ascontiguousarray(P["ffn_down"][l])
    o["g2"] = np.ascontiguousarray(P["norm_ffn_pre"][l].reshape(8, 128).T)
    o["gpost2"] = np.ascontiguousarray(P["norm_ffn_post"][l].reshape(1, D))
    cw = np.concatenate([P["ffn_conv_w"][l], P["ffn_conv_b"][l][None, :]], 0)
    o["cwd"] = np.ascontiguousarray(cw.reshape(4, NFC, 128).transpose(2, 1, 0))
    return o


def _own_tiles(xb, c, halo):
    pad = np.concatenate([np.zeros((halo, D), np.float32), xb], 0)
    out = np.empty((NS, halo + 128, D), np.float32)
    for j in range(NS):
        i = 4 * j + c
        out[j] = pad[128 * i:128 * i + 128 + halo]
    return out


def _scatter_own(res_list, key):
    x = np.empty((NB, SEQ, D), np.float32)
    for core in range(8):
        b, c = divmod(core, 4)
        r = res_list[core][key]
        for j in range(NS):
            i = 4 * j + c
            x[b, 128 * i:128 * (i + 1)] = r[j]
    return x


def _chunk_row(m):
    r, sl = m % 4, m // 4
    return sl // 2, r * 256 + (sl % 2) * 128


def build_all(nslots=NS):
    C = Ctx()
    T = {}
    xg0 = C.dram_in("xg0", [8 * 1024, D])
    xown0 = C.dram_in("xown0", [NS, 128, D])
    for k, shp in MIX_C_SHAPES.items():
        T[k] = C.dram_in(k, shp)
    for l in range(2):
        for k, shp in MIX_W_SHAPES.items():
            T["%s_%d" % (k, l)] = C.dram_in("%s_%d" % (k, l), shp)
        for k, shp in FFN_W_SHAPES.items():
            T["%s_%d" % (k, l)] = C.dram_in("%s_%d" % (k, l), shp)
    out = C.dram_out("out", [NS, 128, D])
    xm_own = C.dram_int("xm_own", [NS, 128, D])
    tails = [C.dram_int("xm_tail%d" % l, [NS * 2, D]) for l in range(2)]
    tail_g = [C.dram_int("tail_g%d" % l, [4 * NS * 2, D]) for l in range(2)]
    x1_own = [C.dram_int("x1_own%d" % g, [256, D]) for g in range(8)]
    x1_g = [C.dram_int("x1_g%d" % g, [1024, D]) for g in range(8)]
    RG = [[0, 1, 2, 3], [4, 5, 6, 7]]

    def gather(src, dst):
        C.S.collective_nowait(lambda e: e.collective_compute("AllGather", ALU.bypass, replica_groups=RG,
                                                      ins=[src.opt()], outs=[dst.opt()]))

    def xg_tile0(m):
        g, ro = _chunk_row(m)
        return xg0[g * 1024 + ro:g * 1024 + ro + 128, :]

    def xg_tile1(m):
        g, ro = _chunk_row(m)
        return x1_g[g][ro:ro + 128, :]

    for l in range(2):
        T["xg_tile"] = xg_tile0 if l == 0 else xg_tile1
        T["xown_tile"] = (lambda j: xown0[j, :, :]) if l == 0 else (lambda j: x1_own[j // 2][(j % 2) * 128:(j % 2) * 128 + 128, :])
        T["xm_own"] = xm_own
        T["xm_tail"] = tails[l]
        mixer_phase(C, T, l, nslots=nslots)
        gather(tails[l], tail_g[l])
        T["tail_g_%d" % l] = tail_g[l]
        T["ffn_out_tile"] = (lambda j: x1_own[j // 2][(j % 2) * 128:(j % 2) * 128 + 128, :]) if l == 0 else (lambda j: out[j, :, :])
        if l == 0:
            def post_group(g):
                C.S.cc_op(lambda e: e.collective_compute("AllGather", ALU.bypass, replica_groups=RG,
                                                         ins=[x1_own[g].opt()], outs=[x1_g[g].opt()]),
                          reads=["ffnout%d" % (2 * g), "ffnout%d" % (2 * g + 1)], writes=["x1g%d" % g])
            T["post_group"] = post_group
        else:
            T["post_group"] = None
        ffn_phase(C, T, l)
    info = C.finish()
    return C.nc, info


_CACHE = {}


def kernel(**inputs):
    P = {k: np.asarray(v, dtype=np.float32) for k, v in inputs.items()}
    x = P["x"]
    if "nc" not in _CACHE:
        _CACHE["nc"] = build_all()[0]
    nc = _CACHE["nc"]
    consts = _consts()
    shared = {k: consts[k] for k in MIX_C_SHAPES if k in consts}
    for l in range(2):
        for k, v in _mixer_weights(P, l).items():
            shared["%s_%d" % (k, l)] = v
        for k, v in _ffn_weights(P, l).items():
            shared["%s_%d" % (k, l)] = v
    in_maps = []
    for core in range(8):
        b, c = divmod(core, 4)
        m = dict(shared)
        m.update(_core_consts(c))
        xt = x[b].reshape(8, 2, 4, 128, D)
        m["xg0"] = np.ascontiguousarray(xt.transpose(0, 2, 1, 3, 4).reshape(8 * 1024, D))
        m["xown0"] = np.ascontiguousarray(x[b].reshape(NS, 4, 128, D)[:, c])
        in_maps.append(m)
    res = run_bass_kernel_spmd(nc, in_maps, core_ids=list(range(8)))
    return _scatter_own(res.results, "out").astype(np.float32)
```

```python
import numpy as np
from contextlib import ExitStack
import concourse.bass as bass
import concourse.mybir as mybir
from concourse.bass_utils import run_bass_kernel_spmd

F32 = mybir.dt.float32
BF16 = mybir.dt.bfloat16
AF = mybir.ActivationFunctionType
ALU = mybir.AluOpType
AX = mybir.AxisListType

D = 1024
SEQ = 8192
NB = 2
NT = 64
NS = 16
FF = 2816
NFC = 44
RMS_EPS = 1e-6
LN_EPS = 1e-5
NEG = -30000.0
GELU_C = 1.5957691216057308


class Sched:
    NDS = 8

    def __init__(self, nc, sems):
        self.nc = nc
        self.eng = {"pe": nc.tensor, "act": nc.scalar, "dve": nc.vector,
                    "pool": nc.gpsimd, "sp": nc.sync}
        self.ops = []
        self.last_writer = {}
        self.readers = {}
        self.sems = sems

    def cc_op(self, fn, reads=(), writes=()):
        idx = self.op("pool", fn, reads, writes)
        self.ops[idx].append("cc")
        return idx

    def op(self, engine, fn, reads=(), writes=()):
        idx = len(self.ops)
        is_dma = engine == "sp"
        deps = {}

        def add(d, kind):
            if d is None:
                return
            if deps.get(d) is None or kind == "raw":
                deps[d] = kind

        for k in reads:
            add(self.last_writer.get(k), "raw")
        for k in writes:
            add(self.last_writer.get(k), "waw")
            for r in self.readers.get(k, ()):
                add(r, "war")
        for k in writes:
            self.last_writer[k] = idx
            self.readers[k] = []
        for k in reads:
            if k not in writes:
                lst = self.readers.setdefault(k, [])
                if not is_dma:
                    lst[:] = [r_ for r_ in lst if self.ops[r_][0] != engine]
                lst.append(idx)
        keep = []
        for d, kind in deps.items():
            de = self.ops[d][0]
            if de == engine and not is_dma:
                if engine == "pe":
                    continue
            keep.append(d)
        self.ops.append([engine, fn, keep, is_dma])
        for d in keep:
            self.ops[d][3] = True
        return idx

    def _init_emit_state(self):
        self.counts = {e: 0 for e in self.eng}
        self.dma_counts = [0] * self.NDS
        self.n_dma = 0
        self.waited = {e: {} for e in self.eng}
        self.sigval = {}
        self.emitted = 0
        self.cc_count = 0

    def flush(self):
        if not hasattr(self, "counts"):
            self._init_emit_state()
        last = {}
        for idx in range(self.emitted, len(self.ops)):
            if len(self.ops[idx]) == 4 and self.ops[idx][0] != "__ccwait__":
                last[self.ops[idx][0]] = idx
        for e, idx in last.items():
            if e != "sp" and len(self.ops[idx]) == 4:
                self.ops[idx][3] = True
        for idx in range(self.emitted, len(self.ops)):
            e, fn, deps, signal = self.ops[idx][:4]
            if e == "__ccwait__":
                if self.cc_count > 0:
                    for e2, eng2 in self.eng.items():
                        if self.waited[e2].get("cc", 0) < self.cc_count:
                            eng2.wait_ge(self.sems["cc"], self.cc_count)
                            self.waited[e2]["cc"] = self.cc_count
                continue
            is_cc = len(self.ops[idx]) > 4
            eng = self.eng[e]
            need = {}
            for d in deps:
                sem, val = self.sigval[d]
                if need.get(id(sem), (None, 0))[1] < val:
                    need[id(sem)] = (sem, val)
            if e == "sp":
                slot = self.n_dma % self.NDS
                if self.dma_counts[slot] > 0:
                    sem = self.sems["dma"][slot]
                    val = self.dma_counts[slot] * 16
                    if need.get(id(sem), (None, 0))[1] < val:
                        need[id(sem)] = (sem, val)
            for key, (sem, val) in need.items():
                if self.waited[e].get(key, 0) >= val:
                    continue
                eng.wait_ge(sem, val)
                self.waited[e][key] = val
            ins = fn(eng)
            if is_cc:
                self.cc_count += 1
                ins.then_inc(self.sems["cc"])
                self.sigval[idx] = (self.sems["cc"], self.cc_count)
                continue
            if e == "sp":
                slot = self.n_dma % self.NDS
                self.n_dma += 1
                self.dma_counts[slot] += 1
                sem = self.sems["dma"][slot]
                ins.then_inc(sem, 16)
                self.sigval[idx] = (sem, self.dma_counts[slot] * 16)
            elif signal:
                self.counts[e] += 1
                ins.then_inc(self.sems[e], 1)
                self.sigval[idx] = (self.sems[e], self.counts[e])
        self.emitted = len(self.ops)

    def barrier(self, engines=None, wait_cc=True):
        self.flush()
        for e, eng in self.eng.items():
            for x in ("pe", "act", "dve", "pool"):
                if x != e and self.counts[x] > 0 and self.waited[e].get(id(self.sems[x]), 0) < self.counts[x]:
                    eng.wait_ge(self.sems[x], self.counts[x])
                    self.waited[e][id(self.sems[x])] = self.counts[x]
            for slot in range(self.NDS):
                if self.dma_counts[slot] > 0:
                    sem = self.sems["dma"][slot]
                    val = self.dma_counts[slot] * 16
                    if self.waited[e].get(id(sem), 0) < val:
                        eng.wait_ge(sem, val)
                        self.waited[e][id(sem)] = val
            if wait_cc and self.cc_count > 0 and self.waited[e].get("cc", 0) < self.cc_count:
                eng.wait_ge(self.sems["cc"], self.cc_count)
                self.waited[e]["cc"] = self.cc_count
        self.last_writer = {}
        self.readers = {}

    def wait_cc(self):
        self.ops.append(["__ccwait__", None, [], False])

    def collective_nowait(self, fn):
        self.barrier()
        self.cc_count += 1
        fn(self.eng["pool"]).then_inc(self.sems["cc"])

    def collective(self, fn):
        self.barrier()
        self.cc_count += 1
        fn(self.eng["pool"]).then_inc(self.sems["cc"])
        for e, eng in self.eng.items():
            eng.wait_ge(self.sems["cc"], self.cc_count)

    def emit(self):
        self.flush()
        sp = self.eng["sp"]
        for slot in range(self.NDS):
            if self.dma_counts[slot] > 0:
                sp.wait_ge(self.sems["dma"][slot], self.dma_counts[slot] * 16)
        return self.counts, self.n_dma


class Ctx:
    def __init__(self):
        self.nc = bass.Bass("TRN2", target_bir_lowering=False)
        self.es = ExitStack()
        nc = self.nc
        sems = {e: self.es.enter_context(nc.semaphore("s_" + e)) for e in ["pe", "act", "dve", "pool"]}
        sems["dma"] = [self.es.enter_context(nc.semaphore("s_dma%d" % i)) for i in range(Sched.NDS)]
        sems["cc"] = self.es.enter_context(nc.semaphore("s_cc"))
        self.S = Sched(nc, sems)
        self.pes = self.es

    def begin_phase(self, sfx):
        self.pes = ExitStack()
        self.sfx = sfx

    def end_phase(self, wait_cc=True):
        self.S.barrier(wait_cc=wait_cc)
        self.pes.close()
        self.pes = self.es

    def dram_int(self, name, shape, dt=F32):
        return self.nc.dram_tensor(name, list(shape), dt).ap()

    def dram_in(self, name, shape, dt=F32):
        return self.nc.dram_tensor(name, list(shape), dt, kind="ExternalInput").ap()

    def dram_out(self, name, shape, dt=F32):
        return self.nc.dram_tensor(name, list(shape), dt, kind="ExternalOutput").ap()

    def sb(self, name, shape, dt=F32):
        return self.pes.enter_context(self.nc.sbuf_tensor(name + getattr(self, "sfx", ""), list(shape), dt))

    def ps(self, name, shape, dt=F32):
        return self.pes.enter_context(self.nc.psum_tensor(name + getattr(self, "sfx", ""), list(shape), dt))

    def dma(self, out, in_, r=(), w=()):
        self.S.op("sp", lambda e: e.dma_start(out=out, in_=in_), r, w)

    def mm(self, out, lhsT, rhs, start=True, stop=True, r=(), w=(), skip=False):
        self.S.op("pe", lambda e: e.matmul(out, lhsT=lhsT, rhs=rhs, start=start, stop=stop,
                                           skip_group_check=skip), r, w)

    def tr(self, out, in_, ident, r=(), w=()):
        self.S.op("pe", lambda e: e.transpose(out, in_, ident), r, w)

    def act(self, out, in_, func, r=(), w=(), scale=1.0, bias=0.0, accum=None, eng="act"):
        if accum is None:
            self.S.op("act", lambda e: e.activation(out=out, in_=in_, func=func, bias=bias, scale=scale), r, w)
        else:
            self.S.op("act", lambda e: e.activation(out=out, in_=in_, func=func, bias=bias, scale=scale,
                                                    accum_out=accum), r, w)

    def cp(self, eng, out, in_, r=(), w=()):
        if eng == "act":
            self.S.op("act", lambda e: e.activation(out=out, in_=in_, func=AF.Copy), r, w)
        else:
            self.S.op(eng, lambda e: e.tensor_copy(out=out, in_=in_), r, w)

    def tt(self, eng, out, in0, in1, op, r=(), w=()):
        self.S.op(eng, lambda e: e.tensor_tensor(out=out, in0=in0, in1=in1, op=op), r, w)

    def ts(self, eng, out, in0, s1, s2, op0, op1=None, r=(), w=()):
        if op1 is None:
            self.S.op(eng, lambda e: e.tensor_scalar(out=out, in0=in0, scalar1=s1, scalar2=None, op0=op0), r, w)
        else:
            self.S.op(eng, lambda e: e.tensor_scalar(out=out, in0=in0, scalar1=s1, scalar2=s2, op0=op0, op1=op1), r, w)

    def stt(self, out, in0, scalar, in1, op0, op1, r=(), w=()):
        self.S.op("dve", lambda e: e.scalar_tensor_tensor(out=out, in0=in0, scalar=scalar, in1=in1,
                                                          op0=op0, op1=op1), r, w)

    def memset(self, eng, ap, val, w=()):
        self.S.op(eng, lambda e: e.memset(ap, val), (), w)

    def recip(self, out, in_, r=(), w=()):
        self.S.op("dve", lambda e: e.reciprocal(out=out, in_=in_), r, w)

    def finish(self):
        res = self.S.emit()
        self.es.close()
        return res


def load_convert(C, dst_bf, src_dram, stage, stage_key, dst_key, ncols, scale_ap=None, parity=0, scale_key="gscale"):
    C.dma(stage[:, 0:ncols], src_dram, w=[stage_key])
    if scale_ap is not None:
        C.ts("dve", dst_bf, stage[:, 0:ncols], scale_ap, None, ALU.mult, r=[stage_key, scale_key], w=[dst_key])
    elif parity % 2 == 0:
        C.cp("dve", dst_bf, stage[:, 0:ncols], r=[stage_key], w=[dst_key])
    else:
        C.cp("act", dst_bf, stage[:, 0:ncols], r=[stage_key], w=[dst_key])


NLOC = 1804


MIX_W = ["w_kv", "w_loc", "w_out", "gpre", "gpost", "convw", "convv", "w1d", "w2d", "ped", "sgv", "sgw", "sgb",
         "poolw", "poolsc"]
MIX_W_SHAPES = {"w_kv": [D, 512], "w_loc": [D, NLOC], "w_out": [D, D], "gpre": [128, 8], "gpost": [1, D],
                "convw": [128, 2, 31], "convv": [128, 2, 3], "w1d": [128, 2048], "w2d": [64, 128], "ped": [128, 32],
                "sgv": [2, 256], "sgw": [128, 4, 128], "sgb": [4, 128], "poolw": [128, 2, 128], "poolsc": [128, 2]}
MIX_C_SHAPES = {"ropeC": [128, SEQ], "ropeS": [128, SEQ], "qC": [NS, 128, 128], "qS": [NS, 128, 128], "indp": [64, SEQ],
                "identd": [128, 128], "wmaskd": [8, 128, 128], "cmaskd": [5, 128, 128], "selA": [NS, 128, 128],
                "selB": [NS, 128, 128], "impM": [4, 128, 128], "trild": [128, 128], "rcnt0": [128, 2, 128],
                "rcntg": [128, 2, 128], "selm": [128, 32], "selm2": [8, 2]}
FFN_W_SHAPES = {"w_up": [D, 2 * FF], "w_dn": [FF, D], "g2": [128, 8], "gpost2": [1, D], "cwd": [128, NFC, 4]}


def mixer_phase(C, T, l, nslots=NS, stages=5):
    C.begin_phase("_m%d" % l)
    nc = C.nc
    xg_tile = T["xg_tile"]; xown_tile = T["xown_tile"]; xm_own = T["xm_own"]; xm_tail = T["xm_tail"]
    w_kv, w_loc, w_out, gpre, gpost = (T[k + "_%d" % l] for k in ("w_kv", "w_loc", "w_out", "gpre", "gpost"))
    convw, convv, w1d, w2d, ped = (T[k + "_%d" % l] for k in ("convw", "convv", "w1d", "w2d", "ped"))
    sgv, sgw, sgb, poolw, poolsc = (T[k + "_%d" % l] for k in ("sgv", "sgw", "sgb", "poolw", "poolsc"))
    ropeC, ropeS, qC, qS, indp, identd = (T[k] for k in ("ropeC", "ropeS", "qC", "qS", "indp", "identd"))
    wmaskd, cmaskd, selA, selB, impM, trild = (T[k] for k in ("wmaskd", "cmaskd", "selA", "selB", "impM", "trild"))
    rcnt0, rcntg, selmd = T["rcnt0"], T["rcntg"], T["selm"]

    sb = C.sb
    wkv = sb("wkv", [128, 8, 512], BF16)
    wloc = sb("wloc", [128, 8, NLOC], BF16)
    wo = sb("wo", [128, 8, D], BF16)
    xt = [sb("xt0", [128, D]), sb("xt1", [128, D])]
    stage = xt
    gpre_t = sb("gpre_t", [128, 8])
    gpost_bc = sb("gpost_bc", [128, D])
    ident = sb("ident", [128, 128], BF16)
    ones_b = sb("ones_b", [128, 128], BF16)
    shi = sb("shi", [128, 2, 128], BF16)
    slo = sb("slo", [128, 2, 128], BF16)
    Dg = sb("Dg", [128, 2, 31, 128], BF16)
    convw_t = sb("convw_t", [128, 2, 31])
    convv_t = sb("convv_t", [128, 2, 3])
    W1k = sb("W1k", [64, 32, 64], BF16)
    W1v = sb("W1v", [64, 32, 64], BF16)
    W2 = sb("W2", [64, 128], BF16)
    peTk = sb("peTk", [64, 32], BF16)
    peTv = sb("peTv", [64, 32], BF16)
    hnc = sb("hnc", [128, D], BF16)
    cbias = sb("cbias", [64, 2])
    Kaug = sb("Kaug", [128, SEQ], BF16)
    Kw = sb("Kw", [128, 12, 128], BF16)
    Vs = sb("Vs", [128, NT, 65], BF16)
    Vw = sb("Vw", [128, 12, 65], BF16)
    kvk = [sb("kvk0", [64, 528], BF16), sb("kvk1", [64, 528], BF16)]
    kvv = [sb("kvv0", [64, 528], BF16), sb("kvv1", [64, 528], BF16)]
    gkT = sb("gkT", [64, 576], BF16)
    gvT = sb("gvT", [64, 576], BF16)
    kcT = sb("kcT", [128, 576], BF16)
    rhsC = sb("rhsC", [128, 4, 193], BF16)
    wmask = sb("wmask", [128, 8, 128], BF16)
    cmask = sb("cmask", [128, 5, 128], BF16)
    wsT = sb("wsT", [128, 4, 128], BF16)
    tril = sb("tril", [128, 128])
    Bs = sb("Bs", [128, 2, 128])
    Wbd = sb("Wbd", [128, 2, 128], BF16)
    poolsc_t = sb("poolsc_t", [128, 2])
    rc0 = sb("rc0", [128, 2, 128])
    rcg = sb("rcg", [128, 2, 128])
    lng_bc = sb("lng_bc", [128, 256])
    lnb_bc = sb("lnb_bc", [128, 256])
    hnb = [sb("hnb0", [128, D], BF16), sb("hnb1", [128, D], BF16)]
    hT = [sb("hT0", [128, 8, 128], BF16), sb("hT1", [128, 8, 128], BF16)]
    rC = [sb("rC0", [128, 128]), sb("rC1", [128, 128])]
    rS = [sb("rS0", [128, 128]), sb("rS1", [128, 128])]
    junk = sb("junk", [128, D], BF16)
    ssq = sb("ssq", [128, 8])
    rstd = sb("rstd", [128, 8])
    ropet = sb("ropet", [128, 2, 128])
    xo = sb("xo", [128, D])
    hno = sb("hno", [128, D], BF16)
    hTo = sb("hTo", [128, 8, 160], BF16)
    sig = sb("sig", [128, 2, 160])
    ub = sb("ub", [128, 2, 160], BF16)
    zdT = sb("zdT", [128, 2, 160])
    s2 = sb("s2", [128, 2, 160]); s4 = sb("s4", [128, 2, 160])
    s8 = sb("s8", [128, 2, 160]); s16 = sb("s16", [128, 2, 160])
    plf = sb("plf", [128, 2, 128]); plb = sb("plb", [128, 2, 128], BF16)
    usg = sb("usg", [128, 2, 128])
    qCt = sb("qCt", [128, 128]); qSt = sb("qSt", [128, 128])
    qt1 = sb("qt1", [128, 128]); qt2 = sb("qt2", [128, 128]); qsum = sb("qsum", [128, 128])
    QR = sb("QR", [128, 2, 4, 128], BF16)
    QC = sb("QC", [128, 4, 128], BF16)
    QW = sb("QW", [128, 4, 128], BF16)
    gts = sb("gts", [128, 12])
    vg = sb("vg", [128, 256]); vn = sb("vn", [128, 256]); vbf = sb("vbf", [128, 256], BF16)
    bnst = sb("bnst", [128, 6]); bnag = sb("bnag", [128, 2]); lnr = sb("lnr", [128, 2])
    ycv = sb("ycv", [128, 2, 128]); ysq = sb("ysq", [128, 2, 128])
    cmean = sb("cmean", [128, 128]); cmsq = sb("cmsq", [128, 128]); cvar = sb("cvar", [128, 128])
    crstd = sb("crstd", [128, 128]); cyn = sb("cyn", [128, 2, 128])
    sgt = sb("sgt", [128, 2, 128])
    yT = sb("yT", [128, 8, 128], BF16)
    Pb = [sb("Pb%d" % i, [128, 4, 128], BF16) for i in range(4)]
    selAt = sb("selAt", [128, 128]); selBt = sb("selBt", [128, 128])
    imp = sb("imp", [128, 128]); imp2 = sb("imp2", [128, 128]); impw = sb("impw", [128, 128])
    top8 = sb("top8", [128, 16]); selt1 = sb("selt1", [128, 128]); selt2 = sb("selt2", [128, 128])
    selbf = sb("selbf", [128, 128], BF16)
    den = sb("den", [128, 12]); rden = sb("rden", [128, 12]); gsc = sb("gsc", [128, 12])
    Ynsa = sb("Ynsa", [128, 256]); Ynb = sb("Ynb", [128, 256], BF16)
    yo = sb("yo", [128, D])
    oss = sb("oss", [128, 4])
    selm = sb("selm", [128, 32], BF16)
    cg1 = sb("cg1", [64, 64]); cg2 = sb("cg2", [64, 64])

    psS = [C.ps("psS0", [128, 512]), C.ps("psS1", [128, 512])]
    psC = C.ps("psC", [128, 512])
    psK = C.ps("psK", [128, 512])
    psSel = C.ps("psSel", [128, 512])
    psWin = C.ps("psWin", [128, 512])
    psG = C.ps("psG", [128, 512])
    psT = C.ps("psT", [128, 1024], BF16)

    C.dma(gpre_t[:], gpre[:, :], w=["gscale"])
    C.dma(gpost_bc[:], gpost.partition_broadcast(128), w=["gpost_bc"])
    C.dma(stage[0][:, 0:128], identd[:, :], w=["xt0"])
    C.cp("dve", ident[:], stage[0][:, 0:128], r=["xt0"], w=["ident"])
    C.memset("dve", ones_b[:], 1.0, w=["ones_b"])
    C.dma(stage[1][:, 0:32], selmd[:, :], w=["xt1"])
    C.cp("dve", selm[:], stage[1][:, 0:32], r=["xt1"], w=["selm"])
    par = 0
    for kc in range(8):
        s = stage[par % 2]; sk = "xt%d" % (par % 2); par += 1
        load_convert(C, wkv[:, kc, :], w_kv[kc * 128:(kc + 1) * 128, :], s, sk, "wkv", 512, scale_ap=gpre_t[:, kc:kc + 1])
        s = stage[par % 2]; sk = "xt%d" % (par % 2); par += 1
        load_convert(C, wloc[:, kc, 0:902], w_loc[kc * 128:(kc + 1) * 128, 0:902], s, sk, "wloc", 902, scale_ap=gpre_t[:, kc:kc + 1])
        s = stage[par % 2]; sk = "xt%d" % (par % 2); par += 1
        load_convert(C, wloc[:, kc, 902:NLOC], w_loc[kc * 128:(kc + 1) * 128, 902:NLOC], s, sk, "wloc", 902, scale_ap=gpre_t[:, kc:kc + 1])
        s = stage[par % 2]; sk = "xt%d" % (par % 2); par += 1
        load_convert(C, wo[:, kc, :], w_out[kc * 128:(kc + 1) * 128, :], s, sk, "wo", D, parity=kc)
    for q in range(8):
        s = stage[par % 2]; sk = "xt%d" % (par % 2); par += 1
        C.dma(s[64:128, 0:1024], indp[:, q * 1024:(q + 1) * 1024], w=[sk])
        C.cp("act" if q % 2 else "dve", Kaug[64:128, q * 1024:(q + 1) * 1024], s[64:128, 0:1024], r=[sk], w=["Kaug_ind"])
    C.memset("pool", Kw[64:128, :, :], 0.0, w=["Kw"])
    C.memset("pool", kcT[:, :], 0.0, w=["kcT"])
    C.memset("pool", gkT[:, :], 0.0, w=["gkT"])
    C.memset("pool", gvT[:, :], 0.0, w=["gvT"])
    for q_ in range(2):
        C.memset("pool", kvk[q_][:, :], 0.0, w=["kvk%d" % q_])
        C.memset("pool", kvv[q_][:, :], 0.0, w=["kvv%d" % q_])
    C.memset("pool", QC[:, :, :], 0.0, w=["QC"])
    C.memset("pool", QW[:, :, :], 0.0, w=["QW"])
    C.memset("pool", Vs[:, :, 64:65], 1.0, w=["Vs"])
    C.memset("pool", Vw[:, :, 64:65], 1.0, w=["Vw"])
    C.memset("pool", rhsC[:, :, :], 0.0, w=["rhsC"])
    C.memset("pool", rhsC[:, :, 192:193], 1.0, w=["rhsC"])
    for q in range(8):
        s = stage[par % 2]; sk = "xt%d" % (par % 2); par += 1
        C.dma(s[:, 0:128], wmaskd[q, :, :], w=[sk])
        C.cp("dve", wmask[:, q, :], s[:, 0:128], r=[sk], w=["wmask"])
    for q in range(4):
        s = stage[par % 2]; sk = "xt%d" % (par % 2); par += 1
        C.dma(s[:, 0:128], cmaskd[q, :, :], w=[sk])
        C.cp("dve", cmask[:, q, :], s[:, 0:128], r=[sk], w=["cmask"])
        if q == 0:
            s = stage[par % 2]; sk = "xt%d" % (par % 2); par += 1
            C.dma(s[:, 0:128], cmaskd[4, :, :], w=[sk])
            C.cp("dve", cmask[:, 4, :], s[:, 0:128], r=[sk], w=["cmask"])
        s = stage[par % 2]; sk = "xt%d" % (par % 2); par += 1
        C.dma(s[:, 0:128], impM[q, :, :], w=[sk])
        C.cp("dve", rhsC[:, q, 64:192], s[:, 0:128], r=[sk], w=["rhsC"])
    C.dma(convw_t[:], convw[:, :, :], w=["convw_t"])
    C.dma(convv_t[:], convv[:, :, :], w=["convv_t"])
    C.dma(stage[0][:, 0:128], identd[:, :], w=["xt0"])
    for cc in range(2):
        for k in range(31):
            if k % 2 == 0:
                C.ts("dve", Dg[:, cc, k, :], stage[0][:, 0:128], convw_t[:, cc, k:k + 1], None, ALU.mult,
                     r=["xt0", "convw_t"], w=["Dg"])
            else:
                C.act(Dg[:, cc, k, :], stage[0][:, 0:128], AF.Copy, r=["xt0", "convw_t"], w=["Dg"],
                      scale=convw_t[:, cc, k:k + 1])
    for wi_, (Wt_, wk_) in enumerate(((W1k, "W1k"), (W1v, "W1v"))):
        for q in range(2):
            C.dma(stage[q][0:64, 0:1024], w1d[64 * wi_:64 * wi_ + 64, q * 1024:(q + 1) * 1024], w=["xt%d" % q])
            C.cp("dve", Wt_[:].rearrange("p r e -> p (r e)")[:, q * 1024:(q + 1) * 1024], stage[q][0:64, 0:1024],
                 r=["xt%d" % q], w=[wk_])
    C.dma(stage[0][0:64, 0:128], w2d[:, :], w=["xt0"])
    C.cp("dve", W2[:], stage[0][0:64, 0:128], r=["xt0"], w=["W2"])
    C.dma(stage[0][0:64, 0:32], ped[0:64, :], w=["xt0"])
    C.cp("dve", peTk[:], stage[0][0:64, 0:32], r=["xt0"], w=["peTk"])
    C.dma(stage[1][0:64, 0:32], ped[64:128, :], w=["xt1"])
    C.cp("dve", peTv[:], stage[1][0:64, 0:32], r=["xt1"], w=["peTv"])
    for wi_, (Wt_, wk_, pt_, pk_) in enumerate(((W1k, "W1k", peTk, "peTk"), (W1v, "W1v", peTv, "peTv"))):
        for r_ in range(32):
            C.mm(psK[0:64, 0:1], Wt_[:, r_, :], pt_[:, r_:r_ + 1], start=(r_ == 0), stop=(r_ == 31),
                 r=[wk_, pk_], w=["psK"])
        C.cp("dve", cbias[:, wi_:wi_ + 1], psK[0:64, 0:1], w=["psK", "cbias"])
    C.dma(lng_bc[:], sgv[0:1, :].partition_broadcast(128), w=["lng_bc"])
    C.dma(lnb_bc[:], sgv[1:2, :].partition_broadcast(128), w=["lnb_bc"])
    C.dma(tril[:], trild[:, :], w=["tril"])
    C.dma(stage[0][:, 0:512], sgw.rearrange("p g t -> p (g t)"), w=["xt0"])
    for g in range(4):
        C.tt("dve", wsT[:, g, :], stage[0][:, g * 128:(g + 1) * 128], tril[:], ALU.mult, r=["xt0", "tril"], w=["wsT"])
        C.dma(Bs[(g % 2) * 64:(g % 2) * 64 + 64, g // 2, :], sgb[g:g + 1, :].partition_broadcast(64), w=["Bs"])
    C.dma(stage[1][:, 0:256], poolw.rearrange("p q d -> p (q d)"), w=["xt1"])
    C.cp("dve", Wbd[:].rearrange("p q d -> p (q d)"), stage[1][:, 0:256], r=["xt1"], w=["Wbd"])
    C.dma(poolsc_t[:], poolsc[:, :], w=["poolsc_t"])
    C.dma(rc0[:], rcnt0[:, :, :], w=["rc0"])
    C.dma(rcg[:], rcntg[:, :, :], w=["rcg"])

    def rms_to_bf(xt_ap, xk, out_bf, ok, np_, col):
        C.act(junk[0:np_, :], xt_ap, AF.Square, r=[xk], w=["junk", "ssq%d" % col], accum=ssq[0:np_, col:col + 1])
        C.act(rstd[0:np_, col:col + 1], ssq[0:np_, col:col + 1], AF.Ln, r=["ssq%d" % col, "epsb"], w=["rstd%d" % col],
              scale=1.0 / D, bias=epsb[0:np_, 0:1])
        C.act(rstd[0:np_, col:col + 1], rstd[0:np_, col:col + 1], AF.Exp, r=["rstd%d" % col], w=["rstd%d" % col], scale=-0.5)
        C.act(out_bf, xt_ap, AF.Copy, r=[xk, "rstd%d" % col], w=[ok], scale=rstd[0:np_, col:col + 1])

    epsb = sb("epsb", [128, 2])
    C.memset("dve", epsb[:, 0:1], RMS_EPS, w=["epsb"])
    C.memset("dve", epsb[:, 1:2], LN_EPS, w=["epsb"])

    def kv_tile(m):
        p = m % 2
        xk, hk, tk = "xt%d" % p, "hnb%d" % p, "hT%d" % p
        C.dma(xt[p][:], xg_tile(m)[:, :], w=[xk])
        C.dma(rC[p][:], ropeC[:, m * 128:(m + 1) * 128], w=["rC%d" % p])
        C.dma(rS[p][:], ropeS[:, m * 128:(m + 1) * 128], w=["rS%d" % p])
        yield
        C.act(junk[:, :], xt[p][:], AF.Square, r=[xk], w=["junk", "ssq%d" % p], accum=ssq[:, p:p + 1])
        yield
        C.act(rstd[:, p:p + 1], ssq[:, p:p + 1], AF.Ln, r=["ssq%d" % p, "epsb"], w=["rstd%d" % p],
              scale=1.0 / D, bias=epsb[:, 0:1])
        yield
        C.act(rstd[:, p:p + 1], rstd[:, p:p + 1], AF.Exp, r=["rstd%d" % p], w=["rstd%d" % p], scale=-0.5)
        yield
        C.ts("dve", hnb[p][:], xt[p][:], rstd[:, p:p + 1], None, ALU.mult, r=[xk, "rstd%d" % p], w=[hk])
        yield
        for half in range(2):
            for q in range(4):
                kc = 4 * half + q
                C.tr(psT[:, q * 128:(q + 1) * 128], hnb[p][:, kc * 128:(kc + 1) * 128], ident[:], r=[hk, "ident"], w=["psT"])
            yield
            C.cp("dve", hT[p][:, 4 * half:4 * half + 4, :], psT[:, 0:512].rearrange("p (k t) -> p k t", k=4), w=["psT", tk])
            yield
        for f in range(3):
            for kc in range(8):
                C.mm(psK[:, f * 128:(f + 1) * 128], wkv[:, kc, f * 128:(f + 1) * 128], hT[p][:, kc, :],
                     start=(kc == 0), stop=(kc == 7), r=["wkv", tk], w=["psK"])
            yield
        for kc in range(8):
            C.mm(psK[:, 384:512], hT[p][:, kc, :], wkv[:, kc, 384:512], start=(kc == 0), stop=(kc == 7),
                 r=["wkv", tk], w=["psK"])
        yield
        grp = (m // 4) % 2
        col0 = 16 + (m % 4) * 128
        C.cp("act", kvk[grp][:, col0:col0 + 128], psK[0:64, 0:128], w=["psK", "kvk%d" % grp])
        C.cp("act", kvv[grp][:, col0:col0 + 128], psK[64:128, 0:128], w=["psK", "kvv%d" % grp])
        C.tt("dve", ropet[:, 0, :], psK[:, 128:256], rC[p][:], ALU.mult, r=["rC%d" % p], w=["psK", "ropet0"])
        C.tt("dve", ropet[:, 1, :], psK[:, 256:384], rS[p][:], ALU.mult, r=["rS%d" % p], w=["psK", "ropet1"])
        yield
        C.cp("dve", Vs[:, m, 0:64], psK[:, 384:448], w=["psK", "Vs%d" % m])
        C.cp("dve", Vw[:, m % 12, 0:64], psK[:, 448:512], w=["psK", "Vw%d" % (m % 12)])
        yield
        C.tt("pool", Kaug[0:64, m * 128:(m + 1) * 128], ropet[0:64, 0, :], ropet[0:64, 1, :], ALU.add,
             r=["ropet0", "ropet1"], w=["Kaug%d" % m])
        C.tt("pool", Kw[0:64, m % 12, :], ropet[64:128, 0, :], ropet[64:128, 1, :], ALU.add,
             r=["ropet0", "ropet1"], w=["Kw%d" % (m % 12)])
        yield

    def compress(j):
        g = j % 2
        n0 = 32 * j - 1
        c0 = 32 + n0
        for which, (Wt_, wk_, kb, kk, gT, gk_) in enumerate(((W1k, "W1k", kvk[g], "kvk%d" % g, gkT, "gkT"),
                                                           (W1v, "W1v", kvv[g], "kvv%d" % g, gvT, "gvT"))):
            for r_ in range(32):
                C.mm(psK[0:64, which * 32:(which + 1) * 32], Wt_[:, r_, :], kb[:, r_:r_ + 16 * 31 + 1:16],
                     start=(r_ == 0), stop=(r_ == 31), r=[wk_, kk], w=["psK"])
            yield
            cgt = cg1[:, which * 32:(which + 1) * 32]
            cgs = cg2[:, which * 32:(which + 1) * 32]
            k1, k2 = ["cg1_%d" % which], ["cg2_%d" % which]
            C.ts("dve", cgt, psK[0:64, which * 32:(which + 1) * 32], cbias[:, which:which + 1], None, ALU.add,
                 r=["cbias"], w=["psK"] + k1)
            C.tt("dve", cgs, cgt, cgt, ALU.mult, r=k1, w=k2)
            C.ts("dve", cgs, cgs, 0.044715, 1.0, ALU.mult, ALU.add, r=k2, w=k2)
            C.tt("dve", cgs, cgs, cgt, ALU.mult, r=k2 + k1, w=k2)
            C.act(cgs, cgs, AF.Exp, r=k2, w=k2, scale=-GELU_C)
            C.ts("dve", cgs, cgs, 1.0, None, ALU.add, r=k2, w=k2)
            C.recip(cgs, cgs, r=k2, w=k2)
            C.tt("dve", gT[:, c0:c0 + 32], cgt, cgs, ALU.mult, r=k1 + k2, w=[gk_])
            other = (kvk, kvv)[which][1 - g]
            C.cp("pool", other[:, 0:16], kb[:, 512:528], r=[kk], w=[("kvk%d", "kvv%d")[which] % (1 - g)])
            yield
        C.mm(psK[0:64, 64:96], W2[:, 0:64], gkT[:, c0:c0 + 32], r=["W2", "gkT"], w=["psK"])
        yield
        C.cp("dve", kcT[0:64, c0:c0 + 32], psK[0:64, 64:96], w=["psK", "kcT"])
        yield
        chunks = sorted(set([max(n0, 0) // 128, (n0 + 31) // 128]))
        for ci in chunks:
            C.mm(psK[:, 128:192], gvT[:, 32 + ci * 128:32 + (ci + 1) * 128], W2[:, 64:128], r=["W2", "gvT"], w=["psK"])
            yield
            C.cp("dve", rhsC[:, ci, 0:64], psK[:, 128:192], w=["psK", "rhsC"])
            yield

    def local_pre(j):
        for cand in range(4):
            m_ = 4 * j - 1 + cand
            if m_ < 0:
                C.memset("pool", yo[0:32, :], 0.0, w=["yo"])
            else:
                C.dma(yo[cand * 32:(cand + 1) * 32, :], xg_tile(m_)[96:128, :], w=["yo"])
        C.dma(qCt[:], qC[j, :, :], w=["qCt"])
        C.dma(qSt[:], qS[j, :, :], w=["qSt"])
        yield
        rms_to_bf(yo[:], "yo", hnc[:], "hnc", 128, 3)
        yield
        for kc in range(8):
            C.mm(psG[:, kc * 32:(kc + 1) * 32], hnc[:, kc * 128:(kc + 1) * 128], selm[:], r=["hnc", "selm"], w=["psG"])
        yield
        C.cp("act", hTo[:, :, 0:32], psG[:, 0:256].rearrange("p (k t) -> p k t", k=8), w=["psG", "hToH"])
        yield

    def local(j):
        yield
        C.dma(xo[:], xown_tile(j)[:, :], w=["xo"])
        C.dma(selAt[:], selA[j, :, :], w=["selAt"])
        C.dma(selBt[:], selB[j, :, :], w=["selBt"])
        rms_to_bf(xo[:], "xo", hno[:], "hno", 128, 2)
        yield
        for half in range(2):
            for q in range(4):
                kc = 4 * half + q
                C.tr(psT[:, 512 + q * 128:512 + (q + 1) * 128], hno[:, kc * 128:(kc + 1) * 128], ident[:],
                     r=["hno", "ident"], w=["psT"])
            yield
            C.cp("act", hTo[:, 4 * half:4 * half + 4, 32:160], psT[:, 512:1024].rearrange("p (k t) -> p k t", k=4),
                 w=["psT", "hTo"])
            yield

        def proj(chunk, ncol, out_ap):
            c0 = 160 - ncol
            for kc in range(8):
                C.mm(out_ap, wloc[:, kc, chunk * 128:(chunk + 1) * 128], hTo[:, kc, c0:160],
                     start=(kc == 0), stop=(kc == 7), r=["wloc", "hTo", "hToH"], w=["psG"])

        yield
        for cc in range(2):
            proj(2 + cc, 160, psG[:, 0:160])
            C.act(sig[:, cc, :], psG[:, 0:160], AF.Sigmoid, w=["psG", "sig"])
        for cc in range(2):
            proj(cc, 160, psG[:, 0:160])
            C.tt("dve", ub[:, cc, :], psG[:, 0:160], sig[:, cc, :], ALU.mult, r=["sig"], w=["psG", "ub"])
        yield
        for cc in range(2):
            proj(4 + cc, 160, psG[:, 0:160])
            C.cp("act", zdT[:, cc, :], psG[:, 0:160], w=["psG", "zdT"])
        yield
        for cc in range(2):
            proj(6 + cc, 128, psG[:, 0:128])
            C.act(usg[:, cc, :], psG[:, 0:128], AF.Gelu_apprx_tanh, w=["psG", "usg"])
        yield
        for hp in range(2):
            for kc in range(8):
                C.mm(psG[:, 0:128], wloc[:, kc, 1024 + hp * 128:1024 + (hp + 1) * 128], hTo[:, kc, 32:160],
                     start=(kc == 0), stop=(kc == 7), r=["wloc", "hTo", "hToH"], w=["psG"])
            for kc in range(8):
                C.mm(psG[:, 128:256], wloc[:, kc, 1280 + hp * 128:1280 + (hp + 1) * 128], hTo[:, kc, 32:160],
                     start=(kc == 0), stop=(kc == 7), r=["wloc", "hTo", "hToH"], w=["psG"])
            C.tt("dve", qt1[:], psG[:, 0:128], qCt[:], ALU.mult, r=["qCt"], w=["psG", "qt1"])
            C.tt("dve", qt2[:], psG[:, 128:256], qSt[:], ALU.mult, r=["qSt"], w=["psG", "qt2"])
            C.act(QC[0:64, 2 * hp, :], psG[0:64, 0:128], AF.Copy, w=["psG", "QC"], scale=0.125)
            C.act(QC[0:64, 2 * hp + 1, :], psG[64:128, 0:128], AF.Copy, w=["psG", "QC"], scale=0.125)
            yield
            C.tt("dve", qsum[:], qt1[:], qt2[:], ALU.add, r=["qt1", "qt2"], w=["qsum"])
            yield
            for v in range(2):
                C.cp("dve", QR[0:64, v, 2 * hp, :], qsum[0:64, :], r=["qsum"], w=["QRq"])
                C.cp("act", QR[0:64, v, 2 * hp + 1, :], qsum[64:128, :], r=["qsum"], w=["QRq"])
            C.cp("pool", QW[0:64, 2 * hp, :], qsum[0:64, :], r=["qsum"], w=["QW"])
            C.cp("act", QW[0:64, 2 * hp + 1, :], qsum[64:128, :], r=["qsum"], w=["QW"])
        yield
        for kc in range(8):
            C.mm(psG[:, 0:268], hTo[:, kc, 32:160], wloc[:, kc, 1536:1804], start=(kc == 0), stop=(kc == 7),
                 r=["wloc", "hTo", "hToH"], w=["psG"])
        C.act(gts[:], psG[:, 256:268], AF.Sigmoid, w=["psG", "gts"])
        C.act(vg[:], psG[:, 0:256], AF.Gelu_apprx_tanh, w=["psG", "vg"])
        C.S.op("dve", lambda e: e.bn_stats(out=bnst[:], in_=vg[:]), ["vg"], ["bnst"])
        C.S.op("dve", lambda e: e.bn_aggr(out=bnag[:], in_=bnst[:]), ["bnst"], ["bnag"])
        C.act(lnr[:, 0:1], bnag[:, 1:2], AF.Ln, r=["bnag", "epsb"], w=["lnr"], bias=epsb[:, 1:2])
        C.act(lnr[:, 1:2], lnr[:, 0:1], AF.Exp, r=["lnr"], w=["lnr1"], scale=-0.5)
        C.ts("dve", vn[:], vg[:], bnag[:, 0:1], lnr[:, 1:2], ALU.subtract, ALU.mult, r=["vg", "bnag", "lnr1"], w=["vn"])
        C.tt("dve", vn[:], vn[:], lng_bc[:], ALU.mult, r=["vn", "lng_bc"], w=["vn"])
        C.tt("dve", vbf[:], vn[:], lnb_bc[:], ALU.add, r=["vn", "lnb_bc"], w=["vbf"])
        yield
        for cc in range(2):
            for k in range(31):
                C.mm(psG[:, 0:128], Dg[:, cc, k, :], ub[:, cc, 2 + k:2 + k + 128], start=(k == 0), stop=(k == 30),
                     r=["Dg", "ub"], w=["psG"])
            C.act(ycv[:, cc, :], psG[:, 0:128], AF.Identity, r=["convv_t"], w=["psG", "ycv"], bias=convv_t[:, cc, 0:1])
            C.act(ysq[:, cc, :], psG[:, 0:128], AF.Square, r=["convv_t"], w=["psG", "ysq"], bias=convv_t[:, cc, 0:1])
        for src, sk_, c0_ in ((ycv, "ycv", 0), (ysq, "ysq", 128)):
            C.cp("dve", shi[:], src[:], r=[sk_], w=["shi"])
            C.tt("dve", slo[:], src[:], shi[:], ALU.subtract, r=[sk_, "shi"], w=["slo"])
            n_ = 0
            for part, pk_ in ((shi, "shi"), (slo, "slo")):
                for cc in range(2):
                    C.mm(psG[:, c0_:c0_ + 128], ones_b[:], part[:, cc, :], start=(n_ == 0), stop=(n_ == 3),
                         r=["ones_b", pk_], w=["psG"])
                    n_ += 1
        C.act(cmean[:], psG[:, 0:128], AF.Copy, w=["psG", "cmean"], scale=1.0 / 256)
        C.tt("dve", cmsq[:], cmean[:], cmean[:], ALU.mult, r=["cmean"], w=["cmsq"])
        C.stt(cvar[:], psG[:, 128:256], 1.0 / 256, cmsq[:], ALU.mult, ALU.subtract, r=["cmsq"], w=["psG", "cvar"])
        C.act(crstd[:], cvar[:], AF.Ln, r=["cvar", "epsb"], w=["crstd"], bias=epsb[:, 1:2])
        C.act(crstd[:], crstd[:], AF.Exp, r=["crstd"], w=["crstd"], scale=-0.5)
        for cc in range(2):
            C.tt("dve", cyn[:, cc, :], ycv[:, cc, :], cmean[:], ALU.subtract, r=["ycv", "cmean"], w=["cyn%d" % cc])
            C.tt("dve", cyn[:, cc, :], cyn[:, cc, :], crstd[:], ALU.mult, r=["cyn%d" % cc, "crstd"], w=["cyn%d" % cc])
            C.act(yT[:, cc, :], cyn[:, cc, :], AF.Silu, r=["cyn%d" % cc, "convv_t"], w=["yT%d" % cc],
                  scale=convv_t[:, cc, 1:2], bias=convv_t[:, cc, 2:3])
        yield
        for g in range(4):
            q = g // 2
            C.mm(psG[:, g * 128:(g + 1) * 128], vbf[:, q * 128:(q + 1) * 128], wsT[:, g, :], r=["vbf", "wsT"], w=["psG"])
        for g in range(4):
            q = g // 2
            lo = (g % 2) * 64
            C.tt("dve", sgt[lo:lo + 64, q, :], psG[lo:lo + 64, g * 128:(g + 1) * 128], Bs[lo:lo + 64, q, :], ALU.add,
                 r=["Bs"], w=["psG", "sgt%d" % g])
            C.tt("pool", yT[lo:lo + 64, 4 + q, :], sgt[lo:lo + 64, q, :], usg[lo:lo + 64, q, :], ALU.mult,
                 r=["sgt%d" % g, "usg"], w=["yT%d" % (4 + q)])
        yield
        C.tt("pool", s2[:, :, 1:160], zdT[:, :, 1:160], zdT[:, :, 0:159], ALU.add, r=["zdT"], w=["s2"])
        C.tt("pool", s4[:, :, 3:160], s2[:, :, 3:160], s2[:, :, 1:158], ALU.add, r=["s2"], w=["s4"])
        C.tt("pool", s8[:, :, 7:160], s4[:, :, 7:160], s4[:, :, 3:156], ALU.add, r=["s4"], w=["s8"])
        C.tt("pool", s16[:, :, 15:160], s8[:, :, 15:160], s8[:, :, 7:152], ALU.add, r=["s8"], w=["s16"])
        rc = rc0 if j == 0 else rcg
        srcs = [(s2, "s2", 0, 0), (s4, "s4", 64, 0), (s8, "s8", 0, 1), (s16, "s16", 64, 1)]
        for gi, (sbuf_, sk, lo, q) in enumerate(srcs):
            C.tt("dve", plf[lo:lo + 64, q, :], sbuf_[lo:lo + 64, q, 32:160], rc[lo:lo + 64, q, :], ALU.mult,
                 r=[sk, "rc0", "rcg"], w=["plf%d" % gi])
            C.tt("dve", plb[lo:lo + 64, q, :], plf[lo:lo + 64, q, :], zdT[lo:lo + 64, q, 32:160], ALU.subtract,
                 r=["plf%d" % gi, "zdT"], w=["plb%d" % q])
        for q in range(2):
            C.mm(psG[:, q * 128:(q + 1) * 128], Wbd[:, q, :], plb[:, q, :], r=["Wbd", "plb%d" % q], w=["psG"])
        for q in range(2):
            C.act(yT[:, 6 + q, :], psG[:, q * 128:(q + 1) * 128], AF.Copy, r=["poolsc_t"], w=["psG", "yT%d" % (6 + q)],
                  scale=poolsc_t[:, q:q + 1])

    pcount = [0]

    def nextP():
        i = pcount[0] % 4
        pcount[0] += 1
        return Pb[i], "Pb%d" % i

    scount = [0]

    def nextS(nb=2):
        i = scount[0] % nb
        scount[0] += 1
        if i == 2:
            return psC, "psC"
        return psS[i], "psS%d" % i

    def nsa(j):
        L = (32 * j + 30) // 128

        def pipeline(chunks, depth=1):
            state = []
            for i, (A, B, Cc) in enumerate(chunks):
                while len(state) < min(len(chunks), i + depth + 1):
                    state.append(chunks[len(state)][0]())
                st = state[i]
                B(st)
                yield
                Cc(st)
                yield

        st_p = {}

        def mk_cmp(ci):
            def A():
                S_, sk = nextS()
                C.mm(S_[:, :], kcT[:, 32 + ci * 128:32 + (ci + 1) * 128], QC[:].rearrange("p h t -> p (h t)"),
                     r=["kcT", "QC"], w=[sk])
                return S_, sk

            def B(st):
                S_, sk = st
                P_, pk = Pb[ci], "Pb%d" % ci
                C.act(P_[:].rearrange("p h t -> p (h t)"), S_[:, :], AF.Exp, w=[sk, pk])
                if ci == L:
                    C.tt("dve", P_[:], P_[:], cmask[:, j % 4:j % 4 + 1, :].to_broadcast([128, 4, 128]), ALU.mult,
                         r=[pk, "cmask"], w=[pk])
                elif ci == L - 1 and j % 4 == 0:
                    C.tt("dve", P_[:], P_[:], cmask[:, 4:5, :].to_broadcast([128, 4, 128]), ALU.mult,
                         r=[pk, "cmask"], w=[pk])

            def Cc(st):
                pass
            return A, B, Cc

        yield from pipeline([mk_cmp(ci) for ci in range(L + 1)])
        for hp in range(2):
            for ci in range(L + 1):
                for h in (2 * hp, 2 * hp + 1):
                    c0 = (h % 2) * 193
                    C.mm(psC[:, c0:c0 + 193], Pb[ci][:, h, :], rhsC[:, ci, :], start=(ci == 0 and h % 2 == 0), stop=(ci == L),
                         r=["Pb%d" % ci, "rhsC"], w=["psC"], skip=True)
            yield
            for h in (2 * hp, 2 * hp + 1):
                c0 = (h % 2) * 193
                C.ts("dve", den[:, h:h + 1], psC[:, c0 + 192:c0 + 193], 1e-30, None, ALU.max, w=["psC", "den%d" % hp])
            C.recip(rden[:, 2 * hp:2 * hp + 2], den[:, 2 * hp:2 * hp + 2], r=["den%d" % hp], w=["rden%d" % hp])
            yield
            for h in (2 * hp, 2 * hp + 1):
                c0 = (h % 2) * 193
                if h == 0:
                    C.ts("dve", imp[:], psC[:, c0 + 64:c0 + 192], rden[:, 0:1], None, ALU.mult, r=["rden0"],
                         w=["psC", "imp"])
                else:
                    C.stt(imp[:], psC[:, c0 + 64:c0 + 192], rden[:, h:h + 1], imp[:], ALU.mult, ALU.add,
                          r=["rden%d" % hp, "imp"], w=["psC", "imp"])
            C.tt("dve", gsc[:, 2 * hp:2 * hp + 2], rden[:, 2 * hp:2 * hp + 2], gts[:, 6 * hp:6 * hp + 6:3], ALU.mult,
                 r=["rden%d" % hp, "gts"], w=["gscc%d" % hp])
            for h in (2 * hp, 2 * hp + 1):
                c0 = (h % 2) * 193
                C.ts("dve", Ynsa[:, h * 64:(h + 1) * 64], psC[:, c0:c0 + 64], gsc[:, h:h + 1], None, ALU.mult,
                     r=["gscc%d" % hp], w=["psC", "Ynsa"])
            yield
        C.tt("dve", imp2[:], imp[:], selAt[:], ALU.mult, r=["imp", "selAt"], w=["imp2"])
        C.tt("dve", imp2[:], imp2[:], selBt[:], ALU.add, r=["imp2", "selBt"], w=["imp2"])
        yield
        C.S.op("dve", lambda e: e.max(out=top8[:, 0:8], in_=imp2[:]), ["imp2"], ["top8a"])
        C.S.op("dve", lambda e: e.match_replace(out=impw[:], in_to_replace=top8[:, 0:8], in_values=imp2[:],
                                                imm_value=-3e38), ["imp2", "top8a"], ["impw"])
        yield
        C.S.op("dve", lambda e: e.max(out=top8[:, 8:16], in_=impw[:]), ["impw"], ["top8b"])
        C.ts("dve", selt1[:], imp2[:], top8[:, 15:16], None, ALU.is_ge, r=["imp2", "top8b"], w=["selt1"])
        C.ts("dve", selt2[:], imp2[:], -1e29, -NEG, ALU.is_gt, ALU.mult, r=["imp2"], w=["selt2"])
        yield
        C.tt("dve", selt1[:], selt1[:], selt2[:], ALU.mult, r=["selt1", "selt2"], w=["selt1"])
        C.ts("dve", selbf[:], selt1[:], NEG, None, ALU.add, r=["selt1"], w=["selbf"])
        yield

        if j + 1 < nslots:
            yield from local_pre(j + 1)
        wl = list(range(max(0, 4 * j - 4), 4 * j + 4))

        def mk_win(wi, m):
            def A():
                S_, sk = nextS(3)
                C.mm(S_[:, :], Kw[:, m % 12, :], QW[:].rearrange("p h t -> p (h t)"),
                     r=["Kw%d" % (m % 12), "Kw", "QW"], w=[sk])
                return S_, sk

            def B(st):
                S_, sk = st
                P_, pk = nextP()
                C.act(P_[:].rearrange("p h t -> p (h t)"), S_[:, :], AF.Exp, w=[sk, pk])
                C.tt("dve", P_[:], P_[:], wmask[:, 4 + m - 4 * j:5 + m - 4 * j, :].to_broadcast([128, 4, 128]), ALU.mult,
                     r=[pk, "wmask"], w=[pk])
                st_p[("w", wi)] = (P_, pk)

            def Cc(st):
                P_, pk = st_p[("w", wi)]
                for h in range(4):
                    C.mm(psWin[:, h * 65:(h + 1) * 65], P_[:, h, :], Vw[:, m % 12, :], start=(wi == 0 and h == 0),
                         stop=(wi == len(wl) - 1), r=[pk, "Vw%d" % (m % 12), "Vw"], w=["psWin"], skip=True)
            return A, B, Cc

        scount[0] = 0
        yield from pipeline([mk_win(wi, m) for wi, m in enumerate(wl)], depth=2)
        C.tr(psT[:, 512:640], selbf[:], ident[:], r=["selbf", "ident"], w=["psT"])
        yield
        C.cp("dve", QR[64:128, 0, :, :], psT[0:64, 512:640].unsqueeze(1).to_broadcast([64, 4, 128]), w=["psT", "QRb"])
        C.cp("act", QR[64:128, 1, :, :], psT[64:128, 512:640].unsqueeze(1).to_broadcast([64, 4, 128]), w=["psT", "QRb"])
        yield

        nsel = 4 * j + 4

        def mk_sel(m):
            def A():
                S_, sk = nextS(3)
                v = 0 if m < 32 else 1
                C.mm(S_[:, :], Kaug[:, m * 128:(m + 1) * 128], QR[:, v, :, :].rearrange("p h t -> p (h t)"),
                     r=["Kaug%d" % m, "Kaug_ind", "QRq", "QRb"], w=[sk])
                return S_, sk

            def B(st):
                S_, sk = st
                P_, pk = nextP()
                C.act(P_[:].rearrange("p h t -> p (h t)"), S_[:, :], AF.Exp, w=[sk, pk])
                if m >= 4 * j:
                    C.tt("dve", P_[:], P_[:], wmask[:, 4 + m - 4 * j:5 + m - 4 * j, :].to_broadcast([128, 4, 128]), ALU.mult,
                         r=[pk, "wmask"], w=[pk])
                st_p[("s", m)] = (P_, pk)

            def Cc(st):
                P_, pk = st_p[("s", m)]
                for h in range(4):
                    C.mm(psSel[:, h * 65:(h + 1) * 65], P_[:, h, :], Vs[:, m, :], start=(m == 0 and h == 0),
                         stop=(m == nsel - 1), r=[pk, "Vs%d" % m, "Vs"], w=["psSel"], skip=True)
            return A, B, Cc

        yield from pipeline([mk_sel(m) for m in range(nsel)], depth=2)
        scount[0] = 0
        C.cp("dve", den[:, 4:8], psSel[:, 64:260:65], w=["psSel", "den"])
        C.cp("dve", den[:, 8:12], psWin[:, 64:260:65], w=["psWin", "den"])
        C.recip(rden[:, 4:12], den[:, 4:12], r=["den"], w=["rden2"])
        C.tt("dve", gsc[:, 4:8], rden[:, 4:8], gts[:, 1:12:3], ALU.mult, r=["rden2", "gts"], w=["gsc1"])
        C.tt("dve", gsc[:, 8:12], rden[:, 8:12], gts[:, 2:12:3], ALU.mult, r=["rden2", "gts"], w=["gsc1"])
        for h in range(4):
            C.stt(Ynsa[:, h * 64:(h + 1) * 64], psSel[:, h * 65:h * 65 + 64], gsc[:, 4 + h:5 + h], Ynsa[:, h * 64:(h + 1) * 64],
                  ALU.mult, ALU.add, r=["gsc1", "Ynsa"], w=["psSel", "Ynsa"])
        for h in range(4):
            C.stt(Ynb[:, h * 64:(h + 1) * 64], psWin[:, h * 65:h * 65 + 64], gsc[:, 8 + h:9 + h], Ynsa[:, h * 64:(h + 1) * 64],
                  ALU.mult, ALU.add, r=["gsc1", "Ynsa"], w=["psWin", "Ynb"])
        yield
        for q in range(2):
            C.tr(psT[:, 512 + q * 128:512 + (q + 1) * 128], Ynb[:, q * 128:(q + 1) * 128], ident[:], r=["Ynb", "ident"], w=["psT"])
        yield
        C.cp("act", yT[:, 2:4, :], psT[:, 512:768].rearrange("p (q t) -> p q t", q=2), w=["psT", "yT2", "yT3"])
        yield

    def outproj(j):
        yk = ["yT%d" % f for f in range(8)]
        for half in range(2):
            for f in range(8):
                C.mm(psG[:, :], yT[:, f, :], wo[:, f, half * 512:(half + 1) * 512], start=(f == 0), stop=(f == 7),
                     r=yk + ["wo"], w=["psG"])
            yield
            C.cp("dve", yo[:, half * 512:(half + 1) * 512], psG[:, :], w=["psG", "yo"])
            yield
        C.act(junk[:, :], yo[:], AF.Square, r=["yo"], w=["junk", "oss"], accum=oss[:, 0:1])
        yield
        C.act(oss[:, 1:2], oss[:, 0:1], AF.Ln, r=["oss", "epsb"], w=["oss1"], scale=1.0 / D, bias=epsb[:, 0:1])
        C.act(oss[:, 2:3], oss[:, 1:2], AF.Exp, r=["oss1"], w=["oss2"], scale=-0.5)
        C.stt(yo[:], yo[:], oss[:, 2:3], gpost_bc[:], ALU.mult, ALU.mult, r=["yo", "oss2", "gpost_bc"], w=["yo"])
        C.tt("dve", yo[:], yo[:], xo[:], ALU.add, r=["yo", "xo"], w=["yo"])
        C.dma(xm_own[j, :, :], yo[:], r=["yo"])
        C.dma(xm_tail[2 * j:2 * j + 2, :], yo[126:128, :], r=["yo"])
        yield

    def stream_b(j):
        for m in range(4 * j, 4 * j + 4):
            yield from kv_tile(m)
        yield from compress(j)

    def stream_a(j):
        yield from local(j)
        yield from nsa(j)
        yield from outproj(j)

    C.S.wait_cc()
    for _ in stream_b(0):
        pass
    for _ in local_pre(0):
        pass
    for j in range(nslots):
        A_ = stream_a(j)
        B_ = stream_b(j + 1) if j + 1 < nslots else None
        doneA = doneB = B_ is None and False
        doneB = B_ is None
        while not doneA:
            try:
                next(A_)
            except StopIteration:
                doneA = True
            if not doneB:
                try:
                    next(B_)
                except StopIteration:
                    doneB = True
        while not doneB:
            try:
                next(B_)
            except StopIteration:
                doneB = True
    C.end_phase()


def ffn_phase(C, T, l):
    C.begin_phase("_f%d" % l)
    nc = C.nc
    xm_own = T["xm_own"]; tail_g = T["tail_g_%d" % l]; out_tile = T["ffn_out_tile"]
    w_up, w_dn, g2, gpost, cwd = (T[k + "_%d" % l] for k in ("w_up", "w_dn", "g2", "gpost2", "cwd"))
    identd = T["identd"]; selm2d = T["selm2"]

    sb = C.sb
    wu = sb("wu", [128, 8, 2 * FF], BF16)
    wd = sb("wd", [128, 22, D], BF16)
    stage = [sb("stage%d" % i, [128, 1024]) for i in range(4)]
    g2_t = sb("g2_t", [128, 8])
    gpost_bc = sb("gpost_bc", [128, D])
    cw = sb("cw", [128, NFC, 4])
    ident = sb("ident", [128, 128], BF16)
    epsb = sb("epsb", [128, 1])
    xs = [sb("xs%d" % i, [128, D]) for i in range(2)]
    xh = sb("xh", [8, D])
    selm2 = sb("selm2", [8, 2], BF16)
    hn = sb("hn", [128, D], BF16)
    hnh = sb("hnh", [8, D], BF16)
    junk = sb("junk", [128, D], BF16)
    ssq = sb("ssq", [128, 4]); rstd = sb("rstd", [128, 4])
    h2T = sb("h2T", [128, 8, 2, 130], BF16)
    actT = sb("actT", [128, 22, 2, 128], BF16)
    gc = [sb("gc%d" % i, [128, 2, 128]) for i in range(2)]
    uc = [sb("uc%d" % i, [128, 2, 128]) for i in range(2)]
    gg = [sb("gg%d" % i, [128, 2, 128]) for i in range(2)]
    yo = sb("yo", [128, D])
    oss = sb("oss", [128, 4])

    psU = [C.ps("psU%d" % i, [128, 512]) for i in range(4)]
    psD = [C.ps("psD%d" % i, [128, 512]) for i in range(2)]
    psT = C.ps("psT", [128, 1024], BF16)
    psH = C.ps("psH", [128, 512])

    C.dma(stage[1][0:8, 0:2], selm2d[:, :], w=["stage1"])
    C.cp("dve", selm2[:], stage[1][0:8, 0:2], r=["stage1"], w=["selm2"])
    C.dma(g2_t[:], g2[:, :], w=["gscale"])
    C.dma(gpost_bc[:], gpost.partition_broadcast(128), w=["gpost_bc"])
    C.dma(cw[:], cwd[:, :, :], w=["cw"])
    C.dma(stage[0][:, 0:128], identd[:, :], w=["stage0"])
    C.cp("dve", ident[:], stage[0][:, 0:128], r=["stage0"], w=["ident"])
    C.memset("dve", epsb[:, 0:1], RMS_EPS, w=["epsb"])
    par = 0
    for kc in range(8):
        for q in range(6):
            c0 = q * 1024
            c1 = min(2 * FF, c0 + 1024)
            s = stage[par % 4]; sk = "stage%d" % (par % 4); par += 1
            load_convert(C, wu[:, kc, c0:c1], w_up[kc * 128:(kc + 1) * 128, c0:c1], s, sk, "wu", c1 - c0,
                         scale_ap=g2_t[:, kc:kc + 1])
    for f in range(22):
        s = stage[par % 4]; sk = "stage%d" % (par % 4); par += 1
        load_convert(C, wd[:, f, :], w_dn[f * 128:(f + 1) * 128, :], s, sk, "wd", D, parity=f)

    def rms_to_bf(x_ap, xk, out_bf, ok, np_, col):
        C.act(junk[0:np_, :], x_ap, AF.Square, r=[xk], w=["junk", "ssq%d" % col], accum=ssq[0:np_, col:col + 1])
        C.act(rstd[0:np_, col:col + 1], ssq[0:np_, col:col + 1], AF.Ln, r=["ssq%d" % col, "epsb"], w=["rstd%d" % col],
              scale=1.0 / D, bias=epsb[0:np_, 0:1])
        C.act(rstd[0:np_, col:col + 1], rstd[0:np_, col:col + 1], AF.Exp, r=["rstd%d" % col], w=["rstd%d" % col], scale=-0.5)
        C.act(out_bf, x_ap, AF.Copy, r=[xk, "rstd%d" % col], w=[ok], scale=rstd[0:np_, col:col + 1])

    C.S.wait_cc()
    ucount = [0]
    for grp in range(NS // 2):
        for sl in range(2):
            j = grp * 2 + sl
            C.dma(xs[sl][:], xm_own[j, :, :], w=["xs%d" % sl])
            for cand in range(4):
                m_ = 4 * j - 1 + cand
                if m_ < 0:
                    C.memset("pool", xh[0:2, :], 0.0, w=["xh"])
                else:
                    row0 = ((m_ % 4) * NS + m_ // 4) * 2
                    C.dma(xh[cand * 2:(cand + 1) * 2, :], tail_g[row0:row0 + 2, :], w=["xh"])
            rms_to_bf(xs[sl][:], "xs%d" % sl, hn[:], "hn", 128, 0)
            rms_to_bf(xh[:], "xh", hnh[:], "hnh", 8, 1)
            for kc in range(8):
                C.tr(psT[:, kc * 128:(kc + 1) * 128], hn[:, kc * 128:(kc + 1) * 128], ident[:], r=["hn", "ident"], w=["psT"])
            C.cp("act", h2T[:, :, sl, 2:130], psT[:, :].rearrange("p (k t) -> p k t", k=8), w=["psT", "h2T"])
            for kc in range(8):
                C.mm(psH[:, kc * 2:(kc + 1) * 2], hnh[:, kc * 128:(kc + 1) * 128], selm2[:], r=["hnh", "selm2"], w=["psH"])
            C.cp("act", h2T[:, :, sl, 0:2], psH[:, 0:16].rearrange("p (k t) -> p k t", k=8), w=["psH", "h2T"])
        def up_mm(fc):
            banks = []
            for ch in (fc, 22 + fc):
                bi = ucount[0] % 4
                ucount[0] += 1
                bank = psU[bi]
                bk = "psU%d" % bi
                for kc in range(8):
                    C.mm(bank[:, 0:260], wu[:, kc, ch * 128:(ch + 1) * 128],
                         h2T[:, kc, :, :].rearrange("p s t -> p (s t)"),
                         start=(kc == 0), stop=(kc == 7), r=["wu", "h2T"], w=[bk])
                banks.append((bank, bk))
            return banks

        def conv_ops(fc, banks):
            p = fc % 2
            for (bank, bk), ch, dst, dk in ((banks[0], fc, gc[p], "gc%d" % p), (banks[1], 22 + fc, uc[p], "uc%d" % p)):
                bv = bank[:, 0:260].rearrange("p (s t) -> p s t", s=2)
                C.act(dst[:], bv[:, :, 2:130], AF.Identity, r=["cw"], w=[bk, dk], scale=cw[:, ch, 2:3], bias=cw[:, ch, 3:4])
                C.stt(dst[:], bv[:, :, 1:129], cw[:, ch, 1:2], dst[:], ALU.mult, ALU.add, r=["cw", dk], w=[bk, dk])
                C.stt(dst[:], bv[:, :, 0:128], cw[:, ch, 0:1], dst[:], ALU.mult, ALU.add, r=["cw", dk], w=[bk, dk])

        def gate_ops(fc):
            p = fc % 2
            C.act(gg[p][:], gc[p][:], AF.Gelu_apprx_tanh, r=["gc%d" % p], w=["gg%d" % p])
            C.tt("pool", actT[:, fc, :, :], gg[p][:], uc[p][:], ALU.mult,
                 r=["gg%d" % p, "uc%d" % p], w=["actT"])

        nxt = up_mm(0)
        for fc in range(22):
            cur = nxt
            if fc + 1 < 22:
                nxt = up_mm(fc + 1)
            conv_ops(fc, cur)
            if fc >= 1:
                gate_ops(fc - 1)
            if fc == 12 and grp >= 1 and T.get("post_group") is not None:
                T["post_group"](grp - 1)
        gate_ops(21)
        for sl in range(2):
            j = grp * 2 + sl
            for half in range(2):
                for f in range(22):
                    C.mm(psD[half][:, :], actT[:, f, sl, :], wd[:, f, half * 512:(half + 1) * 512],
                         start=(f == 0), stop=(f == 21), r=["actT", "wd"], w=["psD%d" % half])
                C.cp("dve", yo[:, half * 512:(half + 1) * 512], psD[half][:, :], w=["psD%d" % half, "yo"])
            C.act(junk[:, :], yo[:], AF.Square, r=["yo"], w=["junk", "oss"], accum=oss[:, 0:1])
            C.act(oss[:, 1:2], oss[:, 0:1], AF.Ln, r=["oss", "epsb"], w=["oss1"], scale=1.0 / D, bias=epsb[:, 0:1])
            C.act(oss[:, 2:3], oss[:, 1:2], AF.Exp, r=["oss1"], w=["oss2"], scale=-0.5)
            C.stt(yo[:], yo[:], oss[:, 2:3], gpost_bc[:], ALU.mult, ALU.mult, r=["yo", "oss2", "gpost_bc"], w=["yo"])
            C.tt("dve", yo[:], yo[:], xs[sl][:], ALU.add, r=["yo", "xs%d" % sl], w=["yo"])
            C.dma(out_tile(j)[:, :], yo[:], r=["yo"], w=["ffnout%d" % j])
    if T.get("post_group") is not None:
        T["post_group"](NS // 2 - 1)
        C.end_phase(wait_cc=False)
        return
    C.end_phase()


OFF_Q, OFF_KV, OFF_G, OFF_C, OFF_D = 512, 768, 1152, 1164, 1676


def _consts():
    c = {}
    half = 32
    inv = (10000.0 ** (-np.arange(half, dtype=np.float32) * 2.0 / 64)).astype(np.float32)
    ang = np.arange(SEQ, dtype=np.float32)[:, None] * inv[None, :]
    cos = np.cos(ang).astype(np.float32).T
    sin = np.sin(ang).astype(np.float32).T
    c64 = np.concatenate([cos, cos], 0)
    s64 = np.concatenate([-sin, sin], 0)
    c["ropeC"] = np.ascontiguousarray(np.concatenate([c64, c64], 0))
    c["ropeS"] = np.ascontiguousarray(np.concatenate([s64, s64], 0))
    blk = np.arange(SEQ) // 64
    c["indp"] = (np.arange(64)[:, None] == (blk % 64)[None, :]).astype(np.float32)
    c["identd"] = np.eye(128, dtype=np.float32)
    n = np.arange(512)[:, None]
    b = np.arange(128)[None, :]
    off = n - 4 * b
    M = np.where((off == -1) | (off == 3), 1.0, np.where((off >= 0) & (off <= 2), 2.0, 0.0)).astype(np.float32)
    M[511, :] = 0.0
    c["impM"] = np.ascontiguousarray(M.reshape(4, 128, 128))
    c["trild"] = (np.arange(128)[:, None] <= np.arange(128)[None, :]).astype(np.float32)
    rg = np.zeros((128, 2, 128), np.float32)
    for gi, w in enumerate((2, 4, 8, 16)):
        rg[(gi % 2) * 64:(gi % 2) * 64 + 64, gi // 2, :] = 1.0 / w
    c["rcntg"] = rg
    return c


def _core_consts(cidx):
    c = cidx
    o = {}
    k = np.arange(128)[:, None]
    t = np.arange(128)[None, :]
    wm = np.zeros((8, 128, 128), np.float32)
    for q in range(8):
        mm = q - 4
        rel = mm - c
        if rel == 0:
            wm[q] = (k <= t)
        elif rel == -4:
            wm[q] = (k > t)
        elif -4 < rel < 0:
            wm[q] = 1.0
    o["wmaskd"] = wm
    cm = np.zeros((5, 128, 128), np.float32)
    for jm in range(4):
        nprime = k - 32 * jm
        cm[jm] = (16 * nprime + 31 <= 128 * c + t)
    cm[4] = 1.0
    if c == 0:
        cm[4][127, :15] = 0.0
    o["cmaskd"] = cm
    A = np.zeros((NS, 128, 128), np.float32)
    B = np.zeros((NS, 128, 128), np.float32)
    tt = np.arange(128)[:, None]
    bb = np.arange(128)[None, :]
    for j in range(NS):
        i = 4 * j + c
        cur = 2 * i + (tt >= 64)
        valid = bb <= cur
        forced = (bb == 0) | (bb == cur) | (bb == cur - 1)
        A[j] = (valid & ~forced)
        B[j] = np.where(valid, np.where(forced, 1e6, 0.0), -1e30)
    o["selA"] = A
    o["selB"] = B
    inv = (10000.0 ** (-np.arange(32, dtype=np.float32) * 2.0 / 64)).astype(np.float32)
    qC = np.zeros((NS, 128, 128), np.float32)
    qS = np.zeros((NS, 128, 128), np.float32)
    for j in range(NS):
        pos = (128 * (4 * j + c) + np.arange(128)).astype(np.float32)
        ang = pos[:, None] * inv[None, :]
        cs = np.cos(ang).astype(np.float32).T * np.float32(0.125)
        sn = np.sin(ang).astype(np.float32).T * np.float32(0.125)
        c64 = np.concatenate([cs, cs], 0)
        s64 = np.concatenate([-sn, sn], 0)
        qC[j] = np.concatenate([c64, c64], 0)
        qS[j] = np.concatenate([s64, s64], 0)
    o["qC"] = qC
    o["qS"] = qS
    r0 = np.zeros((128, 2, 128), np.float32)
    for gi, w in enumerate((2, 4, 8, 16)):
        pos = 128 * c + np.arange(128)
        cnt = np.minimum(pos + 1, w).astype(np.float32)
        r0[(gi % 2) * 64:(gi % 2) * 64 + 64, gi // 2, :] = (1.0 / cnt)[None, :]
    o["rcnt0"] = r0
    sm = np.zeros((128, 32), np.float32)
    sm[c * 32:(c + 1) * 32, :] = np.eye(32, dtype=np.float32)
    o["selm"] = sm
    sm2 = np.zeros((8, 2), np.float32)
    sm2[c * 2:(c + 1) * 2, :] = np.eye(2, dtype=np.float32)
    o["selm2"] = sm2
    return o


def _sw(idx):
    return np.concatenate([idx[32:], idx[:32]])


def _mixer_weights(P, l):
    w_in = P["w_in"][l]
    kv = lambda s: np.arange(OFF_KV + 64 * s, OFF_KV + 64 * s + 64)
    cols_kv = np.concatenate([kv(0), kv(1), kv(2), kv(4), _sw(kv(2)), _sw(kv(4)), kv(3), kv(5)])
    qcols = np.arange(OFF_Q, OFF_Q + 256)
    qsw = np.concatenate([_sw(qcols[h * 64:(h + 1) * 64]) for h in range(4)])
    cols_loc = np.concatenate([np.arange(0, 512), np.arange(OFF_D, OFF_D + 256), np.arange(OFF_C, OFF_C + 256),
                               qcols, qsw, np.arange(OFF_C + 256, OFF_C + 512), np.arange(OFF_G, OFF_G + 12)])
    o = {}
    o["w_kv"] = np.ascontiguousarray(w_in[:, cols_kv])
    o["w_loc"] = np.ascontiguousarray(w_in[:, cols_loc])
    o["w_out"] = np.ascontiguousarray(P["w_out"][l])
    o["gpre"] = np.ascontiguousarray(P["norm_mix_pre"][l].reshape(8, 128).T)
    o["gpost"] = np.ascontiguousarray(P["norm_mix_post"][l].reshape(1, D))
    o["convw"] = np.ascontiguousarray(P["conv_dw_w"][l].reshape(31, 2, 128).transpose(2, 1, 0))
    cv = np.stack([P["conv_dw_b"][l], P["conv_ln_g"][l], P["conv_ln_b"][l]], -1)
    o["convv"] = np.ascontiguousarray(cv.reshape(2, 128, 3).transpose(1, 0, 2))
    w1k = P["nsa_ck_w1"][l].reshape(32, 64, 64).transpose(1, 0, 2).reshape(64, 2048)
    w1v = P["nsa_cv_w1"][l].reshape(32, 64, 64).transpose(1, 0, 2).reshape(64, 2048)
    o["w1d"] = np.ascontiguousarray(np.concatenate([w1k, w1v], 0))
    o["w2d"] = np.ascontiguousarray(np.concatenate([P["nsa_ck_w2"][l], P["nsa_cv_w2"][l]], 1))
    o["ped"] = np.ascontiguousarray(np.concatenate([P["nsa_pe_k"][l].T, P["nsa_pe_v"][l].T], 0))
    o["sgv"] = np.ascontiguousarray(np.stack([P["sgu_ln_g"][l], P["sgu_ln_b"][l]], 0))
    o["sgw"] = np.ascontiguousarray(P["sgu_w"][l].transpose(2, 0, 1))
    o["sgb"] = np.ascontiguousarray(P["sgu_b"][l])
    pw = np.zeros((128, 2, 128), np.float32)
    for gi in range(4):
        lo = (gi % 2) * 64
        pw[lo:lo + 64, gi // 2, lo:lo + 64] = P["pool_w"][l][gi]
    o["poolw"] = pw
    o["poolsc"] = np.ascontiguousarray(P["pool_scale"][l].reshape(2, 128).T)
    return o


def _ffn_weights(P, l):
    o = {}
    o["w_up"] = np.ascontiguousarray(P["ffn_up"][l])
    o["w_dn"] = np.ascontiguousarray(P["ffn_down"][l])
    o["g2"] = np.ascontiguousarray(P["norm_ffn_pre"][l].reshape(8, 128).T)
    o["gpost2"] = np.ascontiguousarray(P["norm_ffn_post"][l].reshape(1, D))
    cw = np.concatenate([P["ffn_conv_w"][l], P["ffn_conv_b"][l][None, :]], 0)
    o["cwd"] = np.ascontiguousarray(cw.reshape(4, NFC, 128).transpose(2, 1, 0))
    return o


def _own_tiles(xb, c, halo):
    pad = np.concatenate([np.zeros((halo, D), np.float32), xb], 0)
    out = np.empty((NS, halo + 128, D), np.float32)
    for j in range(NS):
        i = 4 * j + c
        out[j] = pad[128 * i:128 * i + 128 + halo]
    return out


def _scatter_own(res_list, key):
    x = np.empty((NB, SEQ, D), np.float32)
    for core in range(8):
        b, c = divmod(core, 4)
        r = res_list[core][key]
        for j in range(NS):
            i = 4 * j + c
            x[b, 128 * i:128 * (i + 1)] = r[j]
    return x


def _chunk_row(m):
    r, sl = m % 4, m // 4
    return sl // 2, r * 256 + (sl % 2) * 128


def build_all(nslots=NS):
    C = Ctx()
    T = {}
    xg0 = C.dram_in("xg0", [8 * 1024, D])
    xown0 = C.dram_in("xown0", [NS, 128, D])
    for k, shp in MIX_C_SHAPES.items():
        T[k] = C.dram_in(k, shp)
    for l in range(2):
        for k, shp in MIX_W_SHAPES.items():
            T["%s_%d" % (k, l)] = C.dram_in("%s_%d" % (k, l), shp)
        for k, shp in FFN_W_SHAPES.items():
            T["%s_%d" % (k, l)] = C.dram_in("%s_%d" % (k, l), shp)
    out = C.dram_out("out", [NS, 128, D])
    xm_own = C.dram_int("xm_own", [NS, 128, D])
    tails = [C.dram_int("xm_tail%d" % l, [NS * 2, D]) for l in range(2)]
    tail_g = [C.dram_int("tail_g%d" % l, [4 * NS * 2, D]) for l in range(2)]
    x1_own = [C.dram_int("x1_own%d" % g, [256, D]) for g in range(8)]
    x1_g = [C.dram_int("x1_g%d" % g, [1024, D]) for g in range(8)]
    RG = [[0, 1, 2, 3], [4, 5, 6, 7]]

    def gather(src, dst):
        C.S.collective_nowait(lambda e: e.collective_compute("AllGather", ALU.bypass, replica_groups=RG,
                                                      ins=[src.opt()], outs=[dst.opt()]))

    def xg_tile0(m):
        g, ro = _chunk_row(m)
        return xg0[g * 1024 + ro:g * 1024 + ro + 128, :]

    def xg_tile1(m):
        g, ro = _chunk_row(m)
        return x1_g[g][ro:ro + 128, :]

    for l in range(2):
        T["xg_tile"] = xg_tile0 if l == 0 else xg_tile1
        T["xown_tile"] = (lambda j: xown0[j, :, :]) if l == 0 else (lambda j: x1_own[j // 2][(j % 2) * 128:(j % 2) * 128 + 128, :])
        T["xm_own"] = xm_own
        T["xm_tail"] = tails[l]
        mixer_phase(C, T, l, nslots=nslots)
        gather(tails[l], tail_g[l])
        T["tail_g_%d" % l] = tail_g[l]
        T["ffn_out_tile"] = (lambda j: x1_own[j // 2][(j % 2) * 128:(j % 2) * 128 + 128, :]) if l == 0 else (lambda j: out[j, :, :])
        if l == 0:
            def post_group(g):
                C.S.cc_op(lambda e: e.collective_compute("AllGather", ALU.bypass, replica_groups=RG,
                                                         ins=[x1_own[g].opt()], outs=[x1_g[g].opt()]),
                          reads=["ffnout%d" % (2 * g), "ffnout%d" % (2 * g + 1)], writes=["x1g%d" % g])
            T["post_group"] = post_group
        else:
            T["post_group"] = None
        ffn_phase(C, T, l)
    info = C.finish()
    return C.nc, info


_CACHE = {}


def kernel(**inputs):
    P = {k: np.asarray(v, dtype=np.float32) for k, v in inputs.items()}
    x = P["x"]
    if "nc" not in _CACHE:
        _CACHE["nc"] = build_all()[0]
    nc = _CACHE["nc"]
    consts = _consts()
    shared = {k: consts[k] for k in MIX_C_SHAPES if k in consts}
    for l in range(2):
        for k, v in _mixer_weights(P, l).items():
            shared["%s_%d" % (k, l)] = v
        for k, v in _ffn_weights(P, l).items():
            shared["%s_%d" % (k, l)] = v
    in_maps = []
    for core in range(8):
        b, c = divmod(core, 4)
        m = dict(shared)
        m.update(_core_consts(c))
        xt = x[b].reshape(8, 2, 4, 128, D)
        m["xg0"] = np.ascontiguousarray(xt.transpose(0, 2, 1, 3, 4).reshape(8 * 1024, D))
        m["xown0"] = np.ascontiguousarray(x[b].reshape(NS, 4, 128, D)[:, c])
        in_maps.append(m)
    res = run_bass_kernel_spmd(nc, in_maps, core_ids=list(range(8)))
    return _scatter_own(res.results, "out").astype(np.float32)
```

```python
import numpy as np
from contextlib import ExitStack
import concourse.bass as bass
import concourse.mybir as mybir
from concourse.bass_utils import run_bass_kernel_spmd

F32 = mybir.dt.float32
BF16 = mybir.dt.bfloat16
AF = mybir.ActivationFunctionType
ALU = mybir.AluOpType
AX = mybir.AxisListType

D = 1024
SEQ = 8192
NB = 2
NT = 64
NS = 16
FF = 2816
NFC = 44
RMS_EPS = 1e-6
LN_EPS = 1e-5
NEG = -30000.0
GELU_C = 1.5957691216057308


class Sched:
    NDS = 8

    def __init__(self, nc, sems):
        self.nc = nc
        self.eng = {"pe": nc.tensor, "act": nc.scalar, "dve": nc.vector,
                    "pool": nc.gpsimd, "sp": nc.sync}
        self.ops = []
        self.last_writer = {}
        self.readers = {}
        self.sems = sems

    def cc_op(self, fn, reads=(), writes=()):
        idx = self.op("pool", fn, reads, writes)
        self.ops[idx].append("cc")
        return idx

    def op(self, engine, fn, reads=(), writes=()):
        idx = len(self.ops)
        is_dma = engine == "sp"
        deps = {}

        def add(d, kind):
            if d is None:
                return
            if deps.get(d) is None or kind == "raw":
                deps[d] = kind

        for k in reads:
            add(self.last_writer.get(k), "raw")
        for k in writes:
            add(self.last_writer.get(k), "waw")
            for r in self.readers.get(k, ()):
                add(r, "war")
        for k in writes:
            self.last_writer[k] = idx
            self.readers[k] = []
        for k in reads:
            if k not in writes:
                lst = self.readers.setdefault(k, [])
                if not is_dma:
                    lst[:] = [r_ for r_ in lst if self.ops[r_][0] != engine]
                lst.append(idx)
        keep = []
        for d, kind in deps.items():
            de = self.ops[d][0]
            if de == engine and not is_dma:
                if engine == "pe":
                    continue
            keep.append(d)
        self.ops.append([engine, fn, keep, is_dma])
        for d in keep:
            self.ops[d][3] = True
        return idx

    def _init_emit_state(self):
        self.counts = {e: 0 for e in self.eng}
        self.dma_counts = [0] * self.NDS
        self.n_dma = 0
        self.waited = {e: {} for e in self.eng}
        self.sigval = {}
        self.emitted = 0
        self.cc_count = 0

    def flush(self):
        if not hasattr(self, "counts"):
            self._init_emit_state()
        last = {}
        for idx in range(self.emitted, len(self.ops)):
            if len(self.ops[idx]) == 4 and self.ops[idx][0] != "__ccwait__":
                last[self.ops[idx][0]] = idx
        for e, idx in last.items():
            if e != "sp" and len(self.ops[idx]) == 4:
                self.ops[idx][3] = True
        for idx in range(self.emitted, len(self.ops)):
            e, fn, deps, signal = self.ops[idx][:4]
            if e == "__ccwait__":
                if self.cc_count > 0:
                    for e2, eng2 in self.eng.items():
                        if self.waited[e2].get("cc", 0) < self.cc_count:
                            eng2.wait_ge(self.sems["cc"], self.cc_count)
                            self.waited[e2]["cc"] = self.cc_count
                continue
            is_cc = len(self.ops[idx]) > 4
            eng = self.eng[e]
            need = {}
            for d in deps:
                sem, val = self.sigval[d]
                if need.get(id(sem), (None, 0))[1] < val:
                    need[id(sem)] = (sem, val)
            if e == "sp":
                slot = self.n_dma % self.NDS
                if self.dma_counts[slot] > 0:
                    sem = self.sems["dma"][slot]
                    val = self.dma_counts[slot] * 16
                    if need.get(id(sem), (None, 0))[1] < val:
                        need[id(sem)] = (sem, val)
            for key, (sem, val) in need.items():
                if self.waited[e].get(key, 0) >= val:
                    continue
                eng.wait_ge(sem, val)
                self.waited[e][key] = val
            ins = fn(eng)
            if is_cc:
                self.cc_count += 1
                ins.then_inc(self.sems["cc"])
                self.sigval[idx] = (self.sems["cc"], self.cc_count)
                continue
            if e == "sp":
                slot = self.n_dma % self.NDS
                self.n_dma += 1
                self.dma_counts[slot] += 1
                sem = self.sems["dma"][slot]
                ins.then_inc(sem, 16)
                self.sigval[idx] = (sem, self.dma_counts[slot] * 16)
            elif signal:
                self.counts[e] += 1
                ins.then_inc(self.sems[e], 1)
                self.sigval[idx] = (self.sems[e], self.counts[e])
        self.emitted = len(self.ops)

    def barrier(self, engines=None, wait_cc=True):
        self.flush()
        for e, eng in self.eng.items():
            for x in ("pe", "act", "dve", "pool"):
                if x != e and self.counts[x] > 0 and self.waited[e].get(id(self.sems[x]), 0) < self.counts[x]:
                    eng.wait_ge(self.sems[x], self.counts[x])
                    self.waited[e][id(self.sems[x])] = self.counts[x]
            for slot in range(self.NDS):
                if self.dma_counts[slot] > 0:
                    sem = self.sems["dma"][slot]
                    val = self.dma_counts[slot] * 16
                    if self.waited[e].get(id(sem), 0) < val:
                        eng.wait_ge(sem, val)
                        self.waited[e][id(sem)] = val
            if wait_cc and self.cc_count > 0 and self.waited[e].get("cc", 0) < self.cc_count:
                eng.wait_ge(self.sems["cc"], self.cc_count)
                self.waited[e]["cc"] = self.cc_count
        self.last_writer = {}
        self.readers = {}

    def wait_cc(self):
        self.ops.append(["__ccwait__", None, [], False])

    def collective_nowait(self, fn):
        self.barrier()
        self.cc_count += 1
        fn(self.eng["pool"]).then_inc(self.sems["cc"])

    def collective(self, fn):
        self.barrier()
        self.cc_count += 1
        fn(self.eng["pool"]).then_inc(self.sems["cc"])
        for e, eng in self.eng.items():
            eng.wait_ge(self.sems["cc"], self.cc_count)

    def emit(self):
        self.flush()
        sp = self.eng["sp"]
        for slot in range(self.NDS):
            if self.dma_counts[slot] > 0:
                sp.wait_ge(self.sems["dma"][slot], self.dma_counts[slot] * 16)
        return self.counts, self.n_dma


class Ctx:
    def __init__(self):
        self.nc = bass.Bass("TRN2", target_bir_lowering=False)
        self.es = ExitStack()
        nc = self.nc
        sems = {e: self.es.enter_context(nc.semaphore("s_" + e)) for e in ["pe", "act", "dve", "pool"]}
        sems["dma"] = [self.es.enter_context(nc.semaphore("s_dma%d" % i)) for i in range(Sched.NDS)]
        sems["cc"] = self.es.enter_context(nc.semaphore("s_cc"))
        self.S = Sched(nc, sems)
        self.pes = self.es

    def begin_phase(self, sfx):
        self.pes = ExitStack()
        self.sfx = sfx

    def end_phase(self, wait_cc=True):
        self.S.barrier(wait_cc=wait_cc)
        self.pes.close()
        self.pes = self.es

    def dram_int(self, name, shape, dt=F32):
        return self.nc.dram_tensor(name, list(shape), dt).ap()

    def dram_in(self, name, shape, dt=F32):
        return self.nc.dram_tensor(name, list(shape), dt, kind="ExternalInput").ap()

    def dram_out(self, name, shape, dt=F32):
        return self.nc.dram_tensor(name, list(shape), dt, kind="ExternalOutput").ap()

    def sb(self, name, shape, dt=F32):
        return self.pes.enter_context(self.nc.sbuf_tensor(name + getattr(self, "sfx", ""), list(shape), dt))

    def ps(self, name, shape, dt=F32):
        return self.pes.enter_context(self.nc.psum_tensor(name + getattr(self, "sfx", ""), list(shape), dt))

    def dma(self, out, in_, r=(), w=()):
        self.S.op("sp", lambda e: e.dma_start(out=out, in_=in_), r, w)

    def mm(self, out, lhsT, rhs, start=True, stop=True, r=(), w=(), skip=False):
        self.S.op("pe", lambda e: e.matmul(out, lhsT=lhsT, rhs=rhs, start=start, stop=stop,
                                           skip_group_check=skip), r, w)

    def tr(self, out, in_, ident, r=(), w=()):
        self.S.op("pe", lambda e: e.transpose(out, in_, ident), r, w)

    def act(self, out, in_, func, r=(), w=(), scale=1.0, bias=0.0, accum=None, eng="act"):
        if accum is None:
            self.S.op("act", lambda e: e.activation(out=out, in_=in_, func=func, bias=bias, scale=scale), r, w)
        else:
            self.S.op("act", lambda e: e.activation(out=out, in_=in_, func=func, bias=bias, scale=scale,
                                                    accum_out=accum), r, w)

    def cp(self, eng, out, in_, r=(), w=()):
        if eng == "act":
            self.S.op("act", lambda e: e.activation(out=out, in_=in_, func=AF.Copy), r, w)
        else:
            self.S.op(eng, lambda e: e.tensor_copy(out=out, in_=in_), r, w)

    def tt(self, eng, out, in0, in1, op, r=(), w=()):
        self.S.op(eng, lambda e: e.tensor_tensor(out=out, in0=in0, in1=in1, op=op), r, w)

    def ts(self, eng, out, in0, s1, s2, op0, op1=None, r=(), w=()):
        if op1 is None:
            self.S.op(eng, lambda e: e.tensor_scalar(out=out, in0=in0, scalar1=s1, scalar2=None, op0=op0), r, w)
        else:
            self.S.op(eng, lambda e: e.tensor_scalar(out=out, in0=in0, scalar1=s1, scalar2=s2, op0=op0, op1=op1), r, w)

    def stt(self, out, in0, scalar, in1, op0, op1, r=(), w=()):
        self.S.op("dve", lambda e: e.scalar_tensor_tensor(out=out, in0=in0, scalar=scalar, in1=in1,
                                                          op0=op0, op1=op1), r, w)

    def memset(self, eng, ap, val, w=()):
        self.S.op(eng, lambda e: e.memset(ap, val), (), w)

    def recip(self, out, in_, r=(), w=()):
        self.S.op("dve", lambda e: e.reciprocal(out=out, in_=in_), r, w)

    def finish(self):
        res = self.S.emit()
        self.es.close()
        return res


def load_convert(C, dst_bf, src_dram, stage, stage_key, dst_key, ncols, scale_ap=None, parity=0, scale_key="gscale"):
    C.dma(stage[:, 0:ncols], src_dram, w=[stage_key])
    if scale_ap is not None:
        C.ts("dve", dst_bf, stage[:, 0:ncols], scale_ap, None, ALU.mult, r=[stage_key, scale_key], w=[dst_key])
    elif parity % 2 == 0:
        C.cp("dve", dst_bf, stage[:, 0:ncols], r=[stage_key], w=[dst_key])
    else:
        C.cp("act", dst_bf, stage[:, 0:ncols], r=[stage_key], w=[dst_key])


NLOC = 1804


MIX_W = ["w_kv", "w_loc", "w_out", "gpre", "gpost", "convw", "convv", "w1d", "w2d", "ped", "sgv", "sgw", "sgb",
         "poolw", "poolsc"]
MIX_W_SHAPES = {"w_kv": [D, 512], "w_loc": [D, NLOC], "w_out": [D, D], "gpre": [128, 8], "gpost": [1, D],
                "convw": [128, 2, 31], "convv": [128, 2, 3], "w1d": [128, 2048], "w2d": [64, 128], "ped": [128, 32],
                "sgv": [2, 256], "sgw": [128, 4, 128], "sgb": [4, 128], "poolw": [128, 2, 128], "poolsc": [128, 2]}
MIX_C_SHAPES = {"ropeC": [128, SEQ], "ropeS": [128, SEQ], "qC": [NS, 128, 128], "qS": [NS, 128, 128], "indp": [64, SEQ],
                "identd": [128, 128], "wmaskd": [8, 128, 128], "cmaskd": [5, 128, 128], "selA": [NS, 128, 128],
                "selB": [NS, 128, 128], "impM": [4, 128, 128], "trild": [128, 128], "rcnt0": [128, 2, 128],
                "rcntg": [128, 2, 128], "selm": [128, 32], "selm2": [8, 2]}
FFN_W_SHAPES = {"w_up": [D, 2 * FF], "w_dn": [FF, D], "g2": [128, 8], "gpost2": [1, D], "cwd": [128, NFC, 4]}


def mixer_phase(C, T, l, nslots=NS, stages=5):
    C.begin_phase("_m%d" % l)
    nc = C.nc
    xg_tile = T["xg_tile"]; xown_tile = T["xown_tile"]; xm_own = T["xm_own"]; xm_tail = T["xm_tail"]
    w_kv, w_loc, w_out, gpre, gpost = (T[k + "_%d" % l] for k in ("w_kv", "w_loc", "w_out", "gpre", "gpost"))
    convw, convv, w1d, w2d, ped = (T[k + "_%d" % l] for k in ("convw", "convv", "w1d", "w2d", "ped"))
    sgv, sgw, sgb, poolw, poolsc = (T[k + "_%d" % l] for k in ("sgv", "sgw", "sgb", "poolw", "poolsc"))
    ropeC, ropeS, qC, qS, indp, identd = (T[k] for k in ("ropeC", "ropeS", "qC", "qS", "indp", "identd"))
    wmaskd, cmaskd, selA, selB, impM, trild = (T[k] for k in ("wmaskd", "cmaskd", "selA", "selB", "impM", "trild"))
    rcnt0, rcntg, selmd = T["rcnt0"], T["rcntg"], T["selm"]

    sb = C.sb
    wkv = sb("wkv", [128, 8, 512], BF16)
    wloc = sb("wloc", [128, 8, NLOC], BF16)
    wo = sb("wo", [128, 8, D], BF16)
    xt = [sb("xt0", [128, D]), sb("xt1", [128, D])]
    stage = xt
    gpre_t = sb("gpre_t", [128, 8])
    gpost_bc = sb("gpost_bc", [128, D])
    ident = sb("ident", [128, 128], BF16)
    ones_b = sb("ones_b", [128, 128], BF16)
    shi = sb("shi", [128, 2, 128], BF16)
    slo = sb("slo", [128, 2, 128], BF16)
    Dg = sb("Dg", [128, 2, 31, 128], BF16)
    convw_t = sb("convw_t", [128, 2, 31])
    convv_t = sb("convv_t", [128, 2, 3])
    W1k = sb("W1k", [64, 32, 64], BF16)
    W1v = sb("W1v", [64, 32, 64], BF16)
    W2 = sb("W2", [64, 128], BF16)
    peTk = sb("peTk", [64, 32], BF16)
    peTv = sb("peTv", [64, 32], BF16)
    hnc = sb("hnc", [128, D], BF16)
    cbias = sb("cbias", [64, 2])
    Kaug = sb("Kaug", [128, SEQ], BF16)
    Kw = sb("Kw", [128, 12, 128], BF16)
    Vs = sb("Vs", [128, NT, 65], BF16)
    Vw = sb("Vw", [128, 12, 65], BF16)
    kvk = [sb("kvk0", [64, 528], BF16), sb("kvk1", [64, 528], BF16)]
    kvv = [sb("kvv0", [64, 528], BF16), sb("kvv1", [64, 528], BF16)]
    gkT = sb("gkT", [64, 576], BF16)
    gvT = sb("gvT", [64, 576], BF16)
    kcT = sb("kcT", [128, 576], BF16)
    rhsC = sb("rhsC", [128, 4, 193], BF16)
    wmask = sb("wmask", [128, 8, 128], BF16)
    cmask = sb("cmask", [128, 5, 128], BF16)
    wsT = sb("wsT", [128, 4, 128], BF16)
    tril = sb("tril", [128, 128])
    Bs = sb("Bs", [128, 2, 128])
    Wbd = sb("Wbd", [128, 2, 128], BF16)
    poolsc_t = sb("poolsc_t", [128, 2])
    rc0 = sb("rc0", [128, 2, 128])
    rcg = sb("rcg", [128, 2, 128])
    lng_bc = sb("lng_bc", [128, 256])
    lnb_bc = sb("lnb_bc", [128, 256])
    hnb = [sb("hnb0", [128, D], BF16), sb("hnb1", [128, D], BF16)]
    hT = [sb("hT0", [128, 8, 128], BF16), sb("hT1", [128, 8, 128], BF16)]
    rC = [sb("rC0", [128, 128]), sb("rC1", [128, 128])]
    rS = [sb("rS0", [128, 128]), sb("rS1", [128, 128])]
    junk = sb("junk", [128, D], BF16)
    ssq = sb("ssq", [128, 8])
    rstd = sb("rstd", [128, 8])
    ropet = sb("ropet", [128, 2, 128])
    xo = sb("xo", [128, D])
    hno = sb("hno", [128, D], BF16)
    hTo = sb("hTo", [128, 8, 160], BF16)
    sig = sb("sig", [128, 2, 160])
    ub = sb("ub", [128, 2, 160], BF16)
    zdT = sb("zdT", [128, 2, 160])
    s2 = sb("s2", [128, 2, 160]); s4 = sb("s4", [128, 2, 160])
    s8 = sb("s8", [128, 2, 160]); s16 = sb("s16", [128, 2, 160])
    plf = sb("plf", [128, 2, 128]); plb = sb("plb", [128, 2, 128], BF16)
    usg = sb("usg", [128, 2, 128])
    qCt = sb("qCt", [128, 128]); qSt = sb("qSt", [128, 128])
    qt1 = sb("qt1", [128, 128]); qt2 = sb("qt2", [128, 128]); qsum = sb("qsum", [128, 128])
    QR = sb("QR", [128, 2, 4, 128], BF16)
    QC = sb("QC", [128, 4, 128], BF16)
    QW = sb("QW", [128, 4, 128], BF16)
    gts = sb("gts", [128, 12])
    vg = sb("vg", [128, 256]); vn = sb("vn", [128, 256]); vbf = sb("vbf", [128, 256], BF16)
    bnst = sb("bnst", [128, 6]); bnag = sb("bnag", [128, 2]); lnr = sb("lnr", [128, 2])
    ycv = sb("ycv", [128, 2, 128]); ysq = sb("ysq", [128, 2, 128])
    cmean = sb("cmean", [128, 128]); cmsq = sb("cmsq", [128, 128]); cvar = sb("cvar", [128, 128])
    crstd = sb("crstd", [128, 128]); cyn = sb("cyn", [128, 2, 128])
    sgt = sb("sgt", [128, 2, 128])
    yT = sb("yT", [128, 8, 128], BF16)
    Pb = [sb("Pb%d" % i, [128, 4, 128], BF16) for i in range(4)]
    selAt = sb("selAt", [128, 128]); selBt = sb("selBt", [128, 128])
    imp = sb("imp", [128, 128]); imp2 = sb("imp2", [128, 128]); impw = sb("impw", [128, 128])
    top8 = sb("top8", [128, 16]); selt1 = sb("selt1", [128, 128]); selt2 = sb("selt2", [128, 128])
    selbf = sb("selbf", [128, 128], BF16)
    den = sb("den", [128, 12]); rden = sb("rden", [128, 12]); gsc = sb("gsc", [128, 12])
    Ynsa = sb("Ynsa", [128, 256]); Ynb = sb("Ynb", [128, 256], BF16)
    yo = sb("yo", [128, D])
    oss = sb("oss", [128, 4])
    selm = sb("selm", [128, 32], BF16)
    cg1 = sb("cg1", [64, 64]); cg2 = sb("cg2", [64, 64])

    psS = [C.ps("psS0", [128, 512]), C.ps("psS1", [128, 512])]
    psC = C.ps("psC", [128, 512])
    psK = C.ps("psK", [128, 512])
    psSel = C.ps("psSel", [128, 512])
    psWin = C.ps("psWin", [128, 512])
    psG = C.ps("psG", [128, 512])
    psT = C.ps("psT", [128, 1024], BF16)

    C.dma(gpre_t[:], gpre[:, :], w=["gscale"])
    C.dma(gpost_bc[:], gpost.partition_broadcast(128), w=["gpost_bc"])
    C.dma(stage[0][:, 0:128], identd[:, :], w=["xt0"])
    C.cp("dve", ident[:], stage[0][:, 0:128], r=["xt0"], w=["ident"])
    C.memset("dve", ones_b[:], 1.0, w=["ones_b"])
    C.dma(stage[1][:, 0:32], selmd[:, :], w=["xt1"])
    C.cp("dve", selm[:], stage[1][:, 0:32], r=["xt1"], w=["selm"])
    par = 0
    for kc in range(8):
        s = stage[par % 2]; sk = "xt%d" % (par % 2); par += 1
        load_convert(C, wkv[:, kc, :], w_kv[kc * 128:(kc + 1) * 128, :], s, sk, "wkv", 512, scale_ap=gpre_t[:, kc:kc + 1])
        s = stage[par % 2]; sk = "xt%d" % (par % 2); par += 1
        load_convert(C, wloc[:, kc, 0:902], w_loc[kc * 128:(kc + 1) * 128, 0:902], s, sk, "wloc", 902, scale_ap=gpre_t[:, kc:kc + 1])
        s = stage[par % 2]; sk = "xt%d" % (par % 2); par += 1
        load_convert(C, wloc[:, kc, 902:NLOC], w_loc[kc * 128:(kc + 1) * 128, 902:NLOC], s, sk, "wloc", 902, scale_ap=gpre_t[:, kc:kc + 1])
        s = stage[par % 2]; sk = "xt%d" % (par % 2); par += 1
        load_convert(C, wo[:, kc, :], w_out[kc * 128:(kc + 1) * 128, :], s, sk, "wo", D, parity=kc)
    for q in range(8):
        s = stage[par % 2]; sk = "xt%d" % (par % 2); par += 1
        C.dma(s[64:128, 0:1024], indp[:, q * 1024:(q + 1) * 1024], w=[sk])
        C.cp("act" if q % 2 else "dve", Kaug[64:128, q * 1024:(q + 1) * 1024], s[64:128, 0:1024], r=[sk], w=["Kaug_ind"])
    C.memset("pool", Kw[64:128, :, :], 0.0, w=["Kw"])
    C.memset("pool", kcT[:, :], 0.0, w=["kcT"])
    C.memset("pool", gkT[:, :], 0.0, w=["gkT"])
    C.memset("pool", gvT[:, :], 0.0, w=["gvT"])
    for q_ in range(2):
        C.memset("pool", kvk[q_][:, :], 0.0, w=["kvk%d" % q_])
        C.memset("pool", kvv[q_][:, :], 0.0, w=["kvv%d" % q_])
    C.memset("pool", QC[:, :, :], 0.0, w=["QC"])
    C.memset("pool", QW[:, :, :], 0.0, w=["QW"])
    C.memset("pool", Vs[:, :, 64:65], 1.0, w=["Vs"])
    C.memset("pool", Vw[:, :, 64:65], 1.0, w=["Vw"])
    C.memset("pool", rhsC[:, :, :], 0.0, w=["rhsC"])
    C.memset("pool", rhsC[:, :, 192:193], 1.0, w=["rhsC"])
    for q in range(8):
        s = stage[par % 2]; sk = "xt%d" % (par % 2); par += 1
        C.dma(s[:, 0:128], wmaskd[q, :, :], w=[sk])
        C.cp("dve", wmask[:, q, :], s[:, 0:128], r=[sk], w=["wmask"])
    for q in range(4):
        s = stage[par % 2]; sk = "xt%d" % (par % 2); par += 1
        C.dma(s[:, 0:128], cmaskd[q, :, :], w=[sk])
        C.cp("dve", cmask[:, q, :], s[:, 0:128], r=[sk], w=["cmask"])
        if q == 0:
            s = stage[par % 2]; sk = "xt%d" % (par % 2); par += 1
            C.dma(s[:, 0:128], cmaskd[4, :, :], w=[sk])
            C.cp("dve", cmask[:, 4, :], s[:, 0:128], r=[sk], w=["cmask"])
        s = stage[par % 2]; sk = "xt%d" % (par % 2); par += 1
        C.dma(s[:, 0:128], impM[q, :, :], w=[sk])
        C.cp("dve", rhsC[:, q, 64:192], s[:, 0:128], r=[sk], w=["rhsC"])
    C.dma(convw_t[:], convw[:, :, :], w=["convw_t"])
    C.dma(convv_t[:], convv[:, :, :], w=["convv_t"])
    C.dma(stage[0][:, 0:128], identd[:, :], w=["xt0"])
    for cc in range(2):
        for k in range(31):
            if k % 2 == 0:
                C.ts("dve", Dg[:, cc, k, :], stage[0][:, 0:128], convw_t[:, cc, k:k + 1], None, ALU.mult,
                     r=["xt0", "convw_t"], w=["Dg"])
            else:
                C.act(Dg[:, cc, k, :], stage[0][:, 0:128], AF.Copy, r=["xt0", "convw_t"], w=["Dg"],
                      scale=convw_t[:, cc, k:k + 1])
    for wi_, (Wt_, wk_) in enumerate(((W1k, "W1k"), (W1v, "W1v"))):
        for q in range(2):
            C.dma(stage[q][0:64, 0:1024], w1d[64 * wi_:64 * wi_ + 64, q * 1024:(q + 1) * 1024], w=["xt%d" % q])
            C.cp("dve", Wt_[:].rearrange("p r e -> p (r e)")[:, q * 1024:(q + 1) * 1024], stage[q][0:64, 0:1024],
                 r=["xt%d" % q], w=[wk_])
    C.dma(stage[0][0:64, 0:128], w2d[:, :], w=["xt0"])
    C.cp("dve", W2[:], stage[0][0:64, 0:128], r=["xt0"], w=["W2"])
    C.dma(stage[0][0:64, 0:32], ped[0:64, :], w=["xt0"])
    C.cp("dve", peTk[:], stage[0][0:64, 0:32], r=["xt0"], w=["peTk"])
    C.dma(stage[1][0:64, 0:32], ped[64:128, :], w=["xt1"])
    C.cp("dve", peTv[:], stage[1][0:64, 0:32], r=["xt1"], w=["peTv"])
    for wi_, (Wt_, wk_, pt_, pk_) in enumerate(((W1k, "W1k", peTk, "peTk"), (W1v, "W1v", peTv, "peTv"))):
        for r_ in range(32):
            C.mm(psK[0:64, 0:1], Wt_[:, r_, :], pt_[:, r_:r_ + 1], start=(r_ == 0), stop=(r_ == 31),
                 r=[wk_, pk_], w=["psK"])
        C.cp("dve", cbias[:, wi_:wi_ + 1], psK[0:64, 0:1], w=["psK", "cbias"])
    C.dma(lng_bc[:], sgv[0:1, :].partition_broadcast(128), w=["lng_bc"])
    C.dma(lnb_bc[:], sgv[1:2, :].partition_broadcast(128), w=["lnb_bc"])
    C.dma(tril[:], trild[:, :], w=["tril"])
    C.dma(stage[0][:, 0:512], sgw.rearrange("p g t -> p (g t)"), w=["xt0"])
    for g in range(4):
        C.tt("dve", wsT[:, g, :], stage[0][:, g * 128:(g + 1) * 128], tril[:], ALU.mult, r=["xt0", "tril"], w=["wsT"])
        C.dma(Bs[(g % 2) * 64:(g % 2) * 64 + 64, g // 2, :], sgb[g:g + 1, :].partition_broadcast(64), w=["Bs"])
    C.dma(stage[1][:, 0:256], poolw.rearrange("p q d -> p (q d)"), w=["xt1"])
    C.cp("dve", Wbd[:].rearrange("p q d -> p (q d)"), stage[1][:, 0:256], r=["xt1"], w=["Wbd"])
    C.dma(poolsc_t[:], poolsc[:, :], w=["poolsc_t"])
    C.dma(rc0[:], rcnt0[:, :, :], w=["rc0"])
    C.dma(rcg[:], rcntg[:, :, :], w=["rcg"])

    def rms_to_bf(xt_ap, xk, out_bf, ok, np_, col):
        C.act(junk[0:np_, :], xt_ap, AF.Square, r=[xk], w=["junk", "ssq%d" % col], accum=ssq[0:np_, col:col + 1])
        C.act(rstd[0:np_, col:col + 1], ssq[0:np_, col:col + 1], AF.Ln, r=["ssq%d" % col, "epsb"], w=["rstd%d" % col],
              scale=1.0 / D, bias=epsb[0:np_, 0:1])
        C.act(rstd[0:np_, col:col + 1], rstd[0:np_, col:col + 1], AF.Exp, r=["rstd%d" % col], w=["rstd%d" % col], scale=-0.5)
        C.act(out_bf, xt_ap, AF.Copy, r=[xk, "rstd%d" % col], w=[ok], scale=rstd[0:np_, col:col + 1])

    epsb = sb("epsb", [128, 2])
    C.memset("dve", epsb[:, 0:1], RMS_EPS, w=["epsb"])
    C.memset("dve", epsb[:, 1:2], LN_EPS, w=["epsb"])

    def kv_tile(m):
        p = m % 2
        xk, hk, tk = "xt%d" % p, "hnb%d" % p, "hT%d" % p
        C.dma(xt[p][:], xg_tile(m)[:, :], w=[xk])
        C.dma(rC[p][:], ropeC[:, m * 128:(m + 1) * 128], w=["rC%d" % p])
        C.dma(rS[p][:], ropeS[:, m * 128:(m + 1) * 128], w=["rS%d" % p])
        yield
        C.act(junk[:, :], xt[p][:], AF.Square, r=[xk], w=["junk", "ssq%d" % p], accum=ssq[:, p:p + 1])
        yield
        C.act(rstd[:, p:p + 1], ssq[:, p:p + 1], AF.Ln, r=["ssq%d" % p, "epsb"], w=["rstd%d" % p],
              scale=1.0 / D, bias=epsb[:, 0:1])
        yield
        C.act(rstd[:, p:p + 1], rstd[:, p:p + 1], AF.Exp, r=["rstd%d" % p], w=["rstd%d" % p], scale=-0.5)
        yield
        C.ts("dve", hnb[p][:], xt[p][:], rstd[:, p:p + 1], None, ALU.mult, r=[xk, "rstd%d" % p], w=[hk])
        yield
        for half in range(2):
            for q in range(4):
                kc = 4 * half + q
                C.tr(psT[:, q * 128:(q + 1) * 128], hnb[p][:, kc * 128:(kc + 1) * 128], ident[:], r=[hk, "ident"], w=["psT"])
            yield
            C.cp("dve", hT[p][:, 4 * half:4 * half + 4, :], psT[:, 0:512].rearrange("p (k t) -> p k t", k=4), w=["psT", tk])
            yield
        for f in range(3):
            for kc in range(8):
                C.mm(psK[:, f * 128:(f + 1) * 128], wkv[:, kc, f * 128:(f + 1) * 128], hT[p][:, kc, :],
                     start=(kc == 0), stop=(kc == 7), r=["wkv", tk], w=["psK"])
            yield
        for kc in range(8):
            C.mm(psK[:, 384:512], hT[p][:, kc, :], wkv[:, kc, 384:512], start=(kc == 0), stop=(kc == 7),
                 r=["wkv", tk], w=["psK"])
        yield
        grp = (m // 4) % 2
        col0 = 16 + (m % 4) * 128
        C.cp("act", kvk[grp][:, col0:col0 + 128], psK[0:64, 0:128], w=["psK", "kvk%d" % grp])
        C.cp("act", kvv[grp][:, col0:col0 + 128], psK[64:128, 0:128], w=["psK", "kvv%d" % grp])
        C.tt("dve", ropet[:, 0, :], psK[:, 128:256], rC[p][:], ALU.mult, r=["rC%d" % p], w=["psK", "ropet0"])
        C.tt("dve", ropet[:, 1, :], psK[:, 256:384], rS[p][:], ALU.mult, r=["rS%d" % p], w=["psK", "ropet1"])
        yield
        C.cp("dve", Vs[:, m, 0:64], psK[:, 384:448], w=["psK", "Vs%d" % m])
        C.cp("dve", Vw[:, m % 12, 0:64], psK[:, 448:512], w=["psK", "Vw%d" % (m % 12)])
        yield
        C.tt("pool", Kaug[0:64, m * 128:(m + 1) * 128], ropet[0:64, 0, :], ropet[0:64, 1, :], ALU.add,
             r=["ropet0", "ropet1"], w=["Kaug%d" % m])
        C.tt("pool", Kw[0:64, m % 12, :], ropet[64:128, 0, :], ropet[64:128, 1, :], ALU.add,
             r=["ropet0", "ropet1"], w=["Kw%d" % (m % 12)])
        yield

    def compress(j):
        g = j % 2
        n0 = 32 * j - 1
        c0 = 32 + n0
        for which, (Wt_, wk_, kb, kk, gT, gk_) in enumerate(((W1k, "W1k", kvk[g], "kvk%d" % g, gkT, "gkT"),
                                                           (W1v, "W1v", kvv[g], "kvv%d" % g, gvT, "gvT"))):
            for r_ in range(32):
                C.mm(psK[0:64, which * 32:(which + 1) * 32], Wt_[:, r_, :], kb[:, r_:r_ + 16 * 31 + 1:16],
                     start=(r_ == 0), stop=(r_ == 31), r=[wk_, kk], w=["psK"])
            yield
            cgt = cg1[:, which * 32:(which + 1) * 32]
            cgs = cg2[:, which * 32:(which + 1) * 32]
            k1, k2 = ["cg1_%d" % which], ["cg2_%d" % which]
            C.ts("dve", cgt, psK[0:64, which * 32:(which + 1) * 32], cbias[:, which:which + 1], None, ALU.add,
                 r=["cbias"], w=["psK"] + k1)
            C.tt("dve", cgs, cgt, cgt, ALU.mult, r=k1, w=k2)
            C.ts("dve", cgs, cgs, 0.044715, 1.0, ALU.mult, ALU.add, r=k2, w=k2)
            C.tt("dve", cgs, cgs, cgt, ALU.mult, r=k2 + k1, w=k2)
            C.act(cgs, cgs, AF.Exp, r=k2, w=k2, scale=-GELU_C)
            C.ts("dve", cgs, cgs, 1.0, None, ALU.add, r=k2, w=k2)
            C.recip(cgs, cgs, r=k2, w=k2)
            C.tt("dve", gT[:, c0:c0 + 32], cgt, cgs, ALU.mult, r=k1 + k2, w=[gk_])
            other = (kvk, kvv)[which][1 - g]
            C.cp("pool", other[:, 0:16], kb[:, 512:528], r=[kk], w=[("kvk%d", "kvv%d")[which] % (1 - g)])
            yield
        C.mm(psK[0:64, 64:96], W2[:, 0:64], gkT[:, c0:c0 + 32], r=["W2", "gkT"], w=["psK"])
        yield
        C.cp("dve", kcT[0:64, c0:c0 + 32], psK[0:64, 64:96], w=["psK", "kcT"])
        yield
        chunks = sorted(set([max(n0, 0) // 128, (n0 + 31) // 128]))
        for ci in chunks:
            C.mm(psK[:, 128:192], gvT[:, 32 + ci * 128:32 + (ci + 1) * 128], W2[:, 64:128], r=["W2", "gvT"], w=["psK"])
            yield
            C.cp("dve", rhsC[:, ci, 0:64], psK[:, 128:192], w=["psK", "rhsC"])
            yield

    def local_pre(j):
        for cand in range(4):
            m_ = 4 * j - 1 + cand
            if m_ < 0:
                C.memset("pool", yo[0:32, :], 0.0, w=["yo"])
            else:
                C.dma(yo[cand * 32:(cand + 1) * 32, :], xg_tile(m_)[96:128, :], w=["yo"])
        C.dma(qCt[:], qC[j, :, :], w=["qCt"])
        C.dma(qSt[:], qS[j, :, :], w=["qSt"])
        yield
        rms_to_bf(yo[:], "yo", hnc[:], "hnc", 128, 3)
        yield
        for kc in range(8):
            C.mm(psG[:, kc * 32:(kc + 1) * 32], hnc[:, kc * 128:(kc + 1) * 128], selm[:], r=["hnc", "selm"], w=["psG"])
        yield
        C.cp("act", hTo[:, :, 0:32], psG[:, 0:256].rearrange("p (k t) -> p k t", k=8), w=["psG", "hToH"])
        yield

    def local(j):
        yield
        C.dma(xo[:], xown_tile(j)[:, :], w=["xo"])
        C.dma(selAt[:], selA[j, :, :], w=["selAt"])
        C.dma(selBt[:], selB[j, :, :], w=["selBt"])
        rms_to_bf(xo[:], "xo", hno[:], "hno", 128, 2)
        yield
        for half in range(2):
            for q in range(4):
                kc = 4 * half + q
                C.tr(psT[:, 512 + q * 128:512 + (q + 1) * 128], hno[:, kc * 128:(kc + 1) * 128], ident[:],
                     r=["hno", "ident"], w=["psT"])
            yield
            C.cp("act", hTo[:, 4 * half:4 * half + 4, 32:160], psT[:, 512:1024].rearrange("p (k t) -> p k t", k=4),
                 w=["psT", "hTo"])
            yield

        def proj(chunk, ncol, out_ap):
            c0 = 160 - ncol
            for kc in range(8):
                C.mm(out_ap, wloc[:, kc, chunk * 128:(chunk + 1) * 128], hTo[:, kc, c0:160],
                     start=(kc == 0), stop=(kc == 7), r=["wloc", "hTo", "hToH"], w=["psG"])

        yield
        for cc in range(2):
            proj(2 + cc, 160, psG[:, 0:160])
            C.act(sig[:, cc, :], psG[:, 0:160], AF.Sigmoid, w=["psG", "sig"])
        for cc in range(2):
            proj(cc, 160, psG[:, 0:160])
            C.tt("dve", ub[:, cc, :], psG[:, 0:160], sig[:, cc, :], ALU.mult, r=["sig"], w=["psG", "ub"])
        yield
        for cc in range(2):
            proj(4 + cc, 160, psG[:, 0:160])
            C.cp("act", zdT[:, cc, :], psG[:, 0:160], w=["psG", "zdT"])
        yield
        for cc in range(2):
            proj(6 + cc, 128, psG[:, 0:128])
            C.act(usg[:, cc, :], psG[:, 0:128], AF.Gelu_apprx_tanh, w=["psG", "usg"])
        yield
        for hp in range(2):
            for kc in range(8):
                C.mm(psG[:, 0:128], wloc[:, kc, 1024 + hp * 128:1024 + (hp + 1) * 128], hTo[:, kc, 32:160],
                     start=(kc == 0), stop=(kc == 7), r=["wloc", "hTo", "hToH"], w=["psG"])
            for kc in range(8):
                C.mm(psG[:, 128:256], wloc[:, kc, 1280 + hp * 128:1280 + (hp + 1) * 128], hTo[:, kc, 32:160],
                     start=(kc == 0), stop=(kc == 7), r=["wloc", "hTo", "hToH"], w=["psG"])
            C.tt("dve", qt1[:], psG[:, 0:128], qCt[:], ALU.mult, r=["qCt"], w=["psG", "qt1"])
            C.tt("dve", qt2[:], psG[:, 128:256], qSt[:], ALU.mult, r=["qSt"], w=["psG", "qt2"])
            C.act(QC[0:64, 2 * hp, :], psG[0:64, 0:128], AF.Copy, w=["psG", "QC"], scale=0.125)
            C.act(QC[0:64, 2 * hp + 1, :], psG[64:128, 0:128], AF.Copy, w=["psG", "QC"], scale=0.125)
            yield
            C.tt("dve", qsum[:], qt1[:], qt2[:], ALU.add, r=["qt1", "qt2"], w=["qsum"])
            yield
            for v in range(2):
                C.cp("dve", QR[0:64, v, 2 * hp, :], qsum[0:64, :], r=["qsum"], w=["QRq"])
                C.cp("act", QR[0:64, v, 2 * hp + 1, :], qsum[64:128, :], r=["qsum"], w=["QRq"])
            C.cp("pool", QW[0:64, 2 * hp, :], qsum[0:64, :], r=["qsum"], w=["QW"])
            C.cp("act", QW[0:64, 2 * hp + 1, :], qsum[64:128, :], r=["qsum"], w=["QW"])
        yield
        for kc in range(8):
            C.mm(psG[:, 0:268], hTo[:, kc, 32:160], wloc[:, kc, 1536:1804], start=(kc == 0), stop=(kc == 7),
                 r=["wloc", "hTo", "hToH"], w=["psG"])
        C.act(gts[:], psG[:, 256:268], AF.Sigmoid, w=["psG", "gts"])
        C.act(vg[:], psG[:, 0:256], AF.Gelu_apprx_tanh, w=["psG", "vg"])
        C.S.op("dve", lambda e: e.bn_stats(out=bnst[:], in_=vg[:]), ["vg"], ["bnst"])
        C.S.op("dve", lambda e: e.bn_aggr(out=bnag[:], in_=bnst[:]), ["bnst"], ["bnag"])
        C.act(lnr[:, 0:1], bnag[:, 1:2], AF.Ln, r=["bnag", "epsb"], w=["lnr"], bias=epsb[:, 1:2])
        C.act(lnr[:, 1:2], lnr[:, 0:1], AF.Exp, r=["lnr"], w=["lnr1"], scale=-0.5)
        C.ts("dve", vn[:], vg[:], bnag[:, 0:1], lnr[:, 1:2], ALU.subtract, ALU.mult, r=["vg", "bnag", "lnr1"], w=["vn"])
        C.tt("dve", vn[:], vn[:], lng_bc[:], ALU.mult, r=["vn", "lng_bc"], w=["vn"])
        C.tt("dve", vbf[:], vn[:], lnb_bc[:], ALU.add, r=["vn", "lnb_bc"], w=["vbf"])
        yield
        for cc in range(2):
            for k in range(31):
                C.mm(psG[:, 0:128], Dg[:, cc, k, :], ub[:, cc, 2 + k:2 + k + 128], start=(k == 0), stop=(k == 30),
                     r=["Dg", "ub"], w=["psG"])
            C.act(ycv[:, cc, :], psG[:, 0:128], AF.Identity, r=["convv_t"], w=["psG", "ycv"], bias=convv_t[:, cc, 0:1])
            C.act(ysq[:, cc, :], psG[:, 0:128], AF.Square, r=["convv_t"], w=["psG", "ysq"], bias=convv_t[:, cc, 0:1])
        for src, sk_, c0_ in ((ycv, "ycv", 0), (ysq, "ysq", 128)):
            C.cp("dve", shi[:], src[:], r=[sk_], w=["shi"])
            C.tt("dve", slo[:], src[:], shi[:], ALU.subtract, r=[sk_, "shi"], w=["slo"])
            n_ = 0
            for part, pk_ in ((shi, "shi"), (slo, "slo")):
                for cc in range(2):
                    C.mm(psG[:, c0_:c0_ + 128], ones_b[:], part[:, cc, :], start=(n_ == 0), stop=(n_ == 3),
                         r=["ones_b", pk_], w=["psG"])
                    n_ += 1
        C.act(cmean[:], psG[:, 0:128], AF.Copy, w=["psG", "cmean"], scale=1.0 / 256)
        C.tt("dve", cmsq[:], cmean[:], cmean[:], ALU.mult, r=["cmean"], w=["cmsq"])
        C.stt(cvar[:], psG[:, 128:256], 1.0 / 256, cmsq[:], ALU.mult, ALU.subtract, r=["cmsq"], w=["psG", "cvar"])
        C.act(crstd[:], cvar[:], AF.Ln, r=["cvar", "epsb"], w=["crstd"], bias=epsb[:, 1:2])
        C.act(crstd[:], crstd[:], AF.Exp, r=["crstd"], w=["crstd"], scale=-0.5)
        for cc in range(2):
            C.tt("dve", cyn[:, cc, :], ycv[:, cc, :], cmean[:], ALU.subtract, r=["ycv", "cmean"], w=["cyn%d" % cc])
            C.tt("dve", cyn[:, cc, :], cyn[:, cc, :], crstd[:], ALU.mult, r=["cyn%d" % cc, "crstd"], w=["cyn%d" % cc])
            C.act(yT[:, cc, :], cyn[:, cc, :], AF.Silu, r=["cyn%d" % cc, "convv_t"], w=["yT%d" % cc],
                  scale=convv_t[:, cc, 1:2], bias=convv_t[:, cc, 2:3])
        yield
        for g in range(4):
            q = g // 2
            C.mm(psG[:, g * 128:(g + 1) * 128], vbf[:, q * 128:(q + 1) * 128], wsT[:, g, :], r=["vbf", "wsT"], w=["psG"])
        for g in range(4):
            q = g // 2
            lo = (g % 2) * 64
            C.tt("dve", sgt[lo:lo + 64, q, :], psG[lo:lo + 64, g * 128:(g + 1) * 128], Bs[lo:lo + 64, q, :], ALU.add,
                 r=["Bs"], w=["psG", "sgt%d" % g])
            C.tt("pool", yT[lo:lo + 64, 4 + q, :], sgt[lo:lo + 64, q, :], usg[lo:lo + 64, q, :], ALU.mult,
                 r=["sgt%d" % g, "usg"], w=["yT%d" % (4 + q)])
        yield
        C.tt("pool", s2[:, :, 1:160], zdT[:, :, 1:160], zdT[:, :, 0:159], ALU.add, r=["zdT"], w=["s2"])
        C.tt("pool", s4[:, :, 3:160], s2[:, :, 3:160], s2[:, :, 1:158], ALU.add, r=["s2"], w=["s4"])
        C.tt("pool", s8[:, :, 7:160], s4[:, :, 7:160], s4[:, :, 3:156], ALU.add, r=["s4"], w=["s8"])
        C.tt("pool", s16[:, :, 15:160], s8[:, :, 15:160], s8[:, :, 7:152], ALU.add, r=["s8"], w=["s16"])
        rc = rc0 if j == 0 else rcg
        srcs = [(s2, "s2", 0, 0), (s4, "s4", 64, 0), (s8, "s8", 0, 1), (s16, "s16", 64, 1)]
        for gi, (sbuf_, sk, lo, q) in enumerate(srcs):
            C.tt("dve", plf[lo:lo + 64, q, :], sbuf_[lo:lo + 64, q, 32:160], rc[lo:lo + 64, q, :], ALU.mult,
                 r=[sk, "rc0", "rcg"], w=["plf%d" % gi])
            C.tt("dve", plb[lo:lo + 64, q, :], plf[lo:lo + 64, q, :], zdT[lo:lo + 64, q, 32:160], ALU.subtract,
                 r=["plf%d" % gi, "zdT"], w=["plb%d" % q])
        for q in range(2):
            C.mm(psG[:, q * 128:(q + 1) * 128], Wbd[:, q, :], plb[:, q, :], r=["Wbd", "plb%d" % q], w=["psG"])
        for q in range(2):
            C.act(yT[:, 6 + q, :], psG[:, q * 128:(q + 1) * 128], AF.Copy, r=["poolsc_t"], w=["psG", "yT%d" % (6 + q)],
                  scale=poolsc_t[:, q:q + 1])

    pcount = [0]

    def nextP():
        i = pcount[0] % 4
        pcount[0] += 1
        return Pb[i], "Pb%d" % i

    scount = [0]

    def nextS(nb=2):
        i = scount[0] % nb
        scount[0] += 1
        if i == 2:
            return psC, "psC"
        return psS[i], "psS%d" % i

    def nsa(j):
        L = (32 * j + 30) // 128

        def pipeline(chunks, depth=1):
            state = []
            for i, (A, B, Cc) in enumerate(chunks):
                while len(state) < min(len(chunks), i + depth + 1):
                    state.append(chunks[len(state)][0]())
                st = state[i]
                B(st)
                yield
                Cc(st)
                yield

        st_p = {}

        def mk_cmp(ci):
            def A():
                S_, sk = nextS()
                C.mm(S_[:, :], kcT[:, 32 + ci * 128:32 + (ci + 1) * 128], QC[:].rearrange("p h t -> p (h t)"),
                     r=["kcT", "QC"], w=[sk])
                return S_, sk

            def B(st):
                S_, sk = st
                P_, pk = Pb[ci], "Pb%d" % ci
                C.act(P_[:].rearrange("p h t -> p (h t)"), S_[:, :], AF.Exp, w=[sk, pk])
                if ci == L:
                    C.tt("dve", P_[:], P_[:], cmask[:, j % 4:j % 4 + 1, :].to_broadcast([128, 4, 128]), ALU.mult,
                         r=[pk, "cmask"], w=[pk])
                elif ci == L - 1 and j % 4 == 0:
                    C.tt("dve", P_[:], P_[:], cmask[:, 4:5, :].to_broadcast([128, 4, 128]), ALU.mult,
                         r=[pk, "cmask"], w=[pk])

            def Cc(st):
                pass
            return A, B, Cc

        yield from pipeline([mk_cmp(ci) for ci in range(L + 1)])
        for hp in range(2):
            for ci in range(L + 1):
                for h in (2 * hp, 2 * hp + 1):
                    c0 = (h % 2) * 193
                    C.mm(psC[:, c0:c0 + 193], Pb[ci][:, h, :], rhsC[:, ci, :], start=(ci == 0 and h % 2 == 0), stop=(ci == L),
                         r=["Pb%d" % ci, "rhsC"], w=["psC"], skip=True)
            yield
            for h in (2 * hp, 2 * hp + 1):
                c0 = (h % 2) * 193
                C.ts("dve", den[:, h:h + 1], psC[:, c0 + 192:c0 + 193], 1e-30, None, ALU.max, w=["psC", "den%d" % hp])
            C.recip(rden[:, 2 * hp:2 * hp + 2], den[:, 2 * hp:2 * hp + 2], r=["den%d" % hp], w=["rden%d" % hp])
            yield
            for h in (2 * hp, 2 * hp + 1):
                c0 = (h % 2) * 193
                if h == 0:
                    C.ts("dve", imp[:], psC[:, c0 + 64:c0 + 192], rden[:, 0:1], None, ALU.mult, r=["rden0"],
                         w=["psC", "imp"])
                else:
                    C.stt(imp[:], psC[:, c0 + 64:c0 + 192], rden[:, h:h + 1], imp[:], ALU.mult, ALU.add,
                          r=["rden%d" % hp, "imp"], w=["psC", "imp"])
            C.tt("dve", gsc[:, 2 * hp:2 * hp + 2], rden[:, 2 * hp:2 * hp + 2], gts[:, 6 * hp:6 * hp + 6:3], ALU.mult,
                 r=["rden%d" % hp, "gts"], w=["gscc%d" % hp])
            for h in (2 * hp, 2 * hp + 1):
                c0 = (h % 2) * 193
                C.ts("dve", Ynsa[:, h * 64:(h + 1) * 64], psC[:, c0:c0 + 64], gsc[:, h:h + 1], None, ALU.mult,
                     r=["gscc%d" % hp], w=["psC", "Ynsa"])
            yield
        C.tt("dve", imp2[:], imp[:], selAt[:], ALU.mult, r=["imp", "selAt"], w=["imp2"])
        C.tt("dve", imp2[:], imp2[:], selBt[:], ALU.add, r=["imp2", "selBt"], w=["imp2"])
        yield
        C.S.op("dve", lambda e: e.max(out=top8[:, 0:8], in_=imp2[:]), ["imp2"], ["top8a"])
        C.S.op("dve", lambda e: e.match_replace(out=impw[:], in_to_replace=top8[:, 0:8], in_values=imp2[:],
                                                imm_value=-3e38), ["imp2", "top8a"], ["impw"])
        yield
        C.S.op("dve", lambda e: e.max(out=top8[:, 8:16], in_=impw[:]), ["impw"], ["top8b"])
        C.ts("dve", selt1[:], imp2[:], top8[:, 15:16], None, ALU.is_ge, r=["imp2", "top8b"], w=["selt1"])
        C.ts("dve", selt2[:], imp2[:], -1e29, -NEG, ALU.is_gt, ALU.mult, r=["imp2"], w=["selt2"])
        yield
        C.tt("dve", selt1[:], selt1[:], selt2[:], ALU.mult, r=["selt1", "selt2"], w=["selt1"])
        C.ts("dve", selbf[:], selt1[:], NEG, None, ALU.add, r=["selt1"], w=["selbf"])
        yield

        if j + 1 < nslots:
            yield from local_pre(j + 1)
        wl = list(range(max(0, 4 * j - 4), 4 * j + 4))

        def mk_win(wi, m):
            def A():
                S_, sk = nextS(3)
                C.mm(S_[:, :], Kw[:, m % 12, :], QW[:].rearrange("p h t -> p (h t)"),
                     r=["Kw%d" % (m % 12), "Kw", "QW"], w=[sk])
                return S_, sk

            def B(st):
                S_, sk = st
                P_, pk = nextP()
                C.act(P_[:].rearrange("p h t -> p (h t)"), S_[:, :], AF.Exp, w=[sk, pk])
                C.tt("dve", P_[:], P_[:], wmask[:, 4 + m - 4 * j:5 + m - 4 * j, :].to_broadcast([128, 4, 128]), ALU.mult,
                     r=[pk, "wmask"], w=[pk])
                st_p[("w", wi)] = (P_, pk)

            def Cc(st):
                P_, pk = st_p[("w", wi)]
                for h in range(4):
                    C.mm(psWin[:, h * 65:(h + 1) * 65], P_[:, h, :], Vw[:, m % 12, :], start=(wi == 0 and h == 0),
                         stop=(wi == len(wl) - 1), r=[pk, "Vw%d" % (m % 12), "Vw"], w=["psWin"], skip=True)
            return A, B, Cc

        scount[0] = 0
        yield from pipeline([mk_win(wi, m) for wi, m in enumerate(wl)], depth=2)
        C.tr(psT[:, 512:640], selbf[:], ident[:], r=["selbf", "ident"], w=["psT"])
        yield
        C.cp("dve", QR[64:128, 0, :, :], psT[0:64, 512:640].unsqueeze(1).to_broadcast([64, 4, 128]), w=["psT", "QRb"])
        C.cp("act", QR[64:128, 1, :, :], psT[64:128, 512:640].unsqueeze(1).to_broadcast([64, 4, 128]), w=["psT", "QRb"])
        yield

        nsel = 4 * j + 4

        def mk_sel(m):
            def A():
                S_, sk = nextS(3)
                v = 0 if m < 32 else 1
                C.mm(S_[:, :], Kaug[:, m * 128:(m + 1) * 128], QR[:, v, :, :].rearrange("p h t -> p (h t)"),
                     r=["Kaug%d" % m, "Kaug_ind", "QRq", "QRb"], w=[sk])
                return S_, sk

            def B(st):
                S_, sk = st
                P_, pk = nextP()
                C.act(P_[:].rearrange("p h t -> p (h t)"), S_[:, :], AF.Exp, w=[sk, pk])
                if m >= 4 * j:
                    C.tt("dve", P_[:], P_[:], wmask[:, 4 + m - 4 * j:5 + m - 4 * j, :].to_broadcast([128, 4, 128]), ALU.mult,
                         r=[pk, "wmask"], w=[pk])
                st_p[("s", m)] = (P_, pk)

            def Cc(st):
                P_, pk = st_p[("s", m)]
                for h in range(4):
                    C.mm(psSel[:, h * 65:(h + 1) * 65], P_[:, h, :], Vs[:, m, :], start=(m == 0 and h == 0),
                         stop=(m == nsel - 1), r=[pk, "Vs%d" % m, "Vs"], w=["psSel"], skip=True)
            return A, B, Cc

        yield from pipeline([mk_sel(m) for m in range(nsel)], depth=2)
        scount[0] = 0
        C.cp("dve", den[:, 4:8], psSel[:, 64:260:65], w=["psSel", "den"])
        C.cp("dve", den[:, 8:12], psWin[:, 64:260:65], w=["psWin", "den"])
        C.recip(rden[:, 4:12], den[:, 4:12], r=["den"], w=["rden2"])
        C.tt("dve", gsc[:, 4:8], rden[:, 4:8], gts[:, 1:12:3], ALU.mult, r=["rden2", "gts"], w=["gsc1"])
        C.tt("dve", gsc[:, 8:12], rden[:, 8:12], gts[:, 2:12:3], ALU.mult, r=["rden2", "gts"], w=["gsc1"])
        for h in range(4):
            C.stt(Ynsa[:, h * 64:(h + 1) * 64], psSel[:, h * 65:h * 65 + 64], gsc[:, 4 + h:5 + h], Ynsa[:, h * 64:(h + 1) * 64],
                  ALU.mult, ALU.add, r=["gsc1", "Ynsa"], w=["psSel", "Ynsa"])
        for h in range(4):
            C.stt(Ynb[:, h * 64:(h + 1) * 64], psWin[:, h * 65:h * 65 + 64], gsc[:, 8 + h:9 + h], Ynsa[:, h * 64:(h + 1) * 64],
                  ALU.mult, ALU.add, r=["gsc1", "Ynsa"], w=["psWin", "Ynb"])
        yield
        for q in range(2):
            C.tr(psT[:, 512 + q * 128:512 + (q + 1) * 128], Ynb[:, q * 128:(q + 1) * 128], ident[:], r=["Ynb", "ident"], w=["psT"])
        yield
        C.cp("act", yT[:, 2:4, :], psT[:, 512:768].rearrange("p (q t) -> p q t", q=2), w=["psT", "yT2", "yT3"])
        yield

    def outproj(j):
        yk = ["yT%d" % f for f in range(8)]
        for half in range(2):
            for f in range(8):
                C.mm(psG[:, :], yT[:, f, :], wo[:, f, half * 512:(half + 1) * 512], start=(f == 0), stop=(f == 7),
                     r=yk + ["wo"], w=["psG"])
            yield
            C.cp("dve", yo[:, half * 512:(half + 1) * 512], psG[:, :], w=["psG", "yo"])
            yield
        C.act(junk[:, :], yo[:], AF.Square, r=["yo"], w=["junk", "oss"], accum=oss[:, 0:1])
        yield
        C.act(oss[:, 1:2], oss[:, 0:1], AF.Ln, r=["oss", "epsb"], w=["oss1"], scale=1.0 / D, bias=epsb[:, 0:1])
        C.act(oss[:, 2:3], oss[:, 1:2], AF.Exp, r=["oss1"], w=["oss2"], scale=-0.5)
        C.stt(yo[:], yo[:], oss[:, 2:3], gpost_bc[:], ALU.mult, ALU.mult, r=["yo", "oss2", "gpost_bc"], w=["yo"])
        C.tt("dve", yo[:], yo[:], xo[:], ALU.add, r=["yo", "xo"], w=["yo"])
        C.dma(xm_own[j, :, :], yo[:], r=["yo"])
        C.dma(xm_tail[2 * j:2 * j + 2, :], yo[126:128, :], r=["yo"])
        yield

    def stream_b(j):
        for m in range(4 * j, 4 * j + 4):
            yield from kv_tile(m)
        yield from compress(j)

    def stream_a(j):
        yield from local(j)
        yield from nsa(j)
        yield from outproj(j)

    C.S.wait_cc()
    for _ in stream_b(0):
        pass
    for _ in local_pre(0):
        pass
    for j in range(nslots):
        A_ = stream_a(j)
        B_ = stream_b(j + 1) if j + 1 < nslots else None
        doneA = doneB = B_ is None and False
        doneB = B_ is None
        while not doneA:
            try:
                next(A_)
            except StopIteration:
                doneA = True
            if not doneB:
                try:
                    next(B_)
                except StopIteration:
                    doneB = True
        while not doneB:
            try:
                next(B_)
            except StopIteration:
                doneB = True
    C.end_phase()


def ffn_phase(C, T, l):
    C.begin_phase("_f%d" % l)
    nc = C.nc
    xm_own = T["xm_own"]; tail_g = T["tail_g_%d" % l]; out_tile = T["ffn_out_tile"]
    w_up, w_dn, g2, gpost, cwd = (T[k + "_%d" % l] for k in ("w_up", "w_dn", "g2", "gpost2", "cwd"))
    identd = T["identd"]; selm2d = T["selm2"]

    sb = C.sb
    wu = sb("wu", [128, 8, 2 * FF], BF16)
    wd = sb("wd", [128, 22, D], BF16)
    stage = [sb("stage%d" % i, [128, 1024]) for i in range(3)]
    g2_t = sb("g2_t", [128, 8])
    gpost_bc = sb("gpost_bc", [128, D])
    cw = sb("cw", [128, NFC, 4])
    ident = sb("ident", [128, 128], BF16)
    epsb = sb("epsb", [128, 1])
    xs = [sb("xs%d" % i, [128, D]) for i in range(4)]
    xh = sb("xh", [8, D])
    selm2 = sb("selm2", [8, 2], BF16)
    hn = sb("hn", [128, D], BF16)
    hnh = sb("hnh", [8, D], BF16)
    junk = sb("junk", [128, D], BF16)
    ssq = sb("ssq", [128, 4]); rstd = sb("rstd", [128, 4])
    h2Tb = [sb("h2T%d" % i, [128, 8, 2, 130], BF16) for i in range(2)]
    actT = sb("actT", [128, 22, 2, 128], BF16)
    gc = [sb("gc%d" % i, [128, 2, 128]) for i in range(2)]
    uc = [sb("uc%d" % i, [128, 2, 128]) for i in range(2)]
    gg = [sb("gg%d" % i, [128, 2, 128]) for i in range(2)]
    yo = sb("yo", [128, D])
    oss = sb("oss", [128, 4])

    psU = [C.ps("psU%d" % i, [128, 512]) for i in range(4)]
    psD = [C.ps("psD%d" % i, [128, 512]) for i in range(2)]
    psT = C.ps("psT", [128, 1024], BF16)
    psH = C.ps("psH", [128, 512])

    C.dma(stage[1][0:8, 0:2], selm2d[:, :], w=["stage1"])
    C.cp("dve", selm2[:], stage[1][0:8, 0:2], r=["stage1"], w=["selm2"])
    C.dma(g2_t[:], g2[:, :], w=["gscale"])
    C.dma(gpost_bc[:], gpost.partition_broadcast(128), w=["gpost_bc"])
    C.dma(cw[:], cwd[:, :, :], w=["cw"])
    C.dma(stage[0][:, 0:128], identd[:, :], w=["stage0"])
    C.cp("dve", ident[:], stage[0][:, 0:128], r=["stage0"], w=["ident"])
    C.memset("dve", epsb[:, 0:1], RMS_EPS, w=["epsb"])
    par = 0
    for kc in range(8):
        for q in range(6):
            c0 = q * 1024
            c1 = min(2 * FF, c0 + 1024)
            s = stage[par % 3]; sk = "stage%d" % (par % 3); par += 1
            load_convert(C, wu[:, kc, c0:c1], w_up[kc * 128:(kc + 1) * 128, c0:c1], s, sk, "wu", c1 - c0,
                         scale_ap=g2_t[:, kc:kc + 1])
    for f in range(22):
        s = stage[par % 3]; sk = "stage%d" % (par % 3); par += 1
        load_convert(C, wd[:, f, :], w_dn[f * 128:(f + 1) * 128, :], s, sk, "wd", D, parity=f)

    def rms_to_bf(x_ap, xk, out_bf, ok, np_, col):
        C.act(junk[0:np_, :], x_ap, AF.Square, r=[xk], w=["junk", "ssq%d" % col], accum=ssq[0:np_, col:col + 1])
        C.act(rstd[0:np_, col:col + 1], ssq[0:np_, col:col + 1], AF.Ln, r=["ssq%d" % col, "epsb"], w=["rstd%d" % col],
              scale=1.0 / D, bias=epsb[0:np_, 0:1])
        C.act(rstd[0:np_, col:col + 1], rstd[0:np_, col:col + 1], AF.Exp, r=["rstd%d" % col], w=["rstd%d" % col], scale=-0.5)
        C.act(out_bf, x_ap, AF.Copy, r=[xk, "rstd%d" % col], w=[ok], scale=rstd[0:np_, col:col + 1])

    C.S.wait_cc()
    ucount = [0]
    def prologue(grp):
        gp = grp % 2
        h2T = h2Tb[gp]
        hk = "h2T%d" % gp
        for sl in range(2):
            j = grp * 2 + sl
            xb, xk = xs[gp * 2 + sl], "xs%d" % (gp * 2 + sl)
            C.dma(xb[:], xm_own[j, :, :], w=[xk])
            for cand in range(4):
                m_ = 4 * j - 1 + cand
                if m_ < 0:
                    C.memset("pool", xh[0:2, :], 0.0, w=["xh"])
                else:
                    row0 = ((m_ % 4) * NS + m_ // 4) * 2
                    C.dma(xh[cand * 2:(cand + 1) * 2, :], tail_g[row0:row0 + 2, :], w=["xh"])
            rms_to_bf(xb[:], xk, hn[:], "hn", 128, 0)
            rms_to_bf(xh[:], "xh", hnh[:], "hnh", 8, 1)
            for kc in range(8):
                C.tr(psT[:, kc * 128:(kc + 1) * 128], hn[:, kc * 128:(kc + 1) * 128], ident[:], r=["hn", "ident"], w=["psT"])
            C.cp("act", h2T[:, :, sl, 2:130], psT[:, :].rearrange("p (k t) -> p k t", k=8), w=["psT", hk])
            for kc in range(8):
                C.mm(psH[:, kc * 2:(kc + 1) * 2], hnh[:, kc * 128:(kc + 1) * 128], selm2[:], r=["hnh", "selm2"], w=["psH"])
            C.cp("act", h2T[:, :, sl, 0:2], psH[:, 0:16].rearrange("p (k t) -> p k t", k=8), w=["psH", hk])

    prologue(0)
    for grp in range(NS // 2):
        gp = grp % 2
        h2T = h2Tb[gp]
        hk = "h2T%d" % gp
        def up_mm(fc):
            banks = []
            for ch in (fc, 22 + fc):
                bi = ucount[0] % 4
                ucount[0] += 1
                bank = psU[bi]
                bk = "psU%d" % bi
                for kc in range(8):
                    C.mm(bank[:, 0:260], wu[:, kc, ch * 128:(ch + 1) * 128],
                         h2T[:, kc, :, :].rearrange("p s t -> p (s t)"),
                         start=(kc == 0), stop=(kc == 7), r=["wu", hk], w=[bk])
                banks.append((bank, bk))
            return banks

        def conv_ops(fc, banks):
            p = fc % 2
            for (bank, bk), ch, dst, dk in ((banks[0], fc, gc[p], "gc%d" % p), (banks[1], 22 + fc, uc[p], "uc%d" % p)):
                bv = bank[:, 0:260].rearrange("p (s t) -> p s t", s=2)
                C.act(dst[:], bv[:, :, 2:130], AF.Identity, r=["cw"], w=[bk, dk], scale=cw[:, ch, 2:3], bias=cw[:, ch, 3:4])
                C.stt(dst[:], bv[:, :, 1:129], cw[:, ch, 1:2], dst[:], ALU.mult, ALU.add, r=["cw", dk], w=[bk, dk])
                C.stt(dst[:], bv[:, :, 0:128], cw[:, ch, 0:1], dst[:], ALU.mult, ALU.add, r=["cw", dk], w=[bk, dk])

        def gate_ops(fc):
            p = fc % 2
            C.act(gg[p][:], gc[p][:], AF.Gelu_apprx_tanh, r=["gc%d" % p], w=["gg%d" % p])
            C.tt("pool", actT[:, fc, :, :], gg[p][:], uc[p][:], ALU.mult,
                 r=["gg%d" % p, "uc%d" % p], w=["actT"])

        nxt = up_mm(0)
        for fc in range(22):
            cur = nxt
            if fc + 1 < 22:
                nxt = up_mm(fc + 1)
            conv_ops(fc, cur)
            if fc >= 1:
                gate_ops(fc - 1)
            if fc == 12 and grp >= 1 and T.get("post_group") is not None:
                T["post_group"](grp - 1)
            if fc == 3 and grp + 1 < NS // 2:
                prologue(grp + 1)
        gate_ops(21)
        for sl in range(2):
            j = grp * 2 + sl
            for half in range(2):
                for f in range(22):
                    C.mm(psD[half][:, :], actT[:, f, sl, :], wd[:, f, half * 512:(half + 1) * 512],
                         start=(f == 0), stop=(f == 21), r=["actT", "wd"], w=["psD%d" % half])
                C.cp("dve", yo[:, half * 512:(half + 1) * 512], psD[half][:, :], w=["psD%d" % half, "yo"])
            C.act(junk[:, :], yo[:], AF.Square, r=["yo"], w=["junk", "oss"], accum=oss[:, 0:1])
            C.act(oss[:, 1:2], oss[:, 0:1], AF.Ln, r=["oss", "epsb"], w=["oss1"], scale=1.0 / D, bias=epsb[:, 0:1])
            C.act(oss[:, 2:3], oss[:, 1:2], AF.Exp, r=["oss1"], w=["oss2"], scale=-0.5)
            C.stt(yo[:], yo[:], oss[:, 2:3], gpost_bc[:], ALU.mult, ALU.mult, r=["yo", "oss2", "gpost_bc"], w=["yo"])
            C.tt("dve", yo[:], yo[:], xs[gp * 2 + sl][:], ALU.add, r=["yo", "xs%d" % (gp * 2 + sl)], w=["yo"])
            C.dma(out_tile(j)[:, :], yo[:], r=["yo"], w=["ffnout%d" % j])
    if T.get("post_group") is not None:
        T["post_group"](NS // 2 - 1)
        C.end_phase(wait_cc=False)
        return
    C.end_phase()


OFF_Q, OFF_KV, OFF_G, OFF_C, OFF_D = 512, 768, 1152, 1164, 1676


def _consts():
    c = {}
    half = 32
    inv = (10000.0 ** (-np.arange(half, dtype=np.float32) * 2.0 / 64)).astype(np.float32)
    ang = np.arange(SEQ, dtype=np.float32)[:, None] * inv[None, :]
    cos = np.cos(ang).astype(np.float32).T
    sin = np.sin(ang).astype(np.float32).T
    c64 = np.concatenate([cos, cos], 0)
    s64 = np.concatenate([-sin, sin], 0)
    c["ropeC"] = np.ascontiguousarray(np.concatenate([c64, c64], 0))
    c["ropeS"] = np.ascontiguousarray(np.concatenate([s64, s64], 0))
    blk = np.arange(SEQ) // 64
    c["indp"] = (np.arange(64)[:, None] == (blk % 64)[None, :]).astype(np.float32)
    c["identd"] = np.eye(128, dtype=np.float32)
    n = np.arange(512)[:, None]
    b = np.arange(128)[None, :]
    off = n - 4 * b
    M = np.where((off == -1) | (off == 3), 1.0, np.where((off >= 0) & (off <= 2), 2.0, 0.0)).astype(np.float32)
    M[511, :] = 0.0
    c["impM"] = np.ascontiguousarray(M.reshape(4, 128, 128))
    c["trild"] = (np.arange(128)[:, None] <= np.arange(128)[None, :]).astype(np.float32)
    rg = np.zeros((128, 2, 128), np.float32)
    for gi, w in enumerate((2, 4, 8, 16)):
        rg[(gi % 2) * 64:(gi % 2) * 64 + 64, gi // 2, :] = 1.0 / w
    c["rcntg"] = rg
    return c


def _core_consts(cidx):
    c = cidx
    o = {}
    k = np.arange(128)[:, None]
    t = np.arange(128)[None, :]
    wm = np.zeros((8, 128, 128), np.float32)
    for q in range(8):
        mm = q - 4
        rel = mm - c
        if rel == 0:
            wm[q] = (k <= t)
        elif rel == -4:
            wm[q] = (k > t)
        elif -4 < rel < 0:
            wm[q] = 1.0
    o["wmaskd"] = wm
    cm = np.zeros((5, 128, 128), np.float32)
    for jm in range(4):
        nprime = k - 32 * jm
        cm[jm] = (16 * nprime + 31 <= 128 * c + t)
    cm[4] = 1.0
    if c == 0:
        cm[4][127, :15] = 0.0
    o["cmaskd"] = cm
    A = np.zeros((NS, 128, 128), np.float32)
    B = np.zeros((NS, 128, 128), np.float32)
    tt = np.arange(128)[:, None]
    bb = np.arange(128)[None, :]
    for j in range(NS):
        i = 4 * j + c
        cur = 2 * i + (tt >= 64)
        valid = bb <= cur
        forced = (bb == 0) | (bb == cur) | (bb == cur - 1)
        A[j] = (valid & ~forced)
        B[j] = np.where(valid, np.where(forced, 1e6, 0.0), -1e30)
    o["selA"] = A
    o["selB"] = B
    inv = (10000.0 ** (-np.arange(32, dtype=np.float32) * 2.0 / 64)).astype(np.float32)
    qC = np.zeros((NS, 128, 128), np.float32)
    qS = np.zeros((NS, 128, 128), np.float32)
    for j in range(NS):
        pos = (128 * (4 * j + c) + np.arange(128)).astype(np.float32)
        ang = pos[:, None] * inv[None, :]
        cs = np.cos(ang).astype(np.float32).T * np.float32(0.125)
        sn = np.sin(ang).astype(np.float32).T * np.float32(0.125)
        c64 = np.concatenate([cs, cs], 0)
        s64 = np.concatenate([-sn, sn], 0)
        qC[j] = np.concatenate([c64, c64], 0)
        qS[j] = np.concatenate([s64, s64], 0)
    o["qC"] = qC
    o["qS"] = qS
    r0 = np.zeros((128, 2, 128), np.float32)
    for gi, w in enumerate((2, 4, 8, 16)):
        pos = 128 * c + np.arange(128)
        cnt = np.minimum(pos + 1, w).astype(np.float32)
        r0[(gi % 2) * 64:(gi % 2) * 64 + 64, gi // 2, :] = (1.0 / cnt)[None, :]
    o["rcnt0"] = r0
    sm = np.zeros((128, 32), np.float32)
    sm[c * 32:(c + 1) * 32, :] = np.eye(32, dtype=np.float32)
    o["selm"] = sm
    sm2 = np.zeros((8, 2), np.float32)
    sm2[c * 2:(c + 1) * 2, :] = np.eye(2, dtype=np.float32)
    o["selm2"] = sm2
    return o


def _sw(idx):
    return np.concatenate([idx[32:], idx[:32]])


def _mixer_weights(P, l):
    w_in = P["w_in"][l]
    kv = lambda s: np.arange(OFF_KV + 64 * s, OFF_KV + 64 * s + 64)
    cols_kv = np.concatenate([kv(0), kv(1), kv(2), kv(4), _sw(kv(2)), _sw(kv(4)), kv(3), kv(5)])
    qcols = np.arange(OFF_Q, OFF_Q + 256)
    qsw = np.concatenate([_sw(qcols[h * 64:(h + 1) * 64]) for h in range(4)])
    cols_loc = np.concatenate([np.arange(0, 512), np.arange(OFF_D, OFF_D + 256), np.arange(OFF_C, OFF_C + 256),
                               qcols, qsw, np.arange(OFF_C + 256, OFF_C + 512), np.arange(OFF_G, OFF_G + 12)])
    o = {}
    o["w_kv"] = np.ascontiguousarray(w_in[:, cols_kv])
    o["w_loc"] = np.ascontiguousarray(w_in[:, cols_loc])
    o["w_out"] = np.ascontiguousarray(P["w_out"][l])
    o["gpre"] = np.ascontiguousarray(P["norm_mix_pre"][l].reshape(8, 128).T)
    o["gpost"] = np.ascontiguousarray(P["norm_mix_post"][l].reshape(1, D))
    o["convw"] = np.ascontiguousarray(P["conv_dw_w"][l].reshape(31, 2, 128).transpose(2, 1, 0))
    cv = np.stack([P["conv_dw_b"][l], P["conv_ln_g"][l], P["conv_ln_b"][l]], -1)
    o["convv"] = np.ascontiguousarray(cv.reshape(2, 128, 3).transpose(1, 0, 2))
    w1k = P["nsa_ck_w1"][l].reshape(32, 64, 64).transpose(1, 0, 2).reshape(64, 2048)
    w1v = P["nsa_cv_w1"][l].reshape(32, 64, 64).transpose(1, 0, 2).reshape(64, 2048)
    o["w1d"] = np.ascontiguousarray(np.concatenate([w1k, w1v], 0))
    o["w2d"] = np.ascontiguousarray(np.concatenate([P["nsa_ck_w2"][l], P["nsa_cv_w2"][l]], 1))
    o["ped"] = np.ascontiguousarray(np.concatenate([P["nsa_pe_k"][l].T, P["nsa_pe_v"][l].T], 0))
    o["sgv"] = np.ascontiguousarray(np.stack([P["sgu_ln_g"][l], P["sgu_ln_b"][l]], 0))
    o["sgw"] = np.ascontiguousarray(P["sgu_w"][l].transpose(2, 0, 1))
    o["sgb"] = np.ascontiguousarray(P["sgu_b"][l])
    pw = np.zeros((128, 2, 128), np.float32)
    for gi in range(4):
        lo = (gi % 2) * 64
        pw[lo:lo + 64, gi // 2, lo:lo + 64] = P["pool_w"][l][gi]
    o["poolw"] = pw
    o["poolsc"] = np.ascontiguousarray(P["pool_scale"][l].reshape(2, 128).T)
    return o


def _ffn_weights(P, l):
    o = {}
    o["w_up"] = np.ascontiguousarray(P["ffn_up"][l])
    o["w_dn"] = np.ascontiguousarray(P["ffn_down"][l])
    o["g2"] = np.ascontiguousarray(P["norm_ffn_pre"][l].reshape(8, 128).T)
    o["gpost2"] = np.ascontiguousarray(P["norm_ffn_post"][l].reshape(1, D))
    cw = np.concatenate([P["ffn_conv_w"][l], P["ffn_conv_b"][l][None, :]], 0)
    o["cwd"] = np.ascontiguousarray(cw.reshape(4, NFC, 128).transpose(2, 1, 0))
    return o


def _own_tiles(xb, c, halo):
    pad = np.concatenate([np.zeros((halo, D), np.float32), xb], 0)
    out = np.empty((NS, halo + 128, D), np.float32)
    for j in range(NS):
        i = 4 * j + c
        out[j] = pad[128 * i:128 * i + 128 + halo]
    return out


def _scatter_own(res_list, key):
    x = np.empty((NB, SEQ, D), np.float32)
    for core in range(8):
        b, c = divmod(core, 4)
        r = res_list[core][key]
        for j in range(NS):
            i = 4 * j + c
            x[b, 128 * i:128 * (i + 1)] = r[j]
    return x


def _chunk_row(m):
    r, sl = m % 4, m // 4
    return sl // 2, r * 256 + (sl % 2) * 128


def build_all(nslots=NS):
    C = Ctx()
    T = {}
    xg0 = C.dram_in("xg0", [8 * 1024, D])
    xown0 = C.dram_in("xown0", [NS, 128, D])
    for k, shp in MIX_C_SHAPES.items():
        T[k] = C.dram_in(k, shp)
    for l in range(2):
        for k, shp in MIX_W_SHAPES.items():
            T["%s_%d" % (k, l)] = C.dram_in("%s_%d" % (k, l), shp)
        for k, shp in FFN_W_SHAPES.items():
            T["%s_%d" % (k, l)] = C.dram_in("%s_%d" % (k, l), shp)
    out = C.dram_out("out", [NS, 128, D])
    xm_own = C.dram_int("xm_own", [NS, 128, D])
    tails = [C.dram_int("xm_tail%d" % l, [NS * 2, D]) for l in range(2)]
    tail_g = [C.dram_int("tail_g%d" % l, [4 * NS * 2, D]) for l in range(2)]
    x1_own = [C.dram_int("x1_own%d" % g, [256, D]) for g in range(8)]
    x1_g = [C.dram_int("x1_g%d" % g, [1024, D]) for g in range(8)]
    RG = [[0, 1, 2, 3], [4, 5, 6, 7]]

    def gather(src, dst):
        C.S.collective_nowait(lambda e: e.collective_compute("AllGather", ALU.bypass, replica_groups=RG,
                                                      ins=[src.opt()], outs=[dst.opt()]))

    def xg_tile0(m):
        g, ro = _chunk_row(m)
        return xg0[g * 1024 + ro:g * 1024 + ro + 128, :]

    def xg_tile1(m):
        g, ro = _chunk_row(m)
        return x1_g[g][ro:ro + 128, :]

    for l in range(2):
        T["xg_tile"] = xg_tile0 if l == 0 else xg_tile1
        T["xown_tile"] = (lambda j: xown0[j, :, :]) if l == 0 else (lambda j: x1_own[j // 2][(j % 2) * 128:(j % 2) * 128 + 128, :])
        T["xm_own"] = xm_own
        T["xm_tail"] = tails[l]
        mixer_phase(C, T, l, nslots=nslots)
        gather(tails[l], tail_g[l])
        T["tail_g_%d" % l] = tail_g[l]
        T["ffn_out_tile"] = (lambda j: x1_own[j // 2][(j % 2) * 128:(j % 2) * 128 + 128, :]) if l == 0 else (lambda j: out[j, :, :])
        if l == 0:
            def post_group(g):
                C.S.cc_op(lambda e: e.collective_compute("AllGather", ALU.bypass, replica_groups=RG,
                                                         ins=[x1_own[g].opt()], outs=[x1_g[g].opt()]),
                          reads=["ffnout%d" % (2 * g), "ffnout%d" % (2 * g + 1)], writes=["x1g%d" % g])
            T["post_group"] = post_group
        else:
            T["post_group"] = None
        ffn_phase(C, T, l)
    info = C.finish()
    return C.nc, info


_CACHE = {}


def kernel(**inputs):
    P = {k: np.asarray(v, dtype=np.float32) for k, v in inputs.items()}
    x = P["x"]
    if "nc" not in _CACHE:
        _CACHE["nc"] = build_all()[0]
    nc = _CACHE["nc"]
    consts = _consts()
    shared = {k: consts[k] for k in MIX_C_SHAPES if k in consts}
    for l in range(2):
        for k, v in _mixer_weights(P, l).items():
            shared["%s_%d" % (k, l)] = v
        for k, v in _ffn_weights(P, l).items():
            shared["%s_%d" % (k, l)] = v
    in_maps = []
    for core in range(8):
        b, c = divmod(core, 4)
        m = dict(shared)
        m.update(_core_consts(c))
        xt = x[b].reshape(8, 2, 4, 128, D)
        m["xg0"] = np.ascontiguousarray(xt.transpose(0, 2, 1, 3, 4).reshape(8 * 1024, D))
        m["xown0"] = np.ascontiguousarray(x[b].reshape(NS, 4, 128, D)[:, c])
        in_maps.append(m)
    res = run_bass_kernel_spmd(nc, in_maps, core_ids=list(range(8)))
    return _scatter_own(res.results, "out").astype(np.float32)
```

```python
import numpy as np
from contextlib import ExitStack
import concourse.bass as bass
import concourse.mybir as mybir
from concourse.bass_utils import run_bass_kernel_spmd

F32 = mybir.dt.float32
BF16 = mybir.dt.bfloat16
AF = mybir.ActivationFunctionType
ALU = mybir.AluOpType
AX = mybir.AxisListType

D = 1024
SEQ = 8192
NB = 2
NT = 64
NS = 16
FF = 2816
NFC = 44
RMS_EPS = 1e-6
LN_EPS = 1e-5
NEG = -30000.0
GELU_C = 1.5957691216057308


class Sched:
    NDS = 8

    def __init__(self, nc, sems):
        self.nc = nc
        self.eng = {"pe": nc.tensor, "act": nc.scalar, "dve": nc.vector,
                    "pool": nc.gpsimd, "sp": nc.sync}
        self.ops = []
        self.last_writer = {}
        self.readers = {}
        self.sems = sems

    def cc_op(self, fn, reads=(), writes=()):
        idx = self.op("pool", fn, reads, writes)
        self.ops[idx].append("cc")
        return idx

    def op(self, engine, fn, reads=(), writes=()):
        idx = len(self.ops)
        is_dma = engine == "sp"
        deps = {}

        def add(d, kind):
            if d is None:
                return
            if deps.get(d) is None or kind == "raw":
                deps[d] = kind

        for k in reads:
            add(self.last_writer.get(k), "raw")
        for k in writes:
            add(self.last_writer.get(k), "waw")
            for r in self.readers.get(k, ()):
                add(r, "war")
        for k in writes:
            self.last_writer[k] = idx
            self.readers[k] = []
        for k in reads:
            if k not in writes:
                lst = self.readers.setdefault(k, [])
                if not is_dma:
                    lst[:] = [r_ for r_ in lst if self.ops[r_][0] != engine]
                lst.append(idx)
        keep = []
        for d, kind in deps.items():
            de = self.ops[d][0]
            if de == engine and not is_dma:
                if engine == "pe":
                    continue
            keep.append(d)
        self.ops.append([engine, fn, keep, is_dma])
        for d in keep:
            self.ops[d][3] = True
        return idx

    def _init_emit_state(self):
        self.counts = {e: 0 for e in self.eng}
        self.dma_counts = [0] * self.NDS
        self.n_dma = 0
        self.waited = {e: {} for e in self.eng}
        self.sigval = {}
        self.emitted = 0
        self.cc_count = 0

    def flush(self):
        if not hasattr(self, "counts"):
            self._init_emit_state()
        last = {}
        for idx in range(self.emitted, len(self.ops)):
            if len(self.ops[idx]) == 4 and self.ops[idx][0] != "__ccwait__":
                last[self.ops[idx][0]] = idx
        for e, idx in last.items():
            if e != "sp" and len(self.ops[idx]) == 4:
                self.ops[idx][3] = True
        for idx in range(self.emitted, len(self.ops)):
            e, fn, deps, signal = self.ops[idx][:4]
            if e == "__ccwait__":
                if self.cc_count > 0:
                    for e2, eng2 in self.eng.items():
                        if self.waited[e2].get("cc", 0) < self.cc_count:
                            eng2.wait_ge(self.sems["cc"], self.cc_count)
                            self.waited[e2]["cc"] = self.cc_count
                continue
            is_cc = len(self.ops[idx]) > 4
            eng = self.eng[e]
            need = {}
            for d in deps:
                sem, val = self.sigval[d]
                if need.get(id(sem), (None, 0))[1] < val:
                    need[id(sem)] = (sem, val)
            if e == "sp":
                slot = self.n_dma % self.NDS
                if self.dma_counts[slot] > 0:
                    sem = self.sems["dma"][slot]
                    val = self.dma_counts[slot] * 16
                    if need.get(id(sem), (None, 0))[1] < val:
                        need[id(sem)] = (sem, val)
            for key, (sem, val) in need.items():
                if self.waited[e].get(key, 0) >= val:
                    continue
                eng.wait_ge(sem, val)
                self.waited[e][key] = val
            ins = fn(eng)
            if is_cc:
                self.cc_count += 1
                ins.then_inc(self.sems["cc"])
                self.sigval[idx] = (self.sems["cc"], self.cc_count)
                continue
            if e == "sp":
                slot = self.n_dma % self.NDS
                self.n_dma += 1
                self.dma_counts[slot] += 1
                sem = self.sems["dma"][slot]
                ins.then_inc(sem, 16)
                self.sigval[idx] = (sem, self.dma_counts[slot] * 16)
            elif signal:
                self.counts[e] += 1
                ins.then_inc(self.sems[e], 1)
                self.sigval[idx] = (self.sems[e], self.counts[e])
        self.emitted = len(self.ops)

    def barrier(self, engines=None, wait_cc=True):
        self.flush()
        for e, eng in self.eng.items():
            for x in ("pe", "act", "dve", "pool"):
                if x != e and self.counts[x] > 0 and self.waited[e].get(id(self.sems[x]), 0) < self.counts[x]:
                    eng.wait_ge(self.sems[x], self.counts[x])
                    self.waited[e][id(self.sems[x])] = self.counts[x]
            for slot in range(self.NDS):
                if self.dma_counts[slot] > 0:
                    sem = self.sems["dma"][slot]
                    val = self.dma_counts[slot] * 16
                    if self.waited[e].get(id(sem), 0) < val:
                        eng.wait_ge(sem, val)
                        self.waited[e][id(sem)] = val
            if wait_cc and self.cc_count > 0 and self.waited[e].get("cc", 0) < self.cc_count:
                eng.wait_ge(self.sems["cc"], self.cc_count)
                self.waited[e]["cc"] = self.cc_count
        self.last_writer = {}
        self.readers = {}

    def wait_cc(self):
        self.ops.append(["__ccwait__", None, [], False])

    def collective_nowait(self, fn):
        self.barrier()
        self.cc_count += 1
        fn(self.eng["pool"]).then_inc(self.sems["cc"])

    def collective(self, fn):
        self.barrier()
        self.cc_count += 1
        fn(self.eng["pool"]).then_inc(self.sems["cc"])
        for e, eng in self.eng.items():
            eng.wait_ge(self.sems["cc"], self.cc_count)

    def emit(self):
        self.flush()
        sp = self.eng["sp"]
        for slot in range(self.NDS):
            if self.dma_counts[slot] > 0:
                sp.wait_ge(self.sems["dma"][slot], self.dma_counts[slot] * 16)
        return self.counts, self.n_dma


class Ctx:
    def __init__(self):
        self.nc = bass.Bass("TRN2", target_bir_lowering=False)
        self.es = ExitStack()
        nc = self.nc
        sems = {e: self.es.enter_context(nc.semaphore("s_" + e)) for e in ["pe", "act", "dve", "pool"]}
        sems["dma"] = [self.es.enter_context(nc.semaphore("s_dma%d" % i)) for i in range(Sched.NDS)]
        sems["cc"] = self.es.enter_context(nc.semaphore("s_cc"))
        self.S = Sched(nc, sems)
        self.pes = self.es

    def begin_phase(self, sfx):
        self.pes = ExitStack()
        self.sfx = sfx

    def end_phase(self, wait_cc=True):
        self.S.barrier(wait_cc=wait_cc)
        self.pes.close()
        self.pes = self.es

    def dram_int(self, name, shape, dt=F32):
        return self.nc.dram_tensor(name, list(shape), dt).ap()

    def dram_in(self, name, shape, dt=F32):
        return self.nc.dram_tensor(name, list(shape), dt, kind="ExternalInput").ap()

    def dram_out(self, name, shape, dt=F32):
        return self.nc.dram_tensor(name, list(shape), dt, kind="ExternalOutput").ap()

    def sb(self, name, shape, dt=F32):
        return self.pes.enter_context(self.nc.sbuf_tensor(name + getattr(self, "sfx", ""), list(shape), dt))

    def ps(self, name, shape, dt=F32):
        return self.pes.enter_context(self.nc.psum_tensor(name + getattr(self, "sfx", ""), list(shape), dt))

    def dma(self, out, in_, r=(), w=()):
        self.S.op("sp", lambda e: e.dma_start(out=out, in_=in_), r, w)

    def mm(self, out, lhsT, rhs, start=True, stop=True, r=(), w=(), skip=False):
        self.S.op("pe", lambda e: e.matmul(out, lhsT=lhsT, rhs=rhs, start=start, stop=stop,
                                           skip_group_check=skip), r, w)

    def tr(self, out, in_, ident, r=(), w=()):
        self.S.op("pe", lambda e: e.transpose(out, in_, ident), r, w)

    def act(self, out, in_, func, r=(), w=(), scale=1.0, bias=0.0, accum=None, eng="act"):
        if accum is None:
            self.S.op("act", lambda e: e.activation(out=out, in_=in_, func=func, bias=bias, scale=scale), r, w)
        else:
            self.S.op("act", lambda e: e.activation(out=out, in_=in_, func=func, bias=bias, scale=scale,
                                                    accum_out=accum), r, w)

    def cp(self, eng, out, in_, r=(), w=()):
        if eng == "act":
            self.S.op("act", lambda e: e.activation(out=out, in_=in_, func=AF.Copy), r, w)
        else:
            self.S.op(eng, lambda e: e.tensor_copy(out=out, in_=in_), r, w)

    def tt(self, eng, out, in0, in1, op, r=(), w=()):
        self.S.op(eng, lambda e: e.tensor_tensor(out=out, in0=in0, in1=in1, op=op), r, w)

    def ts(self, eng, out, in0, s1, s2, op0, op1=None, r=(), w=()):
        if op1 is None:
            self.S.op(eng, lambda e: e.tensor_scalar(out=out, in0=in0, scalar1=s1, scalar2=None, op0=op0), r, w)
        else:
            self.S.op(eng, lambda e: e.tensor_scalar(out=out, in0=in0, scalar1=s1, scalar2=s2, op0=op0, op1=op1), r, w)

    def stt(self, out, in0, scalar, in1, op0, op1, r=(), w=()):
        self.S.op("dve", lambda e: e.scalar_tensor_tensor(out=out, in0=in0, scalar=scalar, in1=in1,
                                                          op0=op0, op1=op1), r, w)

    def memset(self, eng, ap, val, w=()):
        self.S.op(eng, lambda e: e.memset(ap, val), (), w)

    def recip(self, out, in_, r=(), w=()):
        self.S.op("dve", lambda e: e.reciprocal(out=out, in_=in_), r, w)

    def finish(self):
        res = self.S.emit()
        self.es.close()
        return res


def load_convert(C, dst_bf, src_dram, stage, stage_key, dst_key, ncols, scale_ap=None, parity=0, scale_key="gscale"):
    C.dma(stage[:, 0:ncols], src_dram, w=[stage_key])
    if scale_ap is not None:
        C.ts("dve", dst_bf, stage[:, 0:ncols], scale_ap, None, ALU.mult, r=[stage_key, scale_key], w=[dst_key])
    elif parity % 2 == 0:
        C.cp("dve", dst_bf, stage[:, 0:ncols], r=[stage_key], w=[dst_key])
    else:
        C.cp("act", dst_bf, stage[:, 0:ncols], r=[stage_key], w=[dst_key])


NLOC = 1804


MIX_W = ["w_kv", "w_loc", "w_out", "gpre", "gpost", "convw", "convv", "w1d", "w2d", "ped", "sgv", "sgw", "sgb",
         "poolw", "poolsc"]
MIX_W_SHAPES = {"w_kv": [D, 512], "w_loc": [D, NLOC], "w_out": [D, D], "gpre": [128, 8], "gpost": [1, D],
                "convw": [128, 2, 31], "convv": [128, 2, 3], "w1d": [128, 2048], "w2d": [64, 128], "ped": [128, 32],
                "sgv": [2, 256], "sgw": [128, 4, 128], "sgb": [4, 128], "poolw": [128, 2, 128], "poolsc": [128, 2]}
MIX_C_SHAPES = {"ropeC": [128, SEQ], "ropeS": [128, SEQ], "qC": [NS, 128, 128], "qS": [NS, 128, 128], "indp": [64, SEQ],
                "identd": [128, 128], "wmaskd": [8, 128, 128], "cmaskd": [5, 128, 128], "selA": [NS, 128, 128],
                "selB": [NS, 128, 128], "impM": [4, 128, 128], "trild": [128, 128], "rcnt0": [128, 2, 128],
                "rcntg": [128, 2, 128], "selm": [128, 32], "selm2": [8, 2]}
FFN_W_SHAPES = {"w_up": [D, 2 * FF], "w_dn": [FF, D], "g2": [128, 8], "gpost2": [1, D], "cwd": [128, NFC, 4]}


def mixer_phase(C, T, l, nslots=NS, stages=5):
    C.begin_phase("_m%d" % l)
    nc = C.nc
    xg_tile = T["xg_tile"]; xown_tile = T["xown_tile"]; xm_own = T["xm_own"]; xm_tail = T["xm_tail"]
    w_kv, w_loc, w_out, gpre, gpost = (T[k + "_%d" % l] for k in ("w_kv", "w_loc", "w_out", "gpre", "gpost"))
    convw, convv, w1d, w2d, ped = (T[k + "_%d" % l] for k in ("convw", "convv", "w1d", "w2d", "ped"))
    sgv, sgw, sgb, poolw, poolsc = (T[k + "_%d" % l] for k in ("sgv", "sgw", "sgb", "poolw", "poolsc"))
    ropeC, ropeS, qC, qS, indp, identd = (T[k] for k in ("ropeC", "ropeS", "qC", "qS", "indp", "identd"))
    wmaskd, cmaskd, selA, selB, impM, trild = (T[k] for k in ("wmaskd", "cmaskd", "selA", "selB", "impM", "trild"))
    rcnt0, rcntg, selmd = T["rcnt0"], T["rcntg"], T["selm"]

    sb = C.sb
    wkv = sb("wkv", [128, 8, 512], BF16)
    wloc = sb("wloc", [128, 8, NLOC], BF16)
    wo = sb("wo", [128, 8, D], BF16)
    xt = [sb("xt0", [128, D]), sb("xt1", [128, D])]
    stage = xt
    gpre_t = sb("gpre_t", [128, 8])
    gpost_bc = sb("gpost_bc", [128, D])
    ident = sb("ident", [128, 128], BF16)
    ones_b = sb("ones_b", [128, 128], BF16)
    shi = sb("shi", [128, 2, 128], BF16)
    slo = sb("slo", [128, 2, 128], BF16)
    Dg = sb("Dg", [128, 2, 31, 128], BF16)
    convw_t = sb("convw_t", [128, 2, 31])
    convv_t = sb("convv_t", [128, 2, 3])
    W1k = sb("W1k", [64, 32, 64], BF16)
    W1v = sb("W1v", [64, 32, 64], BF16)
    W2 = sb("W2", [64, 128], BF16)
    peTk = sb("peTk", [64, 32], BF16)
    peTv = sb("peTv", [64, 32], BF16)
    hnc = sb("hnc", [128, D], BF16)
    cbias = sb("cbias", [64, 2])
    Kaug = sb("Kaug", [128, SEQ], BF16)
    Kw = sb("Kw", [128, 12, 128], BF16)
    Vs = sb("Vs", [128, NT, 65], BF16)
    Vw = sb("Vw", [128, 12, 65], BF16)
    kvk = [sb("kvk0", [64, 528], BF16), sb("kvk1", [64, 528], BF16)]
    kvv = [sb("kvv0", [64, 528], BF16), sb("kvv1", [64, 528], BF16)]
    gkT = sb("gkT", [64, 576], BF16)
    gvT = sb("gvT", [64, 576], BF16)
    kcT = sb("kcT", [128, 576], BF16)
    rhsC = sb("rhsC", [128, 4, 193], BF16)
    wmask = sb("wmask", [128, 8, 128], BF16)
    cmask = sb("cmask", [128, 5, 128], BF16)
    wsT = sb("wsT", [128, 4, 128], BF16)
    tril = sb("tril", [128, 128])
    Bs = sb("Bs", [128, 2, 128])
    Wbd = sb("Wbd", [128, 2, 128], BF16)
    poolsc_t = sb("poolsc_t", [128, 2])
    rc0 = sb("rc0", [128, 2, 128])
    rcg = sb("rcg", [128, 2, 128])
    lng_bc = sb("lng_bc", [128, 256])
    lnb_bc = sb("lnb_bc", [128, 256])
    hnb = [sb("hnb0", [128, D], BF16), sb("hnb1", [128, D], BF16)]
    hT = [sb("hT0", [128, 8, 128], BF16), sb("hT1", [128, 8, 128], BF16)]
    rC = [sb("rC0", [128, 128]), sb("rC1", [128, 128])]
    rS = [sb("rS0", [128, 128]), sb("rS1", [128, 128])]
    junk = sb("junk", [128, D], BF16)
    ssq = sb("ssq", [128, 8])
    rstd = sb("rstd", [128, 8])
    ropet = sb("ropet", [128, 2, 128])
    xo = sb("xo", [128, D])
    hno = sb("hno", [128, D], BF16)
    hTo = sb("hTo", [128, 8, 160], BF16)
    sig = sb("sig", [128, 2, 160])
    ub = sb("ub", [128, 2, 160], BF16)
    zdT = sb("zdT", [128, 2, 160])
    s2 = sb("s2", [128, 2, 160]); s4 = sb("s4", [128, 2, 160])
    s8 = sb("s8", [128, 2, 160]); s16 = sb("s16", [128, 2, 160])
    plf = sb("plf", [128, 2, 128]); plb = sb("plb", [128, 2, 128], BF16)
    usg = sb("usg", [128, 2, 128])
    qCt = sb("qCt", [128, 128]); qSt = sb("qSt", [128, 128])
    qt1 = sb("qt1", [128, 128]); qt2 = sb("qt2", [128, 128]); qsum = sb("qsum", [128, 128])
    QR = sb("QR", [128, 2, 4, 128], BF16)
    QC = sb("QC", [128, 4, 128], BF16)
    QW = sb("QW", [128, 4, 128], BF16)
    gts = sb("gts", [128, 12])
    vg = sb("vg", [128, 256]); vn = sb("vn", [128, 256]); vbf = sb("vbf", [128, 256], BF16)
    bnst = sb("bnst", [128, 6]); bnag = sb("bnag", [128, 2]); lnr = sb("lnr", [128, 2])
    ycv = sb("ycv", [128, 2, 128]); ysq = sb("ysq", [128, 2, 128])
    cmean = sb("cmean", [128, 128]); cmsq = sb("cmsq", [128, 128]); cvar = sb("cvar", [128, 128])
    crstd = sb("crstd", [128, 128]); cyn = sb("cyn", [128, 2, 128])
    sgt = sb("sgt", [128, 2, 128])
    yT = sb("yT", [128, 8, 128], BF16)
    Pb = [sb("Pb%d" % i, [128, 4, 128], BF16) for i in range(4)]
    selAt = sb("selAt", [128, 128]); selBt = sb("selBt", [128, 128])
    imp = sb("imp", [128, 128]); imp2 = sb("imp2", [128, 128]); impw = sb("impw", [128, 128])
    top8 = sb("top8", [128, 16]); selt1 = sb("selt1", [128, 128]); selt2 = sb("selt2", [128, 128])
    selbf = sb("selbf", [128, 128], BF16)
    den = sb("den", [128, 12]); rden = sb("rden", [128, 12]); gsc = sb("gsc", [128, 12])
    Ynsa = sb("Ynsa", [128, 256]); Ynb = sb("Ynb", [128, 256], BF16)
    yo = sb("yo", [128, D])
    oss = sb("oss", [128, 4])
    selm = sb("selm", [128, 32], BF16)
    cg1 = sb("cg1", [64, 64]); cg2 = sb("cg2", [64, 64])

    psS = [C.ps("psS0", [128, 512]), C.ps("psS1", [128, 512])]
    psC = C.ps("psC", [128, 512])
    psK = C.ps("psK", [128, 512])
    psSel = C.ps("psSel", [128, 512])
    psWin = C.ps("psWin", [128, 512])
    psG = C.ps("psG", [128, 512])
    psT = C.ps("psT", [128, 1024], BF16)

    C.dma(gpre_t[:], gpre[:, :], w=["gscale"])
    C.dma(gpost_bc[:], gpost.partition_broadcast(128), w=["gpost_bc"])
    C.dma(stage[0][:, 0:128], identd[:, :], w=["xt0"])
    C.cp("dve", ident[:], stage[0][:, 0:128], r=["xt0"], w=["ident"])
    C.memset("dve", ones_b[:], 1.0, w=["ones_b"])
    C.dma(stage[1][:, 0:32], selmd[:, :], w=["xt1"])
    C.cp("dve", selm[:], stage[1][:, 0:32], r=["xt1"], w=["selm"])
    par = 0
    for kc in range(8):
        s = stage[par % 2]; sk = "xt%d" % (par % 2); par += 1
        load_convert(C, wkv[:, kc, :], w_kv[kc * 128:(kc + 1) * 128, :], s, sk, "wkv", 512, scale_ap=gpre_t[:, kc:kc + 1])
        s = stage[par % 2]; sk = "xt%d" % (par % 2); par += 1
        load_convert(C, wloc[:, kc, 0:902], w_loc[kc * 128:(kc + 1) * 128, 0:902], s, sk, "wloc", 902, scale_ap=gpre_t[:, kc:kc + 1])
        s = stage[par % 2]; sk = "xt%d" % (par % 2); par += 1
        load_convert(C, wloc[:, kc, 902:NLOC], w_loc[kc * 128:(kc + 1) * 128, 902:NLOC], s, sk, "wloc", 902, scale_ap=gpre_t[:, kc:kc + 1])
        s = stage[par % 2]; sk = "xt%d" % (par % 2); par += 1
        load_convert(C, wo[:, kc, :], w_out[kc * 128:(kc + 1) * 128, :], s, sk, "wo", D, parity=kc)
    for q in range(8):
        s = stage[par % 2]; sk = "xt%d" % (par % 2); par += 1
        C.dma(s[64:128, 0:1024], indp[:, q * 1024:(q + 1) * 1024], w=[sk])
        C.cp("act" if q % 2 else "dve", Kaug[64:128, q * 1024:(q + 1) * 1024], s[64:128, 0:1024], r=[sk], w=["Kaug_ind"])
    C.memset("pool", Kw[64:128, :, :], 0.0, w=["Kw"])
    C.memset("pool", kcT[:, :], 0.0, w=["kcT"])
    C.memset("pool", gkT[:, :], 0.0, w=["gkT"])
    C.memset("pool", gvT[:, :], 0.0, w=["gvT"])
    for q_ in range(2):
        C.memset("pool", kvk[q_][:, :], 0.0, w=["kvk%d" % q_])
        C.memset("pool", kvv[q_][:, :], 0.0, w=["kvv%d" % q_])
    C.memset("pool", QC[:, :, :], 0.0, w=["QC"])
    C.memset("pool", QW[:, :, :], 0.0, w=["QW"])
    C.memset("pool", Vs[:, :, 64:65], 1.0, w=["Vs"])
    C.memset("pool", Vw[:, :, 64:65], 1.0, w=["Vw"])
    C.memset("pool", rhsC[:, :, :], 0.0, w=["rhsC"])
    C.memset("pool", rhsC[:, :, 192:193], 1.0, w=["rhsC"])
    for q in range(8):
        s = stage[par % 2]; sk = "xt%d" % (par % 2); par += 1
        C.dma(s[:, 0:128], wmaskd[q, :, :], w=[sk])
        C.cp("dve", wmask[:, q, :], s[:, 0:128], r=[sk], w=["wmask"])
    for q in range(4):
        s = stage[par % 2]; sk = "xt%d" % (par % 2); par += 1
        C.dma(s[:, 0:128], cmaskd[q, :, :], w=[sk])
        C.cp("dve", cmask[:, q, :], s[:, 0:128], r=[sk], w=["cmask"])
        if q == 0:
            s = stage[par % 2]; sk = "xt%d" % (par % 2); par += 1
            C.dma(s[:, 0:128], cmaskd[4, :, :], w=[sk])
            C.cp("dve", cmask[:, 4, :], s[:, 0:128], r=[sk], w=["cmask"])
        s = stage[par % 2]; sk = "xt%d" % (par % 2); par += 1
        C.dma(s[:, 0:128], impM[q, :, :], w=[sk])
        C.cp("dve", rhsC[:, q, 64:192], s[:, 0:128], r=[sk], w=["rhsC"])
    C.dma(convw_t[:], convw[:, :, :], w=["convw_t"])
    C.dma(convv_t[:], convv[:, :, :], w=["convv_t"])
    C.dma(stage[0][:, 0:128], identd[:, :], w=["xt0"])
    for cc in range(2):
        for k in range(31):
            if k % 2 == 0:
                C.ts("dve", Dg[:, cc, k, :], stage[0][:, 0:128], convw_t[:, cc, k:k + 1], None, ALU.mult,
                     r=["xt0", "convw_t"], w=["Dg"])
            else:
                C.act(Dg[:, cc, k, :], stage[0][:, 0:128], AF.Copy, r=["xt0", "convw_t"], w=["Dg"],
                      scale=convw_t[:, cc, k:k + 1])
    for wi_, (Wt_, wk_) in enumerate(((W1k, "W1k"), (W1v, "W1v"))):
        for q in range(2):
            C.dma(stage[q][0:64, 0:1024], w1d[64 * wi_:64 * wi_ + 64, q * 1024:(q + 1) * 1024], w=["xt%d" % q])
            C.cp("dve", Wt_[:].rearrange("p r e -> p (r e)")[:, q * 1024:(q + 1) * 1024], stage[q][0:64, 0:1024],
                 r=["xt%d" % q], w=[wk_])
    C.dma(stage[0][0:64, 0:128], w2d[:, :], w=["xt0"])
    C.cp("dve", W2[:], stage[0][0:64, 0:128], r=["xt0"], w=["W2"])
    C.dma(stage[0][0:64, 0:32], ped[0:64, :], w=["xt0"])
    C.cp("dve", peTk[:], stage[0][0:64, 0:32], r=["xt0"], w=["peTk"])
    C.dma(stage[1][0:64, 0:32], ped[64:128, :], w=["xt1"])
    C.cp("dve", peTv[:], stage[1][0:64, 0:32], r=["xt1"], w=["peTv"])
    for wi_, (Wt_, wk_, pt_, pk_) in enumerate(((W1k, "W1k", peTk, "peTk"), (W1v, "W1v", peTv, "peTv"))):
        for r_ in range(32):
            C.mm(psK[0:64, 0:1], Wt_[:, r_, :], pt_[:, r_:r_ + 1], start=(r_ == 0), stop=(r_ == 31),
                 r=[wk_, pk_], w=["psK"])
        C.cp("dve", cbias[:, wi_:wi_ + 1], psK[0:64, 0:1], w=["psK", "cbias"])
    C.dma(lng_bc[:], sgv[0:1, :].partition_broadcast(128), w=["lng_bc"])
    C.dma(lnb_bc[:], sgv[1:2, :].partition_broadcast(128), w=["lnb_bc"])
    C.dma(tril[:], trild[:, :], w=["tril"])
    C.dma(stage[0][:, 0:512], sgw.rearrange("p g t -> p (g t)"), w=["xt0"])
    for g in range(4):
        C.tt("dve", wsT[:, g, :], stage[0][:, g * 128:(g + 1) * 128], tril[:], ALU.mult, r=["xt0", "tril"], w=["wsT"])
        C.dma(Bs[(g % 2) * 64:(g % 2) * 64 + 64, g // 2, :], sgb[g:g + 1, :].partition_broadcast(64), w=["Bs"])
    C.dma(stage[1][:, 0:256], poolw.rearrange("p q d -> p (q d)"), w=["xt1"])
    C.cp("dve", Wbd[:].rearrange("p q d -> p (q d)"), stage[1][:, 0:256], r=["xt1"], w=["Wbd"])
    C.dma(poolsc_t[:], poolsc[:, :], w=["poolsc_t"])
    C.dma(rc0[:], rcnt0[:, :, :], w=["rc0"])
    C.dma(rcg[:], rcntg[:, :, :], w=["rcg"])

    def rms_to_bf(xt_ap, xk, out_bf, ok, np_, col):
        C.act(junk[0:np_, :], xt_ap, AF.Square, r=[xk], w=["junk", "ssq%d" % col], accum=ssq[0:np_, col:col + 1])
        C.act(rstd[0:np_, col:col + 1], ssq[0:np_, col:col + 1], AF.Ln, r=["ssq%d" % col, "epsb"], w=["rstd%d" % col],
              scale=1.0 / D, bias=epsb[0:np_, 0:1])
        C.act(rstd[0:np_, col:col + 1], rstd[0:np_, col:col + 1], AF.Exp, r=["rstd%d" % col], w=["rstd%d" % col], scale=-0.5)
        C.act(out_bf, xt_ap, AF.Copy, r=[xk, "rstd%d" % col], w=[ok], scale=rstd[0:np_, col:col + 1])

    epsb = sb("epsb", [128, 2])
    C.memset("dve", epsb[:, 0:1], RMS_EPS, w=["epsb"])
    C.memset("dve", epsb[:, 1:2], LN_EPS, w=["epsb"])

    def kv_tile(m):
        p = m % 2
        xk, hk, tk = "xt%d" % p, "hnb%d" % p, "hT%d" % p
        C.dma(xt[p][:], xg_tile(m)[:, :], w=[xk])
        C.dma(rC[p][:], ropeC[:, m * 128:(m + 1) * 128], w=["rC%d" % p])
        C.dma(rS[p][:], ropeS[:, m * 128:(m + 1) * 128], w=["rS%d" % p])
        yield
        C.act(junk[:, :], xt[p][:], AF.Square, r=[xk], w=["junk", "ssq%d" % p], accum=ssq[:, p:p + 1])
        yield
        C.act(rstd[:, p:p + 1], ssq[:, p:p + 1], AF.Ln, r=["ssq%d" % p, "epsb"], w=["rstd%d" % p],
              scale=1.0 / D, bias=epsb[:, 0:1])
        yield
        C.act(rstd[:, p:p + 1], rstd[:, p:p + 1], AF.Exp, r=["rstd%d" % p], w=["rstd%d" % p], scale=-0.5)
        yield
        C.ts("dve", hnb[p][:], xt[p][:], rstd[:, p:p + 1], None, ALU.mult, r=[xk, "rstd%d" % p], w=[hk])
        yield
        for half in range(2):
            for q in range(4):
                kc = 4 * half + q
                C.tr(psT[:, q * 128:(q + 1) * 128], hnb[p][:, kc * 128:(kc + 1) * 128], ident[:], r=[hk, "ident"], w=["psT"])
            yield
            C.cp("dve", hT[p][:, 4 * half:4 * half + 4, :], psT[:, 0:512].rearrange("p (k t) -> p k t", k=4), w=["psT", tk])
            yield
        for f in range(3):
            for kc in range(8):
                C.mm(psK[:, f * 128:(f + 1) * 128], wkv[:, kc, f * 128:(f + 1) * 128], hT[p][:, kc, :],
                     start=(kc == 0), stop=(kc == 7), r=["wkv", tk], w=["psK"])
            yield
        for kc in range(8):
            C.mm(psK[:, 384:512], hT[p][:, kc, :], wkv[:, kc, 384:512], start=(kc == 0), stop=(kc == 7),
                 r=["wkv", tk], w=["psK"])
        yield
        grp = (m // 4) % 2
        col0 = 16 + (m % 4) * 128
        C.cp("act", kvk[grp][:, col0:col0 + 128], psK[0:64, 0:128], w=["psK", "kvk%d" % grp])
        C.cp("act", kvv[grp][:, col0:col0 + 128], psK[64:128, 0:128], w=["psK", "kvv%d" % grp])
        C.tt("dve", ropet[:, 0, :], psK[:, 128:256], rC[p][:], ALU.mult, r=["rC%d" % p], w=["psK", "ropet0"])
        C.tt("dve", ropet[:, 1, :], psK[:, 256:384], rS[p][:], ALU.mult, r=["rS%d" % p], w=["psK", "ropet1"])
        yield
        C.cp("dve", Vs[:, m, 0:64], psK[:, 384:448], w=["psK", "Vs%d" % m])
        C.cp("dve", Vw[:, m % 12, 0:64], psK[:, 448:512], w=["psK", "Vw%d" % (m % 12)])
        yield
        C.tt("pool", Kaug[0:64, m * 128:(m + 1) * 128], ropet[0:64, 0, :], ropet[0:64, 1, :], ALU.add,
             r=["ropet0", "ropet1"], w=["Kaug%d" % m])
        C.tt("pool", Kw[0:64, m % 12, :], ropet[64:128, 0, :], ropet[64:128, 1, :], ALU.add,
             r=["ropet0", "ropet1"], w=["Kw%d" % (m % 12)])
        yield

    def compress(j):
        g = j % 2
        n0 = 32 * j - 1
        c0 = 32 + n0
        for which, (Wt_, wk_, kb, kk, gT, gk_) in enumerate(((W1k, "W1k", kvk[g], "kvk%d" % g, gkT, "gkT"),
                                                           (W1v, "W1v", kvv[g], "kvv%d" % g, gvT, "gvT"))):
            for r_ in range(32):
                C.mm(psK[0:64, which * 32:(which + 1) * 32], Wt_[:, r_, :], kb[:, r_:r_ + 16 * 31 + 1:16],
                     start=(r_ == 0), stop=(r_ == 31), r=[wk_, kk], w=["psK"])
            yield
            cgt = cg1[:, which * 32:(which + 1) * 32]
            cgs = cg2[:, which * 32:(which + 1) * 32]
            k1, k2 = ["cg1_%d" % which], ["cg2_%d" % which]
            C.ts("dve", cgt, psK[0:64, which * 32:(which + 1) * 32], cbias[:, which:which + 1], None, ALU.add,
                 r=["cbias"], w=["psK"] + k1)
            C.tt("dve", cgs, cgt, cgt, ALU.mult, r=k1, w=k2)
            C.ts("dve", cgs, cgs, 0.044715, 1.0, ALU.mult, ALU.add, r=k2, w=k2)
            C.tt("dve", cgs, cgs, cgt, ALU.mult, r=k2 + k1, w=k2)
            C.act(cgs, cgs, AF.Exp, r=k2, w=k2, scale=-GELU_C)
            C.ts("dve", cgs, cgs, 1.0, None, ALU.add, r=k2, w=k2)
            C.recip(cgs, cgs, r=k2, w=k2)
            C.tt("dve", gT[:, c0:c0 + 32], cgt, cgs, ALU.mult, r=k1 + k2, w=[gk_])
            other = (kvk, kvv)[which][1 - g]
            C.cp("pool", other[:, 0:16], kb[:, 512:528], r=[kk], w=[("kvk%d", "kvv%d")[which] % (1 - g)])
            yield
        C.mm(psK[0:64, 64:96], W2[:, 0:64], gkT[:, c0:c0 + 32], r=["W2", "gkT"], w=["psK"])
        yield
        C.cp("dve", kcT[0:64, c0:c0 + 32], psK[0:64, 64:96], w=["psK", "kcT"])
        yield
        chunks = sorted(set([max(n0, 0) // 128, (n0 + 31) // 128]))
        for ci in chunks:
            C.mm(psK[:, 128:192], gvT[:, 32 + ci * 128:32 + (ci + 1) * 128], W2[:, 64:128], r=["W2", "gvT"], w=["psK"])
            yield
            C.cp("dve", rhsC[:, ci, 0:64], psK[:, 128:192], w=["psK", "rhsC"])
            yield

    def local_pre(j):
        for cand in range(4):
            m_ = 4 * j - 1 + cand
            if m_ < 0:
                C.memset("pool", yo[0:32, :], 0.0, w=["yo"])
            else:
                C.dma(yo[cand * 32:(cand + 1) * 32, :], xg_tile(m_)[96:128, :], w=["yo"])
        C.dma(qCt[:], qC[j, :, :], w=["qCt"])
        C.dma(qSt[:], qS[j, :, :], w=["qSt"])
        yield
        rms_to_bf(yo[:], "yo", hnc[:], "hnc", 128, 3)
        yield
        for kc in range(8):
            C.mm(psG[:, kc * 32:(kc + 1) * 32], hnc[:, kc * 128:(kc + 1) * 128], selm[:], r=["hnc", "selm"], w=["psG"])
        yield
        C.cp("act", hTo[:, :, 0:32], psG[:, 0:256].rearrange("p (k t) -> p k t", k=8), w=["psG", "hToH"])
        yield

    def local(j):
        gbanks = [(psG, "psG"), (psS[0], "psS0"), (psS[1], "psS1")]
        gcnt = [0]

        def nextG():
            i = gcnt[0] % 3
            gcnt[0] += 1
            return gbanks[i]

        bk, bkey = nextG()
        yield
        C.dma(xo[:], xown_tile(j)[:, :], w=["xo"])
        C.dma(selAt[:], selA[j, :, :], w=["selAt"])
        C.dma(selBt[:], selB[j, :, :], w=["selBt"])
        rms_to_bf(xo[:], "xo", hno[:], "hno", 128, 2)
        yield
        for half in range(2):
            for q in range(4):
                kc = 4 * half + q
                C.tr(psT[:, 512 + q * 128:512 + (q + 1) * 128], hno[:, kc * 128:(kc + 1) * 128], ident[:],
                     r=["hno", "ident"], w=["psT"])
            yield
            C.cp("act", hTo[:, 4 * half:4 * half + 4, 32:160], psT[:, 512:1024].rearrange("p (k t) -> p k t", k=4),
                 w=["psT", "hTo"])
            yield

        def proj(chunk, ncol, out_ap):
            c0 = 160 - ncol
            for kc in range(8):
                C.mm(out_ap, wloc[:, kc, chunk * 128:(chunk + 1) * 128], hTo[:, kc, c0:160],
                     start=(kc == 0), stop=(kc == 7), r=["wloc", "hTo", "hToH"], w=[bkey])

        yield
        for cc in range(2):
            bk, bkey = nextG()
            proj(2 + cc, 160, bk[:, 0:160])
            C.act(sig[:, cc, :], bk[:, 0:160], AF.Sigmoid, w=[bkey, "sig"])
        for cc in range(2):
            bk, bkey = nextG()
            proj(cc, 160, bk[:, 0:160])
            C.tt("dve", ub[:, cc, :], bk[:, 0:160], sig[:, cc, :], ALU.mult, r=["sig"], w=[bkey, "ub"])
        yield
        for cc in range(2):
            bk, bkey = nextG()
            proj(4 + cc, 160, bk[:, 0:160])
            C.cp("act", zdT[:, cc, :], bk[:, 0:160], w=[bkey, "zdT"])
        yield
        for cc in range(2):
            bk, bkey = nextG()
            proj(6 + cc, 128, bk[:, 0:128])
            C.act(usg[:, cc, :], bk[:, 0:128], AF.Gelu_apprx_tanh, w=[bkey, "usg"])
        yield
        for hp in range(2):
            bk, bkey = nextG()
            for kc in range(8):
                C.mm(bk[:, 0:128], wloc[:, kc, 1024 + hp * 128:1024 + (hp + 1) * 128], hTo[:, kc, 32:160],
                     start=(kc == 0), stop=(kc == 7), r=["wloc", "hTo", "hToH"], w=[bkey])
            for kc in range(8):
                C.mm(bk[:, 128:256], wloc[:, kc, 1280 + hp * 128:1280 + (hp + 1) * 128], hTo[:, kc, 32:160],
                     start=(kc == 0), stop=(kc == 7), r=["wloc", "hTo", "hToH"], w=[bkey])
            C.tt("dve", qt1[:], bk[:, 0:128], qCt[:], ALU.mult, r=["qCt"], w=[bkey, "qt1"])
            C.tt("dve", qt2[:], bk[:, 128:256], qSt[:], ALU.mult, r=["qSt"], w=[bkey, "qt2"])
            C.act(QC[0:64, 2 * hp, :], bk[0:64, 0:128], AF.Copy, w=[bkey, "QC"], scale=0.125)
            C.act(QC[0:64, 2 * hp + 1, :], bk[64:128, 0:128], AF.Copy, w=[bkey, "QC"], scale=0.125)
            yield
            C.tt("dve", qsum[:], qt1[:], qt2[:], ALU.add, r=["qt1", "qt2"], w=["qsum"])
            yield
            for v in range(2):
                C.cp("dve", QR[0:64, v, 2 * hp, :], qsum[0:64, :], r=["qsum"], w=["QRq"])
                C.cp("act", QR[0:64, v, 2 * hp + 1, :], qsum[64:128, :], r=["qsum"], w=["QRq"])
            C.cp("pool", QW[0:64, 2 * hp, :], qsum[0:64, :], r=["qsum"], w=["QW"])
            C.cp("act", QW[0:64, 2 * hp + 1, :], qsum[64:128, :], r=["qsum"], w=["QW"])
        yield
        bk, bkey = nextG()
        for kc in range(8):
            C.mm(bk[:, 0:268], hTo[:, kc, 32:160], wloc[:, kc, 1536:1804], start=(kc == 0), stop=(kc == 7),
                 r=["wloc", "hTo", "hToH"], w=[bkey])
        C.act(gts[:], bk[:, 256:268], AF.Sigmoid, w=[bkey, "gts"])
        C.act(vg[:], bk[:, 0:256], AF.Gelu_apprx_tanh, w=[bkey, "vg"])
        C.S.op("dve", lambda e: e.bn_stats(out=bnst[:], in_=vg[:]), ["vg"], ["bnst"])
        C.S.op("dve", lambda e: e.bn_aggr(out=bnag[:], in_=bnst[:]), ["bnst"], ["bnag"])
        C.act(lnr[:, 0:1], bnag[:, 1:2], AF.Ln, r=["bnag", "epsb"], w=["lnr"], bias=epsb[:, 1:2])
        C.act(lnr[:, 1:2], lnr[:, 0:1], AF.Exp, r=["lnr"], w=["lnr1"], scale=-0.5)
        C.ts("dve", vn[:], vg[:], bnag[:, 0:1], lnr[:, 1:2], ALU.subtract, ALU.mult, r=["vg", "bnag", "lnr1"], w=["vn"])
        C.tt("dve", vn[:], vn[:], lng_bc[:], ALU.mult, r=["vn", "lng_bc"], w=["vn"])
        C.tt("dve", vbf[:], vn[:], lnb_bc[:], ALU.add, r=["vn", "lnb_bc"], w=["vbf"])
        yield
        for cc in range(2):
            bk, bkey = nextG()
            for k in range(31):
                C.mm(bk[:, 0:128], Dg[:, cc, k, :], ub[:, cc, 2 + k:2 + k + 128], start=(k == 0), stop=(k == 30),
                     r=["Dg", "ub"], w=[bkey])
            C.act(ycv[:, cc, :], bk[:, 0:128], AF.Identity, r=["convv_t"], w=[bkey, "ycv"], bias=convv_t[:, cc, 0:1])
            C.act(ysq[:, cc, :], bk[:, 0:128], AF.Square, r=["convv_t"], w=[bkey, "ysq"], bias=convv_t[:, cc, 0:1])
        bk, bkey = nextG()
        for src, sk_, c0_ in ((ycv, "ycv", 0), (ysq, "ysq", 128)):
            C.cp("dve", shi[:], src[:], r=[sk_], w=["shi"])
            C.tt("dve", slo[:], src[:], shi[:], ALU.subtract, r=[sk_, "shi"], w=["slo"])
            n_ = 0
            for part, pk_ in ((shi, "shi"), (slo, "slo")):
                for cc in range(2):
                    C.mm(bk[:, c0_:c0_ + 128], ones_b[:], part[:, cc, :], start=(n_ == 0), stop=(n_ == 3),
                         r=["ones_b", pk_], w=[bkey])
                    n_ += 1
        C.act(cmean[:], bk[:, 0:128], AF.Copy, w=[bkey, "cmean"], scale=1.0 / 256)
        C.tt("dve", cmsq[:], cmean[:], cmean[:], ALU.mult, r=["cmean"], w=["cmsq"])
        C.stt(cvar[:], bk[:, 128:256], 1.0 / 256, cmsq[:], ALU.mult, ALU.subtract, r=["cmsq"], w=[bkey, "cvar"])
        C.act(crstd[:], cvar[:], AF.Ln, r=["cvar", "epsb"], w=["crstd"], bias=epsb[:, 1:2])
        C.act(crstd[:], crstd[:], AF.Exp, r=["crstd"], w=["crstd"], scale=-0.5)
        for cc in range(2):
            C.tt("dve", cyn[:, cc, :], ycv[:, cc, :], cmean[:], ALU.subtract, r=["ycv", "cmean"], w=["cyn%d" % cc])
            C.tt("dve", cyn[:, cc, :], cyn[:, cc, :], crstd[:], ALU.mult, r=["cyn%d" % cc, "crstd"], w=["cyn%d" % cc])
            C.act(yT[:, cc, :], cyn[:, cc, :], AF.Silu, r=["cyn%d" % cc, "convv_t"], w=["yT%d" % cc],
                  scale=convv_t[:, cc, 1:2], bias=convv_t[:, cc, 2:3])
        yield
        bk, bkey = nextG()
        for g in range(4):
            q = g // 2
            C.mm(bk[:, g * 128:(g + 1) * 128], vbf[:, q * 128:(q + 1) * 128], wsT[:, g, :], r=["vbf", "wsT"], w=[bkey])
        for g in range(4):
            q = g // 2
            lo = (g % 2) * 64
            C.tt("dve", sgt[lo:lo + 64, q, :], bk[lo:lo + 64, g * 128:(g + 1) * 128], Bs[lo:lo + 64, q, :], ALU.add,
                 r=["Bs"], w=[bkey, "sgt%d" % g])
            C.tt("pool", yT[lo:lo + 64, 4 + q, :], sgt[lo:lo + 64, q, :], usg[lo:lo + 64, q, :], ALU.mult,
                 r=["sgt%d" % g, "usg"], w=["yT%d" % (4 + q)])
        yield
        C.tt("pool", s2[:, :, 1:160], zdT[:, :, 1:160], zdT[:, :, 0:159], ALU.add, r=["zdT"], w=["s2"])
        C.tt("pool", s4[:, :, 3:160], s2[:, :, 3:160], s2[:, :, 1:158], ALU.add, r=["s2"], w=["s4"])
        C.tt("pool", s8[:, :, 7:160], s4[:, :, 7:160], s4[:, :, 3:156], ALU.add, r=["s4"], w=["s8"])
        C.tt("pool", s16[:, :, 15:160], s8[:, :, 15:160], s8[:, :, 7:152], ALU.add, r=["s8"], w=["s16"])
        rc = rc0 if j == 0 else rcg
        srcs = [(s2, "s2", 0, 0), (s4, "s4", 64, 0), (s8, "s8", 0, 1), (s16, "s16", 64, 1)]
        for gi, (sbuf_, sk, lo, q) in enumerate(srcs):
            C.tt("dve", plf[lo:lo + 64, q, :], sbuf_[lo:lo + 64, q, 32:160], rc[lo:lo + 64, q, :], ALU.mult,
                 r=[sk, "rc0", "rcg"], w=["plf%d" % gi])
            C.tt("dve", plb[lo:lo + 64, q, :], plf[lo:lo + 64, q, :], zdT[lo:lo + 64, q, 32:160], ALU.subtract,
                 r=["plf%d" % gi, "zdT"], w=["plb%d" % q])
        bk, bkey = nextG()
        for q in range(2):
            C.mm(bk[:, q * 128:(q + 1) * 128], Wbd[:, q, :], plb[:, q, :], r=["Wbd", "plb%d" % q], w=[bkey])
        for q in range(2):
            C.act(yT[:, 6 + q, :], bk[:, q * 128:(q + 1) * 128], AF.Copy, r=["poolsc_t"], w=[bkey, "yT%d" % (6 + q)],
                  scale=poolsc_t[:, q:q + 1])

    pcount = [0]

    def nextP():
        i = pcount[0] % 4
        pcount[0] += 1
        return Pb[i], "Pb%d" % i

    scount = [0]

    def nextS(nb=2):
        i = scount[0] % nb
        scount[0] += 1
        if i == 2:
            return psC, "psC"
        return psS[i], "psS%d" % i

    def nsa(j):
        L = (32 * j + 30) // 128

        def pipeline(chunks, depth=1):
            state = []
            for i, (A, B, Cc) in enumerate(chunks):
                while len(state) < min(len(chunks), i + depth + 1):
                    state.append(chunks[len(state)][0]())
                st = state[i]
                B(st)
                yield
                Cc(st)
                yield

        st_p = {}

        def mk_cmp(ci):
            def A():
                S_, sk = nextS()
                C.mm(S_[:, :], kcT[:, 32 + ci * 128:32 + (ci + 1) * 128], QC[:].rearrange("p h t -> p (h t)"),
                     r=["kcT", "QC"], w=[sk])
                return S_, sk

            def B(st):
                S_, sk = st
                P_, pk = Pb[ci], "Pb%d" % ci
                C.act(P_[:].rearrange("p h t -> p (h t)"), S_[:, :], AF.Exp, w=[sk, pk])
                if ci == L:
                    C.tt("dve", P_[:], P_[:], cmask[:, j % 4:j % 4 + 1, :].to_broadcast([128, 4, 128]), ALU.mult,
                         r=[pk, "cmask"], w=[pk])
                elif ci == L - 1 and j % 4 == 0:
                    C.tt("dve", P_[:], P_[:], cmask[:, 4:5, :].to_broadcast([128, 4, 128]), ALU.mult,
                         r=[pk, "cmask"], w=[pk])

            def Cc(st):
                pass
            return A, B, Cc

        yield from pipeline([mk_cmp(ci) for ci in range(L + 1)])
        for hp in range(2):
            for ci in range(L + 1):
                for h in (2 * hp, 2 * hp + 1):
                    c0 = (h % 2) * 193
                    C.mm(psC[:, c0:c0 + 193], Pb[ci][:, h, :], rhsC[:, ci, :], start=(ci == 0 and h % 2 == 0), stop=(ci == L),
                         r=["Pb%d" % ci, "rhsC"], w=["psC"], skip=True)
            yield
            for h in (2 * hp, 2 * hp + 1):
                c0 = (h % 2) * 193
                C.ts("dve", den[:, h:h + 1], psC[:, c0 + 192:c0 + 193], 1e-30, None, ALU.max, w=["psC", "den%d" % hp])
            C.recip(rden[:, 2 * hp:2 * hp + 2], den[:, 2 * hp:2 * hp + 2], r=["den%d" % hp], w=["rden%d" % hp])
            yield
            for h in (2 * hp, 2 * hp + 1):
                c0 = (h % 2) * 193
                if h == 0:
                    C.ts("dve", imp[:], psC[:, c0 + 64:c0 + 192], rden[:, 0:1], None, ALU.mult, r=["rden0"],
                         w=["psC", "imp"])
                else:
                    C.stt(imp[:], psC[:, c0 + 64:c0 + 192], rden[:, h:h + 1], imp[:], ALU.mult, ALU.add,
                          r=["rden%d" % hp, "imp"], w=["psC", "imp"])
            C.tt("dve", gsc[:, 2 * hp:2 * hp + 2], rden[:, 2 * hp:2 * hp + 2], gts[:, 6 * hp:6 * hp + 6:3], ALU.mult,
                 r=["rden%d" % hp, "gts"], w=["gscc%d" % hp])
            for h in (2 * hp, 2 * hp + 1):
                c0 = (h % 2) * 193
                C.ts("dve", Ynsa[:, h * 64:(h + 1) * 64], psC[:, c0:c0 + 64], gsc[:, h:h + 1], None, ALU.mult,
                     r=["gscc%d" % hp], w=["psC", "Ynsa"])
            yield
        C.tt("dve", imp2[:], imp[:], selAt[:], ALU.mult, r=["imp", "selAt"], w=["imp2"])
        C.tt("dve", imp2[:], imp2[:], selBt[:], ALU.add, r=["imp2", "selBt"], w=["imp2"])
        yield
        C.S.op("dve", lambda e: e.max(out=top8[:, 0:8], in_=imp2[:]), ["imp2"], ["top8a"])
        C.S.op("dve", lambda e: e.match_replace(out=impw[:], in_to_replace=top8[:, 0:8], in_values=imp2[:],
                                                imm_value=-3e38), ["imp2", "top8a"], ["impw"])
        yield
        C.S.op("dve", lambda e: e.max(out=top8[:, 8:16], in_=impw[:]), ["impw"], ["top8b"])
        C.ts("dve", selt1[:], imp2[:], top8[:, 15:16], None, ALU.is_ge, r=["imp2", "top8b"], w=["selt1"])
        C.ts("dve", selt2[:], imp2[:], -1e29, -NEG, ALU.is_gt, ALU.mult, r=["imp2"], w=["selt2"])
        yield
        C.tt("dve", selt1[:], selt1[:], selt2[:], ALU.mult, r=["selt1", "selt2"], w=["selt1"])
        C.ts("dve", selbf[:], selt1[:], NEG, None, ALU.add, r=["selt1"], w=["selbf"])
        yield

        if j + 1 < nslots:
            yield from local_pre(j + 1)
        wl = list(range(max(0, 4 * j - 4), 4 * j + 4))

        def mk_win(wi, m):
            def A():
                S_, sk = nextS(3)
                C.mm(S_[:, :], Kw[:, m % 12, :], QW[:].rearrange("p h t -> p (h t)"),
                     r=["Kw%d" % (m % 12), "Kw", "QW"], w=[sk])
                return S_, sk

            def B(st):
                S_, sk = st
                P_, pk = nextP()
                C.act(P_[:].rearrange("p h t -> p (h t)"), S_[:, :], AF.Exp, w=[sk, pk])
                C.tt("dve", P_[:], P_[:], wmask[:, 4 + m - 4 * j:5 + m - 4 * j, :].to_broadcast([128, 4, 128]), ALU.mult,
                     r=[pk, "wmask"], w=[pk])
                st_p[("w", wi)] = (P_, pk)

            def Cc(st):
                P_, pk = st_p[("w", wi)]
                for h in range(4):
                    C.mm(psWin[:, h * 65:(h + 1) * 65], P_[:, h, :], Vw[:, m % 12, :], start=(wi == 0 and h == 0),
                         stop=(wi == len(wl) - 1), r=[pk, "Vw%d" % (m % 12), "Vw"], w=["psWin"], skip=True)
            return A, B, Cc

        scount[0] = 0
        yield from pipeline([mk_win(wi, m) for wi, m in enumerate(wl)], depth=2)
        C.tr(psT[:, 512:640], selbf[:], ident[:], r=["selbf", "ident"], w=["psT"])
        yield
        C.cp("dve", QR[64:128, 0, :, :], psT[0:64, 512:640].unsqueeze(1).to_broadcast([64, 4, 128]), w=["psT", "QRb"])
        C.cp("act", QR[64:128, 1, :, :], psT[64:128, 512:640].unsqueeze(1).to_broadcast([64, 4, 128]), w=["psT", "QRb"])
        yield

        nsel = 4 * j + 4

        def mk_sel(m):
            def A():
                S_, sk = nextS(3)
                v = 0 if m < 32 else 1
                C.mm(S_[:, :], Kaug[:, m * 128:(m + 1) * 128], QR[:, v, :, :].rearrange("p h t -> p (h t)"),
                     r=["Kaug%d" % m, "Kaug_ind", "QRq", "QRb"], w=[sk])
                return S_, sk

            def B(st):
                S_, sk = st
                P_, pk = nextP()
                C.act(P_[:].rearrange("p h t -> p (h t)"), S_[:, :], AF.Exp, w=[sk, pk])
                if m >= 4 * j:
                    C.tt("dve", P_[:], P_[:], wmask[:, 4 + m - 4 * j:5 + m - 4 * j, :].to_broadcast([128, 4, 128]), ALU.mult,
                         r=[pk, "wmask"], w=[pk])
                st_p[("s", m)] = (P_, pk)

            def Cc(st):
                P_, pk = st_p[("s", m)]
                for h in range(4):
                    C.mm(psSel[:, h * 65:(h + 1) * 65], P_[:, h, :], Vs[:, m, :], start=(m == 0 and h == 0),
                         stop=(m == nsel - 1), r=[pk, "Vs%d" % m, "Vs"], w=["psSel"], skip=True)
            return A, B, Cc

        yield from pipeline([mk_sel(m) for m in range(nsel)], depth=2)
        scount[0] = 0
        C.cp("dve", den[:, 4:8], psSel[:, 64:260:65], w=["psSel", "den"])
        C.cp("dve", den[:, 8:12], psWin[:, 64:260:65], w=["psWin", "den"])
        C.recip(rden[:, 4:12], den[:, 4:12], r=["den"], w=["rden2"])
        C.tt("dve", gsc[:, 4:8], rden[:, 4:8], gts[:, 1:12:3], ALU.mult, r=["rden2", "gts"], w=["gsc1"])
        C.tt("dve", gsc[:, 8:12], rden[:, 8:12], gts[:, 2:12:3], ALU.mult, r=["rden2", "gts"], w=["gsc1"])
        for h in range(4):
            C.stt(Ynsa[:, h * 64:(h + 1) * 64], psSel[:, h * 65:h * 65 + 64], gsc[:, 4 + h:5 + h], Ynsa[:, h * 64:(h + 1) * 64],
                  ALU.mult, ALU.add, r=["gsc1", "Ynsa"], w=["psSel", "Ynsa"])
        for h in range(4):
            C.stt(Ynb[:, h * 64:(h + 1) * 64], psWin[:, h * 65:h * 65 + 64], gsc[:, 8 + h:9 + h], Ynsa[:, h * 64:(h + 1) * 64],
                  ALU.mult, ALU.add, r=["gsc1", "Ynsa"], w=["psWin", "Ynb"])
        yield
        for q in range(2):
            C.tr(psT[:, 512 + q * 128:512 + (q + 1) * 128], Ynb[:, q * 128:(q + 1) * 128], ident[:], r=["Ynb", "ident"], w=["psT"])
        yield
        C.cp("act", yT[:, 2:4, :], psT[:, 512:768].rearrange("p (q t) -> p q t", q=2), w=["psT", "yT2", "yT3"])
        yield

    def outproj(j):
        yk = ["yT%d" % f for f in range(8)]
        for half in range(2):
            for f in range(8):
                C.mm(psG[:, :], yT[:, f, :], wo[:, f, half * 512:(half + 1) * 512], start=(f == 0), stop=(f == 7),
                     r=yk + ["wo"], w=["psG"])
            yield
            C.cp("dve", yo[:, half * 512:(half + 1) * 512], psG[:, :], w=["psG", "yo"])
            yield
        C.act(junk[:, :], yo[:], AF.Square, r=["yo"], w=["junk", "oss"], accum=oss[:, 0:1])
        yield
        C.act(oss[:, 1:2], oss[:, 0:1], AF.Ln, r=["oss", "epsb"], w=["oss1"], scale=1.0 / D, bias=epsb[:, 0:1])
        C.act(oss[:, 2:3], oss[:, 1:2], AF.Exp, r=["oss1"], w=["oss2"], scale=-0.5)
        C.stt(yo[:], yo[:], oss[:, 2:3], gpost_bc[:], ALU.mult, ALU.mult, r=["yo", "oss2", "gpost_bc"], w=["yo"])
        C.tt("dve", yo[:], yo[:], xo[:], ALU.add, r=["yo", "xo"], w=["yo"])
        C.dma(xm_own[j, :, :], yo[:], r=["yo"])
        C.dma(xm_tail[2 * j:2 * j + 2, :], yo[126:128, :], r=["yo"])
        yield

    def stream_b(j):
        for m in range(4 * j, 4 * j + 4):
            yield from kv_tile(m)
        yield from compress(j)

    def stream_a(j):
        yield from local(j)
        yield from nsa(j)
        yield from outproj(j)

    C.S.wait_cc()
    for _ in stream_b(0):
        pass
    for _ in local_pre(0):
        pass
    for j in range(nslots):
        A_ = stream_a(j)
        B_ = stream_b(j + 1) if j + 1 < nslots else None
        doneA = doneB = B_ is None and False
        doneB = B_ is None
        while not doneA:
            try:
                next(A_)
            except StopIteration:
                doneA = True
            if not doneB:
                try:
                    next(B_)
                except StopIteration:
                    doneB = True
        while not doneB:
            try:
                next(B_)
            except StopIteration:
                doneB = True
    C.end_phase()


def ffn_phase(C, T, l):
    C.begin_phase("_f%d" % l)
    nc = C.nc
    xm_own = T["xm_own"]; tail_g = T["tail_g_%d" % l]; out_tile = T["ffn_out_tile"]
    w_up, w_dn, g2, gpost, cwd = (T[k + "_%d" % l] for k in ("w_up", "w_dn", "g2", "gpost2", "cwd"))
    identd = T["identd"]; selm2d = T["selm2"]

    sb = C.sb
    wu = sb("wu", [128, 8, 2 * FF], BF16)
    wd = sb("wd", [128, 22, D], BF16)
    stage = [sb("stage%d" % i, [128, 1024]) for i in range(3)]
    g2_t = sb("g2_t", [128, 8])
    gpost_bc = sb("gpost_bc", [128, D])
    cw = sb("cw", [128, NFC, 4])
    ident = sb("ident", [128, 128], BF16)
    epsb = sb("epsb", [128, 1])
    xs = [sb("xs%d" % i, [128, D]) for i in range(4)]
    xh = sb("xh", [8, D])
    selm2 = sb("selm2", [8, 2], BF16)
    hn = sb("hn", [128, D], BF16)
    hnh = sb("hnh", [8, D], BF16)
    junk = sb("junk", [128, D], BF16)
    ssq = sb("ssq", [128, 4]); rstd = sb("rstd", [128, 4])
    h2Tb = [sb("h2T%d" % i, [128, 8, 2, 130], BF16) for i in range(2)]
    actT = sb("actT", [128, 22, 2, 128], BF16)
    gc = [sb("gc%d" % i, [128, 2, 128]) for i in range(2)]
    uc = [sb("uc%d" % i, [128, 2, 128]) for i in range(2)]
    gg = [sb("gg%d" % i, [128, 2, 128]) for i in range(2)]
    yo = sb("yo", [128, D])
    oss = sb("oss", [128, 4])

    psU = [C.ps("psU%d" % i, [128, 512]) for i in range(4)]
    psD = [C.ps("psD%d" % i, [128, 512]) for i in range(2)]
    psT = C.ps("psT", [128, 1024], BF16)
    psH = C.ps("psH", [128, 512])

    C.dma(stage[1][0:8, 0:2], selm2d[:, :], w=["stage1"])
    C.cp("dve", selm2[:], stage[1][0:8, 0:2], r=["stage1"], w=["selm2"])
    C.dma(g2_t[:], g2[:, :], w=["gscale"])
    C.dma(gpost_bc[:], gpost.partition_broadcast(128), w=["gpost_bc"])
    C.dma(cw[:], cwd[:, :, :], w=["cw"])
    C.dma(stage[0][:, 0:128], identd[:, :], w=["stage0"])
    C.cp("dve", ident[:], stage[0][:, 0:128], r=["stage0"], w=["ident"])
    C.memset("dve", epsb[:, 0:1], RMS_EPS, w=["epsb"])
    par = 0
    for kc in range(8):
        for q in range(6):
            c0 = q * 1024
            c1 = min(2 * FF, c0 + 1024)
            s = stage[par % 3]; sk = "stage%d" % (par % 3); par += 1
            load_convert(C, wu[:, kc, c0:c1], w_up[kc * 128:(kc + 1) * 128, c0:c1], s, sk, "wu", c1 - c0,
                         scale_ap=g2_t[:, kc:kc + 1])
    for f in range(22):
        s = stage[par % 3]; sk = "stage%d" % (par % 3); par += 1
        load_convert(C, wd[:, f, :], w_dn[f * 128:(f + 1) * 128, :], s, sk, "wd", D, parity=f)

    def rms_to_bf(x_ap, xk, out_bf, ok, np_, col):
        C.act(junk[0:np_, :], x_ap, AF.Square, r=[xk], w=["junk", "ssq%d" % col], accum=ssq[0:np_, col:col + 1])
        C.act(rstd[0:np_, col:col + 1], ssq[0:np_, col:col + 1], AF.Ln, r=["ssq%d" % col, "epsb"], w=["rstd%d" % col],
              scale=1.0 / D, bias=epsb[0:np_, 0:1])
        C.act(rstd[0:np_, col:col + 1], rstd[0:np_, col:col + 1], AF.Exp, r=["rstd%d" % col], w=["rstd%d" % col], scale=-0.5)
        C.act(out_bf, x_ap, AF.Copy, r=[xk, "rstd%d" % col], w=[ok], scale=rstd[0:np_, col:col + 1])

    C.S.wait_cc()
    ucount = [0]
    def prologue(grp):
        gp = grp % 2
        h2T = h2Tb[gp]
        hk = "h2T%d" % gp
        for sl in range(2):
            j = grp * 2 + sl
            xb, xk = xs[gp * 2 + sl], "xs%d" % (gp * 2 + sl)
            C.dma(xb[:], xm_own[j, :, :], w=[xk])
            for cand in range(4):
                m_ = 4 * j - 1 + cand
                if m_ < 0:
                    C.memset("pool", xh[0:2, :], 0.0, w=["xh"])
                else:
                    row0 = ((m_ % 4) * NS + m_ // 4) * 2
                    C.dma(xh[cand * 2:(cand + 1) * 2, :], tail_g[row0:row0 + 2, :], w=["xh"])
            rms_to_bf(xb[:], xk, hn[:], "hn", 128, 0)
            rms_to_bf(xh[:], "xh", hnh[:], "hnh", 8, 1)
            for kc in range(8):
                C.tr(psT[:, kc * 128:(kc + 1) * 128], hn[:, kc * 128:(kc + 1) * 128], ident[:], r=["hn", "ident"], w=["psT"])
            C.cp("act", h2T[:, :, sl, 2:130], psT[:, :].rearrange("p (k t) -> p k t", k=8), w=["psT", hk])
            for kc in range(8):
                C.mm(psH[:, kc * 2:(kc + 1) * 2], hnh[:, kc * 128:(kc + 1) * 128], selm2[:], r=["hnh", "selm2"], w=["psH"])
            C.cp("act", h2T[:, :, sl, 0:2], psH[:, 0:16].rearrange("p (k t) -> p k t", k=8), w=["psH", hk])

    prologue(0)
    for grp in range(NS // 2):
        gp = grp % 2
        h2T = h2Tb[gp]
        hk = "h2T%d" % gp
        def up_mm(fc):
            banks = []
            for ch in (fc, 22 + fc):
                bi = ucount[0] % 4
                ucount[0] += 1
                bank = psU[bi]
                bk = "psU%d" % bi
                for kc in range(8):
                    C.mm(bank[:, 0:260], wu[:, kc, ch * 128:(ch + 1) * 128],
                         h2T[:, kc, :, :].rearrange("p s t -> p (s t)"),
                         start=(kc == 0), stop=(kc == 7), r=["wu", hk], w=[bk])
                banks.append((bank, bk))
            return banks

        def conv_ops(fc, banks):
            p = fc % 2
            for (bank, bk), ch, dst, dk in ((banks[0], fc, gc[p], "gc%d" % p), (banks[1], 22 + fc, uc[p], "uc%d" % p)):
                bv = bank[:, 0:260].rearrange("p (s t) -> p s t", s=2)
                C.act(dst[:], bv[:, :, 2:130], AF.Identity, r=["cw"], w=[bk, dk], scale=cw[:, ch, 2:3], bias=cw[:, ch, 3:4])
                C.stt(dst[:], bv[:, :, 1:129], cw[:, ch, 1:2], dst[:], ALU.mult, ALU.add, r=["cw", dk], w=[bk, dk])
                C.stt(dst[:], bv[:, :, 0:128], cw[:, ch, 0:1], dst[:], ALU.mult, ALU.add, r=["cw", dk], w=[bk, dk])

        def gate_ops(fc):
            p = fc % 2
            C.act(gg[p][:], gc[p][:], AF.Gelu_apprx_tanh, r=["gc%d" % p], w=["gg%d" % p])
            C.tt("pool", actT[:, fc, :, :], gg[p][:], uc[p][:], ALU.mult,
                 r=["gg%d" % p, "uc%d" % p], w=["actT"])

        nxt = up_mm(0)
        for fc in range(22):
            cur = nxt
            if fc + 1 < 22:
                nxt = up_mm(fc + 1)
            conv_ops(fc, cur)
            if fc >= 1:
                gate_ops(fc - 1)
            if fc == 12 and grp >= 1 and T.get("post_group") is not None:
                T["post_group"](grp - 1)
            if fc == 3 and grp + 1 < NS // 2:
                prologue(grp + 1)
        gate_ops(21)
        for sl in range(2):
            j = grp * 2 + sl
            for half in range(2):
                for f in range(22):
                    C.mm(psD[half][:, :], actT[:, f, sl, :], wd[:, f, half * 512:(half + 1) * 512],
                         start=(f == 0), stop=(f == 21), r=["actT", "wd"], w=["psD%d" % half])
                C.cp("dve", yo[:, half * 512:(half + 1) * 512], psD[half][:, :], w=["psD%d" % half, "yo"])
            C.act(junk[:, :], yo[:], AF.Square, r=["yo"], w=["junk", "oss"], accum=oss[:, 0:1])
            C.act(oss[:, 1:2], oss[:, 0:1], AF.Ln, r=["oss", "epsb"], w=["oss1"], scale=1.0 / D, bias=epsb[:, 0:1])
            C.act(oss[:, 2:3], oss[:, 1:2], AF.Exp, r=["oss1"], w=["oss2"], scale=-0.5)
            C.stt(yo[:], yo[:], oss[:, 2:3], gpost_bc[:], ALU.mult, ALU.mult, r=["yo", "oss2", "gpost_bc"], w=["yo"])
            C.tt("dve", yo[:], yo[:], xs[gp * 2 + sl][:], ALU.add, r=["yo", "xs%d" % (gp * 2 + sl)], w=["yo"])
            C.dma(out_tile(j)[:, :], yo[:], r=["yo"], w=["ffnout%d" % j])
    if T.get("post_group") is not None:
        T["post_group"](NS // 2 - 1)
        C.end_phase(wait_cc=False)
        return
    C.end_phase()


OFF_Q, OFF_KV, OFF_G, OFF_C, OFF_D = 512, 768, 1152, 1164, 1676


def _consts():
    c = {}
    half = 32
    inv = (10000.0 ** (-np.arange(half, dtype=np.float32) * 2.0 / 64)).astype(np.float32)
    ang = np.arange(SEQ, dtype=np.float32)[:, None] * inv[None, :]
    cos = np.cos(ang).astype(np.float32).T
    sin = np.sin(ang).astype(np.float32).T
    c64 = np.concatenate([cos, cos], 0)
    s64 = np.concatenate([-sin, sin], 0)
    c["ropeC"] = np.ascontiguousarray(np.concatenate([c64, c64], 0))
    c["ropeS"] = np.ascontiguousarray(np.concatenate([s64, s64], 0))
    blk = np.arange(SEQ) // 64
    c["indp"] = (np.arange(64)[:, None] == (blk % 64)[None, :]).astype(np.float32)
    c["identd"] = np.eye(128, dtype=np.float32)
    n = np.arange(512)[:, None]
    b = np.arange(128)[None, :]
    off = n - 4 * b
    M = np.where((off == -1) | (off == 3), 1.0, np.where((off >= 0) & (off <= 2), 2.0, 0.0)).astype(np.float32)
    M[511, :] = 0.0
    c["impM"] = np.ascontiguousarray(M.reshape(4, 128, 128))
    c["trild"] = (np.arange(128)[:, None] <= np.arange(128)[None, :]).astype(np.float32)
    rg = np.zeros((128, 2, 128), np.float32)
    for gi, w in enumerate((2, 4, 8, 16)):
        rg[(gi % 2) * 64:(gi % 2) * 64 + 64, gi // 2, :] = 1.0 / w
    c["rcntg"] = rg
    return c


def _core_consts(cidx):
    c = cidx
    o = {}
    k = np.arange(128)[:, None]
    t = np.arange(128)[None, :]
    wm = np.zeros((8, 128, 128), np.float32)
    for q in range(8):
        mm = q - 4
        rel = mm - c
        if rel == 0:
            wm[q] = (k <= t)
        elif rel == -4:
            wm[q] = (k > t)
        elif -4 < rel < 0:
            wm[q] = 1.0
    o["wmaskd"] = wm
    cm = np.zeros((5, 128, 128), np.float32)
    for jm in range(4):
        nprime = k - 32 * jm
        cm[jm] = (16 * nprime + 31 <= 128 * c + t)
    cm[4] = 1.0
    if c == 0:
        cm[4][127, :15] = 0.0
    o["cmaskd"] = cm
    A = np.zeros((NS, 128, 128), np.float32)
    B = np.zeros((NS, 128, 128), np.float32)
    tt = np.arange(128)[:, None]
    bb = np.arange(128)[None, :]
    for j in range(NS):
        i = 4 * j + c
        cur = 2 * i + (tt >= 64)
        valid = bb <= cur
        forced = (bb == 0) | (bb == cur) | (bb == cur - 1)
        A[j] = (valid & ~forced)
        B[j] = np.where(valid, np.where(forced, 1e6, 0.0), -1e30)
    o["selA"] = A
    o["selB"] = B
    inv = (10000.0 ** (-np.arange(32, dtype=np.float32) * 2.0 / 64)).astype(np.float32)
    qC = np.zeros((NS, 128, 128), np.float32)
    qS = np.zeros((NS, 128, 128), np.float32)
    for j in range(NS):
        pos = (128 * (4 * j + c) + np.arange(128)).astype(np.float32)
        ang = pos[:, None] * inv[None, :]
        cs = np.cos(ang).astype(np.float32).T * np.float32(0.125)
        sn = np.sin(ang).astype(np.float32).T * np.float32(0.125)
        c64 = np.concatenate([cs, cs], 0)
        s64 = np.concatenate([-sn, sn], 0)
        qC[j] = np.concatenate([c64, c64], 0)
        qS[j] = np.concatenate([s64, s64], 0)
    o["qC"] = qC
    o["qS"] = qS
    r0 = np.zeros((128, 2, 128), np.float32)
    for gi, w in enumerate((2, 4, 8, 16)):
        pos = 128 * c + np.arange(128)
        cnt = np.minimum(pos + 1, w).astype(np.float32)
        r0[(gi % 2) * 64:(gi % 2) * 64 + 64, gi // 2, :] = (1.0 / cnt)[None, :]
    o["rcnt0"] = r0
    sm = np.zeros((128, 32), np.float32)
    sm[c * 32:(c + 1) * 32, :] = np.eye(32, dtype=np.float32)
    o["selm"] = sm
    sm2 = np.zeros((8, 2), np.float32)
    sm2[c * 2:(c + 1) * 2, :] = np.eye(2, dtype=np.float32)
    o["selm2"] = sm2
    return o


def _sw(idx):
    return np.concatenate([idx[32:], idx[:32]])


def _mixer_weights(P, l):
    w_in = P["w_in"][l]
    kv = lambda s: np.arange(OFF_KV + 64 * s, OFF_KV + 64 * s + 64)
    cols_kv = np.concatenate([kv(0), kv(1), kv(2), kv(4), _sw(kv(2)), _sw(kv(4)), kv(3), kv(5)])
    qcols = np.arange(OFF_Q, OFF_Q + 256)
    qsw = np.concatenate([_sw(qcols[h * 64:(h + 1) * 64]) for h in range(4)])
    cols_loc = np.concatenate([np.arange(0, 512), np.arange(OFF_D, OFF_D + 256), np.arange(OFF_C, OFF_C + 256),
                               qcols, qsw, np.arange(OFF_C + 256, OFF_C + 512), np.arange(OFF_G, OFF_G + 12)])
    o = {}
    o["w_kv"] = np.ascontiguousarray(w_in[:, cols_kv])
    o["w_loc"] = np.ascontiguousarray(w_in[:, cols_loc])
    o["w_out"] = np.ascontiguousarray(P["w_out"][l])
    o["gpre"] = np.ascontiguousarray(P["norm_mix_pre"][l].reshape(8, 128).T)
    o["gpost"] = np.ascontiguousarray(P["norm_mix_post"][l].reshape(1, D))
    o["convw"] = np.ascontiguousarray(P["conv_dw_w"][l].reshape(31, 2, 128).transpose(2, 1, 0))
    cv = np.stack([P["conv_dw_b"][l], P["conv_ln_g"][l], P["conv_ln_b"][l]], -1)
    o["convv"] = np.ascontiguousarray(cv.reshape(2, 128, 3).transpose(1, 0, 2))
    w1k = P["nsa_ck_w1"][l].reshape(32, 64, 64).transpose(1, 0, 2).reshape(64, 2048)
    w1v = P["nsa_cv_w1"][l].reshape(32, 64, 64).transpose(1, 0, 2).reshape(64, 2048)
    o["w1d"] = np.ascontiguousarray(np.concatenate([w1k, w1v], 0))
    o["w2d"] = np.ascontiguousarray(np.concatenate([P["nsa_ck_w2"][l], P["nsa_cv_w2"][l]], 1))
    o["ped"] = np.ascontiguousarray(np.concatenate([P["nsa_pe_k"][l].T, P["nsa_pe_v"][l].T], 0))
    o["sgv"] = np.ascontiguousarray(np.stack([P["sgu_ln_g"][l], P["sgu_ln_b"][l]], 0))
    o["sgw"] = np.ascontiguousarray(P["sgu_w"][l].transpose(2, 0, 1))
    o["sgb"] = np.ascontiguousarray(P["sgu_b"][l])
    pw = np.zeros((128, 2, 128), np.float32)
    for gi in range(4):
        lo = (gi % 2) * 64
        pw[lo:lo + 64, gi // 2, lo:lo + 64] = P["pool_w"][l][gi]
    o["poolw"] = pw
    o["poolsc"] = np.ascontiguousarray(P["pool_scale"][l].reshape(2, 128).T)
    return o


def _ffn_weights(P, l):
    o = {}
    o["w_up"] = np.ascontiguousarray(P["ffn_up"][l])
    o["w_dn"] = np.ascontiguousarray(P["ffn_down"][l])
    o["g2"] = np.ascontiguousarray(P["norm_ffn_pre"][l].reshape(8, 128).T)
    o["gpost2"] = np.ascontiguousarray(P["norm_ffn_post"][l].reshape(1, D))
    cw = np.concatenate([P["ffn_conv_w"][l], P["ffn_conv_b"][l][None, :]], 0)
    o["cwd"] = np.ascontiguousarray(cw.reshape(4, NFC, 128).transpose(2, 1, 0))
    return o


def _own_tiles(xb, c, halo):
    pad = np.concatenate([np.zeros((halo, D), np.float32), xb], 0)
    out = np.empty((NS, halo + 128, D), np.float32)
    for j in range(NS):
        i = 4 * j + c
        out[j] = pad[128 * i:128 * i + 128 + halo]
    return out


def _scatter_own(res_list, key):
    x = np.empty((NB, SEQ, D), np.float32)
    for core in range(8):
        b, c = divmod(core, 4)
        r = res_list[core][key]
        for j in range(NS):
            i = 4 * j + c
            x[b, 128 * i:128 * (i + 1)] = r[j]
    return x


def _chunk_row(m):
    r, sl = m % 4, m // 4
    return sl // 2, r * 256 + (sl % 2) * 128


def build_all(nslots=NS):
    C = Ctx()
    T = {}
    xg0 = C.dram_in("xg0", [8 * 1024, D])
    xown0 = C.dram_in("xown0", [NS, 128, D])
    for k, shp in MIX_C_SHAPES.items():
        T[k] = C.dram_in(k, shp)
    for l in range(2):
        for k, shp in MIX_W_SHAPES.items():
            T["%s_%d" % (k, l)] = C.dram_in("%s_%d" % (k, l), shp)
        for k, shp in FFN_W_SHAPES.items():
            T["%s_%d" % (k, l)] = C.dram_in("%s_%d" % (k, l), shp)
    out = C.dram_out("out", [NS, 128, D])
    xm_own = C.dram_int("xm_own", [NS, 128, D])
    tails = [C.dram_int("xm_tail%d" % l, [NS * 2, D]) for l in range(2)]
    tail_g = [C.dram_int("tail_g%d" % l, [4 * NS * 2, D]) for l in range(2)]
    x1_own = [C.dram_int("x1_own%d" % g, [256, D]) for g in range(8)]
    x1_g = [C.dram_int("x1_g%d" % g, [1024, D]) for g in range(8)]
    RG = [[0, 1, 2, 3], [4, 5, 6, 7]]

    def gather(src, dst):
        C.S.collective_nowait(lambda e: e.collective_compute("AllGather", ALU.bypass, replica_groups=RG,
                                                      ins=[src.opt()], outs=[dst.opt()]))

    def xg_tile0(m):
        g, ro = _chunk_row(m)
        return xg0[g * 1024 + ro:g * 1024 + ro + 128, :]

    def xg_tile1(m):
        g, ro = _chunk_row(m)
        return x1_g[g][ro:ro + 128, :]

    for l in range(2):
        T["xg_tile"] = xg_tile0 if l == 0 else xg_tile1
        T["xown_tile"] = (lambda j: xown0[j, :, :]) if l == 0 else (lambda j: x1_own[j // 2][(j % 2) * 128:(j % 2) * 128 + 128, :])
        T["xm_own"] = xm_own
        T["xm_tail"] = tails[l]
        mixer_phase(C, T, l, nslots=nslots)
        gather(tails[l], tail_g[l])
        T["tail_g_%d" % l] = tail_g[l]
        T["ffn_out_tile"] = (lambda j: x1_own[j // 2][(j % 2) * 128:(j % 2) * 128 + 128, :]) if l == 0 else (lambda j: out[j, :, :])
        if l == 0:
            def post_group(g):
                C.S.cc_op(lambda e: e.collective_compute("AllGather", ALU.bypass, replica_groups=RG,
                                                         ins=[x1_own[g].opt()], outs=[x1_g[g].opt()]),
                          reads=["ffnout%d" % (2 * g), "ffnout%d" % (2 * g + 1)], writes=["x1g%d" % g])
            T["post_group"] = post_group
        else:
            T["post_group"] = None
        ffn_phase(C, T, l)
    info = C.finish()
    return C.nc, info


_CACHE = {}


def kernel(**inputs):
    P = {k: np.asarray(v, dtype=np.float32) for k, v in inputs.items()}
    x = P["x"]
    if "nc" not in _CACHE:
        _CACHE["nc"] = build_all()[0]
    nc = _CACHE["nc"]
    consts = _consts()
    shared = {k: consts[k] for k in MIX_C_SHAPES if k in consts}
    for l in range(2):
        for k, v in _mixer_weights(P, l).items():
            shared["%s_%d" % (k, l)] = v
        for k, v in _ffn_weights(P, l).items():
            shared["%s_%d" % (k, l)] = v
    in_maps = []
    for core in range(8):
        b, c = divmod(core, 4)
        m = dict(shared)
        m.update(_core_consts(c))
        xt = x[b].reshape(8, 2, 4, 128, D)
        m["xg0"] = np.ascontiguousarray(xt.transpose(0, 2, 1, 3, 4).reshape(8 * 1024, D))
        m["xown0"] = np.ascontiguousarray(x[b].reshape(NS, 4, 128, D)[:, c])
        in_maps.append(m)
    res = run_bass_kernel_spmd(nc, in_maps, core_ids=list(range(8)))
    return _scatter_own(res.results, "out").astype(np.float32)
```
